# Optimizing a Trainium2 kernel written in Bass

```python
import math
import jax
import jax.numpy as jnp
from jax import lax
import numpy as np

D_MODEL = 1024
BATCH = 2
SEQ = 8192
DEPTH = 2


HEAD_DIM = 64
N_HEADS = D_MODEL // HEAD_DIM
MEM_LEN = 256
MEM_HEADS = 4
SB_HEADS = (N_HEADS - MEM_HEADS) // 2
DIFF_HEADS = (N_HEADS - MEM_HEADS) // 2
DIFF_QK_DIM = HEAD_DIM // 2
DIFF_V_DIM = 2 * DIFF_QK_DIM
SWA_Q_HEADS = N_HEADS - MEM_HEADS
SWA_KV_HEADS = SWA_Q_HEADS // 4
SWA_GROUP = SWA_Q_HEADS // SWA_KV_HEADS
WINDOW = 128
BLOCK = 128
ROPE_THETA = 500000.0
ROPE_FRACTION = 4
MIX_WIDTH = N_HEADS * HEAD_DIM
DEEPNORM_ALPHA = (2 * DEPTH) ** 0.25
DEEPNORM_BETA = (8 * DEPTH) ** -0.25
LN_EPS = 1e-5
NEG_BIG = -1e30
N_EVEN = (DEPTH + 1) // 2
N_ODD = DEPTH // 2

EVEN_WIDTHS = (SB_HEADS * HEAD_DIM, SB_HEADS * HEAD_DIM, SB_HEADS * HEAD_DIM,
               DIFF_HEADS * 2 * DIFF_QK_DIM, DIFF_HEADS * 2 * DIFF_QK_DIM, DIFF_HEADS * DIFF_V_DIM,
               MEM_HEADS * HEAD_DIM, MIX_WIDTH)
EVEN_VALUE_SLOTS = (2, 5)
ODD_WIDTHS = (SWA_Q_HEADS * HEAD_DIM, SWA_KV_HEADS * HEAD_DIM, SWA_KV_HEADS * HEAD_DIM,
              MEM_HEADS * HEAD_DIM, MIX_WIDTH)
ODD_VALUE_SLOTS = (2,)
EVEN_IN = sum(EVEN_WIDTHS)
ODD_IN = sum(ODD_WIDTHS)

kernel_name = 'hybrid_stickbreak_diff_swa_memory_deepnorm'


def _split(h, widths):
    outs = []
    start = 0
    for w in widths:
        outs.append(h[..., start:start + w])
        start += w
    return outs


def _layer_norm(x, g, b):
    xf = x.astype(jnp.float32)
    mu = jnp.mean(xf, axis=-1, keepdims=True)
    var = jnp.mean(jnp.square(xf - mu), axis=-1, keepdims=True)
    y = (xf - mu) * lax.rsqrt(var + LN_EPS) * g.astype(jnp.float32) + b.astype(jnp.float32)
    return y.astype(x.dtype)


def _rms_norm(x, g):
    xf = x.astype(jnp.float32)
    y = xf * lax.rsqrt(jnp.mean(jnp.square(xf), axis=-1, keepdims=True) + LN_EPS) * g.astype(jnp.float32)
    return y.astype(x.dtype)


def _partial_rope(x, positions):
    rot = x.shape[-1] // ROPE_FRACTION
    half = rot // 2
    inv_freq = jnp.exp(-(jnp.arange(half, dtype=jnp.float32) / half) * math.log(ROPE_THETA))
    ang = positions.astype(jnp.float32)[:, :, None] * inv_freq[None, None, :]
    new_shape = (ang.shape[0], ang.shape[1]) + (1,) * (x.ndim - 3) + (half,)
    ang = ang.reshape(new_shape)
    cos = jnp.cos(ang).astype(x.dtype)
    sin = jnp.sin(ang).astype(x.dtype)
    x1 = x[..., :half]
    x2 = x[..., half:rot]
    rest = x[..., rot:]
    return jnp.concatenate([x1 * cos - x2 * sin, x2 * cos + x1 * sin, rest], axis=-1)


def _block_outputs(out):
    out = jnp.moveaxis(out, 0, 1)
    return out.reshape((out.shape[0], out.shape[1] * out.shape[2]) + out.shape[3:])


def _stick_breaking_attention(q, k, v):
    s_len, d = q.shape[1], q.shape[-1]
    nb = s_len // BLOCK
    key_idx = jnp.arange(s_len, dtype=jnp.int32)
    scale = d ** -0.5

    def one_block(bi):
        qi = lax.dynamic_slice_in_dim(q, bi * BLOCK, BLOCK, axis=1)
        z = jnp.einsum('bqhd,bkhd->bhqk', qi, k, preferred_element_type=jnp.float32) * scale
        q_idx = bi * BLOCK + jnp.arange(BLOCK, dtype=jnp.int32)
        strict = key_idx[None, :] < q_idx[:, None]
        log_1mb = jnp.where(strict, jax.nn.log_sigmoid(-z), 0.0)
        rc = jnp.flip(jnp.cumsum(jnp.flip(log_1mb, axis=-1), axis=-1), axis=-1)
        after = rc - log_1mb
        w = jnp.where(strict, jnp.exp(jax.nn.log_sigmoid(z) + after), 0.0)
        return jnp.einsum('bhqk,bkhd->bqhd', w.astype(v.dtype), v)

    out = lax.map(one_block, jnp.arange(nb, dtype=jnp.int32))
    return _block_outputs(out)


def _differential_attention(q, k, v, lam):
    s_len, d = q.shape[1], q.shape[-1]
    nb = s_len // BLOCK
    key_idx = jnp.arange(s_len, dtype=jnp.int32)
    scale = d ** -0.5

    def one_block(bi):
        qi = lax.dynamic_slice_in_dim(q, bi * BLOCK, BLOCK, axis=1)
        s = jnp.einsum('bqhcd,bkhcd->bhcqk', qi, k, preferred_element_type=jnp.float32) * scale
        q_idx = bi * BLOCK + jnp.arange(BLOCK, dtype=jnp.int32)
        causal = key_idx[None, :] <= q_idx[:, None]
        p = jax.nn.softmax(jnp.where(causal, s, NEG_BIG), axis=-1)
        a = p[:, :, 0] - lam * p[:, :, 1]
        return jnp.einsum('bhqk,bkhd->bqhd', a.astype(v.dtype), v)

    out = lax.map(one_block, jnp.arange(nb, dtype=jnp.int32))
    return _block_outputs(out)


def _sliding_window_attention(q, k, v, sinks):
    b, s_len = q.shape[0], q.shape[1]
    d = q.shape[-1]
    nb = s_len // BLOCK
    scale = d ** -0.5
    pad = jnp.zeros((b, BLOCK) + k.shape[2:], k.dtype)
    kp = jnp.concatenate([pad, k], axis=1)
    vp = jnp.concatenate([pad.astype(v.dtype), v], axis=1)
    sink = sinks.astype(jnp.float32).reshape(SWA_KV_HEADS, SWA_GROUP)[None, :, :, None, None]
    offs_q = jnp.arange(BLOCK, dtype=jnp.int32)
    offs_k = jnp.arange(2 * BLOCK, dtype=jnp.int32) - BLOCK
    rel = offs_k[None, :] - offs_q[:, None]

    def one_block(bi):
        qi = lax.dynamic_slice_in_dim(q, bi * BLOCK, BLOCK, axis=1)
        kb = lax.dynamic_slice_in_dim(kp, bi * BLOCK, 2 * BLOCK, axis=1)
        vb = lax.dynamic_slice_in_dim(vp, bi * BLOCK, 2 * BLOCK, axis=1)
        s = jnp.einsum('bqhgd,bkhd->bhgqk', qi, kb, preferred_element_type=jnp.float32) * scale
        k_pos = bi * BLOCK + offs_k
        band = (rel <= 0) & (rel > -WINDOW) & (k_pos[None, :] >= 0)
        s = jnp.where(band, s, NEG_BIG)
        m = jnp.maximum(jnp.max(s, axis=-1, keepdims=True), sink)
        p = jnp.exp(s - m)
        w = p / (jnp.sum(p, axis=-1, keepdims=True) + jnp.exp(sink - m))
        return jnp.einsum('bhgqk,bkhd->bqhgd', w.astype(vb.dtype), vb)

    out = lax.map(one_block, jnp.arange(nb, dtype=jnp.int32))
    return _block_outputs(out)


def _memory_attention(q, mem, w_memkv):
    b, s_len = q.shape[0], q.shape[1]
    m_len = mem.shape[1]
    q = q.reshape(b, s_len, MEM_HEADS, HEAD_DIM)
    kv = mem @ w_memkv
    mk = kv[..., :MEM_HEADS * HEAD_DIM].reshape(b, m_len, MEM_HEADS, HEAD_DIM)
    mv = kv[..., MEM_HEADS * HEAD_DIM:].reshape(b, m_len, MEM_HEADS, HEAD_DIM)
    s = jnp.einsum('bshd,bmhd->bhsm', q, mk, preferred_element_type=jnp.float32) * (HEAD_DIM ** -0.5)
    p = jax.nn.softmax(s, axis=-1)
    return jnp.einsum('bhsm,bmhd->bshd', p.astype(mv.dtype), mv)


def _even_layer(x, mem, positions, w_in, w_memkv, diff_lambda, diff_subln, w_out, ln_g, ln_b, layer_idx):
    b, s_len = x.shape[0], x.shape[1]
    h = x @ w_in
    sb_q, sb_k, sb_v, df_q, df_k, df_v, m_q, gate = _split(h, EVEN_WIDTHS)
    sb_o = _stick_breaking_attention(sb_q.reshape(b, s_len, SB_HEADS, HEAD_DIM),
                                     sb_k.reshape(b, s_len, SB_HEADS, HEAD_DIM),
                                     sb_v.reshape(b, s_len, SB_HEADS, HEAD_DIM))
    lambda_init = 0.8 - 0.6 * math.exp(-0.3 * layer_idx)
    dl = diff_lambda.astype(jnp.float32)
    lam = jnp.exp(jnp.sum(dl[0] * dl[1])) - jnp.exp(jnp.sum(dl[2] * dl[3])) + lambda_init
    df_q = _partial_rope(df_q.reshape(b, s_len, DIFF_HEADS, 2, DIFF_QK_DIM), positions)
    df_k = _partial_rope(df_k.reshape(b, s_len, DIFF_HEADS, 2, DIFF_QK_DIM), positions)
    df_o = _differential_attention(df_q, df_k, df_v.reshape(b, s_len, DIFF_HEADS, DIFF_V_DIM), lam)
    df_o = _rms_norm(df_o, diff_subln) * (1.0 - lambda_init)
    mem_o = _memory_attention(m_q, mem, w_memkv)
    mixed = jnp.concatenate([sb_o.reshape(b, s_len, SB_HEADS * HEAD_DIM),
                             df_o.reshape(b, s_len, DIFF_HEADS * DIFF_V_DIM),
                             mem_o.reshape(b, s_len, MEM_HEADS * HEAD_DIM)], axis=-1)
    y = (mixed * jax.nn.silu(gate)) @ w_out
    return _layer_norm(DEEPNORM_ALPHA * x + y, ln_g, ln_b)


def _odd_layer(x, mem, positions, w_in, w_memkv, sinks, w_out, ln_g, ln_b):
    b, s_len = x.shape[0], x.shape[1]
    h = x @ w_in
    c_q, c_k, c_v, m_q, gate = _split(h, ODD_WIDTHS)
    c_q = _partial_rope(c_q.reshape(b, s_len, SWA_KV_HEADS, SWA_GROUP, HEAD_DIM), positions)
    c_k = _partial_rope(c_k.reshape(b, s_len, SWA_KV_HEADS, HEAD_DIM), positions)
    c_o = _sliding_window_attention(c_q, c_k, c_v.reshape(b, s_len, SWA_KV_HEADS, HEAD_DIM), sinks)
    mem_o = _memory_attention(m_q, mem, w_memkv)
    mixed = jnp.concatenate([c_o.reshape(b, s_len, SWA_Q_HEADS * HEAD_DIM),
                             mem_o.reshape(b, s_len, MEM_HEADS * HEAD_DIM)], axis=-1)
    y = (mixed * jax.nn.silu(gate)) @ w_out
    return _layer_norm(DEEPNORM_ALPHA * x + y, ln_g, ln_b)


def _col_scale(widths, value_slots):
    scale = np.ones((sum(widths),), np.float32)
    edges = [0]
    for w in widths:
        edges.append(edges[-1] + w)
    for s in value_slots:
        scale[edges[s]:edges[s + 1]] = DEEPNORM_BETA
    return jnp.asarray(scale)


def setup_inputs(seed: int = 0) -> dict:
    key = jax.random.key(seed)
    ks = jax.random.split(key, 16)

    def nrm(k, shape, std):
        return std * jax.random.normal(k, shape, jnp.float32)

    fan = D_MODEL ** -0.5
    x = nrm(ks[0], (BATCH, SEQ, D_MODEL), 1.0)
    mem = nrm(ks[1], (BATCH, MEM_LEN, D_MODEL), 1.0)
    offsets = jax.random.randint(ks[2], (BATCH, 1), 0, 4096, dtype=jnp.int32)
    positions = offsets + jnp.arange(SEQ, dtype=jnp.int32)[None, :]
    memkv_scale = jnp.concatenate([jnp.ones((MEM_HEADS * HEAD_DIM,), jnp.float32),
                                   jnp.full((MEM_HEADS * HEAD_DIM,), DEEPNORM_BETA, jnp.float32)])
    out_std = MIX_WIDTH ** -0.5 * DEEPNORM_BETA
    return {
        'x': x,
        'mem': mem,
        'positions': positions,
        'w_in_even': nrm(ks[3], (N_EVEN, D_MODEL, EVEN_IN), fan) * _col_scale(EVEN_WIDTHS, EVEN_VALUE_SLOTS),
        'w_memkv_even': nrm(ks[4], (N_EVEN, D_MODEL, 2 * MEM_HEADS * HEAD_DIM), fan) * memkv_scale,
        'diff_lambda_even': nrm(ks[5], (N_EVEN, 4, DIFF_QK_DIM), 0.1),
        'diff_subln_even': 1.0 + nrm(ks[6], (N_EVEN, DIFF_V_DIM), 0.02),
        'w_out_even': nrm(ks[7], (N_EVEN, MIX_WIDTH, D_MODEL), out_std),
        'ln_g_even': 1.0 + nrm(ks[8], (N_EVEN, D_MODEL), 0.02),
        'ln_b_even': nrm(ks[9], (N_EVEN, D_MODEL), 0.02),
        'w_in_odd': nrm(ks[10], (N_ODD, D_MODEL, ODD_IN), fan) * _col_scale(ODD_WIDTHS, ODD_VALUE_SLOTS),
        'w_memkv_odd': nrm(ks[11], (N_ODD, D_MODEL, 2 * MEM_HEADS * HEAD_DIM), fan) * memkv_scale,
        'sinks_odd': nrm(ks[12], (N_ODD, SWA_Q_HEADS), 0.5),
        'w_out_odd': nrm(ks[13], (N_ODD, MIX_WIDTH, D_MODEL), out_std),
        'ln_g_odd': 1.0 + nrm(ks[14], (N_ODD, D_MODEL), 0.02),
        'ln_b_odd': nrm(ks[15], (N_ODD, D_MODEL), 0.02),
    }


def reference(x, mem, positions, w_in_even, w_memkv_even, diff_lambda_even, diff_subln_even,
              w_out_even, ln_g_even, ln_b_even, w_in_odd, w_memkv_odd, sinks_odd, w_out_odd,
              ln_g_odd, ln_b_odd):
    for i in range(DEPTH):
        j = i // 2
        if i % 2 == 0:
            x = _even_layer(x, mem, positions, w_in_even[j], w_memkv_even[j], diff_lambda_even[j],
                            diff_subln_even[j], w_out_even[j], ln_g_even[j], ln_b_even[j], i)
        else:
            x = _odd_layer(x, mem, positions, w_in_odd[j], w_memkv_odd[j], sinks_odd[j],
                           w_out_odd[j], ln_g_odd[j], ln_b_odd[j])
    return x
```

```python
import math
import contextlib
import numpy as np
import ml_dtypes
import concourse.bass as bass
import concourse.mybir as mybir
from concourse.bass_utils import run_bass_kernel_spmd

F32 = mybir.dt.float32
BF16 = mybir.dt.bfloat16
I32 = mybir.dt.int32
AF = mybir.ActivationFunctionType
ALU = mybir.AluOpType
AX = mybir.AxisListType

D = 1024
KC = 8
DEPTH = 2
ALPHA = (2 * DEPTH) ** 0.25
LN_EPS = 1e-5
ROPE_THETA = 500000.0
MEM_LEN = 256
PI = math.pi


class Res:
    __slots__ = ("lw", "rd", "name")

    def __init__(self, name=""):
        self.lw = None
        self.rd = []
        self.name = name


class Sched:
    ENGS = ("pe", "act", "dve", "pool", "sp")
    NSLOT = 14

    def __init__(self, nc):
        self.nc = nc
        self.ops = []

    def add(self, eng, fn, reads=(), writes=(), dma=False):
        idx = len(self.ops)
        deps = set()
        for r in reads:
            if r.lw is not None:
                deps.add(r.lw)
        for w in writes:
            if w.lw is not None:
                deps.add(w.lw)
            deps.update(w.rd)
        for r in reads:
            r.rd.append(idx)
        for w in writes:
            w.lw = idx
            w.rd = []
        deps.discard(idx)
        self.ops.append([eng, fn, deps, dma])
        return idx

    def emit(self, final_wait_ops=()):
        nc = self.nc
        ops = self.ops
        n = len(ops)
        has_dep = [False] * n
        for i, (eng, fn, deps, dma) in enumerate(ops):
            for d in deps:
                if ops[d][0] == "pe" and eng == "pe" and not ops[d][3] and not dma:
                    continue
                has_dep[d] = True
        for d in final_wait_ops:
            has_dep[d] = True
        cnt = {e: 0 for e in self.ENGS}
        dcnt = {e: 0 for e in self.ENGS}
        sig = [None] * n
        for i, (eng, fn, deps, dma) in enumerate(ops):
            if dma:
                k = dcnt[eng]
                dcnt[eng] += 1
                sig[i] = ("d", eng, k % self.NSLOT, 16 * (k // self.NSLOT + 1))
            elif has_dep[i]:
                cnt[eng] += 1
                sig[i] = ("c", eng, cnt[eng])
        engs_used = [e for e in self.ENGS if any(o[0] == e for o in ops)]
        with contextlib.ExitStack() as st:
            csem = {e: st.enter_context(nc.semaphore("c_" + e)) for e in engs_used}
            dsem = {}
            for e in engs_used:
                if dcnt[e] > 0:
                    dsem[e] = [st.enter_context(nc.semaphore("d_%s_%d" % (e, s)))
                               for s in range(min(self.NSLOT, dcnt[e]))]
            block = st.enter_context(nc.Block())
            engobj = {"pe": "tensor", "act": "scalar", "dve": "vector", "pool": "gpsimd", "sp": "sync"}

            def make_stream(ename):
                def stream(eng):
                    waited_c = {}
                    waited_d = {}
                    for i, (e, fn, deps, dma) in enumerate(ops):
                        if e != ename:
                            continue
                        need_c = {}
                        need_d = {}
                        for d in deps:
                            s = sig[d]
                            if s is None:
                                continue
                            if s[0] == "c":
                                if s[1] == "pe" and ename == "pe" and not dma:
                                    continue
                                need_c[s[1]] = max(need_c.get(s[1], 0), s[2])
                            else:
                                key = (s[1], s[2])
                                need_d[key] = max(need_d.get(key, 0), s[3])
                        if dma:
                            s = sig[i]
                            if s[3] > 16:
                                key = (s[1], s[2])
                                need_d[key] = max(need_d.get(key, 0), s[3] - 16)
                        for se, v in need_c.items():
                            if waited_c.get(se, 0) < v:
                                eng.wait_ge(csem[se], v)
                                waited_c[se] = v
                        for key, v in need_d.items():
                            if waited_d.get(key, 0) < v:
                                eng.wait_ge(dsem[key[0]][key[1]], v)
                                waited_d[key] = v
                        ins = fn(eng)
                        s = sig[i]
                        if s is not None:
                            if s[0] == "c":
                                ins.then_inc(csem[ename], 1)
                            else:
                                ins.then_inc(dsem[ename][s[2]], 16)
                    if ename == "sp":
                        for d in final_wait_ops:
                            s = sig[d]
                            if s[0] == "c":
                                eng.wait_ge(csem[s[1]], s[2])
                            else:
                                eng.wait_ge(dsem[s[1]][s[2]], s[3])
                return stream

            for e in engs_used:
                getattr(block, engobj[e])(make_stream(e))


class SbufAlloc:
    def __init__(self, nc, nbytes=207 * 1024):
        self.nc = nc
        self.arena = nc.alloc_sbuf_tensor("arena", [128, nbytes], mybir.dt.uint8)
        self.off = 0
        self.limit = nbytes
        self.peak = 0

    def mark(self):
        return self.off

    def reset(self, m):
        self.off = m

    def tile(self, shape, dtype):
        assert shape[0] == 128
        esz = {F32: 4, BF16: 2, I32: 4}[dtype]
        nel = int(np.prod(shape[1:]))
        nbytes = (esz * nel + 63) // 64 * 64
        off = self.off
        self.off += nbytes
        self.peak = max(self.peak, self.off)
        assert self.off <= self.limit, ("SBUF overflow", self.off)
        ap = self.arena.ap()[:, off:off + esz * nel].bitcast(dtype)
        if len(shape) == 3:
            ap = ap.rearrange("p (a b) -> p a b", a=shape[1])
        elif len(shape) == 4:
            ap = ap.rearrange("p (a b c) -> p a b c", a=shape[1], b=shape[2])
        return ap


class T:
    def __init__(self, ap, name=""):
        self.ap = ap
        self.res = Res(name)


class Builder:
    def __init__(self, S, layer):
        self.S = S
        self.layer = layer
        self.NB = S // 128
        self.NJ = self.NB // 4
        self.NO = self.NJ * 128
        self.QGB = min(4, self.NJ)
        self.GW = self.QGB * 128
        self.NQG = self.NJ // self.QGB
        self.TW = min(512, S)
        self.NT = S // self.TW
        self.nc = bass.Bass("TRN2", target_bir_lowering=False)
        self.sch = Sched(self.nc)
        self.sa = SbufAlloc(self.nc)
        self.dram = {}
        self.psum = [T(self.nc.alloc_psum_tensor("ps%d" % i, [128, 512], F32).ap(), "ps%d" % i) for i in range(8)]

    def din(self, name, shape, dtype=F32):
        t = T(self.nc.dram_tensor(name, list(shape), dtype, kind="ExternalInput").ap(), name)
        self.dram[name] = t
        return t

    def dout(self, name, shape, dtype=F32):
        t = T(self.nc.dram_tensor(name, list(shape), dtype, kind="ExternalOutput").ap(), name)
        self.dram[name] = t
        return t

    def dscr(self, name, shape, dtype):
        t = T(self.nc.dram_tensor(name, list(shape), dtype).ap(), name)
        self.dram[name] = t
        return t

    def tile(self, shape, dtype, name=""):
        return T(self.sa.tile(shape, dtype), name)

    def op(self, eng, fn, reads=(), writes=(), dma=False):
        return self.sch.add(eng, fn, [t.res for t in reads], [t.res for t in writes], dma)

    def dma(self, out_ap, in_ap, reads, writes, q="sp"):
        return self.op(q, lambda e: e.dma_start(out=out_ap, in_=in_ap), reads, writes, dma=True)

    def mm(self, out_ap, lhsT, rhs, start, stop, reads, writes, skip=False):
        if skip:
            return self.op("pe", lambda e: e.matmul(out_ap, lhsT=lhsT, rhs=rhs, start=start, stop=stop,
                                                    skip_group_check=True), reads, writes)
        return self.op("pe", lambda e: e.matmul(out_ap, lhsT=lhsT, rhs=rhs, start=start, stop=stop), reads, writes)

    def act(self, out_ap, in_ap, func, reads, writes, scale=1.0, bias=0.0):
        return self.op("act", lambda e: e.activation(out=out_ap, in_=in_ap, func=func, bias=bias, scale=scale),
                       reads, writes)

    def tt(self, eng, out_ap, a, b, op, reads, writes):
        return self.op(eng, lambda e: e.tensor_tensor(out_ap, a, b, op), reads, writes)

    def ts(self, eng, out_ap, a, s1, s2, op0, op1, reads, writes):
        if op1 is None:
            return self.op(eng, lambda e: e.tensor_scalar(out_ap, a, s1, None, op0), reads, writes)
        return self.op(eng, lambda e: e.tensor_scalar(out_ap, a, s1, s2, op0, op1), reads, writes)

    def copy(self, eng, out_ap, in_ap, reads, writes):
        if eng == "act":
            return self.op("act", lambda e: e.copy(out_ap, in_ap), reads, writes)
        return self.op(eng, lambda e: e.tensor_copy(out_ap, in_ap), reads, writes)

    def memset(self, eng, ap, val, writes):
        return self.op(eng, lambda e: e.memset(ap, val), (), writes)

    def load_w(self, wd, n, wb, stage, c0=0):
        src = wd.ap.rearrange("(kc p) n -> p kc n", p=128)
        i = 0
        CW = stage[0].ap.shape[2]
        for a in range(0, n, CW):
            w = min(CW, n - a)
            stg = stage[i % len(stage)]
            i += 1
            self.dma(stg.ap[:, :, 0:w], src[:, :, a:a + w], [wd], [stg])
            self.copy("pool" if (i % 2) else "dve", wb.ap[:, :, c0 + a:c0 + a + w], stg.ap[:, :, 0:w], [stg], [wb])

    def rope_tables(self, pos_d, a, w, posi, posf, t1, cosT, sinS, invf, coefp):
        SC = 2 * PI * (1.0 - 1e-6)
        self.dma(posi.ap[:, 0:w], pos_d.ap[0:1, a:a + w].partition_broadcast(128), [pos_d], [posi])
        self.copy("dve", posf.ap[:, 0:w], posi.ap[:, 0:w], [posi], [posf])
        self.ts("dve", posf.ap[:, 0:w], posf.ap[:, 0:w], invf.ap[:, 0:1], None, ALU.mult, None, [posf, invf], [posf])
        self.copy("dve", posi.ap[:, 0:w], posf.ap[:, 0:w], [posf], [posi])
        self.copy("pool", t1.ap[:, 0:w], posi.ap[:, 0:w], [posi], [t1])
        self.tt("dve", t1.ap[:, 0:w], posf.ap[:, 0:w], t1.ap[:, 0:w], ALU.subtract, [posf, t1], [t1])
        self.act(sinS.ap[:, 0:w], t1.ap[:, 0:w], AF.Sin, [t1], [sinS], scale=SC)
        self.ts("dve", posf.ap[:, 0:w], posf.ap[:, 0:w], 0.25, None, ALU.add, None, [posf], [posf])
        self.copy("dve", posi.ap[:, 0:w], posf.ap[:, 0:w], [posf], [posi])
        self.copy("pool", t1.ap[:, 0:w], posi.ap[:, 0:w], [posi], [t1])
        self.tt("dve", t1.ap[:, 0:w], posf.ap[:, 0:w], t1.ap[:, 0:w], ALU.subtract, [posf, t1], [t1])
        self.act(cosT.ap[:, 0:w], t1.ap[:, 0:w], AF.Sin, [t1], [cosT], scale=SC)
        self.ts("pool", sinS.ap[:, 0:w], sinS.ap[:, 0:w], coefp.ap[:, 0:1], None, ALU.mult, None, [sinS, coefp], [sinS])

    def proj_fm(self, ps, wb, c0, xb, w, m=128):
        for kc in range(KC):
            self.mm(ps.ap[0:m, 0:w], wb.ap[:, kc, c0:c0 + m], xb.ap[:, kc, 0:w], kc == 0, kc == KC - 1, [wb, xb], [ps])


    def nextbank4(self):
        b = self.psum[self.b4[0] % 4]
        self.b4[0] += 1
        return b

    def softmax_norm(self, acc, rdcols, post_scale_col=None, add_row=None, W=None):
        rd, bcs, ones_f = self.rd, self.bcs, self.ones_f
        W = W or self.GW
        if add_row is not None:
            r2 = slice(rdcols.start + W, rdcols.stop + W)
            self.tt("dve", rd.ap[64:65, r2], acc.ap[64:65, 0:W], add_row[0], ALU.add, [acc, add_row[1]], [rd])
            self.op("dve", lambda e: e.reciprocal(rd.ap[64:65, rdcols], rd.ap[64:65, r2]), [rd], [rd])
        else:
            self.op("dve", lambda e: e.reciprocal(rd.ap[64:65, rdcols], acc.ap[64:65, 0:W]), [acc], [rd])
        if post_scale_col is not None:
            self.ts("dve", rd.ap[64:65, rdcols], rd.ap[64:65, rdcols], post_scale_col, None, ALU.mult, None, [rd, self.lamw], [rd])
        bc = self.nextbank4()
        self.mm(bc.ap[0:64, 0:W], ones_f.ap[64:65, 0:64], rd.ap[64:65, rdcols], True, True, [ones_f, rd], [bc])
        self.copy("act", bcs.ap[0:64, rdcols], bc.ap[0:64, 0:W], [bc], [bcs])

    def mem_units(self, mkT, mv, mqT, mixT, PT, ea, tq, zc, accc):
        B = self
        GW, NQG, PS = self.GW, self.NQG, self.psum
        units = []
        for hm in range(4):
            cg, po = hm // 2, (hm % 2) * 64
            for G in range(NQG):
                acc = PS[4 + accc[0] % 4]
                accc[0] += 1
                gsl = slice(G * GW, (G + 1) * GW)
                for mb in range(2):
                    zi = zc[0]
                    zc[0] += 1
                    Sb = B.nextbank4()
                    PTt = PT[zi % len(PT)]

                    def s0(Sb=Sb, PTt=PTt, mb=mb, cg=cg, po=po, gsl=gsl):
                        B.mm(Sb.ap[:, 0:GW], mkT.ap[po:po + 64, cg, mb * 128:(mb + 1) * 128], mqT.ap[po:po + 64, cg, gsl],
                             True, True, [mkT, mqT], [Sb])
                        B.act(PTt.ap[:, 0:GW], Sb.ap[:, 0:GW], AF.Exp, [Sb], [PTt], scale=0.125)

                    def s1(acc=acc, PTt=PTt, mb=mb, hm=hm, cg=cg, po=po, gsl=gsl):
                        B.mm(acc.ap[0:65, 0:GW], mv.ap[:, mb, hm, 0:65], PTt.ap[:, 0:GW], mb == 0, mb == 1, [mv, PTt], [acc], skip=True)
                        if mb == 1:
                            B.softmax_norm(acc, slice(0, GW))
                            B.tt("dve", ea.ap[0:64, 0:GW], acc.ap[0:64, 0:GW], B.bcs.ap[0:64, 0:GW], ALU.mult, [acc, B.bcs], [ea])
                            B.copy("act", tq.ap[po:po + 64, 0:GW], ea.ap[0:64, 0:GW], [ea], [tq])
                            B.tt("dve", mixT.ap[po:po + 64, 6 + cg, gsl], tq.ap[po:po + 64, 0:GW], mixT.ap[po:po + 64, 6 + cg, gsl],
                                 ALU.mult, [tq, mixT], [mixT])

                    units.append([s0, s1])
        return units

    def mem_kv(self, memT, w_mkv, wb, wst, xs0, memb, mkT, mv, nextbank):
        B = self
        B.load_w(w_mkv, 512, wb, wst)
        B.dma(xs0.ap[:, :, 0:MEM_LEN], memT.ap.rearrange("(kc p) m -> p kc m", p=128), [memT], [xs0])
        B.copy("dve", memb.ap[:, :, 0:MEM_LEN], xs0.ap[:, :, 0:MEM_LEN], [xs0], [memb])
        for cg in range(2):
            ps = nextbank()
            B.proj_fm(ps, wb, cg * 128, memb, MEM_LEN)
            B.copy("act", mkT.ap[:, cg, :], ps.ap[:, 0:MEM_LEN], [ps], [mkT])
        B.memset("pool", mv.ap, 1.0, [mv])
        for mb in range(2):
            ps = nextbank()
            for kc in range(KC):
                B.mm(ps.ap[:, 0:256], memb.ap[:, kc, mb * 128:(mb + 1) * 128], wb.ap[:, kc, 256:512], kc == 0, kc == KC - 1,
                     [memb, wb], [ps])
            B.copy("act", mv.ap[:, mb, :, 0:64], ps.ap[:, 0:256].rearrange("p (h d) -> p h d", h=4), [ps], [mv])

    def out_block(self, j, mixT, wo, x_src, xr_t, vv_t, st, gbc, bbc, y_dst, nextbank):
        B = self
        B.dma(xr_t.ap, x_src.ap[j * 128:(j + 1) * 128, :], [x_src], [xr_t])
        for n in range(2):
            ps = nextbank()
            for kc in range(KC):
                B.mm(ps.ap[:, 0:512], mixT.ap[:, kc, j * 128:(j + 1) * 128], wo.ap[:, kc, n * 512:(n + 1) * 512],
                     kc == 0, kc == KC - 1, [mixT, wo], [ps])
            B.op("dve", lambda e, ps=ps, n=n: e.scalar_tensor_tensor(
                vv_t.ap[:, n * 512:(n + 1) * 512], xr_t.ap[:, n * 512:(n + 1) * 512], ALPHA, ps.ap[:, 0:512],
                ALU.mult, ALU.add), [ps, xr_t], [vv_t])
        return layer_norm_store(B, vv_t, xr_t, st, gbc, bbc, y_dst, j)

    def rope_evac(self, psK, psP, out_ap, w, cosT, sinS, tmpa, tmpb, out_t, scale=None):
        self.tt("dve", tmpa.ap[:, 0:w], psK.ap[:, 0:w], cosT.ap[:, 0:w], ALU.mult, [psK, cosT], [tmpa])
        self.tt("dve", tmpb.ap[:, 0:w], psP.ap[:, 0:w], sinS.ap[:, 0:w], ALU.mult, [psP, sinS], [tmpb])
        self.tt("pool", out_ap, tmpa.ap[:, 0:w], tmpb.ap[:, 0:w], ALU.add, [tmpa, tmpb], [out_t])


def _host_consts(layer):
    if layer == 0:
        hd, rot = 32, 8
    else:
        hd, rot = 64, 16
    half = rot // 2
    inv = np.exp(-(np.arange(half, dtype=np.float32) / half) * math.log(ROPE_THETA)).astype(np.float32)
    invf = np.zeros((128, 1), np.float32)
    coef = np.zeros((128, 1), np.float32)
    for r in range(128):
        d = r % hd
        if d < rot:
            invf[r, 0] = inv[d % half] / np.float32(2 * PI)
            coef[r, 0] = -1.0 if d < half else 1.0
    return invf, coef


def _partner_perm(ncols, hd, rot):
    half = rot // 2
    perm = np.arange(ncols)
    for c in range(ncols):
        d = c % hd
        if d < half:
            perm[c] = c + half
        elif d < rot:
            perm[c] = c - half
    return perm


def _masks(c):
    k = np.arange(128)[:, None]
    q = np.arange(128)[None, :]
    out = np.zeros((9, 128, 128), np.float32)
    out[8] = -1.0 * (k >= q)
    for r in range(4):
        if r < c:
            out[r] = 1.0
            out[4 + r] = 1.0
        elif r == c:
            out[r] = (k <= q)
            out[4 + r] = (k < q)
    return out.astype(ml_dtypes.bfloat16)


def run_pipeline(units, nst):
    n = len(units)
    for step in range(n + nst - 1):
        for s in range(nst):
            u = step - s
            if 0 <= u < n and units[u][s] is not None:
                units[u][s]()


def build_l0(S, lambda_init):
    B = Builder(S, 0)
    outs = l0_body(B, lambda_init, False)
    B.sch.emit(final_wait_ops=outs)
    return B


def l0_body(B, lambda_init, fused):
    nc = B.nc
    S = B.S
    NB, NJ, NO, QGB, GW, NQG, TW, NT = B.NB, B.NJ, B.NO, B.QGB, B.GW, B.NQG, B.TW, B.NT
    OW = min(512, NO)
    NOT = NO // OW
    xT_all = B.din("xT_all", [D, S])
    xT_own = B.din("xT_own", [D, NO])
    x_own = B.din("x_own", [NO, D])
    pos_all = B.din("pos_all", [1, S], I32)
    pos_own = B.din("pos_own", [1, NO], I32)
    w_k = B.din("w_k", [D, 1152])
    w_v = B.din("w_v", [D, 768])
    w_q = B.din("w_q", [D, 1408])
    w_g = B.din("w_g", [D, 1024])
    memT = B.din("memT", [D, MEM_LEN])
    w_mkv = B.din("w_mkv", [D, 512])
    w_o = B.din("w_o", [D, D])
    ln_g = B.din("ln_g", [1, D])
    ln_b = B.din("ln_b", [1, D])
    dlam = B.din("dlam", [1, 128])
    subln = B.din("subln", [128, 1])
    invf_d = B.din("invf", [128, 1])
    coef_d = B.din("coefp", [128, 1])
    masks_d = B.din("masks", [128, 9 * 128], BF16)
    y_out = B.dscr("x1_send", [NO, D], F32) if fused else B.dout("y", [NO, D])
    B.x1_own_d = y_out
    kT_scr = B.dscr("kT_scr", [128, 6, S], BF16)
    v_scr = B.dscr("v_scr", [128, 12, NB * 65], BF16)

    masks = B.tile([128, 9, 128], BF16, "masks")
    negtri = T(masks.ap[:, 8, :], "negtri")
    negtri.res = masks.res
    negones = B.tile([128, 128], BF16, "negones")
    ones_f = B.tile([128, 128], F32, "ones_f")
    invf = B.tile([128, 1], F32, "invf")
    coefp = B.tile([128, 1], F32, "coefp")
    B.pi_col = B.tile([128, 1], F32, "pi")
    g1col = B.tile([128, 1], F32, "g1col")
    lam_t = B.tile([128, 128], F32, "lam")
    lamw = B.tile([128, 8], F32, "lamw")
    qT_sb = B.tile([128, 3, NO], BF16, "qT_sb")
    qT_df = B.tile([128, 3, NO], BF16, "qT_df")
    mqT = B.tile([128, 2, NO], BF16, "mqT")
    mixT = B.tile([128, 8, NO], BF16, "mixT")
    mkT = B.tile([128, 2, MEM_LEN], BF16, "mkT")
    mv = B.tile([128, 2, 4, 65], BF16, "mv")
    PS = B.psum
    pbc = [0]

    def nextbank():
        b = PS[pbc[0] % 8]
        pbc[0] += 1
        return b

    B.dma(masks.ap.rearrange("p a b -> p (a b)"), masks_d.ap[:, :], [masks_d], [masks])
    B.dma(invf.ap, invf_d.ap[:, :], [invf_d], [invf])
    B.dma(coefp.ap, coef_d.ap[:, :], [coef_d], [coefp])
    B.memset("pool", B.pi_col.ap, PI, [B.pi_col])
    B.eps_col = B.tile([128, 1], F32, "eps")
    B.memset("pool", B.eps_col.ap, LN_EPS, [B.eps_col])
    B.memset("pool", ones_f.ap, 1.0, [ones_f])
    B.memset("pool", negones.ap, -1.0, [negones])
    B.dma(lam_t.ap[64:65, 0:128], dlam.ap[0:1, :], [dlam], [lam_t])
    prod = B.tile([128, 64], F32, "prod")
    B.tt("dve", prod.ap[64:65, 0:32], lam_t.ap[64:65, 0:32], lam_t.ap[64:65, 32:64], ALU.mult, [lam_t], [prod])
    B.tt("dve", prod.ap[64:65, 32:64], lam_t.ap[64:65, 64:96], lam_t.ap[64:65, 96:128], ALU.mult, [lam_t], [prod])
    B.op("dve", lambda e: e.tensor_reduce(lamw.ap[64:65, 0:1], prod.ap[64:65, 0:32], AX.X, ALU.add), [prod], [lamw])
    B.op("dve", lambda e: e.tensor_reduce(lamw.ap[64:65, 1:2], prod.ap[64:65, 32:64], AX.X, ALU.add), [prod], [lamw])
    B.act(lamw.ap[64:65, 2:4], lamw.ap[64:65, 0:2], AF.Exp, [lamw], [lamw])
    B.tt("dve", lamw.ap[64:65, 4:5], lamw.ap[64:65, 2:3], lamw.ap[64:65, 3:4], ALU.subtract, [lamw], [lamw])
    B.ts("dve", lamw.ap[64:65, 5:6], lamw.ap[64:65, 4:5], lambda_init, -1.0, ALU.add, ALU.mult, [lamw], [lamw])
    B.dma(g1col.ap, subln.ap[:, :], [subln], [g1col])
    B.ts("dve", g1col.ap, g1col.ap, 1.0 - lambda_init, None, ALU.mult, None, [g1col], [g1col])

    mA = B.sa.mark()
    wb = B.tile([128, KC, 1920], BF16, "wb")
    wst = [B.tile([128, KC, 128], F32, "wst%d" % i) for i in range(2)]
    xs = [B.tile([128, KC, TW], F32, "xs%d" % i) for i in range(2)]
    xb = [B.tile([128, KC, TW], BF16, "xb%d" % i) for i in range(2)]
    posi = B.tile([128, TW], I32, "posi")
    posf = B.tile([128, TW], F32, "posf")
    t1 = B.tile([128, TW], F32, "t1")
    cosT = B.tile([128, TW], F32, "cosT")
    sinS = B.tile([128, TW], F32, "sinS")
    tmpa = B.tile([128, TW], F32, "tmpa")
    tmpb = B.tile([128, TW], F32, "tmpb")
    ktst = [B.tile([128, 6, TW], BF16, "ktst%d" % i) for i in range(2)]
    vst = [B.tile([128, 12, TW // 128, 65], BF16, "vst%d" % i) for i in range(2)]

    B.mem_kv(memT, w_mkv, wb, wst, xs[0], xb[0], mkT, mv, nextbank)

    B.load_w(w_k, 1152, wb, wst, 0)
    B.load_w(w_v, 768, wb, wst, 1152)
    for i in range(2):
        B.memset("pool", vst[i].ap, 1.0, [vst[i]])
    xsrc = xT_all.ap.rearrange("(kc p) s -> p kc s", p=128)
    for t in range(NT):
        xs_t, xb_t = xs[t % 2], xb[t % 2]
        kt_t, v_t = ktst[t % 2], vst[t % 2]
        for hlf in range(2):
            B.dma(xs_t.ap[:, hlf * 4:(hlf + 1) * 4, :], xsrc[:, hlf * 4:(hlf + 1) * 4, t * TW:(t + 1) * TW], [xT_all], [xs_t])
        for kc in range(KC):
            B.copy("pool" if kc % 2 == 0 else "dve", xb_t.ap[:, kc, :], xs_t.ap[:, kc, :], [xs_t], [xb_t])
        B.rope_tables(pos_all, t * TW, TW, posi, posf, t1, cosT, sinS, invf, coefp)
        for cg in range(3):
            ps = nextbank()
            B.proj_fm(ps, wb, cg * 128, xb_t, TW)
            B.copy("act", kt_t.ap[:, cg, :], ps.ap[:, 0:TW], [ps], [kt_t])
        for cg in range(3):
            psK = nextbank()
            B.proj_fm(psK, wb, 384 + cg * 128, xb_t, TW)
            psP = nextbank()
            B.proj_fm(psP, wb, 768 + cg * 128, xb_t, TW)
            B.rope_evac(psK, psP, kt_t.ap[:, 3 + cg, :], TW, cosT, sinS, tmpa, tmpb, kt_t)
        B.dma(kT_scr.ap[:, :, t * TW:(t + 1) * TW], kt_t.ap, [kt_t], [kT_scr])
        for blk in range(TW // 128):
            for half in range(2):
                ps = nextbank()
                for kc in range(KC):
                    B.mm(ps.ap[:, 0:384], xb_t.ap[:, kc, blk * 128:(blk + 1) * 128],
                         wb.ap[:, kc, 1152 + half * 384:1152 + (half + 1) * 384], kc == 0, kc == KC - 1, [xb_t, wb], [ps])
                B.copy("act" if half == 0 else "dve", v_t.ap[:, half * 6:(half + 1) * 6, blk, 0:64],
                       ps.ap[:, 0:384].rearrange("p (h d) -> p h d", h=6), [ps], [v_t])
        nb_t = TW // 128
        B.dma(v_scr.ap[:, :, t * nb_t * 65:(t + 1) * nb_t * 65], v_t.ap.rearrange("p h b c -> p h (b c)"), [v_t], [v_scr])

    xosrc = xT_own.ap.rearrange("(kc p) s -> p kc s", p=128)
    for rnd in range(2):
        if rnd == 0:
            B.load_w(w_q, 1408, wb, wst, 0)
        else:
            B.load_w(w_g, 1024, wb, wst, 0)
        for u in range(NOT):
            xs_t, xb_t = xs[u % 2], xb[u % 2]
            osl = slice(u * OW, (u + 1) * OW)
            for hlf in range(2):
                B.dma(xs_t.ap[:, hlf * 4:(hlf + 1) * 4, 0:OW], xosrc[:, hlf * 4:(hlf + 1) * 4, osl], [xT_own], [xs_t])
            for kc in range(KC):
                B.copy("pool" if kc % 2 == 0 else "dve", xb_t.ap[:, kc, 0:OW], xs_t.ap[:, kc, 0:OW], [xs_t], [xb_t])
            if rnd == 0:
                B.rope_tables(pos_own, u * OW, OW, posi, posf, t1, cosT, sinS, invf, coefp)
                for cg in range(3):
                    ps = nextbank()
                    B.proj_fm(ps, wb, cg * 128, xb_t, OW)
                    B.act(qT_sb.ap[:, cg, osl], ps.ap[:, 0:OW], AF.Copy, [ps], [qT_sb], scale=0.125)
                for cg in range(3):
                    psK = nextbank()
                    B.proj_fm(psK, wb, 384 + cg * 128, xb_t, OW)
                    psP = nextbank()
                    B.proj_fm(psP, wb, 768 + cg * 128, xb_t, OW)
                    B.rope_evac(psK, psP, qT_df.ap[:, cg, osl], OW, cosT, sinS, tmpa, tmpb, qT_df)
                for cg in range(2):
                    ps = nextbank()
                    B.proj_fm(ps, wb, 1152 + cg * 128, xb_t, OW)
                    B.copy("act", mqT.ap[:, cg, osl], ps.ap[:, 0:OW], [ps], [mqT])
            else:
                for cg in range(8):
                    ps = nextbank()
                    B.proj_fm(ps, wb, cg * 128, xb_t, OW)
                    B.act(mixT.ap[:, cg, osl], ps.ap[:, 0:OW], AF.Silu, [ps], [mixT])

    phaseA_tiles = [wb] + wst + xs + xb + [posi, posf, t1, cosT, sinS, tmpa, tmpb] + ktst + vst
    B.sa.reset(mA)
    kTp = [B.tile([128, S], BF16, "kTp%d" % i) for i in range(2)]
    vtp = [B.tile([128, 2, NB, 65], BF16, "vtp%d" % i) for i in range(2)]
    E = [B.tile([128, GW], F32, "E%d" % i) for i in range(2)]
    SP = [B.tile([128, GW], BF16, "SP%d" % i) for i in range(3)]
    PT = [B.tile([128, GW], BF16, "PT%d" % i) for i in range(4)]
    R32 = B.tile([128, GW], F32, "R32")
    Rb = [B.tile([128, GW], BF16, "Rb%d" % i) for i in range(2)]
    tmpo = [B.tile([128, GW], F32, "tmpo%d" % i) for i in range(2)]
    rd = B.tile([128, 2 * GW], F32, "rd")
    bcs = B.tile([128, 2 * GW], F32, "bcs")
    ea = B.tile([128, GW], F32, "ea")
    eb = B.tile([128, GW], F32, "eb")
    ec = B.tile([128, GW], F32, "ec")
    fz = B.tile([128, 16], F32, "fz")
    phaseB_tiles = kTp + vtp + E + SP + PT + [R32] + Rb + tmpo + [rd, bcs, ea, eb, ec, fz]
    B.op("pool", lambda e: e.memset(fz.ap, 0.0), [], phaseA_tiles + phaseB_tiles)

    kview = kT_scr.ap
    vview = v_scr.ap

    def load_pair(kcg, vh0, slot):
        B.dma(kTp[slot].ap, kview[:, kcg, :], [kT_scr], [kTp[slot]])
        B.dma(vtp[slot].ap.rearrange("p h b c -> p h (b c)"), vview[:, vh0:vh0 + 2, :], [v_scr], [vtp[slot]])

    zc = [0]
    accc = [0]

    def kb_list(G):
        zs = 4 * QGB * G
        out = []
        for kb in range(4 * QGB * (G + 1) - 1, -1, -1):
            if kb >= zs:
                jd, r = (kb - zs) // 4, (kb - zs) % 4
                out.append((kb, jd * 128, r))
            else:
                out.append((kb, 0, None))
        return out

    def sb_units(h, slot):
        cg, po = h // 2, (h % 2) * 64
        kT, vt = kTp[slot], vtp[slot]
        units = []
        for G in range(NQG):
            kbs = kb_list(G)
            acc = PS[4 + accc[0] % 2]
            accc[0] += 1
            for ui, (kb, c0, r) in enumerate(kbs):
                first, last = ui == 0, ui == len(kbs) - 1
                zi = zc[0]
                zc[0] += 1
                Z, ARG = PS[zi % 2], PS[2 + zi % 2]
                Et, SPt, PTt = E[zi % 2], SP[zi % 3], PT[zi % 4]
                Rcur, Rnext = Rb[zi % 2], Rb[(zi + 1) % 2]
                kap = kT.ap[po:po + 64, kb * 128:(kb + 1) * 128]
                qap = qT_sb.ap[po:po + 64, cg, G * GW + c0:(G + 1) * GW]
                cs = slice(c0, GW)
                ms = slice(c0, c0 + 128)

                def s0(Z=Z, Et=Et, SPt=SPt, kap=kap, qap=qap, cs=cs, ms=ms, r=r, kT=kT):
                    B.mm(Z.ap[:, cs], kap, qap, True, True, [kT, qT_sb], [Z])
                    B.act(Et.ap[:, cs], Z.ap[:, cs], AF.Exp, [Z], [Et])
                    B.act(SPt.ap[:, cs], Et.ap[:, cs], AF.Ln, [Et], [SPt], bias=1.0)
                    if r is not None:
                        B.tt("dve", SPt.ap[:, ms], SPt.ap[:, ms], masks.ap[:, 4 + r, :], ALU.mult, [SPt, masks], [SPt])

                def s1(ARG=ARG, SPt=SPt, PTt=PTt, kap=kap, qap=qap, cs=cs, ms=ms, r=r, first=first, last=last,
                       Rcur=Rcur, Rnext=Rnext, kT=kT):
                    B.mm(ARG.ap[:, cs], kap, qap, True, False, [kT, qT_sb], [ARG])
                    B.mm(ARG.ap[:, cs], negtri.ap, SPt.ap[:, cs], False, first, [negtri, SPt], [ARG])
                    if not first:
                        B.mm(ARG.ap[:, cs], negones.ap, Rcur.ap[:, cs], False, True, [negones, Rcur], [ARG])
                    if first:
                        B.memset("pool", R32.ap, 0.0, [R32])
                    if not last:
                        B.tt("pool", R32.ap[:, cs], R32.ap[:, cs], SPt.ap[:, cs], ALU.add, [R32, SPt], [R32])
                        B.copy("pool", Rnext.ap, R32.ap, [R32], [Rnext])
                    B.act(PTt.ap[:, cs], ARG.ap[:, cs], AF.Exp, [ARG], [PTt])
                    if r is not None:
                        B.tt("dve", PTt.ap[:, ms], PTt.ap[:, ms], masks.ap[:, 4 + r, :], ALU.mult, [PTt, masks], [PTt])

                def s2(acc=acc, PTt=PTt, cs=cs, kb=kb, first=first, last=last, G=G, vt=vt):
                    B.mm(acc.ap[0:64, cs], vt.ap[:, h % 2, kb, 0:64], PTt.ap[:, cs], first, last, [vt, PTt], [acc], skip=True)
                    if last:
                        tq = tmpo[G % 2]
                        gsl = slice(G * GW, (G + 1) * GW)
                        B.copy("act", tq.ap[po:po + 64, :], acc.ap[0:64, 0:GW], [acc], [tq])
                        B.tt("dve", mixT.ap[po:po + 64, cg, gsl], tq.ap[po:po + 64, :], mixT.ap[po:po + 64, cg, gsl],
                             ALU.mult, [tq, mixT], [mixT])

                units.append([s0, s1, s2])
        return units

    B.rd, B.bcs, B.ones_f, B.b4, B.lamw = rd, bcs, ones_f, [0], lamw
    softmax_norm = B.softmax_norm
    nextbank4 = B.nextbank4

    def df_units(h, slot):
        cgk, po = h // 2, (h % 2) * 64
        kT, vt = kTp[slot], vtp[slot]
        units = []
        scale = 32 ** -0.5
        for G in range(NQG):
            kbs = kb_list(G)
            accs = [PS[4 + 2 * (accc[0] % 2)], PS[5 + 2 * (accc[0] % 2)]]
            accc[0] += 1
            gsl = slice(G * GW, (G + 1) * GW)
            for ui, (kb, c0, r) in enumerate(kbs):
                first, last = ui == 0, ui == len(kbs) - 1
                for cm in range(2):
                    zi = zc[0]
                    zc[0] += 1
                    Sb = nextbank4()
                    PTt = PT[zi % 4]
                    rows = slice(po + cm * 32, po + cm * 32 + 32)
                    kap = kT.ap[rows, kb * 128:(kb + 1) * 128]
                    qap = qT_df.ap[rows, cgk, G * GW + c0:(G + 1) * GW]
                    cs = slice(c0, GW)
                    ms = slice(c0, c0 + 128)
                    tp = (po + cm * 32, 0)

                    def s0(Sb=Sb, PTt=PTt, kap=kap, qap=qap, cs=cs, ms=ms, r=r, tp=tp, kT=kT):
                        B.op("pe", lambda e: e.matmul(Sb.ap[:, cs], lhsT=kap, rhs=qap, start=True, stop=True,
                                                      tile_position=tp), [kT, qT_df], [Sb])
                        B.act(PTt.ap[:, cs], Sb.ap[:, cs], AF.Exp, [Sb], [PTt], scale=scale)
                        if r is not None:
                            B.tt("dve", PTt.ap[:, ms], PTt.ap[:, ms], masks.ap[:, r, :], ALU.mult, [PTt, masks], [PTt])

                    def s1(acc=accs[cm], PTt=PTt, cs=cs, kb=kb, first=first, last=last, cm=cm, accs=accs, gsl=gsl, vt=vt):
                        B.mm(acc.ap[0:65, cs], vt.ap[:, h % 2, kb, 0:65], PTt.ap[:, cs], first, last, [vt, PTt], [acc], skip=True)
                        if last and cm == 1:
                            softmax_norm(accs[0], slice(0, GW))
                            softmax_norm(accs[1], slice(GW, 2 * GW), post_scale_col=lamw.ap[64:65, 5:6])
                            B.tt("dve", ea.ap[0:64, :], accs[0].ap[0:64, 0:GW], bcs.ap[0:64, 0:GW], ALU.mult, [accs[0], bcs], [ea])
                            B.tt("dve", eb.ap[0:64, :], accs[1].ap[0:64, 0:GW], bcs.ap[0:64, GW:2 * GW], ALU.mult, [accs[1], bcs], [eb])
                            B.tt("pool", ea.ap[0:64, :], ea.ap[0:64, :], eb.ap[0:64, :], ALU.add, [ea, eb], [ea])
                            B.tt("pool", eb.ap[0:64, :], ea.ap[0:64, :], ea.ap[0:64, :], ALU.mult, [ea], [eb])
                            ss = nextbank4()
                            B.mm(ss.ap[0:64, 0:GW], ones_f.ap[0:64, 0:64], eb.ap[0:64, :], True, True, [ones_f, eb], [ss])
                            B.act(ec.ap[0:64, :], ss.ap[0:64, 0:GW], AF.Ln, [ss, B.eps_col], [ec], scale=1.0 / 64, bias=B.eps_col.ap[0:64, 0:1])
                            B.act(ec.ap[0:64, :], ec.ap[0:64, :], AF.Exp, [ec], [ec], scale=-0.5)
                            B.tt("dve", ea.ap[0:64, :], ea.ap[0:64, :], ec.ap[0:64, :], ALU.mult, [ea, ec], [ea])
                            tq = tmpo[0]
                            B.copy("act", tq.ap[po:po + 64, :], ea.ap[0:64, :], [ea], [tq])
                            mcg = 3 + h // 2
                            B.op("dve", lambda e: e.scalar_tensor_tensor(
                                mixT.ap[po:po + 64, mcg, gsl], tq.ap[po:po + 64, :], g1col.ap[po:po + 64, 0:1],
                                mixT.ap[po:po + 64, mcg, gsl], ALU.mult, ALU.mult), [tq, g1col, mixT], [mixT])

                    units.append([s0, s1])
        return units

    pairs = [("sb", p) for p in range(3)] + [("df", p) for p in range(3)]
    load_pair(0, 0, 0)
    run_pipeline(B.mem_units(mkT, mv, mqT, mixT, PT, ea, tmpo[1], zc, accc), 2)
    for pi, (kind, p) in enumerate(pairs):
        slot = pi % 2
        if pi + 1 < len(pairs):
            k2, p2 = pairs[pi + 1]
            load_pair(p2 if k2 == "sb" else 3 + p2, 2 * p2 if k2 == "sb" else 6 + 2 * p2, (pi + 1) % 2)
        for h in (2 * p, 2 * p + 1):
            if kind == "sb":
                run_pipeline(sb_units(h, slot), 3)
            else:
                run_pipeline(df_units(h, slot), 2)

    B.sa.reset(mA)
    wo = B.tile([128, KC, D], BF16, "wo")
    wst2 = [B.tile([128, KC, 128], F32, "wst2_%d" % i) for i in range(2)]
    gbc = B.tile([128, D], F32, "gbc")
    bbc = B.tile([128, D], F32, "bbc")
    xr = [B.tile([128, D], F32, "xr%d" % i) for i in range(2)]
    vv = [B.tile([128, D], F32, "vv%d" % i) for i in range(2)]
    st = B.tile([128, 8], F32, "st")
    fz2 = B.tile([128, 16], F32, "fz2")
    phaseC_tiles = [wo, gbc, bbc, st, fz2] + wst2 + xr + vv
    B.op("pool", lambda e: e.memset(fz2.ap, 0.0), [], phaseB_tiles + phaseC_tiles)
    B.load_w(w_o, D, wo, wst2)
    B.dma(gbc.ap, ln_g.ap[0:1, :].partition_broadcast(128), [ln_g], [gbc])
    B.dma(bbc.ap, ln_b.ap[0:1, :].partition_broadcast(128), [ln_b], [bbc])
    outs = []
    for j in range(NJ):
        outs.append(B.out_block(j, mixT, wo, x_own, xr[j % 2], vv[j % 2], st, gbc, bbc, y_out, nextbank))
    B.l0_tiles = phaseB_tiles + phaseC_tiles + [qT_sb, qT_df, mqT, mixT, mkT, mv]
    B.eps_ready = True
    return outs


DBG = {}
QCOLS = [0, 4, 1, 5, 2, 6, 3, 7, 8, 8, 9, 9, 10, 10, 11, 11]


def _qpos(hq):
    if hq < 4:
        return hq, 0
    if hq < 8:
        return hq - 4, 64
    return 4 + hq - 8, 0


def build_l1(S, stop=None):
    B = Builder(S, 1)
    outs = l1_body(B, False, stop)
    B.sch.emit(final_wait_ops=outs)
    return B


def l1_body(B, fused, stop=None):
    nc = B.nc
    S = B.S
    NB, NJ, NO, GW, NQG = B.NB, B.NJ, B.NO, B.GW, B.NQG
    OW = min(512, NO)
    NOT = NO // OW
    BPT = OW // 128
    PS = B.psum
    if fused:
        x1_own = B.x1_own_d
        x1_halo = B.x1_halo_d
        pos_own = B.dram["pos_own"]
        memT = B.dram["memT"]
    else:
        x1_own = B.din("x1_own", [NO, D])
        x1_halo = B.din("x1_halo", [NO, D])
        pos_own = B.din("pos_own", [1, NO], I32)
        memT = B.din("memT", [D, MEM_LEN])
    pos_halo = B.din("pos_halo", [1, NO], I32)
    w1_kv = B.din("w1_kv", [D, 960])
    w1_q = B.din("w1_q", [D, 2048])
    w1_g = B.din("w1_g", [D, 1024])
    w1_mkv = B.din("w1_mkv", [D, 512])
    w1_o = B.din("w1_o", [D, D])
    ln1_g = B.din("ln1_g", [1, D])
    ln1_b = B.din("ln1_b", [1, D])
    sinks_x = B.din("sinks_x", [1, 1536])
    invf1_d = B.din("invf1", [128, 1])
    coef1_d = B.din("coefp1", [128, 1])
    masks1_d = B.din("masks1", [128, 3 * 128], BF16)
    ident_d = B.din("ident", [128, 128])
    y_out = B.dout("y", [NO, D])

    pbc = [0]

    def nextbank():
        b = PS[pbc[0] % 8]
        pbc[0] += 1
        return b

    if fused:
        B.sa.reset(B.mark_consts)
    new_tiles = []

    def tl(shape, dt, name):
        t = B.tile(shape, dt, name)
        new_tiles.append(t)
        return t

    ident = tl([128, 128], F32, "ident")
    masks1 = tl([128, 3, 128], BF16, "masks1")
    invf1 = tl([128, 1], F32, "invf1")
    coef1 = tl([128, 1], F32, "coef1")
    if not fused:
        B.ones_f = tl([128, 128], F32, "ones_f")
        B.eps_col = tl([128, 1], F32, "eps")
        B.lamw = B.eps_col
    qT1 = tl([128, 8, NO], BF16, "qT1")
    kT1o = tl([128, 2, NO], BF16, "kT1o")
    kT1h = tl([128, 2, NO], BF16, "kT1h")
    V1o = tl([128, NJ, 3, 65], BF16, "V1o")
    V1h = tl([128, NJ, 3, 65], BF16, "V1h")
    mqT1 = tl([128, 2, NO], BF16, "mqT1")
    mixT1 = tl([128, 8, NO], BF16, "mixT1")
    mkT1 = tl([128, 2, MEM_LEN], BF16, "mkT1")
    mv1 = tl([128, 2, 4, 65], BF16, "mv1")
    fz = tl([128, 16], F32, "fz1")
    mP = B.sa.mark()
    xT1o = tl([128, KC, NO], BF16, "xT1o")
    xT1h = [tl([128, KC, OW], BF16, "xT1h%d" % i) for i in range(2)]
    xin = [tl([128, D], F32, "xin%d" % i) for i in range(2)]
    wb1 = tl([128, KC, 1024], BF16, "wb1")
    wst = [tl([128, KC, 128], F32, "wst1_%d" % i) for i in range(2)]
    posi = tl([128, OW], I32, "posi1")
    posf = tl([128, OW], F32, "posf1")
    t1 = tl([128, OW], F32, "t11")
    cosT = tl([128, OW], F32, "cosT1")
    sinS = tl([128, OW], F32, "sinS1")
    tmpa = tl([128, OW], F32, "tmpa1")
    tmpb = tl([128, OW], F32, "tmpb1")
    xs_m = tl([128, KC, MEM_LEN], F32, "xs_m")
    if fused:
        B.op("pool", lambda e: e.memset(fz.ap, 0.0), [], B.l0_tiles + new_tiles)
    else:
        B.memset("pool", B.ones_f.ap, 1.0, [B.ones_f])
        B.memset("pool", B.eps_col.ap, LN_EPS, [B.eps_col])
    B.dma(ident.ap, ident_d.ap[:, :], [ident_d], [ident])
    B.dma(masks1.ap.rearrange("p a b -> p (a b)"), masks1_d.ap[:, :], [masks1_d], [masks1])
    B.dma(invf1.ap, invf1_d.ap[:, :], [invf1_d], [invf1])
    B.dma(coef1.ap, coef1_d.ap[:, :], [coef1_d], [coef1])

    B.mem_kv(memT, w1_mkv, wb1, wst, xs_m, xT1h[0], mkT1, mv1, nextbank)
    B.memset("pool", V1o.ap, 1.0, [V1o])
    B.memset("pool", V1h.ap, 1.0, [V1h])

    def transpose_block(src_d, row0, xin_t, dst, c0):
        B.dma(xin_t.ap, src_d.ap[row0:row0 + 128, :], [src_d], [xin_t])
        for half in range(2):
            ps = nextbank()
            for q in range(4):
                kc = half * 4 + q
                B.op("pe", lambda e, ps=ps, q=q, kc=kc: e.transpose(ps.ap[:, q * 128:(q + 1) * 128],
                                                                   xin_t.ap[:, kc * 128:(kc + 1) * 128], ident.ap),
                     [xin_t, ident], [ps])
            B.copy("act" if half == 0 else "dve", dst.ap[:, half * 4:(half + 1) * 4, c0:c0 + 128],
                   ps.ap[:, 0:512].rearrange("p (a b) -> p a b", a=4), [ps], [dst])

    B.load_w(w1_kv, 960, wb1, wst, 0)
    xc = [0]
    for u in range(NOT):
        osl = slice(u * OW, (u + 1) * OW)
        xh = xT1h[u % 2]
        for blk in range(BPT):
            j = u * BPT + blk
            transpose_block(x1_own, j * 128, xin[xc[0] % 2], xT1o, j * 128)
            xc[0] += 1
            transpose_block(x1_halo, j * 128, xin[xc[0] % 2], xh, blk * 128)
            xc[0] += 1
        xo_t = T(xT1o.ap[:, :, osl], "xo_t")
        xo_t.res = xT1o.res
        for (xsrc_t, pos_d, kdst, vdst) in ((xo_t, pos_own, kT1o, V1o), (xh, pos_halo, kT1h, V1h)):
            B.rope_tables(pos_d, u * OW, OW, posi, posf, t1, cosT, sinS, invf1, coef1)
            for cg in range(2):
                psK = nextbank()
                B.proj_fm(psK, wb1, cg * 128, xsrc_t, OW)
                psP = nextbank()
                B.proj_fm(psP, wb1, 256 + cg * 128, xsrc_t, OW)
                B.rope_evac(psK, psP, kdst.ap[:, cg, osl], OW, cosT, sinS, tmpa, tmpb, kdst)
            for blk in range(BPT):
                j = u * BPT + blk
                ps = nextbank()
                for kc in range(KC):
                    B.mm(ps.ap[:, 0:192], xsrc_t.ap[:, kc, blk * 128:(blk + 1) * 128], wb1.ap[:, kc, 512:704],
                         kc == 0, kc == KC - 1, [xsrc_t, wb1], [ps])
                B.copy("act", vdst.ap[:, j, :, 0:64], ps.ap[:, 0:192].rearrange("p (h d) -> p h d", h=3), [ps], [vdst])
        for cg in range(2):
            ps = nextbank()
            B.proj_fm(ps, wb1, 704 + cg * 128, xo_t, OW)
            B.copy("act", mqT1.ap[:, cg, osl], ps.ap[:, 0:OW], [ps], [mqT1])
    if stop == "proj0":
        return [B.dma(y_out.ap[0:128, :], xin[0].ap, [xin[0], kT1o, kT1h, V1o, V1h, mqT1], [y_out])]
    for rnd in range(3):
        if rnd < 2:
            B.load_w(T(w1_q.ap[:, rnd * 1024:(rnd + 1) * 1024], "w1q"), 1024, wb1, wst, 0)
        else:
            B.load_w(w1_g, 1024, wb1, wst, 0)
        for u in range(NOT):
            osl = slice(u * OW, (u + 1) * OW)
            xo_t = T(xT1o.ap[:, :, osl], "xo_t")
            xo_t.res = xT1o.res
            if rnd < 2:
                B.rope_tables(pos_own, u * OW, OW, posi, posf, t1, cosT, sinS, invf1, coef1)
                for cg in range(4):
                    psK = nextbank()
                    B.proj_fm(psK, wb1, cg * 128, xo_t, OW)
                    psP = nextbank()
                    B.proj_fm(psP, wb1, 512 + cg * 128, xo_t, OW)
                    B.rope_evac(psK, psP, qT1.ap[:, rnd * 4 + cg, osl], OW, cosT, sinS, tmpa, tmpb, qT1)
            else:
                for cg in range(8):
                    ps = nextbank()
                    B.proj_fm(ps, wb1, cg * 128, xo_t, OW)
                    B.act(mixT1.ap[:, cg, osl], ps.ap[:, 0:OW], AF.Silu, [ps], [mixT1])

    if stop == "proj":
        return [B.dma(y_out.ap[0:128, :], xin[0].ap, [xin[0], kT1o, kT1h, V1o, V1h, mqT1, qT1, mixT1], [y_out])]
    proj_tiles = [xT1o] + xT1h + xin + [wb1] + wst + [posi, posf, t1, cosT, sinS, tmpa, tmpb, xs_m]
    B.sa.reset(mP)
    esink = B.tile([128, 1536], F32, "esink")
    PT1 = [B.tile([128, 512], BF16, "PT1_%d" % i) for i in range(4)]
    rd = B.tile([128, 1024], F32, "rd1")
    bcs = B.tile([128, 1024], F32, "bcs1")
    ea = B.tile([128, 512], F32, "ea1")
    tq = [B.tile([128, 512], F32, "tq1_%d" % i) for i in range(2)]
    wo1 = B.tile([128, KC, D], BF16, "wo1")
    wst2 = [B.tile([128, KC, 128], F32, "wst1b_%d" % i) for i in range(2)]
    gbc = B.tile([128, D], F32, "gbc1")
    bbc = B.tile([128, D], F32, "bbc1")
    xr = [B.tile([128, D], F32, "xr1_%d" % i) for i in range(2)]
    vv = [B.tile([128, D], F32, "vv1_%d" % i) for i in range(2)]
    st = B.tile([128, 8], F32, "st1")
    att_tiles = [esink] + PT1 + [rd, bcs, ea] + tq + [wo1] + wst2 + [gbc, bbc] + xr + vv + [st]
    B.op("pool", lambda e: e.memset(fz.ap, 0.0), [], proj_tiles + att_tiles)
    B.rd, B.bcs, B.b4 = rd, bcs, [0]
    esraw = B.tile([128, 1536], F32, "esraw")
    B.op("pool", lambda e: e.memset(fz.ap, 0.0), [], [esraw, fz])
    B.dma(esraw.ap, sinks_x.ap[0:1, :].partition_broadcast(128), [sinks_x], [esraw])
    B.act(esink.ap, esraw.ap, AF.Exp, [esraw], [esink])
    B.load_w(w1_o, D, wo1, wst2)
    B.dma(gbc.ap, ln1_g.ap[0:1, :].partition_broadcast(128), [ln1_g], [gbc])
    B.dma(bbc.ap, ln1_b.ap[0:1, :].partition_broadcast(128), [ln1_b], [bbc])

    zc, accc = [0], [0]
    if stop == "pre_att":
        return [B.dma(y_out.ap[0:128, :], gbc.ap, [gbc, bbc, wo1, esink], [y_out])]
    run_pipeline(B.mem_units(mkT1, mv1, mqT1, mixT1, PT1, ea, tq[1], zc, accc), 2)
    if stop == "mem":
        return [B.dma(y_out.ap[0:128, :], gbc.ap, [gbc, bbc, wo1, esink, mixT1], [y_out])]
    units = []
    for j in range(NJ):
        jsl = slice(j * 128, (j + 1) * 128)
        for kvh in range(3):
            acc = PS[4 + accc[0] % 4]
            accc[0] += 1
            for bi, (kTx, Vx) in enumerate(((kT1h, V1h), (kT1o, V1o))):
                zi = zc[0]
                zc[0] += 1
                Sb = B.nextbank4()
                PTt = PT1[zi % 4]
                mi = 0 if bi == 1 else (2 if j == 0 else 1)

                def s0(Sb=Sb, PTt=PTt, kTx=kTx, kvh=kvh, jsl=jsl, mi=mi):
                    for g in range(4):
                        hq = kvh * 4 + g
                        cgq, po = _qpos(hq)
                        cgk = 0 if kvh < 2 else 1
                        B.mm(Sb.ap[:, g * 128:(g + 1) * 128], kTx.ap[po:po + 64, cgk, jsl], qT1.ap[po:po + 64, cgq, jsl],
                             True, True, [kTx, qT1], [Sb])
                    B.act(PTt.ap[:, 0:512], Sb.ap[:, 0:512], AF.Exp, [Sb], [PTt], scale=0.125)
                    for g in range(4):
                        if DBG.get("nomask"):
                            break
                        B.tt("dve" if (g % 2 == 0 or DBG.get("nopool")) else "pool", PTt.ap[:, g * 128:(g + 1) * 128], PTt.ap[:, g * 128:(g + 1) * 128],
                             masks1.ap[:, mi, :], ALU.mult, [PTt, masks1], [PTt])

                def s1(acc=acc, PTt=PTt, Vx=Vx, kvh=kvh, j=j, jsl=jsl, bi=bi):
                    B.mm(acc.ap[0:65, 0:512], Vx.ap[:, j, kvh, 0:65], PTt.ap[:, 0:512], bi == 0, bi == 1, [Vx, PTt], [acc], skip=True)
                    if bi == 1:
                        B.softmax_norm(acc, slice(0, 512), add_row=None if DBG.get("noadd") else (esink.ap[64:65, kvh * 512:(kvh + 1) * 512], esink), W=512)
                        B.tt("dve", ea.ap[0:64, :], acc.ap[0:64, 0:512], bcs.ap[0:64, 0:512], ALU.mult, [acc, bcs], [ea])
                        tqt = tq[0]
                        for g in range(4):
                            hq = kvh * 4 + g
                            mcg, po = hq // 2, (hq % 2) * 64
                            gs = slice(g * 128, (g + 1) * 128)
                            B.copy("act", tqt.ap[po:po + 64, gs], ea.ap[0:64, gs], [ea], [tqt])
                            B.tt("dve", mixT1.ap[po:po + 64, mcg, jsl], tqt.ap[po:po + 64, gs],
                                 mixT1.ap[po:po + 64, mcg, jsl], ALU.mult, [tqt, mixT1], [mixT1])

                units.append([s0, s1])
    run_pipeline(units, 2)
    if stop == "att":
        return [B.dma(y_out.ap[0:128, :], gbc.ap, [gbc, bbc, wo1, esink, mixT1], [y_out])]

    outs = []
    for j in range(NJ):
        outs.append(B.out_block(j, mixT1, wo1, x1_own, xr[j % 2], vv[j % 2], st, gbc, bbc, y_out, nextbank))
    return outs


def layer_norm_store(B, vv_t, scratch, st, gbc, bbc, y_out, j):
    B.op("dve", lambda e: e.tensor_reduce(st.ap[:, 0:1], vv_t.ap, AX.X, ALU.add), [vv_t], [st])
    B.ts("dve", st.ap[:, 1:2], st.ap[:, 0:1], -1.0 / D, None, ALU.mult, None, [st], [st])
    B.op("act", lambda e: e.activation(out=scratch.ap, in_=vv_t.ap, func=AF.Square, bias=st.ap[:, 1:2], scale=1.0,
                                       accum_out=st.ap[:, 2:3]), [vv_t, st], [scratch, st])
    B.act(st.ap[:, 3:4], st.ap[:, 2:3], AF.Ln, [st, B.eps_col], [st], scale=1.0 / D, bias=B.eps_col.ap[:, 0:1])
    B.act(st.ap[:, 3:4], st.ap[:, 3:4], AF.Exp, [st], [st], scale=-0.5)
    B.ts("dve", vv_t.ap, vv_t.ap, st.ap[:, 1:2], st.ap[:, 3:4], ALU.add, ALU.mult, [vv_t, st], [vv_t])
    B.tt("pool", vv_t.ap, vv_t.ap, gbc.ap, ALU.mult, [vv_t, gbc], [vv_t])
    B.tt("pool", vv_t.ap, vv_t.ap, bbc.ap, ALU.add, [vv_t, bbc], [vv_t])
    return B.dma(y_out.ap[j * 128:(j + 1) * 128, :], vv_t.ap, [vv_t], [y_out])


def _own_idx(S, c):
    NB = S // 128
    return np.concatenate([np.arange((4 * j + c) * 128, (4 * j + c + 1) * 128) for j in range(NB // 4)])


def _mask_dram(c):
    m = _masks(c)
    return np.ascontiguousarray(np.transpose(m, (1, 0, 2)).reshape(128, 9 * 128))


def prep_l0(x, mem, positions, w_in, w_memkv, diff_lambda, diff_subln, w_out, ln_g, ln_b):
    S = x.shape[1]
    sbq, sbk, sbv = w_in[:, 0:384], w_in[:, 384:768], w_in[:, 768:1152]
    dfq, dfk, dfv = w_in[:, 1152:1536], w_in[:, 1536:1920], w_in[:, 1920:2304]
    mq, gate = w_in[:, 2304:2560], w_in[:, 2560:3584]
    perm = _partner_perm(384, 32, 8)
    w_k = np.ascontiguousarray(np.concatenate([sbk, dfk, dfk[:, perm]], axis=1))
    w_v = np.ascontiguousarray(np.concatenate([sbv, dfv], axis=1))
    w_q = np.ascontiguousarray(np.concatenate([sbq, dfq, dfq[:, perm], mq], axis=1))
    w_g = np.ascontiguousarray(gate)
    invf, coefp = _host_consts(0)
    maps = []
    for core in range(8):
        b, c = core // 4, core % 4
        own = _own_idx(S, c)
        maps.append({
            "xT_all": np.ascontiguousarray(x[b].T),
            "xT_own": np.ascontiguousarray(x[b][own].T),
            "x_own": np.ascontiguousarray(x[b][own]),
            "pos_all": np.ascontiguousarray(positions[b][None, :]).astype(np.int32),
            "pos_own": np.ascontiguousarray(positions[b][own][None, :]).astype(np.int32),
            "w_k": w_k, "w_v": w_v, "w_q": w_q, "w_g": w_g,
            "memT": np.ascontiguousarray(mem[b].T),
            "w_mkv": np.ascontiguousarray(w_memkv),
            "w_o": np.ascontiguousarray(w_out),
            "ln_g": np.ascontiguousarray(ln_g[None, :]),
            "ln_b": np.ascontiguousarray(ln_b[None, :]),
            "dlam": np.ascontiguousarray(diff_lambda.reshape(1, 128)),
            "subln": np.ascontiguousarray(np.concatenate([diff_subln, diff_subln])[:, None]),
            "invf": invf, "coefp": coefp,
            "masks": _mask_dram(c),
        })
    return maps


def gather_own(results, key, S, nb=2):
    out = np.zeros((nb, S, D), np.float32)
    for core in range(8):
        b, c = core // 4, core % 4
        out[b, _own_idx(S, c)] = results[core][key]
    return out


def _halo_idx(S, c):
    NB = S // 128
    out = []
    for j in range(NB // 4):
        g = 4 * j + c - 1
        out.append(np.arange(g * 128, (g + 1) * 128) if g >= 0 else np.full(128, -1))
    return np.concatenate(out)


def _masks1(c):
    k = np.arange(128)[:, None]
    q = np.arange(128)[None, :]
    m = np.zeros((3, 128, 128), np.float32)
    m[0] = (k <= q)
    m[1] = (k > q)
    m[2] = (k > q) if c > 0 else 0.0
    return np.ascontiguousarray(np.transpose(m, (1, 0, 2)).reshape(128, 3 * 128)).astype(ml_dtypes.bfloat16)


def l1_weights(w_in, w_memkv, sinks, w_out, ln_g, ln_b):
    cq, ck, cv = w_in[:, 0:768], w_in[:, 768:960], w_in[:, 960:1152]
    mq, gate = w_in[:, 1152:1408], w_in[:, 1408:2432]
    kd = np.concatenate([ck[:, 0:64], ck[:, 64:128], ck[:, 128:192], ck[:, 128:192]], axis=1)
    pk = _partner_perm(256, 64, 16)
    qr = np.concatenate([cq[:, h * 64:(h + 1) * 64] for h in QCOLS], axis=1)
    pq = _partner_perm(512, 64, 16)
    q0, q1 = qr[:, 0:512], qr[:, 512:1024]
    invf1, coef1 = _host_consts(1)
    return {
        "w1_kv": np.ascontiguousarray(np.concatenate([kd, kd[:, pk], cv, mq], axis=1)),
        "w1_q": np.ascontiguousarray(np.concatenate([q0, q0[:, pq], q1, q1[:, pq]], axis=1)),
        "w1_g": np.ascontiguousarray(gate),
        "w1_mkv": np.ascontiguousarray(w_memkv),
        "w1_o": np.ascontiguousarray(w_out),
        "ln1_g": np.ascontiguousarray(ln_g[None, :]),
        "ln1_b": np.ascontiguousarray(ln_b[None, :]),
        "sinks_x": np.ascontiguousarray(np.repeat(sinks, 128)[None, :]),
        "invf1": invf1, "coefp1": coef1,
        "ident": np.eye(128, dtype=np.float32),
    }


def prep_l1(x1, mem, positions, w_in, w_memkv, sinks, w_out, ln_g, ln_b):
    S = x1.shape[1]
    wd = l1_weights(w_in, w_memkv, sinks, w_out, ln_g, ln_b)
    maps = []
    for core in range(8):
        b, c = core // 4, core % 4
        own = _own_idx(S, c)
        hidx = _halo_idx(S, c)
        xh = np.where((hidx >= 0)[:, None], x1[b][np.maximum(hidx, 0)], np.float32(0))
        ph = np.where(hidx >= 0, positions[b][np.maximum(hidx, 0)], 0)
        m = dict(wd)
        m.update({
            "x1_own": np.ascontiguousarray(x1[b][own]),
            "x1_halo": np.ascontiguousarray(xh.astype(np.float32)),
            "pos_own": np.ascontiguousarray(positions[b][own][None, :]).astype(np.int32),
            "pos_halo": np.ascontiguousarray(ph[None, :]).astype(np.int32),
            "memT": np.ascontiguousarray(mem[b].T),
            "masks1": _masks1(c),
        })
        maps.append(m)
    return maps


LAMBDA_INIT0 = 0.8 - 0.6 * math.exp(-0.3 * 0)


def kernel(x, mem, positions, w_in_even, w_memkv_even, diff_lambda_even, diff_subln_even, w_out_even, ln_g_even,
           ln_b_even, w_in_odd, w_memkv_odd, sinks_odd, w_out_odd, ln_g_odd, ln_b_odd):
    f = lambda a: np.asarray(a)
    x, mem, positions = f(x), f(mem), f(positions)
    S = x.shape[1]
    cores = list(range(8))
    B0 = build_l0(S, LAMBDA_INIT0)
    maps0 = prep_l0(x, mem, positions, f(w_in_even)[0], f(w_memkv_even)[0], f(diff_lambda_even)[0], f(diff_subln_even)[0],
                    f(w_out_even)[0], f(ln_g_even)[0], f(ln_b_even)[0])
    res0 = run_bass_kernel_spmd(B0.nc, maps0, core_ids=cores)
    x1 = gather_own(res0.results, "y", S)
    B1 = build_l1(S)
    maps1 = prep_l1(x1, mem, positions, f(w_in_odd)[0], f(w_memkv_odd)[0], f(sinks_odd)[0], f(w_out_odd)[0],
                    f(ln_g_odd)[0], f(ln_b_odd)[0])
    res1 = run_bass_kernel_spmd(B1.nc, maps1, core_ids=cores)
    return gather_own(res1.results, "y", S)
```

```python
import math
import contextlib
import numpy as np
import ml_dtypes
import concourse.bass as bass
import concourse.mybir as mybir
from concourse.bass_utils import run_bass_kernel_spmd

F32 = mybir.dt.float32
BF16 = mybir.dt.bfloat16
I32 = mybir.dt.int32
AF = mybir.ActivationFunctionType
ALU = mybir.AluOpType
AX = mybir.AxisListType

D = 1024
KC = 8
DEPTH = 2
ALPHA = (2 * DEPTH) ** 0.25
LN_EPS = 1e-5
ROPE_THETA = 500000.0
MEM_LEN = 256
PI = math.pi


class Res:
    __slots__ = ("lw", "rd", "name")

    def __init__(self, name=""):
        self.lw = None
        self.rd = []
        self.name = name


class Sched:
    ENGS = ("pe", "act", "dve", "pool", "sp")
    NSLOT = 14

    def __init__(self, nc):
        self.nc = nc
        self.ops = []

    def add(self, eng, fn, reads=(), writes=(), dma=False):
        idx = len(self.ops)
        deps = set()
        for r in reads:
            if r.lw is not None:
                deps.add(r.lw)
        for w in writes:
            if w.lw is not None:
                deps.add(w.lw)
            deps.update(w.rd)
        for r in reads:
            r.rd.append(idx)
        for w in writes:
            w.lw = idx
            w.rd = []
        deps.discard(idx)
        self.ops.append([eng, fn, deps, dma])
        return idx

    def emit(self, final_wait_ops=()):
        nc = self.nc
        ops = self.ops
        n = len(ops)
        has_dep = [False] * n
        for i, (eng, fn, deps, dma) in enumerate(ops):
            for d in deps:
                if ops[d][0] == "pe" and eng == "pe" and not ops[d][3] and not dma:
                    continue
                has_dep[d] = True
        for d in final_wait_ops:
            has_dep[d] = True
        cnt = {e: 0 for e in self.ENGS}
        dcnt = {e: 0 for e in self.ENGS}
        sig = [None] * n
        for i, (eng, fn, deps, dma) in enumerate(ops):
            if dma:
                k = dcnt[eng]
                dcnt[eng] += 1
                sig[i] = ("d", eng, k % self.NSLOT, 16 * (k // self.NSLOT + 1))
            elif has_dep[i]:
                cnt[eng] += 1
                sig[i] = ("c", eng, cnt[eng])
        engs_used = [e for e in self.ENGS if any(o[0] == e for o in ops)]
        with contextlib.ExitStack() as st:
            csem = {e: st.enter_context(nc.semaphore("c_" + e)) for e in engs_used}
            dsem = {}
            for e in engs_used:
                if dcnt[e] > 0:
                    dsem[e] = [st.enter_context(nc.semaphore("d_%s_%d" % (e, s)))
                               for s in range(min(self.NSLOT, dcnt[e]))]
            block = st.enter_context(nc.Block())
            engobj = {"pe": "tensor", "act": "scalar", "dve": "vector", "pool": "gpsimd", "sp": "sync"}

            def make_stream(ename):
                def stream(eng):
                    waited_c = {}
                    waited_d = {}
                    for i, (e, fn, deps, dma) in enumerate(ops):
                        if e != ename:
                            continue
                        need_c = {}
                        need_d = {}
                        for d in deps:
                            s = sig[d]
                            if s is None:
                                continue
                            if s[0] == "c":
                                if s[1] == "pe" and ename == "pe" and not dma:
                                    continue
                                need_c[s[1]] = max(need_c.get(s[1], 0), s[2])
                            else:
                                key = (s[1], s[2])
                                need_d[key] = max(need_d.get(key, 0), s[3])
                        if dma:
                            s = sig[i]
                            if s[3] > 16:
                                key = (s[1], s[2])
                                need_d[key] = max(need_d.get(key, 0), s[3] - 16)
                        for se, v in need_c.items():
                            if waited_c.get(se, 0) < v:
                                eng.wait_ge(csem[se], v)
                                waited_c[se] = v
                        for key, v in need_d.items():
                            if waited_d.get(key, 0) < v:
                                eng.wait_ge(dsem[key[0]][key[1]], v)
                                waited_d[key] = v
                        ins = fn(eng)
                        s = sig[i]
                        if s is not None:
                            if s[0] == "c":
                                ins.then_inc(csem[ename], 1)
                            else:
                                ins.then_inc(dsem[ename][s[2]], 16)
                    if ename == "sp":
                        for d in final_wait_ops:
                            s = sig[d]
                            if s[0] == "c":
                                eng.wait_ge(csem[s[1]], s[2])
                            else:
                                eng.wait_ge(dsem[s[1]][s[2]], s[3])
                return stream

            for e in engs_used:
                getattr(block, engobj[e])(make_stream(e))


class SbufAlloc:
    def __init__(self, nc, nbytes=207 * 1024):
        self.nc = nc
        self.arena = nc.alloc_sbuf_tensor("arena", [128, nbytes], mybir.dt.uint8)
        self.off = 0
        self.limit = nbytes
        self.peak = 0

    def mark(self):
        return self.off

    def reset(self, m):
        self.off = m

    def tile(self, shape, dtype):
        assert shape[0] == 128
        esz = {F32: 4, BF16: 2, I32: 4}[dtype]
        nel = int(np.prod(shape[1:]))
        nbytes = (esz * nel + 63) // 64 * 64
        off = self.off
        self.off += nbytes
        self.peak = max(self.peak, self.off)
        assert self.off <= self.limit, ("SBUF overflow", self.off)
        ap = self.arena.ap()[:, off:off + esz * nel].bitcast(dtype)
        if len(shape) == 3:
            ap = ap.rearrange("p (a b) -> p a b", a=shape[1])
        elif len(shape) == 4:
            ap = ap.rearrange("p (a b c) -> p a b c", a=shape[1], b=shape[2])
        return ap


class T:
    def __init__(self, ap, name=""):
        self.ap = ap
        self.res = Res(name)


class Builder:
    def __init__(self, S, layer):
        self.S = S
        self.layer = layer
        self.NB = S // 128
        self.NJ = self.NB // 4
        self.NO = self.NJ * 128
        self.NR = self.NB // 16
        self.NE = self.NO + self.NR * 128
        self.NEB = self.NE // 128
        self.QGB = 4
        self.GW = 512
        self.NQG = self.NR
        self.groups = [(i * 512, 512) for i in range(self.NR)] + [(self.NO, self.NR * 128)]
        self.TW = min(512, S)
        self.NT = S // self.TW
        self.nc = bass.Bass("TRN2", target_bir_lowering=False)
        self.sch = Sched(self.nc)
        self.sa = SbufAlloc(self.nc)
        self.dram = {}
        self.psum = [T(self.nc.alloc_psum_tensor("ps%d" % i, [128, 512], F32).ap(), "ps%d" % i) for i in range(8)]

    def din(self, name, shape, dtype=F32):
        t = T(self.nc.dram_tensor(name, list(shape), dtype, kind="ExternalInput").ap(), name)
        self.dram[name] = t
        return t

    def dout(self, name, shape, dtype=F32):
        t = T(self.nc.dram_tensor(name, list(shape), dtype, kind="ExternalOutput").ap(), name)
        self.dram[name] = t
        return t

    def dscr(self, name, shape, dtype):
        t = T(self.nc.dram_tensor(name, list(shape), dtype).ap(), name)
        self.dram[name] = t
        return t

    def tile(self, shape, dtype, name=""):
        return T(self.sa.tile(shape, dtype), name)

    def op(self, eng, fn, reads=(), writes=(), dma=False):
        return self.sch.add(eng, fn, [t.res for t in reads], [t.res for t in writes], dma)

    def dma(self, out_ap, in_ap, reads, writes, q="sp"):
        return self.op(q, lambda e: e.dma_start(out=out_ap, in_=in_ap), reads, writes, dma=True)

    def mm(self, out_ap, lhsT, rhs, start, stop, reads, writes, skip=False):
        if skip:
            return self.op("pe", lambda e: e.matmul(out_ap, lhsT=lhsT, rhs=rhs, start=start, stop=stop,
                                                    skip_group_check=True), reads, writes)
        return self.op("pe", lambda e: e.matmul(out_ap, lhsT=lhsT, rhs=rhs, start=start, stop=stop), reads, writes)

    def act(self, out_ap, in_ap, func, reads, writes, scale=1.0, bias=0.0):
        return self.op("act", lambda e: e.activation(out=out_ap, in_=in_ap, func=func, bias=bias, scale=scale),
                       reads, writes)

    def tt(self, eng, out_ap, a, b, op, reads, writes):
        return self.op(eng, lambda e: e.tensor_tensor(out_ap, a, b, op), reads, writes)

    def ts(self, eng, out_ap, a, s1, s2, op0, op1, reads, writes):
        if op1 is None:
            return self.op(eng, lambda e: e.tensor_scalar(out_ap, a, s1, None, op0), reads, writes)
        return self.op(eng, lambda e: e.tensor_scalar(out_ap, a, s1, s2, op0, op1), reads, writes)

    def copy(self, eng, out_ap, in_ap, reads, writes):
        if eng == "act":
            return self.op("act", lambda e: e.copy(out_ap, in_ap), reads, writes)
        return self.op(eng, lambda e: e.tensor_copy(out_ap, in_ap), reads, writes)

    def memset(self, eng, ap, val, writes):
        return self.op(eng, lambda e: e.memset(ap, val), (), writes)

    def load_w(self, wd, n, wb, stage=None, c0=0):
        src = wd.ap.rearrange("(kc p) n -> p kc n", p=128)
        for k2 in range(0, KC, 2):
            self.dma(wb.ap[:, k2:k2 + 2, c0:c0 + n], src[:, k2:k2 + 2, :], [wd], [wb], q="pool")

    def rope_tables(self, pos_d, a, w, posi, posf, t1, cosT, sinS, invf, coefp):
        SC = 2 * PI * (1.0 - 1e-6)
        self.dma(posi.ap[:, 0:w], pos_d.ap[0:1, a:a + w].partition_broadcast(128), [pos_d], [posi])
        self.copy("dve", posf.ap[:, 0:w], posi.ap[:, 0:w], [posi], [posf])
        self.ts("dve", posf.ap[:, 0:w], posf.ap[:, 0:w], invf.ap[:, 0:1], None, ALU.mult, None, [posf, invf], [posf])
        self.copy("dve", posi.ap[:, 0:w], posf.ap[:, 0:w], [posf], [posi])
        self.copy("pool", t1.ap[:, 0:w], posi.ap[:, 0:w], [posi], [t1])
        self.tt("dve", t1.ap[:, 0:w], posf.ap[:, 0:w], t1.ap[:, 0:w], ALU.subtract, [posf, t1], [t1])
        self.act(sinS.ap[:, 0:w], t1.ap[:, 0:w], AF.Sin, [t1], [sinS], scale=SC)
        self.ts("dve", posf.ap[:, 0:w], posf.ap[:, 0:w], 0.25, None, ALU.add, None, [posf], [posf])
        self.copy("dve", posi.ap[:, 0:w], posf.ap[:, 0:w], [posf], [posi])
        self.copy("pool", t1.ap[:, 0:w], posi.ap[:, 0:w], [posi], [t1])
        self.tt("dve", t1.ap[:, 0:w], posf.ap[:, 0:w], t1.ap[:, 0:w], ALU.subtract, [posf, t1], [t1])
        self.act(cosT.ap[:, 0:w], t1.ap[:, 0:w], AF.Sin, [t1], [cosT], scale=SC)
        self.ts("pool", sinS.ap[:, 0:w], sinS.ap[:, 0:w], coefp.ap[:, 0:1], None, ALU.mult, None, [sinS, coefp], [sinS])

    def proj_fm(self, ps, wb, c0, xb, w, m=128):
        for kc in range(KC):
            self.mm(ps.ap[0:m, 0:w], wb.ap[:, kc, c0:c0 + m], xb.ap[:, kc, 0:w], kc == 0, kc == KC - 1, [wb, xb], [ps])


    def nextbank4(self):
        b = self.psum[self.b4[0] % 4]
        self.b4[0] += 1
        return b

    def softmax_norm(self, acc, rdcols, post_scale_col=None, add_row=None, W=None):
        rd, bcs, ones_f = self.rd, self.bcs, self.ones_f
        W = W or self.GW
        if add_row is not None:
            r2 = slice(rdcols.start + W, rdcols.stop + W)
            self.tt("dve", rd.ap[64:65, r2], acc.ap[64:65, 0:W], add_row[0], ALU.add, [acc, add_row[1]], [rd])
            self.op("dve", lambda e: e.reciprocal(rd.ap[64:65, rdcols], rd.ap[64:65, r2]), [rd], [rd])
        else:
            self.ts("dve", rd.ap[64:65, rdcols], acc.ap[64:65, 0:W], 1e-30, None, ALU.add, None, [acc], [rd])
            self.op("dve", lambda e: e.reciprocal(rd.ap[64:65, rdcols], rd.ap[64:65, rdcols]), [rd], [rd])
        if post_scale_col is not None:
            self.ts("dve", rd.ap[64:65, rdcols], rd.ap[64:65, rdcols], post_scale_col, None, ALU.mult, None, [rd, self.lamw], [rd])
        bc = self.nextbank4()
        self.mm(bc.ap[0:64, 0:W], ones_f.ap[64:65, 0:64], rd.ap[64:65, rdcols], True, True, [ones_f, rd], [bc])
        self.copy("act", bcs.ap[0:64, rdcols], bc.ap[0:64, 0:W], [bc], [bcs])

    def mem_units(self, mkT, mv, mqT, mixT, PT, ea, tq, zc, accc, groups):
        B = self
        PS = self.psum
        units = []
        for hm in range(4):
            cg, po = hm // 2, (hm % 2) * 64
            for (col0, W) in groups:
                acc = PS[4 + accc[0] % 4]
                accc[0] += 1
                gsl = slice(col0, col0 + W)
                for mb in range(2):
                    zi = zc[0]
                    zc[0] += 1
                    Sb = B.nextbank4()
                    PTt = PT[zi % len(PT)]

                    def s0(Sb=Sb, PTt=PTt, mb=mb, cg=cg, po=po, gsl=gsl, W=W):
                        B.mm(Sb.ap[:, 0:W], mkT.ap[po:po + 64, cg, mb * 128:(mb + 1) * 128], mqT.ap[po:po + 64, cg, gsl],
                             True, True, [mkT, mqT], [Sb])
                        B.act(PTt.ap[:, 0:W], Sb.ap[:, 0:W], AF.Exp, [Sb], [PTt], scale=0.125)

                    def s1(acc=acc, PTt=PTt, mb=mb, hm=hm, cg=cg, po=po, gsl=gsl, W=W):
                        B.mm(acc.ap[0:65, 0:W], mv.ap[:, mb, hm, 0:65], PTt.ap[:, 0:W], mb == 0, mb == 1, [mv, PTt], [acc], skip=True)
                        if mb == 1:
                            B.softmax_norm(acc, slice(0, W), W=W)
                            B.tt("dve", ea.ap[0:64, 0:W], acc.ap[0:64, 0:W], B.bcs.ap[0:64, 0:W], ALU.mult, [acc, B.bcs], [ea])
                            B.copy("act", tq.ap[po:po + 64, 0:W], ea.ap[0:64, 0:W], [ea], [tq])
                            B.tt("dve", mixT.ap[po:po + 64, 6 + cg, gsl], tq.ap[po:po + 64, 0:W], mixT.ap[po:po + 64, 6 + cg, gsl],
                                 ALU.mult, [tq, mixT], [mixT])

                    units.append([s0, s1])
        return units

    def mem_kv(self, memT, w_mkv, wb, wst, xs0, memb, mkT, mv, nextbank):
        B = self
        B.load_w(w_mkv, 512, wb, wst)
        B.dma(xs0.ap[:, :, 0:MEM_LEN], memT.ap.rearrange("(kc p) m -> p kc m", p=128), [memT], [xs0])
        B.copy("dve", memb.ap[:, :, 0:MEM_LEN], xs0.ap[:, :, 0:MEM_LEN], [xs0], [memb])
        for cg in range(2):
            ps = nextbank()
            B.proj_fm(ps, wb, cg * 128, memb, MEM_LEN)
            B.copy("act", mkT.ap[:, cg, :], ps.ap[:, 0:MEM_LEN], [ps], [mkT])
        B.memset("pool", mv.ap, 1.0, [mv])
        for mb in range(2):
            ps = nextbank()
            for kc in range(KC):
                B.mm(ps.ap[:, 0:256], memb.ap[:, kc, mb * 128:(mb + 1) * 128], wb.ap[:, kc, 256:512], kc == 0, kc == KC - 1,
                     [memb, wb], [ps])
            B.copy("act", mv.ap[:, mb, :, 0:64], ps.ap[:, 0:256].rearrange("p (h d) -> p h d", h=4), [ps], [mv])

    def out_block(self, j, mixT, wo, x_src, xr_t, vv_t, st, gbc, bbc, y_dst, nextbank):
        B = self
        B.dma(xr_t.ap, x_src.ap[j * 128:(j + 1) * 128, :], [x_src], [xr_t])
        for n in range(2):
            ps = nextbank()
            for kc in range(KC):
                B.mm(ps.ap[:, 0:512], mixT.ap[:, kc, j * 128:(j + 1) * 128], wo.ap[:, kc, n * 512:(n + 1) * 512],
                     kc == 0, kc == KC - 1, [mixT, wo], [ps])
            B.op("dve", lambda e, ps=ps, n=n: e.scalar_tensor_tensor(
                vv_t.ap[:, n * 512:(n + 1) * 512], xr_t.ap[:, n * 512:(n + 1) * 512], ALPHA, ps.ap[:, 0:512],
                ALU.mult, ALU.add), [ps, xr_t], [vv_t])
        return layer_norm_store(B, vv_t, xr_t, st, gbc, bbc, y_dst, j)

    def rope_evac(self, psK, psP, out_ap, w, cosT, sinS, tmpa, tmpb, out_t, scale=None):
        self.tt("dve", tmpa.ap[:, 0:w], psK.ap[:, 0:w], cosT.ap[:, 0:w], ALU.mult, [psK, cosT], [tmpa])
        self.tt("dve", tmpb.ap[:, 0:w], psP.ap[:, 0:w], sinS.ap[:, 0:w], ALU.mult, [psP, sinS], [tmpb])
        self.tt("pool", out_ap, tmpa.ap[:, 0:w], tmpb.ap[:, 0:w], ALU.add, [tmpa, tmpb], [out_t])


def _host_consts(layer):
    if layer == 0:
        hd, rot = 32, 8
    else:
        hd, rot = 64, 16
    half = rot // 2
    inv = np.exp(-(np.arange(half, dtype=np.float32) / half) * math.log(ROPE_THETA)).astype(np.float32)
    invf = np.zeros((128, 1), np.float32)
    coef = np.zeros((128, 1), np.float32)
    for r in range(128):
        d = r % hd
        if d < rot:
            invf[r, 0] = inv[d % half] / np.float32(2 * PI)
            coef[r, 0] = -1.0 if d < half else 1.0
    return invf, coef


def _partner_perm(ncols, hd, rot):
    half = rot // 2
    perm = np.arange(ncols)
    for c in range(ncols):
        d = c % hd
        if d < half:
            perm[c] = c + half
        elif d < rot:
            perm[c] = c - half
    return perm


def _masks(c):
    k = np.arange(128)[:, None]
    q = np.arange(128)[None, :]
    out = np.zeros((9, 128, 128), np.float32)
    out[8] = -1.0 * (k >= q)
    for r in range(4):
        if r < c:
            out[r] = 1.0
            out[4 + r] = 1.0
        elif r == c:
            out[r] = (k <= q)
            out[4 + r] = (k < q)
    return out.astype(ml_dtypes.bfloat16)


def run_pipeline(units, nst):
    n = len(units)
    for step in range(n + nst - 1):
        for s in range(nst):
            u = step - s
            if 0 <= u < n and units[u][s] is not None:
                units[u][s]()


def build_l0(S, lambda_init):
    B = Builder(S, 0)
    outs = l0_body(B, lambda_init, False)
    B.sch.emit(final_wait_ops=outs)
    return B


def l0_body(B, lambda_init, fused):
    nc = B.nc
    S = B.S
    NB, NJ, NO, QGB, GW, NQG, TW, NT = B.NB, B.NJ, B.NO, B.QGB, B.GW, B.NQG, B.TW, B.NT
    NR, NE, NEB = B.NR, B.NE, B.NEB
    etiles = [(a, min(512, NE - a)) for a in range(0, NE, 512)]
    xT_all = B.din("xT_all", [D, S])
    xT_own = B.din("xT_ext", [D, NE])
    x_own = B.din("x_ext", [NE, D])
    pos_all = B.din("pos_all", [1, S], I32)
    pos_own = B.din("pos_ext", [1, NE], I32)
    w_k = B.din("w_k", [D, 1152])
    w_v = B.din("w_v", [D, 768])
    w_q = B.din("w_q", [D, 1408])
    w_g = B.din("w_g", [D, 1024])
    memT = B.din("memT", [D, MEM_LEN])
    w_mkv = B.din("w_mkv", [D, 512])
    w_o = B.din("w_o", [D, D])
    ln_g = B.din("ln_g", [1, D])
    ln_b = B.din("ln_b", [1, D])
    dlam = B.din("dlam", [1, 128])
    subln = B.din("subln", [128, 1])
    invf_d = B.din("invf", [128, 1])
    coef_d = B.din("coefp", [128, 1])
    masks_d = B.din("masks", [128, 9 * 128], BF16)
    qcol_d = B.din("qcol", [128, 1024])
    stab_d = B.din("stab", [128, 16 + 64])
    y_out = B.dscr("x1_ext", [NE, D], F32)
    B.x1_ext_d = y_out
    kT_scr = B.dscr("kT_scr", [128, 6, S], BF16)
    v_scr = B.dscr("v_scr", [128, 12, NB * 65], BF16)

    masks = B.tile([128, 9, 128], BF16, "masks")
    negtri = T(masks.ap[:, 8, :], "negtri")
    negtri.res = masks.res
    negones = B.tile([128, 128], BF16, "negones")
    ones_f = B.tile([128, 128], F32, "ones_f")
    invf = B.tile([128, 1], F32, "invf")
    coefp = B.tile([128, 1], F32, "coefp")
    B.pi_col = B.tile([128, 1], F32, "pi")
    g1col = B.tile([128, 1], F32, "g1col")
    lam_t = B.tile([128, 128], F32, "lam")
    lamw = B.tile([128, 8], F32, "lamw")
    B.eps_col = B.tile([128, 1], F32, "eps")
    prod = B.tile([128, 64], F32, "prod")
    qcol = B.tile([128, 1024], F32, "qcol")
    stab = B.tile([128, 80], F32, "stab")
    B.mark_consts = B.sa.mark()
    qT_sb = B.tile([128, 3, NE], BF16, "qT_sb")
    qT_df = B.tile([128, 3, NE], BF16, "qT_df")
    mqT = B.tile([128, 2, NE], BF16, "mqT")
    mixT = B.tile([128, 8, NE], BF16, "mixT")
    mkT = B.tile([128, 2, MEM_LEN], BF16, "mkT")
    mv = B.tile([128, 2, 4, 65], BF16, "mv")
    PS = B.psum
    pbc = [0]

    def nextbank():
        b = PS[pbc[0] % 8]
        pbc[0] += 1
        return b

    B.dma(masks.ap.rearrange("p a b -> p (a b)"), masks_d.ap[:, :], [masks_d], [masks])
    B.dma(invf.ap, invf_d.ap[:, :], [invf_d], [invf])
    B.dma(coefp.ap, coef_d.ap[:, :], [coef_d], [coefp])
    B.dma(qcol.ap, qcol_d.ap[:, :], [qcol_d], [qcol])
    B.dma(stab.ap, stab_d.ap[:, :], [stab_d], [stab])
    B.memset("pool", B.pi_col.ap, PI, [B.pi_col])
    B.memset("pool", B.eps_col.ap, LN_EPS, [B.eps_col])
    B.memset("pool", ones_f.ap, 1.0, [ones_f])
    B.memset("pool", negones.ap, -1.0, [negones])
    B.dma(lam_t.ap[64:65, 0:128], dlam.ap[0:1, :], [dlam], [lam_t])
    B.tt("dve", prod.ap[64:65, 0:32], lam_t.ap[64:65, 0:32], lam_t.ap[64:65, 32:64], ALU.mult, [lam_t], [prod])
    B.tt("dve", prod.ap[64:65, 32:64], lam_t.ap[64:65, 64:96], lam_t.ap[64:65, 96:128], ALU.mult, [lam_t], [prod])
    B.op("dve", lambda e: e.tensor_reduce(lamw.ap[64:65, 0:1], prod.ap[64:65, 0:32], AX.X, ALU.add), [prod], [lamw])
    B.op("dve", lambda e: e.tensor_reduce(lamw.ap[64:65, 1:2], prod.ap[64:65, 32:64], AX.X, ALU.add), [prod], [lamw])
    B.act(lamw.ap[64:65, 2:4], lamw.ap[64:65, 0:2], AF.Exp, [lamw], [lamw])
    B.tt("dve", lamw.ap[64:65, 4:5], lamw.ap[64:65, 2:3], lamw.ap[64:65, 3:4], ALU.subtract, [lamw], [lamw])
    B.ts("dve", lamw.ap[64:65, 5:6], lamw.ap[64:65, 4:5], lambda_init, -1.0, ALU.add, ALU.mult, [lamw], [lamw])
    B.dma(g1col.ap, subln.ap[:, :], [subln], [g1col])
    B.ts("dve", g1col.ap, g1col.ap, 1.0 - lambda_init, None, ALU.mult, None, [g1col], [g1col])

    mA = B.sa.mark()
    wb = B.tile([128, KC, 1920], BF16, "wb")
    wst = [B.tile([128, KC, 64], F32, "wst%d" % i) for i in range(2)]
    xs = [B.tile([128, KC, TW], F32, "xs%d" % i) for i in range(2)]
    xb = [B.tile([128, KC, TW], BF16, "xb%d" % i) for i in range(2)]
    posi = B.tile([128, TW], I32, "posi")
    posf = B.tile([128, TW], F32, "posf")
    t1 = B.tile([128, TW], F32, "t1")
    cosT = B.tile([128, TW], F32, "cosT")
    sinS = B.tile([128, TW], F32, "sinS")
    tmpa = B.tile([128, TW], F32, "tmpa")
    tmpb = B.tile([128, TW], F32, "tmpb")
    ktst = [B.tile([128, 6, TW], BF16, "ktst%d" % i) for i in range(1)]
    vst = [B.tile([128, 12, TW // 128, 65], BF16, "vst%d" % i) for i in range(2)]

    B.mem_kv(memT, w_mkv, wb, wst, xs[0], xb[0], mkT, mv, nextbank)

    B.load_w(w_k, 1152, wb, wst, 0)
    B.load_w(w_v, 768, wb, wst, 1152)
    for i in range(2):
        B.memset("pool", vst[i].ap, 1.0, [vst[i]])
    xsrc = xT_all.ap.rearrange("(kc p) s -> p kc s", p=128)
    for t in range(NT):
        xs_t, xb_t = xs[t % 2], xb[t % 2]
        kt_t, v_t = ktst[0], vst[t % 2]
        for hlf in range(2):
            B.dma(xs_t.ap[:, hlf * 4:(hlf + 1) * 4, :], xsrc[:, hlf * 4:(hlf + 1) * 4, t * TW:(t + 1) * TW], [xT_all], [xs_t])
        for kc in range(KC):
            B.copy("pool" if kc % 2 == 0 else "dve", xb_t.ap[:, kc, :], xs_t.ap[:, kc, :], [xs_t], [xb_t])
        B.rope_tables(pos_all, t * TW, TW, posi, posf, t1, cosT, sinS, invf, coefp)
        for cg in range(3):
            ps = nextbank()
            B.proj_fm(ps, wb, cg * 128, xb_t, TW)
            B.copy("act", kt_t.ap[:, cg, :], ps.ap[:, 0:TW], [ps], [kt_t])
        for cg in range(3):
            psK = nextbank()
            B.proj_fm(psK, wb, 384 + cg * 128, xb_t, TW)
            psP = nextbank()
            B.proj_fm(psP, wb, 768 + cg * 128, xb_t, TW)
            B.rope_evac(psK, psP, kt_t.ap[:, 3 + cg, :], TW, cosT, sinS, tmpa, tmpb, kt_t)
        B.dma(kT_scr.ap[:, :, t * TW:(t + 1) * TW], kt_t.ap, [kt_t], [kT_scr])
        for blk in range(TW // 128):
            for half in range(2):
                ps = nextbank()
                for kc in range(KC):
                    B.mm(ps.ap[:, 0:384], xb_t.ap[:, kc, blk * 128:(blk + 1) * 128],
                         wb.ap[:, kc, 1152 + half * 384:1152 + (half + 1) * 384], kc == 0, kc == KC - 1, [xb_t, wb], [ps])
                B.copy("act" if half == 0 else "dve", v_t.ap[:, half * 6:(half + 1) * 6, blk, 0:64],
                       ps.ap[:, 0:384].rearrange("p (h d) -> p h d", h=6), [ps], [v_t])
        nb_t = TW // 128
        B.dma(v_scr.ap[:, :, t * nb_t * 65:(t + 1) * nb_t * 65], v_t.ap.rearrange("p h b c -> p h (b c)"), [v_t], [v_scr])

    xosrc = xT_own.ap.rearrange("(kc p) s -> p kc s", p=128)
    for rnd in range(2):
        if rnd == 0:
            B.load_w(w_q, 1408, wb, wst, 0)
        else:
            B.load_w(w_g, 1024, wb, wst, 0)
        for u, (ea0, OW) in enumerate(etiles):
            xs_t, xb_t = xs[u % 2], xb[u % 2]
            osl = slice(ea0, ea0 + OW)
            for hlf in range(2):
                B.dma(xs_t.ap[:, hlf * 4:(hlf + 1) * 4, 0:OW], xosrc[:, hlf * 4:(hlf + 1) * 4, osl], [xT_own], [xs_t])
            for kc in range(KC):
                B.copy("pool" if kc % 2 == 0 else "dve", xb_t.ap[:, kc, 0:OW], xs_t.ap[:, kc, 0:OW], [xs_t], [xb_t])
            if rnd == 0:
                B.rope_tables(pos_own, ea0, OW, posi, posf, t1, cosT, sinS, invf, coefp)
                for cg in range(3):
                    ps = nextbank()
                    B.proj_fm(ps, wb, cg * 128, xb_t, OW)
                    B.act(qT_sb.ap[:, cg, osl], ps.ap[:, 0:OW], AF.Copy, [ps], [qT_sb], scale=0.125)
                for cg in range(3):
                    psK = nextbank()
                    B.proj_fm(psK, wb, 384 + cg * 128, xb_t, OW)
                    psP = nextbank()
                    B.proj_fm(psP, wb, 768 + cg * 128, xb_t, OW)
                    B.rope_evac(psK, psP, qT_df.ap[:, cg, osl], OW, cosT, sinS, tmpa, tmpb, qT_df)
                for cg in range(2):
                    ps = nextbank()
                    B.proj_fm(ps, wb, 1152 + cg * 128, xb_t, OW)
                    B.copy("act", mqT.ap[:, cg, osl], ps.ap[:, 0:OW], [ps], [mqT])
            else:
                for cg in range(8):
                    ps = nextbank()
                    B.proj_fm(ps, wb, cg * 128, xb_t, OW)
                    B.act(mixT.ap[:, cg, osl], ps.ap[:, 0:OW], AF.Silu, [ps], [mixT])

    phaseA_tiles = [wb] + wst + xs + xb + [posi, posf, t1, cosT, sinS, tmpa, tmpb] + ktst + vst
    B.sa.reset(mA)
    kTp = [B.tile([128, S], BF16, "kTp%d" % i) for i in range(2)]
    vtp = [B.tile([128, 2, NB * 65 + 64], BF16, "vtp%d" % i) for i in range(2)]
    E = [B.tile([128, GW], F32, "E%d" % i) for i in range(2)]
    SP = [B.tile([128, GW], BF16, "SP%d" % i) for i in range(3)]
    PT = [B.tile([128, GW], BF16, "PT%d" % i) for i in range(4)]
    R32 = B.tile([128, GW], F32, "R32")
    Rb = [B.tile([128, GW], BF16, "Rb%d" % i) for i in range(2)]
    tmpo = [B.tile([128, GW], F32, "tmpo%d" % i) for i in range(2)]
    rd = B.tile([128, 2 * GW], F32, "rd")
    bcs = B.tile([128, 2 * GW], F32, "bcs")
    ea = B.tile([128, GW], F32, "ea")
    eb = B.tile([128, GW], F32, "eb")
    ec = B.tile([128, GW], F32, "ec")
    fz = B.tile([128, 16], F32, "fz")
    phaseB_tiles = kTp + vtp + E + SP + PT + [R32] + Rb + tmpo + [rd, bcs, ea, eb, ec, fz]
    B.op("pool", lambda e: e.memset(fz.ap, 0.0), [], phaseA_tiles + phaseB_tiles)
    for i in range(2):
        B.memset("pool", vtp[i].ap[:, :, NB * 65:NB * 65 + 64], 0.0, [vtp[i]])

    kview = kT_scr.ap
    vview = v_scr.ap

    def load_pair(kcg, vh0, slot):
        B.dma(kTp[slot].ap, kview[:, kcg, :], [kT_scr], [kTp[slot]])
        B.dma(vtp[slot].ap[:, :, 0:NB * 65], vview[:, vh0:vh0 + 2, :], [v_scr], [vtp[slot]])

    zc = [0]
    accc = [0]

    def kb_list(gi):
        out = []
        if gi < NR:
            for kb in range(16 * gi + 15, -1, -1):
                if kb >= 16 * gi:
                    m = kb - 16 * gi
                    out.append((kb, max(0, m - 12) * 128, m))
                else:
                    out.append((kb, 0, None))
            return gi * 512, 512, qcol.ap[:, 0:512], out
        W = NR * 128
        for kb in range(16 * (NR - 1) + 11, -1, -1):
            c0 = ((kb - 11 + 15) // 16) * 128 if kb > 11 else 0
            out.append((kb, c0, 16 + kb))
        return NO, W, qcol.ap[:, 512:512 + W], out

    def mask_op(Pt, qc, cs, midx, strict):
        B.op("dve", lambda e: e.scalar_tensor_tensor(Pt.ap[:, cs], qc[:, cs], stab.ap[:, midx:midx + 1], Pt.ap[:, cs],
                                                     ALU.is_gt if strict else ALU.is_ge, ALU.mult),
             [Pt, qcol, stab], [Pt])

    def sb_units(h, slot):
        cg, po = h // 2, (h % 2) * 64
        kT, vt = kTp[slot], vtp[slot]
        units = []
        for gi in range(NR + 1):
            col0, W, qc, kbs = kb_list(gi)
            acc = PS[4 + accc[0] % 2]
            accc[0] += 1
            for ui, (kb, c0, midx) in enumerate(kbs):
                first, last = ui == 0, ui == len(kbs) - 1
                zi = zc[0]
                zc[0] += 1
                Z, ARG = PS[zi % 2], PS[2 + zi % 2]
                Et, SPt, PTt = E[zi % 2], SP[zi % 3], PT[zi % 4]
                Rcur, Rnext = Rb[zi % 2], Rb[(zi + 1) % 2]
                kap = kT.ap[po:po + 64, kb * 128:(kb + 1) * 128]
                qap = qT_sb.ap[po:po + 64, cg, col0 + c0:col0 + W]
                cs = slice(c0, W)

                def s0(Z=Z, Et=Et, SPt=SPt, kap=kap, qap=qap, cs=cs, midx=midx, kT=kT, qc=qc):
                    B.mm(Z.ap[:, cs], kap, qap, True, True, [kT, qT_sb], [Z])
                    B.act(Et.ap[:, cs], Z.ap[:, cs], AF.Exp, [Z], [Et])
                    B.act(SPt.ap[:, cs], Et.ap[:, cs], AF.Ln, [Et], [SPt], bias=1.0)
                    if midx is not None:
                        mask_op(SPt, qc, cs, midx, True)

                def s1(ARG=ARG, SPt=SPt, PTt=PTt, kap=kap, qap=qap, cs=cs, midx=midx, first=first, last=last,
                       Rcur=Rcur, Rnext=Rnext, kT=kT, qc=qc, W=W):
                    B.mm(ARG.ap[:, cs], kap, qap, True, False, [kT, qT_sb], [ARG])
                    B.mm(ARG.ap[:, cs], negtri.ap, SPt.ap[:, cs], False, first, [negtri, SPt], [ARG])
                    if not first:
                        B.mm(ARG.ap[:, cs], negones.ap, Rcur.ap[:, cs], False, True, [negones, Rcur], [ARG])
                    if first:
                        B.memset("pool", R32.ap, 0.0, [R32])
                    if not last:
                        B.tt("dve", R32.ap[:, cs], R32.ap[:, cs], SPt.ap[:, cs], ALU.add, [R32, SPt], [R32])
                        B.copy("pool", Rnext.ap[:, 0:W], R32.ap[:, 0:W], [R32], [Rnext])
                    B.act(PTt.ap[:, cs], ARG.ap[:, cs], AF.Exp, [ARG], [PTt])
                    if midx is not None:
                        mask_op(PTt, qc, cs, midx, True)

                def s2(acc=acc, PTt=PTt, cs=cs, kb=kb, first=first, last=last, gi=gi, vt=vt, col0=col0, W=W):
                    B.mm(acc.ap[:, cs], vt.ap[:, h % 2, kb * 65:kb * 65 + 128], PTt.ap[:, cs], first, last, [vt, PTt], [acc], skip=True)
                    if last:
                        tq = tmpo[gi % 2]
                        gsl = slice(col0, col0 + W)
                        B.copy("act", tq.ap[po:po + 64, 0:W], acc.ap[0:64, 0:W], [acc], [tq])
                        B.tt("dve", mixT.ap[po:po + 64, cg, gsl], tq.ap[po:po + 64, 0:W], mixT.ap[po:po + 64, cg, gsl],
                             ALU.mult, [tq, mixT], [mixT])

                units.append([s0, s1, s2])
        return units

    B.rd, B.bcs, B.ones_f, B.b4, B.lamw = rd, bcs, ones_f, [0], lamw
    softmax_norm = B.softmax_norm
    nextbank4 = B.nextbank4

    def df_units(h, slot):
        cgk, po = h // 2, (h % 2) * 64
        kT, vt = kTp[slot], vtp[slot]
        units = []
        scale = 32 ** -0.5
        for gi in range(NR + 1):
            col0, W, qc, kbs = kb_list(gi)
            accs = [PS[4 + 2 * (accc[0] % 2)], PS[5 + 2 * (accc[0] % 2)]]
            accc[0] += 1
            gsl = slice(col0, col0 + W)
            for ui, (kb, c0, midx) in enumerate(kbs):
                first, last = ui == 0, ui == len(kbs) - 1
                cs = slice(c0, W)
                Sbs, PTs, kaps, qaps, tps = [], [], [], [], []
                for cm in range(2):
                    zi = zc[0]
                    zc[0] += 1
                    Sbs.append(nextbank4())
                    PTs.append(PT[zi % 4])
                    rows = slice(po + cm * 32, po + cm * 32 + 32)
                    kaps.append(kT.ap[rows, kb * 128:(kb + 1) * 128])
                    qaps.append(qT_df.ap[rows, cgk, col0 + c0:col0 + W])
                    tps.append((po + cm * 32, 0))

                def s0(Sbs=Sbs, PTs=PTs, kaps=kaps, qaps=qaps, cs=cs, midx=midx, tps=tps, kT=kT, qc=qc):
                    for cm in range(2):
                        B.op("pe", lambda e, cm=cm: e.matmul(Sbs[cm].ap[:, cs], lhsT=kaps[cm], rhs=qaps[cm], start=True, stop=True,
                                                             tile_position=tps[cm]), [kT, qT_df], [Sbs[cm]])
                    for cm in range(2):
                        B.act(PTs[cm].ap[:, cs], Sbs[cm].ap[:, cs], AF.Exp, [Sbs[cm]], [PTs[cm]], scale=scale)
                        if midx is not None:
                            mask_op(PTs[cm], qc, cs, midx, False)

                def s1(PTs=PTs, cs=cs, kb=kb, first=first, last=last, accs=accs, gsl=gsl, vt=vt, W=W):
                    for cm in range(2):
                        B.mm(accs[cm].ap[:, cs], vt.ap[:, h % 2, kb * 65:kb * 65 + 128], PTs[cm].ap[:, cs], first, last,
                             [vt, PTs[cm]], [accs[cm]], skip=True)
                    if last:
                        softmax_norm(accs[0], slice(0, W), W=W)
                        softmax_norm(accs[1], slice(512, 512 + W), post_scale_col=lamw.ap[64:65, 5:6], W=W)
                        B.tt("dve", ea.ap[0:64, 0:W], accs[0].ap[0:64, 0:W], bcs.ap[0:64, 0:W], ALU.mult, [accs[0], bcs], [ea])
                        B.tt("dve", eb.ap[0:64, 0:W], accs[1].ap[0:64, 0:W], bcs.ap[0:64, 512:512 + W], ALU.mult, [accs[1], bcs], [eb])
                        B.tt("pool", ea.ap[0:64, 0:W], ea.ap[0:64, 0:W], eb.ap[0:64, 0:W], ALU.add, [ea, eb], [ea])
                        B.tt("pool", eb.ap[0:64, 0:W], ea.ap[0:64, 0:W], ea.ap[0:64, 0:W], ALU.mult, [ea], [eb])
                        ss = nextbank4()
                        B.mm(ss.ap[0:64, 0:W], ones_f.ap[0:64, 0:64], eb.ap[0:64, 0:W], True, True, [ones_f, eb], [ss])
                        B.act(ec.ap[0:64, 0:W], ss.ap[0:64, 0:W], AF.Ln, [ss, B.eps_col], [ec], scale=1.0 / 64, bias=B.eps_col.ap[0:64, 0:1])
                        B.act(ec.ap[0:64, 0:W], ec.ap[0:64, 0:W], AF.Exp, [ec], [ec], scale=-0.5)
                        B.tt("dve", ea.ap[0:64, 0:W], ea.ap[0:64, 0:W], ec.ap[0:64, 0:W], ALU.mult, [ea, ec], [ea])
                        tq = tmpo[0]
                        B.copy("act", tq.ap[po:po + 64, 0:W], ea.ap[0:64, 0:W], [ea], [tq])
                        mcg = 3 + h // 2
                        B.op("dve", lambda e: e.scalar_tensor_tensor(
                            mixT.ap[po:po + 64, mcg, gsl], tq.ap[po:po + 64, 0:W], g1col.ap[po:po + 64, 0:1],
                            mixT.ap[po:po + 64, mcg, gsl], ALU.mult, ALU.mult), [tq, g1col, mixT], [mixT])

                units.append([s0, s1])
        return units

    pairs = [("sb", p) for p in range(3)] + [("df", p) for p in range(3)]
    load_pair(0, 0, 0)
    run_pipeline(B.mem_units(mkT, mv, mqT, mixT, PT, ea, tmpo[1], zc, accc, B.groups), 2)
    for pi, (kind, p) in enumerate(pairs):
        slot = pi % 2
        if pi + 1 < len(pairs):
            k2, p2 = pairs[pi + 1]
            load_pair(p2 if k2 == "sb" else 3 + p2, 2 * p2 if k2 == "sb" else 6 + 2 * p2, (pi + 1) % 2)
        for h in (2 * p, 2 * p + 1):
            if kind == "sb":
                run_pipeline(sb_units(h, slot), 3)
            else:
                run_pipeline(df_units(h, slot), 2)

    B.sa.reset(mA)
    wo = B.tile([128, KC, D], BF16, "wo")
    wst2 = [B.tile([128, KC, 128], F32, "wst2_%d" % i) for i in range(2)]
    gbc = B.tile([128, D], F32, "gbc")
    bbc = B.tile([128, D], F32, "bbc")
    xr = [B.tile([128, D], F32, "xr%d" % i) for i in range(2)]
    vv = [B.tile([128, D], F32, "vv%d" % i) for i in range(2)]
    st = B.tile([128, 8], F32, "st")
    fz2 = B.tile([128, 16], F32, "fz2")
    phaseC_tiles = [wo, gbc, bbc, st, fz2] + wst2 + xr + vv
    B.op("pool", lambda e: e.memset(fz2.ap, 0.0), [], phaseB_tiles + phaseC_tiles)
    B.load_w(w_o, D, wo, wst2)
    B.dma(gbc.ap, ln_g.ap[0:1, :].partition_broadcast(128), [ln_g], [gbc])
    B.dma(bbc.ap, ln_b.ap[0:1, :].partition_broadcast(128), [ln_b], [bbc])
    outs = []
    for j in range(NEB):
        outs.append(B.out_block(j, mixT, wo, x_own, xr[j % 2], vv[j % 2], st, gbc, bbc, y_out, nextbank))
    B.l0_tiles = phaseB_tiles + phaseC_tiles + [qT_sb, qT_df, mqT, mixT, mkT, mv]
    B.eps_ready = True
    return outs


DBG = {}
QCOLS = [0, 4, 1, 5, 2, 6, 3, 7, 8, 8, 9, 9, 10, 10, 11, 11]


def _qpos(hq):
    if hq < 4:
        return hq, 0
    if hq < 8:
        return hq - 4, 64
    return 4 + hq - 8, 0


def l1_body(B):
    nc = B.nc
    S = B.S
    NB, NJ, NO, NR, NE, NEB = B.NB, B.NJ, B.NO, B.NR, B.NE, B.NEB
    PS = B.psum
    etiles = [(a, min(512, NE - a)) for a in range(0, NE, 512)]
    x1_ext = B.x1_ext_d
    pos_ext = B.dram["pos_ext"]
    memT = B.dram["memT"]
    w1_kv = B.din("w1_kv", [D, 960])
    w1_q = B.din("w1_q", [D, 2048])
    w1_g = B.din("w1_g", [D, 1024])
    w1_mkv = B.din("w1_mkv", [D, 512])
    w1_o = B.din("w1_o", [D, D])
    ln1_g = B.din("ln1_g", [1, D])
    ln1_b = B.din("ln1_b", [1, D])
    sinks_x = B.din("sinks_x", [1, 1536])
    invf1_d = B.din("invf1", [128, 1])
    coef1_d = B.din("coefp1", [128, 1])
    masks1_d = B.din("masks1", [128, 3 * 128], BF16)
    ident_d = B.din("ident", [128, 128])
    y_out = B.dout("y", [NO, D])

    pbc = [0]

    def nextbank():
        b = PS[pbc[0] % 8]
        pbc[0] += 1
        return b

    B.sa.reset(B.mark_consts)
    new_tiles = []

    def tl(shape, dt, name):
        t = B.tile(shape, dt, name)
        new_tiles.append(t)
        return t

    ident = tl([128, 128], F32, "ident")
    masks1 = tl([128, 3, 128], BF16, "masks1")
    invf1 = tl([128, 1], F32, "invf1")
    coef1 = tl([128, 1], F32, "coef1")
    qT1 = tl([128, 8, NO], BF16, "qT1")
    kT1 = tl([128, 2, NE], BF16, "kT1")
    V1 = tl([128, NEB, 3, 65], BF16, "V1")
    mqT1 = tl([128, 2, NO], BF16, "mqT1")
    mixT1 = tl([128, 8, NO], BF16, "mixT1")
    mkT1 = tl([128, 2, MEM_LEN], BF16, "mkT1")
    mv1 = tl([128, 2, 4, 65], BF16, "mv1")
    fz = tl([128, 16], F32, "fz1")
    mP = B.sa.mark()
    xT1 = tl([128, KC, NE], BF16, "xT1")
    xin = [tl([128, D], F32, "xin%d" % i) for i in range(2)]
    wb1 = tl([128, KC, 1024], BF16, "wb1")
    wst = [tl([128, KC, 64], F32, "wst1_%d" % i) for i in range(2)]
    posi = tl([128, 512], I32, "posi1")
    posf = tl([128, 512], F32, "posf1")
    t1 = tl([128, 512], F32, "t11")
    cosT = tl([128, 512], F32, "cosT1")
    sinS = tl([128, 512], F32, "sinS1")
    tmpa = tl([128, 512], F32, "tmpa1")
    tmpb = tl([128, 512], F32, "tmpb1")
    xs_m = tl([128, KC, MEM_LEN], F32, "xs_m")
    memb = tl([128, KC, MEM_LEN], BF16, "memb1")
    B.op("pool", lambda e: e.memset(fz.ap, 0.0), [], B.l0_tiles + new_tiles)
    B.dma(ident.ap, ident_d.ap[:, :], [ident_d], [ident])
    B.dma(masks1.ap.rearrange("p a b -> p (a b)"), masks1_d.ap[:, :], [masks1_d], [masks1])
    B.dma(invf1.ap, invf1_d.ap[:, :], [invf1_d], [invf1])
    B.dma(coef1.ap, coef1_d.ap[:, :], [coef1_d], [coef1])

    B.mem_kv(memT, w1_mkv, wb1, wst, xs_m, memb, mkT1, mv1, nextbank)
    B.memset("pool", V1.ap, 1.0, [V1])

    xc = [0]

    def transpose_block(e):
        xin_t = xin[xc[0] % 2]
        xc[0] += 1
        B.dma(xin_t.ap, x1_ext.ap[e * 128:(e + 1) * 128, :], [x1_ext], [xin_t])
        for half in range(2):
            ps = nextbank()
            for q in range(4):
                kc = half * 4 + q
                B.op("pe", lambda en, ps=ps, q=q, kc=kc: en.transpose(ps.ap[:, q * 128:(q + 1) * 128],
                                                                     xin_t.ap[:, kc * 128:(kc + 1) * 128], ident.ap),
                     [xin_t, ident], [ps])
            B.copy("act" if half == 0 else "dve", xT1.ap[:, half * 4:(half + 1) * 4, e * 128:(e + 1) * 128],
                   ps.ap[:, 0:512].rearrange("p (a b) -> p a b", a=4), [ps], [xT1])

    def xtile(a0, w):
        t = T(xT1.ap[:, :, a0:a0 + w], "xo_t")
        t.res = xT1.res
        return t

    B.load_w(w1_kv, 960, wb1, wst, 0)
    for (a0, w) in etiles:
        for e in range(a0 // 128, (a0 + w) // 128):
            transpose_block(e)
        xo_t = xtile(a0, w)
        B.rope_tables(pos_ext, a0, w, posi, posf, t1, cosT, sinS, invf1, coef1)
        for cg in range(2):
            psK = nextbank()
            B.proj_fm(psK, wb1, cg * 128, xo_t, w)
            psP = nextbank()
            B.proj_fm(psP, wb1, 256 + cg * 128, xo_t, w)
            B.rope_evac(psK, psP, kT1.ap[:, cg, a0:a0 + w], w, cosT, sinS, tmpa, tmpb, kT1)
        for blk in range(w // 128):
            e = a0 // 128 + blk
            ps = nextbank()
            for kc in range(KC):
                B.mm(ps.ap[:, 0:192], xo_t.ap[:, kc, blk * 128:(blk + 1) * 128], wb1.ap[:, kc, 512:704],
                     kc == 0, kc == KC - 1, [xo_t, wb1], [ps])
            B.copy("act", V1.ap[:, e, :, 0:64], ps.ap[:, 0:192].rearrange("p (h d) -> p h d", h=3), [ps], [V1])
        if a0 < NO:
            for cg in range(2):
                ps = nextbank()
                B.proj_fm(ps, wb1, 704 + cg * 128, xo_t, w)
                B.copy("act", mqT1.ap[:, cg, a0:a0 + w], ps.ap[:, 0:w], [ps], [mqT1])
    for rnd in range(3):
        if rnd < 2:
            B.load_w(T(w1_q.ap[:, rnd * 1024:(rnd + 1) * 1024], "w1q"), 1024, wb1, wst, 0)
        else:
            B.load_w(w1_g, 1024, wb1, wst, 0)
        for (a0, w) in etiles:
            if a0 >= NO:
                continue
            osl = slice(a0, a0 + w)
            xo_t = xtile(a0, w)
            if rnd < 2:
                B.rope_tables(pos_ext, a0, w, posi, posf, t1, cosT, sinS, invf1, coef1)
                for cg in range(4):
                    psK = nextbank()
                    B.proj_fm(psK, wb1, cg * 128, xo_t, w)
                    psP = nextbank()
                    B.proj_fm(psP, wb1, 512 + cg * 128, xo_t, w)
                    B.rope_evac(psK, psP, qT1.ap[:, rnd * 4 + cg, osl], w, cosT, sinS, tmpa, tmpb, qT1)
            else:
                for cg in range(8):
                    ps = nextbank()
                    B.proj_fm(ps, wb1, cg * 128, xo_t, w)
                    B.act(mixT1.ap[:, cg, osl], ps.ap[:, 0:w], AF.Silu, [ps], [mixT1])

    proj_tiles = [xT1] + xin + [wb1] + wst + [posi, posf, t1, cosT, sinS, tmpa, tmpb, xs_m, memb]
    B.sa.reset(mP)
    esink = B.tile([128, 1536], F32, "esink")
    esraw = B.tile([128, 1536], F32, "esraw")
    PT1 = [B.tile([128, 512], BF16, "PT1_%d" % i) for i in range(4)]
    rd = B.tile([128, 1024], F32, "rd1")
    bcs = B.tile([128, 1024], F32, "bcs1")
    ea = B.tile([128, 512], F32, "ea1")
    tq = [B.tile([128, 512], F32, "tq1_%d" % i) for i in range(2)]
    wo1 = B.tile([128, KC, D], BF16, "wo1")
    wst2 = [B.tile([128, KC, 128], F32, "wst1b_%d" % i) for i in range(2)]
    gbc = B.tile([128, D], F32, "gbc1")
    bbc = B.tile([128, D], F32, "bbc1")
    xr = [B.tile([128, D], F32, "xr1_%d" % i) for i in range(2)]
    vv = [B.tile([128, D], F32, "vv1_%d" % i) for i in range(2)]
    st = B.tile([128, 8], F32, "st1")
    att_tiles = [esink, esraw] + PT1 + [rd, bcs, ea] + tq + [wo1] + wst2 + [gbc, bbc] + xr + vv + [st]
    B.op("pool", lambda e: e.memset(fz.ap, 0.0), [], proj_tiles + att_tiles + [fz])
    B.rd, B.bcs, B.b4 = rd, bcs, [0]
    B.dma(esraw.ap, sinks_x.ap[0:1, :].partition_broadcast(128), [sinks_x], [esraw])
    B.act(esink.ap, esraw.ap, AF.Exp, [esraw], [esink])
    B.load_w(w1_o, D, wo1, wst2)
    B.dma(gbc.ap, ln1_g.ap[0:1, :].partition_broadcast(128), [ln1_g], [gbc])
    B.dma(bbc.ap, ln1_b.ap[0:1, :].partition_broadcast(128), [ln1_b], [bbc])

    zc, accc = [0], [0]
    run_pipeline(B.mem_units(mkT1, mv1, mqT1, mixT1, PT1, ea, tq[1], zc, accc, [(i * 512, 512) for i in range(NR)]), 2)
    units = []
    for j in range(NJ):
        i_run, r = j // 4, j % 4
        e_prev = j - 1 if r > 0 else NJ + i_run
        jsl = slice(j * 128, (j + 1) * 128)
        for kvh in range(3):
            acc = PS[4 + accc[0] % 4]
            accc[0] += 1
            for bi, e_k in enumerate((e_prev, j)):
                zi = zc[0]
                zc[0] += 1
                Sb = B.nextbank4()
                PTt = PT1[zi % 4]
                mi = 0 if bi == 1 else (2 if j == 0 else 1)
                ksl = slice(e_k * 128, (e_k + 1) * 128)

                def s0(Sb=Sb, PTt=PTt, kvh=kvh, jsl=jsl, ksl=ksl, mi=mi):
                    for g in range(4):
                        hq = kvh * 4 + g
                        cgq, po = _qpos(hq)
                        cgk = 0 if kvh < 2 else 1
                        B.mm(Sb.ap[:, g * 128:(g + 1) * 128], kT1.ap[po:po + 64, cgk, ksl], qT1.ap[po:po + 64, cgq, jsl],
                             True, True, [kT1, qT1], [Sb])
                    B.act(PTt.ap[:, 0:512], Sb.ap[:, 0:512], AF.Exp, [Sb], [PTt], scale=0.125)
                    for g in range(4):
                        B.tt("dve" if g % 2 == 0 else "pool", PTt.ap[:, g * 128:(g + 1) * 128], PTt.ap[:, g * 128:(g + 1) * 128],
                             masks1.ap[:, mi, :], ALU.mult, [PTt, masks1], [PTt])

                def s1(acc=acc, PTt=PTt, kvh=kvh, j=j, jsl=jsl, bi=bi, e_k=e_k):
                    B.mm(acc.ap[0:65, 0:512], V1.ap[:, e_k, kvh, 0:65], PTt.ap[:, 0:512], bi == 0, bi == 1, [V1, PTt], [acc], skip=True)
                    if bi == 1:
                        B.softmax_norm(acc, slice(0, 512), add_row=(esink.ap[64:65, kvh * 512:(kvh + 1) * 512], esink), W=512)
                        B.tt("dve", ea.ap[0:64, :], acc.ap[0:64, 0:512], bcs.ap[0:64, 0:512], ALU.mult, [acc, bcs], [ea])
                        tqt = tq[0]
                        for g in range(4):
                            hq = kvh * 4 + g
                            mcg, po = hq // 2, (hq % 2) * 64
                            gs = slice(g * 128, (g + 1) * 128)
                            B.copy("act", tqt.ap[po:po + 64, gs], ea.ap[0:64, gs], [ea], [tqt])
                            B.tt("dve", mixT1.ap[po:po + 64, mcg, jsl], tqt.ap[po:po + 64, gs],
                                 mixT1.ap[po:po + 64, mcg, jsl], ALU.mult, [tqt, mixT1], [mixT1])

                units.append([s0, s1])
    run_pipeline(units, 2)

    outs = []
    for j in range(NJ):
        outs.append(B.out_block(j, mixT1, wo1, x1_ext, xr[j % 2], vv[j % 2], st, gbc, bbc, y_out, nextbank))
    return outs


def layer_norm_store(B, vv_t, scratch, st, gbc, bbc, y_out, j):
    B.op("dve", lambda e: e.tensor_reduce(st.ap[:, 0:1], vv_t.ap, AX.X, ALU.add), [vv_t], [st])
    B.ts("dve", st.ap[:, 1:2], st.ap[:, 0:1], -1.0 / D, None, ALU.mult, None, [st], [st])
    B.op("act", lambda e: e.activation(out=scratch.ap, in_=vv_t.ap, func=AF.Square, bias=st.ap[:, 1:2], scale=1.0,
                                       accum_out=st.ap[:, 2:3]), [vv_t, st], [scratch, st])
    B.act(st.ap[:, 3:4], st.ap[:, 2:3], AF.Ln, [st, B.eps_col], [st], scale=1.0 / D, bias=B.eps_col.ap[:, 0:1])
    B.act(st.ap[:, 3:4], st.ap[:, 3:4], AF.Exp, [st], [st], scale=-0.5)
    B.ts("dve", vv_t.ap, vv_t.ap, st.ap[:, 1:2], st.ap[:, 3:4], ALU.add, ALU.mult, [vv_t, st], [vv_t])
    B.tt("pool", vv_t.ap, vv_t.ap, gbc.ap, ALU.mult, [vv_t, gbc], [vv_t])
    B.tt("pool", vv_t.ap, vv_t.ap, bbc.ap, ALU.add, [vv_t, bbc], [vv_t])
    return B.dma(y_out.ap[j * 128:(j + 1) * 128, :], vv_t.ap, [vv_t], [y_out])


def _own_idx(S, c):
    NR = S // 2048
    return np.concatenate([np.arange((16 * i + 4 * c) * 128, (16 * i + 4 * c + 4) * 128) for i in range(NR)])


def _halo_idx(S, c):
    NR = S // 2048
    out = []
    for i in range(NR):
        g = 16 * i + 4 * c - 1
        out.append(np.arange(g * 128, (g + 1) * 128) if g >= 0 else np.full(128, -1))
    return np.concatenate(out)


def _mask_dram(c):
    m = _masks(c)
    return np.ascontiguousarray(np.transpose(m, (1, 0, 2)).reshape(128, 9 * 128))


def _mask_tables(c):
    col = np.arange(512, dtype=np.float32)
    qcol = np.zeros((128, 1024), np.float32)
    qcol[:, 0:512] = col[None, :]
    qcol[:, 512:1024] = (col + 1920.0 * np.floor(col / 128.0))[None, :]
    k = np.arange(128, dtype=np.float32)[:, None]
    stab = np.zeros((128, 80), np.float32)
    stab[:, 0:16] = k + 128.0 * np.arange(16, dtype=np.float32)[None, :] - 512.0 * c
    stab[:, 16:80] = k + 128.0 * np.arange(64, dtype=np.float32)[None, :] + 128.0 - 512.0 * c
    return qcol, stab


def _masks1(c):
    k = np.arange(128)[:, None]
    q = np.arange(128)[None, :]
    m = np.zeros((3, 128, 128), np.float32)
    m[0] = (k <= q)
    m[1] = (k > q)
    m[2] = (k > q) if c > 0 else 0.0
    return np.ascontiguousarray(np.transpose(m, (1, 0, 2)).reshape(128, 3 * 128)).astype(ml_dtypes.bfloat16)


def l1_weights(w_in, w_memkv, sinks, w_out, ln_g, ln_b):
    cq, ck, cv = w_in[:, 0:768], w_in[:, 768:960], w_in[:, 960:1152]
    mq, gate = w_in[:, 1152:1408], w_in[:, 1408:2432]
    kd = np.concatenate([ck[:, 0:64], ck[:, 64:128], ck[:, 128:192], ck[:, 128:192]], axis=1)
    pk = _partner_perm(256, 64, 16)
    qr = np.concatenate([cq[:, h * 64:(h + 1) * 64] for h in QCOLS], axis=1)
    pq = _partner_perm(512, 64, 16)
    q0, q1 = qr[:, 0:512], qr[:, 512:1024]
    invf1, coef1 = _host_consts(1)
    return {
        "w1_kv": np.ascontiguousarray(np.concatenate([kd, kd[:, pk], cv, mq], axis=1)),
        "w1_q": np.ascontiguousarray(np.concatenate([q0, q0[:, pq], q1, q1[:, pq]], axis=1)),
        "w1_g": np.ascontiguousarray(gate),
        "w1_mkv": np.ascontiguousarray(w_memkv),
        "w1_o": np.ascontiguousarray(w_out),
        "ln1_g": np.ascontiguousarray(ln_g[None, :]),
        "ln1_b": np.ascontiguousarray(ln_b[None, :]),
        "sinks_x": np.ascontiguousarray(np.repeat(sinks, 128)[None, :]),
        "invf1": invf1, "coefp1": coef1,
        "ident": np.eye(128, dtype=np.float32),
    }


def prep_fused(inp):
    f = lambda a: np.asarray(a)
    x, mem, positions = f(inp["x"]), f(inp["mem"]), f(inp["positions"])
    S = x.shape[1]
    w_in = f(inp["w_in_even"])[0]
    sbq, sbk, sbv = w_in[:, 0:384], w_in[:, 384:768], w_in[:, 768:1152]
    dfq, dfk, dfv = w_in[:, 1152:1536], w_in[:, 1536:1920], w_in[:, 1920:2304]
    mq, gate = w_in[:, 2304:2560], w_in[:, 2560:3584]
    perm = _partner_perm(384, 32, 8)
    invf, coefp = _host_consts(0)
    dsub = f(inp["diff_subln_even"])[0]
    shared = {
        "w_k": np.ascontiguousarray(np.concatenate([sbk, dfk, dfk[:, perm]], axis=1)),
        "w_v": np.ascontiguousarray(np.concatenate([sbv, dfv], axis=1)),
        "w_q": np.ascontiguousarray(np.concatenate([sbq, dfq, dfq[:, perm], mq], axis=1)),
        "w_g": np.ascontiguousarray(gate),
        "w_mkv": np.ascontiguousarray(f(inp["w_memkv_even"])[0]),
        "w_o": np.ascontiguousarray(f(inp["w_out_even"])[0]),
        "ln_g": np.ascontiguousarray(f(inp["ln_g_even"])[0][None, :]),
        "ln_b": np.ascontiguousarray(f(inp["ln_b_even"])[0][None, :]),
        "dlam": np.ascontiguousarray(f(inp["diff_lambda_even"])[0].reshape(1, 128)),
        "subln": np.ascontiguousarray(np.concatenate([dsub, dsub])[:, None]),
        "invf": invf, "coefp": coefp,
    }
    shared.update(l1_weights(f(inp["w_in_odd"])[0], f(inp["w_memkv_odd"])[0], f(inp["sinks_odd"])[0], f(inp["w_out_odd"])[0],
                             f(inp["ln_g_odd"])[0], f(inp["ln_b_odd"])[0]))
    xT = [np.ascontiguousarray(x[b].T) for b in range(x.shape[0])]
    maps = []
    for core in range(8):
        b, c = core // 4, core % 4
        own = _own_idx(S, c)
        hidx = _halo_idx(S, c)
        ext = np.concatenate([own, np.maximum(hidx, 0)])
        qcol, stab = _mask_tables(c)
        m = dict(shared)
        m.update({
            "xT_all": xT[b],
            "xT_ext": np.ascontiguousarray(x[b][ext].T),
            "x_ext": np.ascontiguousarray(x[b][ext]),
            "pos_all": np.ascontiguousarray(positions[b][None, :]).astype(np.int32),
            "pos_ext": np.ascontiguousarray(positions[b][ext][None, :]).astype(np.int32),
            "memT": np.ascontiguousarray(mem[b].T),
            "masks": _mask_dram(c),
            "qcol": qcol, "stab": stab,
            "masks1": _masks1(c),
        })
        maps.append(m)
    return maps


def gather_own(results, key, S, nb=2):
    out = np.zeros((nb, S, D), np.float32)
    for core in range(8):
        b, c = core // 4, core % 4
        out[b, _own_idx(S, c)] = results[core][key]
    return out


def build_fused(S, lambda_init):
    B = Builder(S, 0)
    l0_body(B, lambda_init, True)
    outs = l1_body(B)
    B.sch.emit(final_wait_ops=outs)
    return B


LAMBDA_INIT0 = 0.8 - 0.6 * math.exp(-0.3 * 0)


def kernel(**inputs):
    S = np.asarray(inputs["x"]).shape[1]
    B = build_fused(S, LAMBDA_INIT0)
    maps = prep_fused(inputs)
    res = run_bass_kernel_spmd(B.nc, maps, core_ids=list(range(8)))
    return gather_own(res.results, "y", S)
```

```python
import math
import contextlib
import numpy as np
import ml_dtypes
import concourse.bass as bass
import concourse.mybir as mybir
from concourse.bass_utils import run_bass_kernel_spmd

F32 = mybir.dt.float32
BF16 = mybir.dt.bfloat16
I32 = mybir.dt.int32
AF = mybir.ActivationFunctionType
ALU = mybir.AluOpType
AX = mybir.AxisListType

D = 1024
KC = 8
DEPTH = 2
ALPHA = (2 * DEPTH) ** 0.25
LN_EPS = 1e-5
ROPE_THETA = 500000.0
MEM_LEN = 256
PI = math.pi


class Res:
    __slots__ = ("lw", "rd", "name")

    def __init__(self, name=""):
        self.lw = None
        self.rd = []
        self.name = name


class Sched:
    ENGS = ("pe", "act", "dve", "pool", "sp")
    NSLOT = 14

    def __init__(self, nc):
        self.nc = nc
        self.ops = []

    def add(self, eng, fn, reads=(), writes=(), dma=False):
        idx = len(self.ops)
        deps = set()
        for r in reads:
            if r.lw is not None:
                deps.add(r.lw)
        for w in writes:
            if w.lw is not None:
                deps.add(w.lw)
            deps.update(w.rd)
        for r in reads:
            r.rd.append(idx)
        for w in writes:
            w.lw = idx
            w.rd = []
        deps.discard(idx)
        self.ops.append([eng, fn, deps, dma])
        return idx

    def emit(self, final_wait_ops=()):
        nc = self.nc
        ops = self.ops
        n = len(ops)
        has_dep = [False] * n
        for i, (eng, fn, deps, dma) in enumerate(ops):
            for d in deps:
                if ops[d][0] == "pe" and eng == "pe" and not ops[d][3] and not dma:
                    continue
                has_dep[d] = True
        for d in final_wait_ops:
            has_dep[d] = True
        cnt = {e: 0 for e in self.ENGS}
        dcnt = {e: 0 for e in self.ENGS}
        sig = [None] * n
        for i, (eng, fn, deps, dma) in enumerate(ops):
            if dma:
                k = dcnt[eng]
                dcnt[eng] += 1
                sig[i] = ("d", eng, k % self.NSLOT, 16 * (k // self.NSLOT + 1))
            elif has_dep[i]:
                cnt[eng] += 1
                sig[i] = ("c", eng, cnt[eng])
        engs_used = [e for e in self.ENGS if any(o[0] == e for o in ops)]
        with contextlib.ExitStack() as st:
            csem = {e: st.enter_context(nc.semaphore("c_" + e)) for e in engs_used}
            dsem = {}
            for e in engs_used:
                if dcnt[e] > 0:
                    dsem[e] = [st.enter_context(nc.semaphore("d_%s_%d" % (e, s)))
                               for s in range(min(self.NSLOT, dcnt[e]))]
            block = st.enter_context(nc.Block())
            engobj = {"pe": "tensor", "act": "scalar", "dve": "vector", "pool": "gpsimd", "sp": "sync"}

            def make_stream(ename):
                def stream(eng):
                    waited_c = {}
                    waited_d = {}
                    for i, (e, fn, deps, dma) in enumerate(ops):
                        if e != ename:
                            continue
                        need_c = {}
                        need_d = {}
                        for d in deps:
                            s = sig[d]
                            if s is None:
                                continue
                            if s[0] == "c":
                                if s[1] == "pe" and ename == "pe" and not dma:
                                    continue
                                need_c[s[1]] = max(need_c.get(s[1], 0), s[2])
                            else:
                                key = (s[1], s[2])
                                need_d[key] = max(need_d.get(key, 0), s[3])
                        if dma:
                            s = sig[i]
                            if s[3] > 16:
                                key = (s[1], s[2])
                                need_d[key] = max(need_d.get(key, 0), s[3] - 16)
                        for se, v in need_c.items():
                            if waited_c.get(se, 0) < v:
                                eng.wait_ge(csem[se], v)
                                waited_c[se] = v
                        for key, v in need_d.items():
                            if waited_d.get(key, 0) < v:
                                eng.wait_ge(dsem[key[0]][key[1]], v)
                                waited_d[key] = v
                        ins = fn(eng)
                        s = sig[i]
                        if s is not None:
                            if s[0] == "c":
                                ins.then_inc(csem[ename], 1)
                            else:
                                ins.then_inc(dsem[ename][s[2]], 16)
                    if ename == "sp":
                        for d in final_wait_ops:
                            s = sig[d]
                            if s[0] == "c":
                                eng.wait_ge(csem[s[1]], s[2])
                            else:
                                eng.wait_ge(dsem[s[1]][s[2]], s[3])
                return stream

            for e in engs_used:
                getattr(block, engobj[e])(make_stream(e))


class SbufAlloc:
    def __init__(self, nc, nbytes=207 * 1024):
        self.nc = nc
        self.arena = nc.alloc_sbuf_tensor("arena", [128, nbytes], mybir.dt.uint8)
        self.off = 0
        self.limit = nbytes
        self.peak = 0

    def mark(self):
        return self.off

    def reset(self, m):
        self.off = m

    def tile(self, shape, dtype):
        assert shape[0] == 128
        esz = {F32: 4, BF16: 2, I32: 4}[dtype]
        nel = int(np.prod(shape[1:]))
        nbytes = (esz * nel + 63) // 64 * 64
        off = self.off
        self.off += nbytes
        self.peak = max(self.peak, self.off)
        assert self.off <= self.limit, ("SBUF overflow", self.off)
        ap = self.arena.ap()[:, off:off + esz * nel].bitcast(dtype)
        if len(shape) == 3:
            ap = ap.rearrange("p (a b) -> p a b", a=shape[1])
        elif len(shape) == 4:
            ap = ap.rearrange("p (a b c) -> p a b c", a=shape[1], b=shape[2])
        return ap


class T:
    def __init__(self, ap, name=""):
        self.ap = ap
        self.res = Res(name)


class Builder:
    def __init__(self, S, layer):
        self.S = S
        self.layer = layer
        self.NB = S // 128
        self.NJ = self.NB // 4
        self.NO = self.NJ * 128
        self.NR = self.NB // 16
        self.NE = self.NO + self.NR * 128
        self.NEB = self.NE // 128
        self.QGB = 4
        self.GW = 512
        self.NQG = self.NR
        self.groups = [(i * 512, 512) for i in range(self.NR)] + [(self.NO, self.NR * 128)]
        self.TW = min(512, S)
        self.NT = S // self.TW
        self.nc = bass.Bass("TRN2", target_bir_lowering=False)
        self.sch = Sched(self.nc)
        self.sa = SbufAlloc(self.nc)
        self.dram = {}
        self.psum = [T(self.nc.alloc_psum_tensor("ps%d" % i, [128, 512], F32).ap(), "ps%d" % i) for i in range(8)]

    def din(self, name, shape, dtype=F32):
        t = T(self.nc.dram_tensor(name, list(shape), dtype, kind="ExternalInput").ap(), name)
        self.dram[name] = t
        return t

    def dout(self, name, shape, dtype=F32):
        t = T(self.nc.dram_tensor(name, list(shape), dtype, kind="ExternalOutput").ap(), name)
        self.dram[name] = t
        return t

    def dscr(self, name, shape, dtype):
        t = T(self.nc.dram_tensor(name, list(shape), dtype).ap(), name)
        self.dram[name] = t
        return t

    def tile(self, shape, dtype, name=""):
        return T(self.sa.tile(shape, dtype), name)

    def op(self, eng, fn, reads=(), writes=(), dma=False):
        return self.sch.add(eng, fn, [t.res for t in reads], [t.res for t in writes], dma)

    def dma(self, out_ap, in_ap, reads, writes, q="sp"):
        return self.op(q, lambda e: e.dma_start(out=out_ap, in_=in_ap), reads, writes, dma=True)

    def mm(self, out_ap, lhsT, rhs, start, stop, reads, writes, skip=False):
        if skip:
            return self.op("pe", lambda e: e.matmul(out_ap, lhsT=lhsT, rhs=rhs, start=start, stop=stop,
                                                    skip_group_check=True), reads, writes)
        return self.op("pe", lambda e: e.matmul(out_ap, lhsT=lhsT, rhs=rhs, start=start, stop=stop), reads, writes)

    def act(self, out_ap, in_ap, func, reads, writes, scale=1.0, bias=0.0):
        return self.op("act", lambda e: e.activation(out=out_ap, in_=in_ap, func=func, bias=bias, scale=scale),
                       reads, writes)

    def tt(self, eng, out_ap, a, b, op, reads, writes):
        return self.op(eng, lambda e: e.tensor_tensor(out_ap, a, b, op), reads, writes)

    def ts(self, eng, out_ap, a, s1, s2, op0, op1, reads, writes):
        if op1 is None:
            return self.op(eng, lambda e: e.tensor_scalar(out_ap, a, s1, None, op0), reads, writes)
        return self.op(eng, lambda e: e.tensor_scalar(out_ap, a, s1, s2, op0, op1), reads, writes)

    def copy(self, eng, out_ap, in_ap, reads, writes):
        if eng == "act":
            return self.op("act", lambda e: e.copy(out_ap, in_ap), reads, writes)
        return self.op(eng, lambda e: e.tensor_copy(out_ap, in_ap), reads, writes)

    def memset(self, eng, ap, val, writes):
        return self.op(eng, lambda e: e.memset(ap, val), (), writes)

    def load_w(self, wd, n, wb, stage=None, c0=0):
        src = wd.ap.rearrange("(kc p) n -> p kc n", p=128)
        for k2 in range(0, KC, 2):
            self.dma(wb.ap[:, k2:k2 + 2, c0:c0 + n], src[:, k2:k2 + 2, :], [wd], [wb], q="pool")

    def rope_tables(self, pos_d, a, w, posi, posf, t1, cosT, sinS, invf, coefp):
        SC = 2 * PI * (1.0 - 1e-6)
        self.dma(posi.ap[:, 0:w], pos_d.ap[0:1, a:a + w].partition_broadcast(128), [pos_d], [posi])
        self.copy("dve", posf.ap[:, 0:w], posi.ap[:, 0:w], [posi], [posf])
        self.ts("dve", posf.ap[:, 0:w], posf.ap[:, 0:w], invf.ap[:, 0:1], None, ALU.mult, None, [posf, invf], [posf])
        self.copy("dve", posi.ap[:, 0:w], posf.ap[:, 0:w], [posf], [posi])
        self.copy("pool", t1.ap[:, 0:w], posi.ap[:, 0:w], [posi], [t1])
        self.tt("dve", t1.ap[:, 0:w], posf.ap[:, 0:w], t1.ap[:, 0:w], ALU.subtract, [posf, t1], [t1])
        self.act(sinS.ap[:, 0:w], t1.ap[:, 0:w], AF.Sin, [t1], [sinS], scale=SC)
        self.ts("dve", posf.ap[:, 0:w], posf.ap[:, 0:w], 0.25, None, ALU.add, None, [posf], [posf])
        self.copy("dve", posi.ap[:, 0:w], posf.ap[:, 0:w], [posf], [posi])
        self.copy("pool", t1.ap[:, 0:w], posi.ap[:, 0:w], [posi], [t1])
        self.tt("dve", t1.ap[:, 0:w], posf.ap[:, 0:w], t1.ap[:, 0:w], ALU.subtract, [posf, t1], [t1])
        self.act(cosT.ap[:, 0:w], t1.ap[:, 0:w], AF.Sin, [t1], [cosT], scale=SC)
        self.ts("pool", sinS.ap[:, 0:w], sinS.ap[:, 0:w], coefp.ap[:, 0:1], None, ALU.mult, None, [sinS, coefp], [sinS])

    def proj_fm(self, ps, wb, c0, xb, w, m=128):
        for kc in range(KC):
            self.mm(ps.ap[0:m, 0:w], wb.ap[:, kc, c0:c0 + m], xb.ap[:, kc, 0:w], kc == 0, kc == KC - 1, [wb, xb], [ps])


    def nextbank4(self):
        b = self.psum[self.b4[0] % 4]
        self.b4[0] += 1
        return b

    def softmax_norm(self, acc, rdcols, post_scale_col=None, add_row=None, W=None):
        rd, bcs, ones_f = self.rd, self.bcs, self.ones_f
        W = W or self.GW
        if add_row is not None:
            r2 = slice(rdcols.start + W, rdcols.stop + W)
            self.tt("dve", rd.ap[64:65, r2], acc.ap[64:65, 0:W], add_row[0], ALU.add, [acc, add_row[1]], [rd])
            self.op("dve", lambda e: e.reciprocal(rd.ap[64:65, rdcols], rd.ap[64:65, r2]), [rd], [rd])
        else:
            self.ts("dve", rd.ap[64:65, rdcols], acc.ap[64:65, 0:W], 1e-30, None, ALU.add, None, [acc], [rd])
            self.op("dve", lambda e: e.reciprocal(rd.ap[64:65, rdcols], rd.ap[64:65, rdcols]), [rd], [rd])
        if post_scale_col is not None:
            self.ts("dve", rd.ap[64:65, rdcols], rd.ap[64:65, rdcols], post_scale_col, None, ALU.mult, None, [rd, self.lamw], [rd])
        bc = self.nextbank4()
        self.mm(bc.ap[0:64, 0:W], ones_f.ap[64:65, 0:64], rd.ap[64:65, rdcols], True, True, [ones_f, rd], [bc])
        self.copy("act", bcs.ap[0:64, rdcols], bc.ap[0:64, 0:W], [bc], [bcs])

    def mem_units(self, mkT, mv, mqT, mixT, PT, ea, tq, zc, accc, groups):
        B = self
        PS = self.psum
        units = []
        for hm in range(4):
            cg, po = hm // 2, (hm % 2) * 64
            for (col0, W) in groups:
                acc = PS[4 + accc[0] % 4]
                accc[0] += 1
                gsl = slice(col0, col0 + W)
                for mb in range(2):
                    zi = zc[0]
                    zc[0] += 1
                    Sb = B.nextbank4()
                    PTt = PT[zi % len(PT)]

                    def s0(Sb=Sb, PTt=PTt, mb=mb, cg=cg, po=po, gsl=gsl, W=W):
                        B.mm(Sb.ap[:, 0:W], mkT.ap[po:po + 64, cg, mb * 128:(mb + 1) * 128], mqT.ap[po:po + 64, cg, gsl],
                             True, True, [mkT, mqT], [Sb])
                        B.act(PTt.ap[:, 0:W], Sb.ap[:, 0:W], AF.Exp, [Sb], [PTt], scale=0.125)

                    def s1(acc=acc, PTt=PTt, mb=mb, hm=hm, cg=cg, po=po, gsl=gsl, W=W):
                        B.mm(acc.ap[0:65, 0:W], mv.ap[:, mb, hm, 0:65], PTt.ap[:, 0:W], mb == 0, mb == 1, [mv, PTt], [acc], skip=True)
                        if mb == 1:
                            B.softmax_norm(acc, slice(0, W), W=W)
                            B.tt("dve", ea.ap[0:64, 0:W], acc.ap[0:64, 0:W], B.bcs.ap[0:64, 0:W], ALU.mult, [acc, B.bcs], [ea])
                            B.copy("act", tq.ap[po:po + 64, 0:W], ea.ap[0:64, 0:W], [ea], [tq])
                            B.tt("dve", mixT.ap[po:po + 64, 6 + cg, gsl], tq.ap[po:po + 64, 0:W], mixT.ap[po:po + 64, 6 + cg, gsl],
                                 ALU.mult, [tq, mixT], [mixT])

                    units.append([s0, s1])
        return units

    def mem_kv(self, memT, w_mkv, wb, wst, xs0, memb, mkT, mv, nextbank):
        B = self
        B.load_w(w_mkv, 512, wb, wst)
        B.dma(xs0.ap[:, :, 0:MEM_LEN], memT.ap.rearrange("(kc p) m -> p kc m", p=128), [memT], [xs0])
        B.copy("dve", memb.ap[:, :, 0:MEM_LEN], xs0.ap[:, :, 0:MEM_LEN], [xs0], [memb])
        for cg in range(2):
            ps = nextbank()
            B.proj_fm(ps, wb, cg * 128, memb, MEM_LEN)
            B.copy("act", mkT.ap[:, cg, :], ps.ap[:, 0:MEM_LEN], [ps], [mkT])
        B.memset("pool", mv.ap, 1.0, [mv])
        for mb in range(2):
            ps = nextbank()
            for kc in range(KC):
                B.mm(ps.ap[:, 0:256], memb.ap[:, kc, mb * 128:(mb + 1) * 128], wb.ap[:, kc, 256:512], kc == 0, kc == KC - 1,
                     [memb, wb], [ps])
            B.copy("act", mv.ap[:, mb, :, 0:64], ps.ap[:, 0:256].rearrange("p (h d) -> p h d", h=4), [ps], [mv])

    def out_block(self, j, mixT, wo, x_src, xr_t, vv_t, st, gbc, bbc, y_dst, nextbank):
        B = self
        B.dma(xr_t.ap, x_src.ap[j * 128:(j + 1) * 128, :], [x_src], [xr_t])
        for n in range(2):
            ps = nextbank()
            for kc in range(KC):
                B.mm(ps.ap[:, 0:512], mixT.ap[:, kc, j * 128:(j + 1) * 128], wo.ap[:, kc, n * 512:(n + 1) * 512],
                     kc == 0, kc == KC - 1, [mixT, wo], [ps])
            B.op("dve", lambda e, ps=ps, n=n: e.scalar_tensor_tensor(
                vv_t.ap[:, n * 512:(n + 1) * 512], xr_t.ap[:, n * 512:(n + 1) * 512], ALPHA, ps.ap[:, 0:512],
                ALU.mult, ALU.add), [ps, xr_t], [vv_t])
        return layer_norm_store(B, vv_t, xr_t, st, gbc, bbc, y_dst, j)

    def rope_evac(self, psK, psP, out_ap, w, cosT, sinS, tmpa, tmpb, out_t, scale=None):
        self.tt("dve", tmpa.ap[:, 0:w], psK.ap[:, 0:w], cosT.ap[:, 0:w], ALU.mult, [psK, cosT], [tmpa])
        self.tt("dve", tmpb.ap[:, 0:w], psP.ap[:, 0:w], sinS.ap[:, 0:w], ALU.mult, [psP, sinS], [tmpb])
        self.tt("pool", out_ap, tmpa.ap[:, 0:w], tmpb.ap[:, 0:w], ALU.add, [tmpa, tmpb], [out_t])


def _host_consts(layer):
    if layer == 0:
        hd, rot = 32, 8
    else:
        hd, rot = 64, 16
    half = rot // 2
    inv = np.exp(-(np.arange(half, dtype=np.float32) / half) * math.log(ROPE_THETA)).astype(np.float32)
    invf = np.zeros((128, 1), np.float32)
    coef = np.zeros((128, 1), np.float32)
    for r in range(128):
        d = r % hd
        if d < rot:
            invf[r, 0] = inv[d % half] / np.float32(2 * PI)
            coef[r, 0] = -1.0 if d < half else 1.0
    return invf, coef


def _partner_perm(ncols, hd, rot):
    half = rot // 2
    perm = np.arange(ncols)
    for c in range(ncols):
        d = c % hd
        if d < half:
            perm[c] = c + half
        elif d < rot:
            perm[c] = c - half
    return perm


def _masks(c):
    k = np.arange(128)[:, None]
    q = np.arange(128)[None, :]
    out = np.zeros((9, 128, 128), np.float32)
    out[8] = -1.0 * (k >= q)
    for r in range(4):
        if r < c:
            out[r] = 1.0
            out[4 + r] = 1.0
        elif r == c:
            out[r] = (k <= q)
            out[4 + r] = (k < q)
    return out.astype(ml_dtypes.bfloat16)


def run_pipeline(units, nst):
    n = len(units)
    for step in range(n + nst - 1):
        for s in range(nst):
            u = step - s
            if 0 <= u < n and units[u][s] is not None:
                units[u][s]()


def build_l0(S, lambda_init):
    B = Builder(S, 0)
    outs = l0_body(B, lambda_init, False)
    B.sch.emit(final_wait_ops=outs)
    return B


def l0_body(B, lambda_init, fused):
    nc = B.nc
    S = B.S
    NB, NJ, NO, QGB, GW, NQG, TW, NT = B.NB, B.NJ, B.NO, B.QGB, B.GW, B.NQG, B.TW, B.NT
    NR, NE, NEB = B.NR, B.NE, B.NEB
    etiles = [(a, min(512, NE - a)) for a in range(0, NE, 512)]
    xT_all = B.din("xT_all", [D, S])
    xT_own = B.din("xT_ext", [D, NE])
    x_own = B.din("x_ext", [NE, D])
    pos_all = B.din("pos_all", [1, S], I32)
    pos_own = B.din("pos_ext", [1, NE], I32)
    w_k = B.din("w_k", [D, 1152])
    w_v = B.din("w_v", [D, 768])
    w_q = B.din("w_q", [D, 1408])
    w_g = B.din("w_g", [D, 1024])
    memT = B.din("memT", [D, MEM_LEN])
    w_mkv = B.din("w_mkv", [D, 512])
    w_o = B.din("w_o", [D, D])
    ln_g = B.din("ln_g", [1, D])
    ln_b = B.din("ln_b", [1, D])
    dlam = B.din("dlam", [1, 128])
    subln = B.din("subln", [128, 1])
    invf_d = B.din("invf", [128, 1])
    coef_d = B.din("coefp", [128, 1])
    masks_d = B.din("masks", [128, 9 * 128], BF16)
    qcol_d = B.din("qcol", [128, 1024])
    stab_d = B.din("stab", [128, 16 + 64])
    y_out = B.dscr("x1_ext", [NE, D], F32)
    B.x1_ext_d = y_out
    kT_scr = B.dscr("kT_scr", [128, 6, S], BF16)
    v_scr = B.dscr("v_scr", [128, 12, NB * 65], BF16)

    masks = B.tile([128, 9, 128], BF16, "masks")
    negtri = T(masks.ap[:, 8, :], "negtri")
    negtri.res = masks.res
    negones = B.tile([128, 128], BF16, "negones")
    ones_f = B.tile([128, 128], F32, "ones_f")
    invf = B.tile([128, 1], F32, "invf")
    coefp = B.tile([128, 1], F32, "coefp")
    B.pi_col = B.tile([128, 1], F32, "pi")
    g1col = B.tile([128, 1], F32, "g1col")
    lam_t = B.tile([128, 128], F32, "lam")
    lamw = B.tile([128, 8], F32, "lamw")
    B.eps_col = B.tile([128, 1], F32, "eps")
    prod = B.tile([128, 64], F32, "prod")
    qcol = B.tile([128, 1024], F32, "qcol")
    stab = B.tile([128, 80], F32, "stab")
    B.mark_consts = B.sa.mark()
    qT_sb = B.tile([128, 3, NE], BF16, "qT_sb")
    qT_df = B.tile([128, 3, NE], BF16, "qT_df")
    mqT = B.tile([128, 2, NE], BF16, "mqT")
    mixT = B.tile([128, 8, NE], BF16, "mixT")
    mkT = B.tile([128, 2, MEM_LEN], BF16, "mkT")
    mv = B.tile([128, 2, 4, 65], BF16, "mv")
    PS = B.psum
    pbc = [0]

    def nextbank():
        b = PS[pbc[0] % 8]
        pbc[0] += 1
        return b

    B.dma(masks.ap.rearrange("p a b -> p (a b)"), masks_d.ap[:, :], [masks_d], [masks])
    B.dma(invf.ap, invf_d.ap[:, :], [invf_d], [invf])
    B.dma(coefp.ap, coef_d.ap[:, :], [coef_d], [coefp])
    B.dma(qcol.ap, qcol_d.ap[:, :], [qcol_d], [qcol])
    B.dma(stab.ap, stab_d.ap[:, :], [stab_d], [stab])
    B.memset("pool", B.pi_col.ap, PI, [B.pi_col])
    B.memset("pool", B.eps_col.ap, LN_EPS, [B.eps_col])
    B.memset("pool", ones_f.ap, 1.0, [ones_f])
    B.memset("pool", negones.ap, -1.0, [negones])
    B.dma(lam_t.ap[64:65, 0:128], dlam.ap[0:1, :], [dlam], [lam_t])
    B.tt("dve", prod.ap[64:65, 0:32], lam_t.ap[64:65, 0:32], lam_t.ap[64:65, 32:64], ALU.mult, [lam_t], [prod])
    B.tt("dve", prod.ap[64:65, 32:64], lam_t.ap[64:65, 64:96], lam_t.ap[64:65, 96:128], ALU.mult, [lam_t], [prod])
    B.op("dve", lambda e: e.tensor_reduce(lamw.ap[64:65, 0:1], prod.ap[64:65, 0:32], AX.X, ALU.add), [prod], [lamw])
    B.op("dve", lambda e: e.tensor_reduce(lamw.ap[64:65, 1:2], prod.ap[64:65, 32:64], AX.X, ALU.add), [prod], [lamw])
    B.act(lamw.ap[64:65, 2:4], lamw.ap[64:65, 0:2], AF.Exp, [lamw], [lamw])
    B.tt("dve", lamw.ap[64:65, 4:5], lamw.ap[64:65, 2:3], lamw.ap[64:65, 3:4], ALU.subtract, [lamw], [lamw])
    B.ts("dve", lamw.ap[64:65, 5:6], lamw.ap[64:65, 4:5], lambda_init, -1.0, ALU.add, ALU.mult, [lamw], [lamw])
    B.dma(g1col.ap, subln.ap[:, :], [subln], [g1col])
    B.ts("dve", g1col.ap, g1col.ap, 1.0 - lambda_init, None, ALU.mult, None, [g1col], [g1col])

    mA = B.sa.mark()
    wb = B.tile([128, KC, 1920], BF16, "wb")
    wst = [B.tile([128, KC, 64], F32, "wst%d" % i) for i in range(2)]
    xs = [B.tile([128, KC, TW], F32, "xs%d" % i) for i in range(2)]
    xb = [B.tile([128, KC, TW], BF16, "xb%d" % i) for i in range(2)]
    posi = B.tile([128, TW], I32, "posi")
    posf = B.tile([128, TW], F32, "posf")
    t1 = B.tile([128, TW], F32, "t1")
    cosT = B.tile([128, TW], F32, "cosT")
    sinS = B.tile([128, TW], F32, "sinS")
    tmpa = B.tile([128, TW], F32, "tmpa")
    tmpb = B.tile([128, TW], F32, "tmpb")
    ktst = [B.tile([128, 6, TW], BF16, "ktst%d" % i) for i in range(1)]
    vst = [B.tile([128, 12, TW // 128, 65], BF16, "vst%d" % i) for i in range(2)]

    B.mem_kv(memT, w_mkv, wb, wst, xs[0], xb[0], mkT, mv, nextbank)

    B.load_w(w_k, 1152, wb, wst, 0)
    B.load_w(w_v, 768, wb, wst, 1152)
    for i in range(2):
        B.memset("pool", vst[i].ap, 1.0, [vst[i]])
    xsrc = xT_all.ap.rearrange("(kc p) s -> p kc s", p=128)
    for t in range(NT):
        xs_t, xb_t = xs[t % 2], xb[t % 2]
        kt_t, v_t = ktst[0], vst[t % 2]
        for hlf in range(2):
            B.dma(xs_t.ap[:, hlf * 4:(hlf + 1) * 4, :], xsrc[:, hlf * 4:(hlf + 1) * 4, t * TW:(t + 1) * TW], [xT_all], [xs_t])
        for kc in range(KC):
            B.copy("pool" if kc % 2 == 0 else "dve", xb_t.ap[:, kc, :], xs_t.ap[:, kc, :], [xs_t], [xb_t])
        B.rope_tables(pos_all, t * TW, TW, posi, posf, t1, cosT, sinS, invf, coefp)
        for cg in range(3):
            ps = nextbank()
            B.proj_fm(ps, wb, cg * 128, xb_t, TW)
            B.copy("act", kt_t.ap[:, cg, :], ps.ap[:, 0:TW], [ps], [kt_t])
        for cg in range(3):
            psK = nextbank()
            B.proj_fm(psK, wb, 384 + cg * 128, xb_t, TW)
            psP = nextbank()
            B.proj_fm(psP, wb, 768 + cg * 128, xb_t, TW)
            B.rope_evac(psK, psP, kt_t.ap[:, 3 + cg, :], TW, cosT, sinS, tmpa, tmpb, kt_t)
        B.dma(kT_scr.ap[:, :, t * TW:(t + 1) * TW], kt_t.ap, [kt_t], [kT_scr])
        for blk in range(TW // 128):
            for half in range(2):
                ps = nextbank()
                for kc in range(KC):
                    B.mm(ps.ap[:, 0:384], xb_t.ap[:, kc, blk * 128:(blk + 1) * 128],
                         wb.ap[:, kc, 1152 + half * 384:1152 + (half + 1) * 384], kc == 0, kc == KC - 1, [xb_t, wb], [ps])
                B.copy("act" if half == 0 else "dve", v_t.ap[:, half * 6:(half + 1) * 6, blk, 0:64],
                       ps.ap[:, 0:384].rearrange("p (h d) -> p h d", h=6), [ps], [v_t])
        nb_t = TW // 128
        B.dma(v_scr.ap[:, :, t * nb_t * 65:(t + 1) * nb_t * 65], v_t.ap.rearrange("p h b c -> p h (b c)"), [v_t], [v_scr])

    xosrc = xT_own.ap.rearrange("(kc p) s -> p kc s", p=128)
    for rnd in range(2):
        if rnd == 0:
            B.load_w(w_q, 1408, wb, wst, 0)
        else:
            B.load_w(w_g, 1024, wb, wst, 0)
        for u, (ea0, OW) in enumerate(etiles):
            xs_t, xb_t = xs[u % 2], xb[u % 2]
            osl = slice(ea0, ea0 + OW)
            for hlf in range(2):
                B.dma(xs_t.ap[:, hlf * 4:(hlf + 1) * 4, 0:OW], xosrc[:, hlf * 4:(hlf + 1) * 4, osl], [xT_own], [xs_t])
            for kc in range(KC):
                B.copy("pool" if kc % 2 == 0 else "dve", xb_t.ap[:, kc, 0:OW], xs_t.ap[:, kc, 0:OW], [xs_t], [xb_t])
            if rnd == 0:
                B.rope_tables(pos_own, ea0, OW, posi, posf, t1, cosT, sinS, invf, coefp)
                for cg in range(3):
                    ps = nextbank()
                    B.proj_fm(ps, wb, cg * 128, xb_t, OW)
                    B.act(qT_sb.ap[:, cg, osl], ps.ap[:, 0:OW], AF.Copy, [ps], [qT_sb], scale=0.125)
                for cg in range(3):
                    psK = nextbank()
                    B.proj_fm(psK, wb, 384 + cg * 128, xb_t, OW)
                    psP = nextbank()
                    B.proj_fm(psP, wb, 768 + cg * 128, xb_t, OW)
                    B.rope_evac(psK, psP, qT_df.ap[:, cg, osl], OW, cosT, sinS, tmpa, tmpb, qT_df)
                for cg in range(2):
                    ps = nextbank()
                    B.proj_fm(ps, wb, 1152 + cg * 128, xb_t, OW)
                    B.copy("act", mqT.ap[:, cg, osl], ps.ap[:, 0:OW], [ps], [mqT])
            else:
                for cg in range(8):
                    ps = nextbank()
                    B.proj_fm(ps, wb, cg * 128, xb_t, OW)
                    B.act(mixT.ap[:, cg, osl], ps.ap[:, 0:OW], AF.Silu, [ps], [mixT])

    phaseA_tiles = [wb] + wst + xs + xb + [posi, posf, t1, cosT, sinS, tmpa, tmpb] + ktst + vst
    B.sa.reset(mA)
    kTp = [B.tile([128, S], BF16, "kTp%d" % i) for i in range(2)]
    vtp = [B.tile([128, 2, NB * 65 + 64], BF16, "vtp%d" % i) for i in range(2)]
    E = [B.tile([128, GW], F32, "E%d" % i) for i in range(4)]
    SP = [B.tile([128, GW], BF16, "SP%d" % i) for i in range(6)]
    PT = [B.tile([128, GW], BF16, "PT%d" % i) for i in range(6)]
    R32s = [B.tile([128, GW], F32, "R32_%d" % i) for i in range(2)]
    Rbs = [[B.tile([128, GW], BF16, "Rb%d_%d" % (ch, i)) for i in range(2)] for ch in range(2)]
    tmpo = [B.tile([128, GW], F32, "tmpo%d" % i) for i in range(2)]
    rd = B.tile([128, 2 * GW], F32, "rd")
    bcs = B.tile([128, 2 * GW], F32, "bcs")
    ea = B.tile([128, GW], F32, "ea")
    eb = B.tile([128, GW], F32, "eb")
    ec = B.tile([128, GW], F32, "ec")
    fz = B.tile([128, 16], F32, "fz")
    phaseB_tiles = kTp + vtp + E + SP + PT + R32s + Rbs[0] + Rbs[1] + tmpo + [rd, bcs, ea, eb, ec, fz]
    B.op("pool", lambda e: e.memset(fz.ap, 0.0), [], phaseA_tiles + phaseB_tiles)
    for i in range(2):
        B.memset("pool", vtp[i].ap[:, :, NB * 65:NB * 65 + 64], 0.0, [vtp[i]])

    kview = kT_scr.ap
    vview = v_scr.ap

    def load_pair(kcg, vh0, slot):
        B.dma(kTp[slot].ap, kview[:, kcg, :], [kT_scr], [kTp[slot]])
        B.dma(vtp[slot].ap[:, :, 0:NB * 65], vview[:, vh0:vh0 + 2, :], [v_scr], [vtp[slot]])

    zc = [0]
    accc = [0]

    def kb_list(gi):
        out = []
        if gi < NR:
            for kb in range(16 * gi + 15, -1, -1):
                if kb >= 16 * gi:
                    m = kb - 16 * gi
                    out.append((kb, max(0, m - 12) * 128, m))
                else:
                    out.append((kb, 0, None))
            return gi * 512, 512, qcol.ap[:, 0:512], out
        W = NR * 128
        for kb in range(16 * (NR - 1) + 11, -1, -1):
            c0 = ((kb - 11 + 15) // 16) * 128 if kb > 11 else 0
            out.append((kb, c0, 16 + kb))
        return NO, W, qcol.ap[:, 512:512 + W], out

    def mask_op(Pt, qc, cs, midx, strict):
        B.op("dve", lambda e: e.scalar_tensor_tensor(Pt.ap[:, cs], qc[:, cs], stab.ap[:, midx:midx + 1], Pt.ap[:, cs],
                                                     ALU.is_gt if strict else ALU.is_ge, ALU.mult),
             [Pt, qcol, stab], [Pt])

    def sb_units(h, slot):
        cg, po = h // 2, (h % 2) * 64
        ch = h % 2
        R32, Rb = R32s[ch], Rbs[ch]
        kT, vt = kTp[slot], vtp[slot]
        units = []
        ucount = 0
        for gi in range(NR + 1):
            col0, W, qc, kbs = kb_list(gi)
            acc = PS[4 + ch + 2 * (gi % 2)]
            for ui, (kb, c0, midx) in enumerate(kbs):
                first, last = ui == 0, ui == len(kbs) - 1
                zi = 2 * ucount + ch
                ucount += 1
                Z, ARG = PS[zi % 2], PS[2 + zi % 2]
                Et, SPt, PTt = E[zi % 4], SP[zi % 6], PT[zi % 6]
                Rcur, Rnext = Rb[(zi // 2) % 2], Rb[(zi // 2 + 1) % 2]
                kap = kT.ap[po:po + 64, kb * 128:(kb + 1) * 128]
                qap = qT_sb.ap[po:po + 64, cg, col0 + c0:col0 + W]
                cs = slice(c0, W)

                def s0(Z=Z, Et=Et, SPt=SPt, kap=kap, qap=qap, cs=cs, midx=midx, kT=kT, qc=qc):
                    B.mm(Z.ap[:, cs], kap, qap, True, True, [kT, qT_sb], [Z])
                    B.act(Et.ap[:, cs], Z.ap[:, cs], AF.Exp, [Z], [Et])
                    B.act(SPt.ap[:, cs], Et.ap[:, cs], AF.Ln, [Et], [SPt], bias=1.0)
                    if midx is not None:
                        mask_op(SPt, qc, cs, midx, True)

                def s1(ARG=ARG, SPt=SPt, PTt=PTt, kap=kap, qap=qap, cs=cs, midx=midx, first=first, last=last,
                       Rcur=Rcur, Rnext=Rnext, kT=kT, qc=qc, W=W):
                    B.mm(ARG.ap[:, cs], kap, qap, True, False, [kT, qT_sb], [ARG])
                    B.mm(ARG.ap[:, cs], negtri.ap, SPt.ap[:, cs], False, first, [negtri, SPt], [ARG])
                    if not first:
                        B.mm(ARG.ap[:, cs], negones.ap, Rcur.ap[:, cs], False, True, [negones, Rcur], [ARG])
                    if first:
                        B.memset("pool", R32.ap, 0.0, [R32])
                    if not last:
                        B.tt("dve", R32.ap[:, cs], R32.ap[:, cs], SPt.ap[:, cs], ALU.add, [R32, SPt], [R32])
                        B.copy("pool", Rnext.ap[:, 0:W], R32.ap[:, 0:W], [R32], [Rnext])
                    B.act(PTt.ap[:, cs], ARG.ap[:, cs], AF.Exp, [ARG], [PTt])
                    if midx is not None:
                        mask_op(PTt, qc, cs, midx, True)

                def s2(acc=acc, PTt=PTt, cs=cs, kb=kb, first=first, last=last, gi=gi, vt=vt, col0=col0, W=W):
                    B.mm(acc.ap[:, cs], vt.ap[:, h % 2, kb * 65:kb * 65 + 128], PTt.ap[:, cs], first, last, [vt, PTt], [acc], skip=True)
                    if last:
                        tq = tmpo[ch]
                        gsl = slice(col0, col0 + W)
                        B.copy("act", tq.ap[po:po + 64, 0:W], acc.ap[0:64, 0:W], [acc], [tq])
                        B.tt("dve", mixT.ap[po:po + 64, cg, gsl], tq.ap[po:po + 64, 0:W], mixT.ap[po:po + 64, cg, gsl],
                             ALU.mult, [tq, mixT], [mixT])

                units.append([s0, s1, s2])
        return units

    B.rd, B.bcs, B.ones_f, B.b4, B.lamw = rd, bcs, ones_f, [0], lamw
    softmax_norm = B.softmax_norm
    nextbank4 = B.nextbank4

    def df_units(h, slot):
        cgk, po = h // 2, (h % 2) * 64
        kT, vt = kTp[slot], vtp[slot]
        units = []
        scale = 32 ** -0.5
        for gi in range(NR + 1):
            col0, W, qc, kbs = kb_list(gi)
            accs = [PS[4 + 2 * (accc[0] % 2)], PS[5 + 2 * (accc[0] % 2)]]
            accc[0] += 1
            gsl = slice(col0, col0 + W)
            for ui, (kb, c0, midx) in enumerate(kbs):
                first, last = ui == 0, ui == len(kbs) - 1
                cs = slice(c0, W)
                Sbs, PTs, kaps, qaps, tps = [], [], [], [], []
                for cm in range(2):
                    zi = zc[0]
                    zc[0] += 1
                    Sbs.append(nextbank4())
                    PTs.append(PT[zi % 4])
                    rows = slice(po + cm * 32, po + cm * 32 + 32)
                    kaps.append(kT.ap[rows, kb * 128:(kb + 1) * 128])
                    qaps.append(qT_df.ap[rows, cgk, col0 + c0:col0 + W])
                    tps.append((po + cm * 32, 0))

                def s0(Sbs=Sbs, PTs=PTs, kaps=kaps, qaps=qaps, cs=cs, midx=midx, tps=tps, kT=kT, qc=qc):
                    for cm in range(2):
                        B.op("pe", lambda e, cm=cm: e.matmul(Sbs[cm].ap[:, cs], lhsT=kaps[cm], rhs=qaps[cm], start=True, stop=True,
                                                             tile_position=tps[cm]), [kT, qT_df], [Sbs[cm]])
                    for cm in range(2):
                        B.act(PTs[cm].ap[:, cs], Sbs[cm].ap[:, cs], AF.Exp, [Sbs[cm]], [PTs[cm]], scale=scale)
                        if midx is not None:
                            mask_op(PTs[cm], qc, cs, midx, False)

                def s1(PTs=PTs, cs=cs, kb=kb, first=first, last=last, accs=accs, gsl=gsl, vt=vt, W=W):
                    for cm in range(2):
                        B.mm(accs[cm].ap[:, cs], vt.ap[:, h % 2, kb * 65:kb * 65 + 128], PTs[cm].ap[:, cs], first, last,
                             [vt, PTs[cm]], [accs[cm]], skip=True)
                    if last:
                        softmax_norm(accs[0], slice(0, W), W=W)
                        softmax_norm(accs[1], slice(512, 512 + W), post_scale_col=lamw.ap[64:65, 5:6], W=W)
                        B.tt("dve", ea.ap[0:64, 0:W], accs[0].ap[0:64, 0:W], bcs.ap[0:64, 0:W], ALU.mult, [accs[0], bcs], [ea])
                        B.tt("dve", eb.ap[0:64, 0:W], accs[1].ap[0:64, 0:W], bcs.ap[0:64, 512:512 + W], ALU.mult, [accs[1], bcs], [eb])
                        B.tt("pool", ea.ap[0:64, 0:W], ea.ap[0:64, 0:W], eb.ap[0:64, 0:W], ALU.add, [ea, eb], [ea])
                        B.tt("pool", eb.ap[0:64, 0:W], ea.ap[0:64, 0:W], ea.ap[0:64, 0:W], ALU.mult, [ea], [eb])
                        ss = nextbank4()
                        B.mm(ss.ap[0:64, 0:W], ones_f.ap[0:64, 0:64], eb.ap[0:64, 0:W], True, True, [ones_f, eb], [ss])
                        B.act(ec.ap[0:64, 0:W], ss.ap[0:64, 0:W], AF.Ln, [ss, B.eps_col], [ec], scale=1.0 / 64, bias=B.eps_col.ap[0:64, 0:1])
                        B.act(ec.ap[0:64, 0:W], ec.ap[0:64, 0:W], AF.Exp, [ec], [ec], scale=-0.5)
                        B.tt("dve", ea.ap[0:64, 0:W], ea.ap[0:64, 0:W], ec.ap[0:64, 0:W], ALU.mult, [ea, ec], [ea])
                        tq = tmpo[0]
                        B.copy("act", tq.ap[po:po + 64, 0:W], ea.ap[0:64, 0:W], [ea], [tq])
                        mcg = 3 + h // 2
                        B.op("dve", lambda e: e.scalar_tensor_tensor(
                            mixT.ap[po:po + 64, mcg, gsl], tq.ap[po:po + 64, 0:W], g1col.ap[po:po + 64, 0:1],
                            mixT.ap[po:po + 64, mcg, gsl], ALU.mult, ALU.mult), [tq, g1col, mixT], [mixT])

                units.append([s0, s1])
        return units

    pairs = [("sb", p) for p in range(3)] + [("df", p) for p in range(3)]
    load_pair(0, 0, 0)
    run_pipeline(B.mem_units(mkT, mv, mqT, mixT, PT, ea, tmpo[1], zc, accc, B.groups), 2)
    for pi, (kind, p) in enumerate(pairs):
        slot = pi % 2
        if pi + 1 < len(pairs):
            k2, p2 = pairs[pi + 1]
            load_pair(p2 if k2 == "sb" else 3 + p2, 2 * p2 if k2 == "sb" else 6 + 2 * p2, (pi + 1) % 2)
        if kind == "sb":
            ua, ub = sb_units(2 * p, slot), sb_units(2 * p + 1, slot)
            run_pipeline([u for pair in zip(ua, ub) for u in pair], 3)
        else:
            for h in (2 * p, 2 * p + 1):
                run_pipeline(df_units(h, slot), 2)

    B.sa.reset(mA)
    wo = B.tile([128, KC, D], BF16, "wo")
    wst2 = [B.tile([128, KC, 128], F32, "wst2_%d" % i) for i in range(2)]
    gbc = B.tile([128, D], F32, "gbc")
    bbc = B.tile([128, D], F32, "bbc")
    xr = [B.tile([128, D], F32, "xr%d" % i) for i in range(2)]
    vv = [B.tile([128, D], F32, "vv%d" % i) for i in range(2)]
    st = B.tile([128, 8], F32, "st")
    fz2 = B.tile([128, 16], F32, "fz2")
    phaseC_tiles = [wo, gbc, bbc, st, fz2] + wst2 + xr + vv
    B.op("pool", lambda e: e.memset(fz2.ap, 0.0), [], phaseB_tiles + phaseC_tiles)
    B.load_w(w_o, D, wo, wst2)
    B.dma(gbc.ap, ln_g.ap[0:1, :].partition_broadcast(128), [ln_g], [gbc])
    B.dma(bbc.ap, ln_b.ap[0:1, :].partition_broadcast(128), [ln_b], [bbc])
    outs = []
    for j in range(NEB):
        outs.append(B.out_block(j, mixT, wo, x_own, xr[j % 2], vv[j % 2], st, gbc, bbc, y_out, nextbank))
    B.l0_tiles = phaseB_tiles + phaseC_tiles + [qT_sb, qT_df, mqT, mixT, mkT, mv]
    B.eps_ready = True
    return outs


DBG = {}
QCOLS = [0, 4, 1, 5, 2, 6, 3, 7, 8, 8, 9, 9, 10, 10, 11, 11]


def _qpos(hq):
    if hq < 4:
        return hq, 0
    if hq < 8:
        return hq - 4, 64
    return 4 + hq - 8, 0


def l1_body(B):
    nc = B.nc
    S = B.S
    NB, NJ, NO, NR, NE, NEB = B.NB, B.NJ, B.NO, B.NR, B.NE, B.NEB
    PS = B.psum
    etiles = [(a, min(512, NE - a)) for a in range(0, NE, 512)]
    x1_ext = B.x1_ext_d
    pos_ext = B.dram["pos_ext"]
    memT = B.dram["memT"]
    w1_kv = B.din("w1_kv", [D, 960])
    w1_q = B.din("w1_q", [D, 2048])
    w1_g = B.din("w1_g", [D, 1024])
    w1_mkv = B.din("w1_mkv", [D, 512])
    w1_o = B.din("w1_o", [D, D])
    ln1_g = B.din("ln1_g", [1, D])
    ln1_b = B.din("ln1_b", [1, D])
    sinks_x = B.din("sinks_x", [1, 1536])
    invf1_d = B.din("invf1", [128, 1])
    coef1_d = B.din("coefp1", [128, 1])
    masks1_d = B.din("masks1", [128, 3 * 128], BF16)
    ident_d = B.din("ident", [128, 128])
    y_out = B.dout("y", [NO, D])

    pbc = [0]

    def nextbank():
        b = PS[pbc[0] % 8]
        pbc[0] += 1
        return b

    B.sa.reset(B.mark_consts)
    new_tiles = []

    def tl(shape, dt, name):
        t = B.tile(shape, dt, name)
        new_tiles.append(t)
        return t

    ident = tl([128, 128], F32, "ident")
    masks1 = tl([128, 3, 128], BF16, "masks1")
    invf1 = tl([128, 1], F32, "invf1")
    coef1 = tl([128, 1], F32, "coef1")
    qT1 = tl([128, 8, NO], BF16, "qT1")
    kT1 = tl([128, 2, NE], BF16, "kT1")
    V1 = tl([128, NEB, 3, 65], BF16, "V1")
    mqT1 = tl([128, 2, NO], BF16, "mqT1")
    mixT1 = tl([128, 8, NO], BF16, "mixT1")
    mkT1 = tl([128, 2, MEM_LEN], BF16, "mkT1")
    mv1 = tl([128, 2, 4, 65], BF16, "mv1")
    fz = tl([128, 16], F32, "fz1")
    mP = B.sa.mark()
    xT1 = tl([128, KC, NE], BF16, "xT1")
    xin = [tl([128, D], F32, "xin%d" % i) for i in range(2)]
    wb1 = tl([128, KC, 1024], BF16, "wb1")
    wst = [tl([128, KC, 64], F32, "wst1_%d" % i) for i in range(2)]
    posi = tl([128, 512], I32, "posi1")
    posf = tl([128, 512], F32, "posf1")
    t1 = tl([128, 512], F32, "t11")
    cosT = tl([128, 512], F32, "cosT1")
    sinS = tl([128, 512], F32, "sinS1")
    tmpa = tl([128, 512], F32, "tmpa1")
    tmpb = tl([128, 512], F32, "tmpb1")
    xs_m = tl([128, KC, MEM_LEN], F32, "xs_m")
    memb = tl([128, KC, MEM_LEN], BF16, "memb1")
    B.op("pool", lambda e: e.memset(fz.ap, 0.0), [], B.l0_tiles + new_tiles)
    B.dma(ident.ap, ident_d.ap[:, :], [ident_d], [ident])
    B.dma(masks1.ap.rearrange("p a b -> p (a b)"), masks1_d.ap[:, :], [masks1_d], [masks1])
    B.dma(invf1.ap, invf1_d.ap[:, :], [invf1_d], [invf1])
    B.dma(coef1.ap, coef1_d.ap[:, :], [coef1_d], [coef1])

    B.mem_kv(memT, w1_mkv, wb1, wst, xs_m, memb, mkT1, mv1, nextbank)
    B.memset("pool", V1.ap, 1.0, [V1])

    xc = [0]

    def transpose_block(e):
        xin_t = xin[xc[0] % 2]
        xc[0] += 1
        B.dma(xin_t.ap, x1_ext.ap[e * 128:(e + 1) * 128, :], [x1_ext], [xin_t])
        for half in range(2):
            ps = nextbank()
            for q in range(4):
                kc = half * 4 + q
                B.op("pe", lambda en, ps=ps, q=q, kc=kc: en.transpose(ps.ap[:, q * 128:(q + 1) * 128],
                                                                     xin_t.ap[:, kc * 128:(kc + 1) * 128], ident.ap),
                     [xin_t, ident], [ps])
            B.copy("act" if half == 0 else "dve", xT1.ap[:, half * 4:(half + 1) * 4, e * 128:(e + 1) * 128],
                   ps.ap[:, 0:512].rearrange("p (a b) -> p a b", a=4), [ps], [xT1])

    def xtile(a0, w):
        t = T(xT1.ap[:, :, a0:a0 + w], "xo_t")
        t.res = xT1.res
        return t

    B.load_w(w1_kv, 960, wb1, wst, 0)
    for (a0, w) in etiles:
        for e in range(a0 // 128, (a0 + w) // 128):
            transpose_block(e)
        xo_t = xtile(a0, w)
        B.rope_tables(pos_ext, a0, w, posi, posf, t1, cosT, sinS, invf1, coef1)
        for cg in range(2):
            psK = nextbank()
            B.proj_fm(psK, wb1, cg * 128, xo_t, w)
            psP = nextbank()
            B.proj_fm(psP, wb1, 256 + cg * 128, xo_t, w)
            B.rope_evac(psK, psP, kT1.ap[:, cg, a0:a0 + w], w, cosT, sinS, tmpa, tmpb, kT1)
        for blk in range(w // 128):
            e = a0 // 128 + blk
            ps = nextbank()
            for kc in range(KC):
                B.mm(ps.ap[:, 0:192], xo_t.ap[:, kc, blk * 128:(blk + 1) * 128], wb1.ap[:, kc, 512:704],
                     kc == 0, kc == KC - 1, [xo_t, wb1], [ps])
            B.copy("act", V1.ap[:, e, :, 0:64], ps.ap[:, 0:192].rearrange("p (h d) -> p h d", h=3), [ps], [V1])
        if a0 < NO:
            for cg in range(2):
                ps = nextbank()
                B.proj_fm(ps, wb1, 704 + cg * 128, xo_t, w)
                B.copy("act", mqT1.ap[:, cg, a0:a0 + w], ps.ap[:, 0:w], [ps], [mqT1])
    for rnd in range(3):
        if rnd < 2:
            B.load_w(T(w1_q.ap[:, rnd * 1024:(rnd + 1) * 1024], "w1q"), 1024, wb1, wst, 0)
        else:
            B.load_w(w1_g, 1024, wb1, wst, 0)
        for (a0, w) in etiles:
            if a0 >= NO:
                continue
            osl = slice(a0, a0 + w)
            xo_t = xtile(a0, w)
            if rnd < 2:
                B.rope_tables(pos_ext, a0, w, posi, posf, t1, cosT, sinS, invf1, coef1)
                for cg in range(4):
                    psK = nextbank()
                    B.proj_fm(psK, wb1, cg * 128, xo_t, w)
                    psP = nextbank()
                    B.proj_fm(psP, wb1, 512 + cg * 128, xo_t, w)
                    B.rope_evac(psK, psP, qT1.ap[:, rnd * 4 + cg, osl], w, cosT, sinS, tmpa, tmpb, qT1)
            else:
                for cg in range(8):
                    ps = nextbank()
                    B.proj_fm(ps, wb1, cg * 128, xo_t, w)
                    B.act(mixT1.ap[:, cg, osl], ps.ap[:, 0:w], AF.Silu, [ps], [mixT1])

    proj_tiles = [xT1] + xin + [wb1] + wst + [posi, posf, t1, cosT, sinS, tmpa, tmpb, xs_m, memb]
    B.sa.reset(mP)
    esink = B.tile([128, 1536], F32, "esink")
    esraw = B.tile([128, 1536], F32, "esraw")
    PT1 = [B.tile([128, 512], BF16, "PT1_%d" % i) for i in range(4)]
    rd = B.tile([128, 1024], F32, "rd1")
    bcs = B.tile([128, 1024], F32, "bcs1")
    ea = B.tile([128, 512], F32, "ea1")
    tq = [B.tile([128, 512], F32, "tq1_%d" % i) for i in range(2)]
    wo1 = B.tile([128, KC, D], BF16, "wo1")
    wst2 = [B.tile([128, KC, 128], F32, "wst1b_%d" % i) for i in range(2)]
    gbc = B.tile([128, D], F32, "gbc1")
    bbc = B.tile([128, D], F32, "bbc1")
    xr = [B.tile([128, D], F32, "xr1_%d" % i) for i in range(2)]
    vv = [B.tile([128, D], F32, "vv1_%d" % i) for i in range(2)]
    st = B.tile([128, 8], F32, "st1")
    att_tiles = [esink, esraw] + PT1 + [rd, bcs, ea] + tq + [wo1] + wst2 + [gbc, bbc] + xr + vv + [st]
    B.op("pool", lambda e: e.memset(fz.ap, 0.0), [], proj_tiles + att_tiles + [fz])
    B.rd, B.bcs, B.b4 = rd, bcs, [0]
    B.dma(esraw.ap, sinks_x.ap[0:1, :].partition_broadcast(128), [sinks_x], [esraw])
    B.act(esink.ap, esraw.ap, AF.Exp, [esraw], [esink])
    B.load_w(w1_o, D, wo1, wst2)
    B.dma(gbc.ap, ln1_g.ap[0:1, :].partition_broadcast(128), [ln1_g], [gbc])
    B.dma(bbc.ap, ln1_b.ap[0:1, :].partition_broadcast(128), [ln1_b], [bbc])

    zc, accc = [0], [0]
    run_pipeline(B.mem_units(mkT1, mv1, mqT1, mixT1, PT1, ea, tq[1], zc, accc, [(i * 512, 512) for i in range(NR)]), 2)
    units = []
    for j in range(NJ):
        i_run, r = j // 4, j % 4
        e_prev = j - 1 if r > 0 else NJ + i_run
        jsl = slice(j * 128, (j + 1) * 128)
        for kvh in range(3):
            acc = PS[4 + accc[0] % 4]
            accc[0] += 1
            for bi, e_k in enumerate((e_prev, j)):
                zi = zc[0]
                zc[0] += 1
                Sb = B.nextbank4()
                PTt = PT1[zi % 4]
                mi = 0 if bi == 1 else (2 if j == 0 else 1)
                ksl = slice(e_k * 128, (e_k + 1) * 128)

                def s0(Sb=Sb, PTt=PTt, kvh=kvh, jsl=jsl, ksl=ksl, mi=mi):
                    for g in range(4):
                        hq = kvh * 4 + g
                        cgq, po = _qpos(hq)
                        cgk = 0 if kvh < 2 else 1
                        B.mm(Sb.ap[:, g * 128:(g + 1) * 128], kT1.ap[po:po + 64, cgk, ksl], qT1.ap[po:po + 64, cgq, jsl],
                             True, True, [kT1, qT1], [Sb])
                    B.act(PTt.ap[:, 0:512], Sb.ap[:, 0:512], AF.Exp, [Sb], [PTt], scale=0.125)
                    for g in range(4):
                        B.tt("dve" if g % 2 == 0 else "pool", PTt.ap[:, g * 128:(g + 1) * 128], PTt.ap[:, g * 128:(g + 1) * 128],
                             masks1.ap[:, mi, :], ALU.mult, [PTt, masks1], [PTt])

                def s1(acc=acc, PTt=PTt, kvh=kvh, j=j, jsl=jsl, bi=bi, e_k=e_k):
                    B.mm(acc.ap[0:65, 0:512], V1.ap[:, e_k, kvh, 0:65], PTt.ap[:, 0:512], bi == 0, bi == 1, [V1, PTt], [acc], skip=True)
                    if bi == 1:
                        B.softmax_norm(acc, slice(0, 512), add_row=(esink.ap[64:65, kvh * 512:(kvh + 1) * 512], esink), W=512)
                        B.tt("dve", ea.ap[0:64, :], acc.ap[0:64, 0:512], bcs.ap[0:64, 0:512], ALU.mult, [acc, bcs], [ea])
                        tqt = tq[0]
                        for g in range(4):
                            hq = kvh * 4 + g
                            mcg, po = hq // 2, (hq % 2) * 64
                            gs = slice(g * 128, (g + 1) * 128)
                            B.copy("act", tqt.ap[po:po + 64, gs], ea.ap[0:64, gs], [ea], [tqt])
                            B.tt("dve", mixT1.ap[po:po + 64, mcg, jsl], tqt.ap[po:po + 64, gs],
                                 mixT1.ap[po:po + 64, mcg, jsl], ALU.mult, [tqt, mixT1], [mixT1])

                units.append([s0, s1])
    run_pipeline(units, 2)

    outs = []
    for j in range(NJ):
        outs.append(B.out_block(j, mixT1, wo1, x1_ext, xr[j % 2], vv[j % 2], st, gbc, bbc, y_out, nextbank))
    return outs


def layer_norm_store(B, vv_t, scratch, st, gbc, bbc, y_out, j):
    B.op("dve", lambda e: e.tensor_reduce(st.ap[:, 0:1], vv_t.ap, AX.X, ALU.add), [vv_t], [st])
    B.ts("dve", st.ap[:, 1:2], st.ap[:, 0:1], -1.0 / D, None, ALU.mult, None, [st], [st])
    B.op("act", lambda e: e.activation(out=scratch.ap, in_=vv_t.ap, func=AF.Square, bias=st.ap[:, 1:2], scale=1.0,
                                       accum_out=st.ap[:, 2:3]), [vv_t, st], [scratch, st])
    B.act(st.ap[:, 3:4], st.ap[:, 2:3], AF.Ln, [st, B.eps_col], [st], scale=1.0 / D, bias=B.eps_col.ap[:, 0:1])
    B.act(st.ap[:, 3:4], st.ap[:, 3:4], AF.Exp, [st], [st], scale=-0.5)
    B.ts("dve", vv_t.ap, vv_t.ap, st.ap[:, 1:2], st.ap[:, 3:4], ALU.add, ALU.mult, [vv_t, st], [vv_t])
    B.tt("pool", vv_t.ap, vv_t.ap, gbc.ap, ALU.mult, [vv_t, gbc], [vv_t])
    B.tt("pool", vv_t.ap, vv_t.ap, bbc.ap, ALU.add, [vv_t, bbc], [vv_t])
    return B.dma(y_out.ap[j * 128:(j + 1) * 128, :], vv_t.ap, [vv_t], [y_out])


def _own_idx(S, c):
    NR = S // 2048
    return np.concatenate([np.arange((16 * i + 4 * c) * 128, (16 * i + 4 * c + 4) * 128) for i in range(NR)])


def _halo_idx(S, c):
    NR = S // 2048
    out = []
    for i in range(NR):
        g = 16 * i + 4 * c - 1
        out.append(np.arange(g * 128, (g + 1) * 128) if g >= 0 else np.full(128, -1))
    return np.concatenate(out)


def _mask_dram(c):
    m = _masks(c)
    return np.ascontiguousarray(np.transpose(m, (1, 0, 2)).reshape(128, 9 * 128))


def _mask_tables(c):
    col = np.arange(512, dtype=np.float32)
    qcol = np.zeros((128, 1024), np.float32)
    qcol[:, 0:512] = col[None, :]
    qcol[:, 512:1024] = (col + 1920.0 * np.floor(col / 128.0))[None, :]
    k = np.arange(128, dtype=np.float32)[:, None]
    stab = np.zeros((128, 80), np.float32)
    stab[:, 0:16] = k + 128.0 * np.arange(16, dtype=np.float32)[None, :] - 512.0 * c
    stab[:, 16:80] = k + 128.0 * np.arange(64, dtype=np.float32)[None, :] + 128.0 - 512.0 * c
    return qcol, stab


def _masks1(c):
    k = np.arange(128)[:, None]
    q = np.arange(128)[None, :]
    m = np.zeros((3, 128, 128), np.float32)
    m[0] = (k <= q)
    m[1] = (k > q)
    m[2] = (k > q) if c > 0 else 0.0
    return np.ascontiguousarray(np.transpose(m, (1, 0, 2)).reshape(128, 3 * 128)).astype(ml_dtypes.bfloat16)


def l1_weights(w_in, w_memkv, sinks, w_out, ln_g, ln_b):
    cq, ck, cv = w_in[:, 0:768], w_in[:, 768:960], w_in[:, 960:1152]
    mq, gate = w_in[:, 1152:1408], w_in[:, 1408:2432]
    kd = np.concatenate([ck[:, 0:64], ck[:, 64:128], ck[:, 128:192], ck[:, 128:192]], axis=1)
    pk = _partner_perm(256, 64, 16)
    qr = np.concatenate([cq[:, h * 64:(h + 1) * 64] for h in QCOLS], axis=1)
    pq = _partner_perm(512, 64, 16)
    q0, q1 = qr[:, 0:512], qr[:, 512:1024]
    invf1, coef1 = _host_consts(1)
    return {
        "w1_kv": np.ascontiguousarray(np.concatenate([kd, kd[:, pk], cv, mq], axis=1)),
        "w1_q": np.ascontiguousarray(np.concatenate([q0, q0[:, pq], q1, q1[:, pq]], axis=1)),
        "w1_g": np.ascontiguousarray(gate),
        "w1_mkv": np.ascontiguousarray(w_memkv),
        "w1_o": np.ascontiguousarray(w_out),
        "ln1_g": np.ascontiguousarray(ln_g[None, :]),
        "ln1_b": np.ascontiguousarray(ln_b[None, :]),
        "sinks_x": np.ascontiguousarray(np.repeat(sinks, 128)[None, :]),
        "invf1": invf1, "coefp1": coef1,
        "ident": np.eye(128, dtype=np.float32),
    }


def prep_fused(inp):
    f = lambda a: np.asarray(a)
    x, mem, positions = f(inp["x"]), f(inp["mem"]), f(inp["positions"])
    S = x.shape[1]
    w_in = f(inp["w_in_even"])[0]
    sbq, sbk, sbv = w_in[:, 0:384], w_in[:, 384:768], w_in[:, 768:1152]
    dfq, dfk, dfv = w_in[:, 1152:1536], w_in[:, 1536:1920], w_in[:, 1920:2304]
    mq, gate = w_in[:, 2304:2560], w_in[:, 2560:3584]
    perm = _partner_perm(384, 32, 8)
    invf, coefp = _host_consts(0)
    dsub = f(inp["diff_subln_even"])[0]
    shared = {
        "w_k": np.ascontiguousarray(np.concatenate([sbk, dfk, dfk[:, perm]], axis=1)),
        "w_v": np.ascontiguousarray(np.concatenate([sbv, dfv], axis=1)),
        "w_q": np.ascontiguousarray(np.concatenate([sbq, dfq, dfq[:, perm], mq], axis=1)),
        "w_g": np.ascontiguousarray(gate),
        "w_mkv": np.ascontiguousarray(f(inp["w_memkv_even"])[0]),
        "w_o": np.ascontiguousarray(f(inp["w_out_even"])[0]),
        "ln_g": np.ascontiguousarray(f(inp["ln_g_even"])[0][None, :]),
        "ln_b": np.ascontiguousarray(f(inp["ln_b_even"])[0][None, :]),
        "dlam": np.ascontiguousarray(f(inp["diff_lambda_even"])[0].reshape(1, 128)),
        "subln": np.ascontiguousarray(np.concatenate([dsub, dsub])[:, None]),
        "invf": invf, "coefp": coefp,
    }
    shared.update(l1_weights(f(inp["w_in_odd"])[0], f(inp["w_memkv_odd"])[0], f(inp["sinks_odd"])[0], f(inp["w_out_odd"])[0],
                             f(inp["ln_g_odd"])[0], f(inp["ln_b_odd"])[0]))
    xT = [np.ascontiguousarray(x[b].T) for b in range(x.shape[0])]
    maps = []
    for core in range(8):
        b, c = core // 4, core % 4
        own = _own_idx(S, c)
        hidx = _halo_idx(S, c)
        ext = np.concatenate([own, np.maximum(hidx, 0)])
        qcol, stab = _mask_tables(c)
        m = dict(shared)
        m.update({
            "xT_all": xT[b],
            "xT_ext": np.ascontiguousarray(x[b][ext].T),
            "x_ext": np.ascontiguousarray(x[b][ext]),
            "pos_all": np.ascontiguousarray(positions[b][None, :]).astype(np.int32),
            "pos_ext": np.ascontiguousarray(positions[b][ext][None, :]).astype(np.int32),
            "memT": np.ascontiguousarray(mem[b].T),
            "masks": _mask_dram(c),
            "qcol": qcol, "stab": stab,
            "masks1": _masks1(c),
        })
        maps.append(m)
    return maps


def gather_own(results, key, S, nb=2):
    out = np.zeros((nb, S, D), np.float32)
    for core in range(8):
        b, c = core // 4, core % 4
        out[b, _own_idx(S, c)] = results[core][key]
    return out


def build_fused(S, lambda_init):
    B = Builder(S, 0)
    l0_body(B, lambda_init, True)
    outs = l1_body(B)
    B.sch.emit(final_wait_ops=outs)
    return B


LAMBDA_INIT0 = 0.8 - 0.6 * math.exp(-0.3 * 0)


def kernel(**inputs):
    S = np.asarray(inputs["x"]).shape[1]
    B = build_fused(S, LAMBDA_INIT0)
    maps = prep_fused(inputs)
    res = run_bass_kernel_spmd(B.nc, maps, core_ids=list(range(8)))
    return gather_own(res.results, "y", S)
```

```python
import math
import contextlib
import numpy as np
import ml_dtypes
import concourse.bass as bass
import concourse.mybir as mybir
from concourse.bass_utils import run_bass_kernel_spmd

F32 = mybir.dt.float32
BF16 = mybir.dt.bfloat16
I32 = mybir.dt.int32
AF = mybir.ActivationFunctionType
ALU = mybir.AluOpType
AX = mybir.AxisListType

D = 1024
KC = 8
DEPTH = 2
ALPHA = (2 * DEPTH) ** 0.25
LN_EPS = 1e-5
ROPE_THETA = 500000.0
MEM_LEN = 256
PI = math.pi


class Res:
    __slots__ = ("lw", "rd", "name")

    def __init__(self, name=""):
        self.lw = None
        self.rd = []
        self.name = name


class Sched:
    ENGS = ("pe", "act", "dve", "pool", "sp")
    NSLOT = 14

    def __init__(self, nc):
        self.nc = nc
        self.ops = []

    def add(self, eng, fn, reads=(), writes=(), dma=False):
        idx = len(self.ops)
        deps = set()
        for r in reads:
            if r.lw is not None:
                deps.add(r.lw)
        for w in writes:
            if w.lw is not None:
                deps.add(w.lw)
            deps.update(w.rd)
        for r in reads:
            r.rd.append(idx)
        for w in writes:
            w.lw = idx
            w.rd = []
        deps.discard(idx)
        self.ops.append([eng, fn, deps, dma])
        return idx

    def emit(self, final_wait_ops=()):
        nc = self.nc
        ops = self.ops
        n = len(ops)
        has_dep = [False] * n
        for i, (eng, fn, deps, dma) in enumerate(ops):
            for d in deps:
                if ops[d][0] == "pe" and eng == "pe" and not ops[d][3] and not dma:
                    continue
                has_dep[d] = True
        for d in final_wait_ops:
            has_dep[d] = True
        cnt = {e: 0 for e in self.ENGS}
        dcnt = {e: 0 for e in self.ENGS}
        sig = [None] * n
        for i, (eng, fn, deps, dma) in enumerate(ops):
            if dma:
                k = dcnt[eng]
                dcnt[eng] += 1
                sig[i] = ("d", eng, k % self.NSLOT, 16 * (k // self.NSLOT + 1))
            elif has_dep[i]:
                cnt[eng] += 1
                sig[i] = ("c", eng, cnt[eng])
        engs_used = [e for e in self.ENGS if any(o[0] == e for o in ops)]
        with contextlib.ExitStack() as st:
            csem = {e: st.enter_context(nc.semaphore("c_" + e)) for e in engs_used}
            dsem = {}
            for e in engs_used:
                if dcnt[e] > 0:
                    dsem[e] = [st.enter_context(nc.semaphore("d_%s_%d" % (e, s)))
                               for s in range(min(self.NSLOT, dcnt[e]))]
            block = st.enter_context(nc.Block())
            engobj = {"pe": "tensor", "act": "scalar", "dve": "vector", "pool": "gpsimd", "sp": "sync"}

            def make_stream(ename):
                def stream(eng):
                    waited_c = {}
                    waited_d = {}
                    for i, (e, fn, deps, dma) in enumerate(ops):
                        if e != ename:
                            continue
                        need_c = {}
                        need_d = {}
                        for d in deps:
                            s = sig[d]
                            if s is None:
                                continue
                            if s[0] == "c":
                                if s[1] == "pe" and ename == "pe" and not dma:
                                    continue
                                need_c[s[1]] = max(need_c.get(s[1], 0), s[2])
                            else:
                                key = (s[1], s[2])
                                need_d[key] = max(need_d.get(key, 0), s[3])
                        if dma:
                            s = sig[i]
                            if s[3] > 16:
                                key = (s[1], s[2])
                                need_d[key] = max(need_d.get(key, 0), s[3] - 16)
                        for se, v in need_c.items():
                            if waited_c.get(se, 0) < v:
                                eng.wait_ge(csem[se], v)
                                waited_c[se] = v
                        for key, v in need_d.items():
                            if waited_d.get(key, 0) < v:
                                eng.wait_ge(dsem[key[0]][key[1]], v)
                                waited_d[key] = v
                        ins = fn(eng)
                        s = sig[i]
                        if s is not None:
                            if s[0] == "c":
                                ins.then_inc(csem[ename], 1)
                            else:
                                ins.then_inc(dsem[ename][s[2]], 16)
                    if ename == "sp":
                        for d in final_wait_ops:
                            s = sig[d]
                            if s[0] == "c":
                                eng.wait_ge(csem[s[1]], s[2])
                            else:
                                eng.wait_ge(dsem[s[1]][s[2]], s[3])
                return stream

            for e in engs_used:
                getattr(block, engobj[e])(make_stream(e))


class SbufAlloc:
    def __init__(self, nc, nbytes=207 * 1024):
        self.nc = nc
        self.arena = nc.alloc_sbuf_tensor("arena", [128, nbytes], mybir.dt.uint8)
        self.off = 0
        self.limit = nbytes
        self.peak = 0

    def mark(self):
        return self.off

    def reset(self, m):
        self.off = m

    def tile(self, shape, dtype):
        assert shape[0] == 128
        esz = {F32: 4, BF16: 2, I32: 4}[dtype]
        nel = int(np.prod(shape[1:]))
        nbytes = (esz * nel + 63) // 64 * 64
        off = self.off
        self.off += nbytes
        self.peak = max(self.peak, self.off)
        assert self.off <= self.limit, ("SBUF overflow", self.off)
        ap = self.arena.ap()[:, off:off + esz * nel].bitcast(dtype)
        if len(shape) == 3:
            ap = ap.rearrange("p (a b) -> p a b", a=shape[1])
        elif len(shape) == 4:
            ap = ap.rearrange("p (a b c) -> p a b c", a=shape[1], b=shape[2])
        return ap


class T:
    def __init__(self, ap, name=""):
        self.ap = ap
        self.res = Res(name)


class Builder:
    def __init__(self, S, layer):
        self.S = S
        self.layer = layer
        self.NB = S // 128
        self.NJ = self.NB // 4
        self.NO = self.NJ * 128
        self.NR = self.NB // 16
        self.NE = self.NO + self.NR * 128
        self.NEB = self.NE // 128
        self.QGB = 4
        self.GW = 512
        self.NQG = self.NR
        self.groups = [(i * 512, 512) for i in range(self.NR)] + [(self.NO, self.NR * 128)]
        self.TW = min(512, S)
        self.NT = S // self.TW
        self.nc = bass.Bass("TRN2", target_bir_lowering=False)
        self.sch = Sched(self.nc)
        self.sa = SbufAlloc(self.nc)
        self.dram = {}
        self.psum = [T(self.nc.alloc_psum_tensor("ps%d" % i, [128, 512], F32).ap(), "ps%d" % i) for i in range(8)]

    def din(self, name, shape, dtype=F32):
        t = T(self.nc.dram_tensor(name, list(shape), dtype, kind="ExternalInput").ap(), name)
        self.dram[name] = t
        return t

    def dout(self, name, shape, dtype=F32):
        t = T(self.nc.dram_tensor(name, list(shape), dtype, kind="ExternalOutput").ap(), name)
        self.dram[name] = t
        return t

    def dscr(self, name, shape, dtype):
        t = T(self.nc.dram_tensor(name, list(shape), dtype).ap(), name)
        self.dram[name] = t
        return t

    def tile(self, shape, dtype, name=""):
        return T(self.sa.tile(shape, dtype), name)

    def op(self, eng, fn, reads=(), writes=(), dma=False):
        return self.sch.add(eng, fn, [t.res for t in reads], [t.res for t in writes], dma)

    def dma(self, out_ap, in_ap, reads, writes, q="sp"):
        return self.op(q, lambda e: e.dma_start(out=out_ap, in_=in_ap), reads, writes, dma=True)

    def mm(self, out_ap, lhsT, rhs, start, stop, reads, writes, skip=False):
        if skip:
            return self.op("pe", lambda e: e.matmul(out_ap, lhsT=lhsT, rhs=rhs, start=start, stop=stop,
                                                    skip_group_check=True), reads, writes)
        return self.op("pe", lambda e: e.matmul(out_ap, lhsT=lhsT, rhs=rhs, start=start, stop=stop), reads, writes)

    def act(self, out_ap, in_ap, func, reads, writes, scale=1.0, bias=0.0):
        return self.op("act", lambda e: e.activation(out=out_ap, in_=in_ap, func=func, bias=bias, scale=scale),
                       reads, writes)

    def tt(self, eng, out_ap, a, b, op, reads, writes):
        return self.op(eng, lambda e: e.tensor_tensor(out_ap, a, b, op), reads, writes)

    def ts(self, eng, out_ap, a, s1, s2, op0, op1, reads, writes):
        if op1 is None:
            return self.op(eng, lambda e: e.tensor_scalar(out_ap, a, s1, None, op0), reads, writes)
        return self.op(eng, lambda e: e.tensor_scalar(out_ap, a, s1, s2, op0, op1), reads, writes)

    def copy(self, eng, out_ap, in_ap, reads, writes):
        if eng == "act":
            return self.op("act", lambda e: e.copy(out_ap, in_ap), reads, writes)
        return self.op(eng, lambda e: e.tensor_copy(out_ap, in_ap), reads, writes)

    def memset(self, eng, ap, val, writes):
        return self.op(eng, lambda e: e.memset(ap, val), (), writes)

    def load_w(self, wd, n, wb, stage=None, c0=0):
        src = wd.ap.rearrange("(kc p) n -> p kc n", p=128)
        for k2 in range(0, KC, 2):
            self.dma(wb.ap[:, k2:k2 + 2, c0:c0 + n], src[:, k2:k2 + 2, :], [wd], [wb], q="pool")

    def rope_tables(self, pos_d, a, w, posi, posf, t1, cosT, sinS, invf, coefp):
        SC = 2 * PI * (1.0 - 1e-6)
        self.dma(posi.ap[:, 0:w], pos_d.ap[0:1, a:a + w].partition_broadcast(128), [pos_d], [posi])
        self.copy("act", posf.ap[:, 0:w], posi.ap[:, 0:w], [posi], [posf])
        self.ts("dve", posf.ap[:, 0:w], posf.ap[:, 0:w], invf.ap[:, 0:1], None, ALU.mult, None, [posf, invf], [posf])
        for (dst, off) in ((sinS, 0.0), (cosT, 0.25)):
            if off:
                self.ts("dve", posf.ap[:, 0:w], posf.ap[:, 0:w], off, None, ALU.add, None, [posf], [posf])
            self.copy("dve", posi.ap[:, 0:w], posf.ap[:, 0:w], [posf], [posi])
            self.copy("act", t1.ap[:, 0:w], posi.ap[:, 0:w], [posi], [t1])
            self.op("dve", lambda e: e.scalar_tensor_tensor(t1.ap[:, 0:w], t1.ap[:, 0:w], -1.0, posf.ap[:, 0:w],
                                                            ALU.mult, ALU.add), [t1, posf], [t1])
            self.act(dst.ap[:, 0:w], t1.ap[:, 0:w], AF.Sin, [t1], [dst], scale=SC)
        self.ts("dve", sinS.ap[:, 0:w], sinS.ap[:, 0:w], coefp.ap[:, 0:1], None, ALU.mult, None, [sinS, coefp], [sinS])

    def proj_fm(self, ps, wb, c0, xb, w, m=128):
        for kc in range(KC):
            self.mm(ps.ap[0:m, 0:w], wb.ap[:, kc, c0:c0 + m], xb.ap[:, kc, 0:w], kc == 0, kc == KC - 1, [wb, xb], [ps])


    def nextbank4(self):
        b = self.psum[self.b4[0] % 4]
        self.b4[0] += 1
        return b

    def softmax_norm(self, acc, W, add_row=None):
        rd, bcs, ones_f = self.rd, self.bcs, self.ones_f
        k = self.b4[0] % 2
        cols = slice(k * 512, k * 512 + W)
        if add_row is not None:
            self.tt("dve", rd.ap[64:65, cols], acc.ap[64:65, 0:W], add_row[0], ALU.add, [acc, add_row[1]], [rd])
        else:
            self.ts("dve", rd.ap[64:65, cols], acc.ap[64:65, 0:W], 1e-18, None, ALU.add, None, [acc], [rd])
        bc = self.nextbank4()
        self.mm(bc.ap[0:64, 0:W], ones_f.ap[64:65, 0:64], rd.ap[64:65, cols], True, True, [ones_f, rd], [bc])
        self.act(bcs.ap[0:64, cols], bc.ap[0:64, 0:W], AF.Ln, [bc], [bcs])
        self.act(bcs.ap[0:64, cols], bcs.ap[0:64, cols], AF.Exp, [bcs], [bcs], scale=-1.0)
        return cols

    def mem_units(self, mkT, mv, mqT, mixT, PT, ea, tq, zc, accc, groups):
        B = self
        PS = self.psum
        units = []
        for hm in range(4):
            cg, po = hm // 2, (hm % 2) * 64
            for (col0, W) in groups:
                acc = PS[4 + accc[0] % 4]
                accc[0] += 1
                gsl = slice(col0, col0 + W)
                for mb in range(2):
                    zi = zc[0]
                    zc[0] += 1
                    Sb = B.nextbank4()
                    PTt = PT[zi % len(PT)]

                    def s0(Sb=Sb, PTt=PTt, mb=mb, cg=cg, po=po, gsl=gsl, W=W):
                        B.mm(Sb.ap[:, 0:W], mkT.ap[po:po + 64, cg, mb * 128:(mb + 1) * 128], mqT.ap[po:po + 64, cg, gsl],
                             True, True, [mkT, mqT], [Sb])
                        B.act(PTt.ap[:, 0:W], Sb.ap[:, 0:W], AF.Exp, [Sb], [PTt], scale=0.125)

                    def s1(acc=acc, PTt=PTt, mb=mb, hm=hm, cg=cg, po=po, gsl=gsl, W=W):
                        B.mm(acc.ap[0:65, 0:W], mv.ap[:, mb, hm, 0:65], PTt.ap[:, 0:W], mb == 0, mb == 1, [mv, PTt], [acc], skip=True)
                        if mb == 1:
                            bcol = B.softmax_norm(acc, W)
                            B.tt("dve", ea.ap[0:64, 0:W], acc.ap[0:64, 0:W], B.bcs.ap[0:64, bcol], ALU.mult, [acc, B.bcs], [ea])
                            B.copy("act", tq.ap[po:po + 64, 0:W], ea.ap[0:64, 0:W], [ea], [tq])
                            B.tt("dve", mixT.ap[po:po + 64, 6 + cg, gsl], tq.ap[po:po + 64, 0:W], mixT.ap[po:po + 64, 6 + cg, gsl],
                                 ALU.mult, [tq, mixT], [mixT])

                    units.append([s0, s1])
        return units

    def mem_kv(self, memT, w_mkv, wb, wst, xs0, memb, mkT, mv, nextbank):
        B = self
        B.load_w(w_mkv, 512, wb, wst)
        B.dma(xs0.ap[:, :, 0:MEM_LEN], memT.ap.rearrange("(kc p) m -> p kc m", p=128), [memT], [xs0])
        B.copy("dve", memb.ap[:, :, 0:MEM_LEN], xs0.ap[:, :, 0:MEM_LEN], [xs0], [memb])
        for cg in range(2):
            ps = nextbank()
            B.proj_fm(ps, wb, cg * 128, memb, MEM_LEN)
            B.copy("act", mkT.ap[:, cg, :], ps.ap[:, 0:MEM_LEN], [ps], [mkT])
        B.memset("pool", mv.ap, 1.0, [mv])
        for mb in range(2):
            ps = nextbank()
            for kc in range(KC):
                B.mm(ps.ap[:, 0:256], memb.ap[:, kc, mb * 128:(mb + 1) * 128], wb.ap[:, kc, 256:512], kc == 0, kc == KC - 1,
                     [memb, wb], [ps])
            B.copy("act", mv.ap[:, mb, :, 0:64], ps.ap[:, 0:256].rearrange("p (h d) -> p h d", h=4), [ps], [mv])

    def out_block(self, j, mixT, wo, x_src, xr_t, vv_t, st, gbc, bbc, y_dst, nextbank):
        B = self
        B.dma(xr_t.ap, x_src.ap[j * 128:(j + 1) * 128, :], [x_src], [xr_t])
        for n in range(2):
            ps = nextbank()
            for kc in range(KC):
                B.mm(ps.ap[:, 0:512], mixT.ap[:, kc, j * 128:(j + 1) * 128], wo.ap[:, kc, n * 512:(n + 1) * 512],
                     kc == 0, kc == KC - 1, [mixT, wo], [ps])
            B.op("dve", lambda e, ps=ps, n=n: e.scalar_tensor_tensor(
                vv_t.ap[:, n * 512:(n + 1) * 512], xr_t.ap[:, n * 512:(n + 1) * 512], ALPHA, ps.ap[:, 0:512],
                ALU.mult, ALU.add), [ps, xr_t], [vv_t])
        return layer_norm_store(B, vv_t, xr_t, st, gbc, bbc, y_dst, j)

    def rope_evac(self, psK, psP, out_ap, w, cosT, sinS, tmpa, tmpb, out_t, scale=None):
        self.tt("dve", tmpa.ap[:, 0:w], psK.ap[:, 0:w], cosT.ap[:, 0:w], ALU.mult, [psK, cosT], [tmpa])
        self.tt("dve", tmpb.ap[:, 0:w], psP.ap[:, 0:w], sinS.ap[:, 0:w], ALU.mult, [psP, sinS], [tmpb])
        self.tt("pool", out_ap, tmpa.ap[:, 0:w], tmpb.ap[:, 0:w], ALU.add, [tmpa, tmpb], [out_t])


def _host_consts(layer):
    if layer == 0:
        hd, rot = 32, 8
    else:
        hd, rot = 64, 16
    half = rot // 2
    inv = np.exp(-(np.arange(half, dtype=np.float32) / half) * math.log(ROPE_THETA)).astype(np.float32)
    invf = np.zeros((128, 1), np.float32)
    coef = np.zeros((128, 1), np.float32)
    for r in range(128):
        d = r % hd
        if d < rot:
            invf[r, 0] = inv[d % half] / np.float32(2 * PI)
            coef[r, 0] = -1.0 if d < half else 1.0
    return invf, coef


def _partner_perm(ncols, hd, rot):
    half = rot // 2
    perm = np.arange(ncols)
    for c in range(ncols):
        d = c % hd
        if d < half:
            perm[c] = c + half
        elif d < rot:
            perm[c] = c - half
    return perm


def _masks(c):
    k = np.arange(128)[:, None]
    q = np.arange(128)[None, :]
    out = np.zeros((9, 128, 128), np.float32)
    out[8] = -1.0 * (k >= q)
    for r in range(4):
        if r < c:
            out[r] = 1.0
            out[4 + r] = 1.0
        elif r == c:
            out[r] = (k <= q)
            out[4 + r] = (k < q)
    return out.astype(ml_dtypes.bfloat16)


def run_pipeline(units, nst):
    n = len(units)
    for step in range(n + nst - 1):
        for s in range(nst):
            u = step - s
            if 0 <= u < n and units[u][s] is not None:
                units[u][s]()


def build_l0(S, lambda_init):
    B = Builder(S, 0)
    outs = l0_body(B, lambda_init, False)
    B.sch.emit(final_wait_ops=outs)
    return B


def l0_body(B, lambda_init, fused):
    nc = B.nc
    S = B.S
    NB, NJ, NO, QGB, GW, NQG, TW, NT = B.NB, B.NJ, B.NO, B.QGB, B.GW, B.NQG, B.TW, B.NT
    NR, NE, NEB = B.NR, B.NE, B.NEB
    etiles = [(a, min(512, NE - a)) for a in range(0, NE, 512)]
    xT_all = B.din("xT_all", [D, S])
    xT_own = B.din("xT_ext", [D, NE])
    x_own = B.din("x_ext", [NE, D])
    pos_all = B.din("pos_all", [1, S], I32)
    pos_own = B.din("pos_ext", [1, NE], I32)
    w_k = B.din("w_k", [D, 1152])
    w_v = B.din("w_v", [D, 768])
    w_q = B.din("w_q", [D, 1408])
    w_g = B.din("w_g", [D, 1024])
    memT = B.din("memT", [D, MEM_LEN])
    w_mkv = B.din("w_mkv", [D, 512])
    w_o = B.din("w_o", [D, D])
    ln_g = B.din("ln_g", [1, D])
    ln_b = B.din("ln_b", [1, D])
    dlam = B.din("dlam", [1, 128])
    subln = B.din("subln", [128, 1])
    invf_d = B.din("invf", [128, 1])
    coef_d = B.din("coefp", [128, 1])
    masks_d = B.din("masks", [128, 9 * 128], BF16)
    qcol_d = B.din("qcol", [128, 1024])
    stab_d = B.din("stab", [128, 16 + 64])
    y_out = B.dscr("x1_ext", [NE, D], F32)
    B.x1_ext_d = y_out
    kT_scr = B.dscr("kT_scr", [128, 6, S], BF16)
    v_scr = B.dscr("v_scr", [128, 12, NB * 65], BF16)

    masks = B.tile([128, 9, 128], BF16, "masks")
    negtri = T(masks.ap[:, 8, :], "negtri")
    negtri.res = masks.res
    negones = B.tile([128, 128], BF16, "negones")
    ones_f = B.tile([128, 128], F32, "ones_f")
    invf = B.tile([128, 1], F32, "invf")
    coefp = B.tile([128, 1], F32, "coefp")
    B.pi_col = B.tile([128, 1], F32, "pi")
    g1col = B.tile([128, 1], F32, "g1col")
    lam_t = B.tile([128, 128], F32, "lam")
    lamw = B.tile([128, 8], F32, "lamw")
    nlcol = B.tile([128, 2], F32, "nlcol")
    B.eps_col = B.tile([128, 1], F32, "eps")
    prod = B.tile([128, 64], F32, "prod")
    qcol = B.tile([128, 1024], F32, "qcol")
    stab = B.tile([128, 80], F32, "stab")
    B.mark_consts = B.sa.mark()
    qT_sb = B.tile([128, 3, NE], BF16, "qT_sb")
    qT_df = B.tile([128, 3, NE], BF16, "qT_df")
    mqT = B.tile([128, 2, NE], BF16, "mqT")
    mixT = B.tile([128, 8, NE], BF16, "mixT")
    mkT = B.tile([128, 2, MEM_LEN], BF16, "mkT")
    mv = B.tile([128, 2, 4, 65], BF16, "mv")
    PS = B.psum
    pbc = [0]

    def nextbank():
        b = PS[pbc[0] % 8]
        pbc[0] += 1
        return b

    B.dma(masks.ap.rearrange("p a b -> p (a b)"), masks_d.ap[:, :], [masks_d], [masks])
    B.dma(invf.ap, invf_d.ap[:, :], [invf_d], [invf])
    B.dma(coefp.ap, coef_d.ap[:, :], [coef_d], [coefp])
    B.dma(qcol.ap, qcol_d.ap[:, :], [qcol_d], [qcol])
    B.dma(stab.ap, stab_d.ap[:, :], [stab_d], [stab])
    B.memset("pool", B.pi_col.ap, PI, [B.pi_col])
    B.memset("pool", B.eps_col.ap, LN_EPS, [B.eps_col])
    B.memset("pool", ones_f.ap, 1.0, [ones_f])
    B.memset("pool", negones.ap, -1.0, [negones])
    B.dma(lam_t.ap[64:65, 0:128], dlam.ap[0:1, :], [dlam], [lam_t])
    B.tt("dve", prod.ap[64:65, 0:32], lam_t.ap[64:65, 0:32], lam_t.ap[64:65, 32:64], ALU.mult, [lam_t], [prod])
    B.tt("dve", prod.ap[64:65, 32:64], lam_t.ap[64:65, 64:96], lam_t.ap[64:65, 96:128], ALU.mult, [lam_t], [prod])
    B.op("dve", lambda e: e.tensor_reduce(lamw.ap[64:65, 0:1], prod.ap[64:65, 0:32], AX.X, ALU.add), [prod], [lamw])
    B.op("dve", lambda e: e.tensor_reduce(lamw.ap[64:65, 1:2], prod.ap[64:65, 32:64], AX.X, ALU.add), [prod], [lamw])
    B.act(lamw.ap[64:65, 2:4], lamw.ap[64:65, 0:2], AF.Exp, [lamw], [lamw])
    B.tt("dve", lamw.ap[64:65, 4:5], lamw.ap[64:65, 2:3], lamw.ap[64:65, 3:4], ALU.subtract, [lamw], [lamw])
    B.ts("dve", lamw.ap[64:65, 5:6], lamw.ap[64:65, 4:5], lambda_init, -1.0, ALU.add, ALU.mult, [lamw], [lamw])
    nlps = nextbank()
    B.mm(nlps.ap[0:64, 0:2], ones_f.ap[64:65, 0:64], lamw.ap[64:65, 4:6], True, True, [ones_f, lamw], [nlps])
    B.copy("dve", nlcol.ap[0:64, 0:2], nlps.ap[0:64, 0:2], [nlps], [nlcol])
    B.dma(g1col.ap, subln.ap[:, :], [subln], [g1col])
    B.ts("dve", g1col.ap, g1col.ap, 1.0 - lambda_init, None, ALU.mult, None, [g1col], [g1col])

    mA = B.sa.mark()
    wb = B.tile([128, KC, 1920], BF16, "wb")
    wst = [B.tile([128, KC, 64], F32, "wst%d" % i) for i in range(2)]
    xs = [B.tile([128, KC, TW], F32, "xs%d" % i) for i in range(2)]
    xb = [B.tile([128, KC, TW], BF16, "xb%d" % i) for i in range(2)]
    posi = B.tile([128, TW], I32, "posi")
    posf = B.tile([128, TW], F32, "posf")
    t1 = B.tile([128, TW], F32, "t1")
    cosT = B.tile([128, TW], F32, "cosT")
    sinS = B.tile([128, TW], F32, "sinS")
    tmpa = B.tile([128, TW], F32, "tmpa")
    tmpb = B.tile([128, TW], F32, "tmpb")
    ktst = [B.tile([128, 6, TW], BF16, "ktst%d" % i) for i in range(1)]
    vst = [B.tile([128, 12, TW // 128, 65], BF16, "vst%d" % i) for i in range(2)]

    B.mem_kv(memT, w_mkv, wb, wst, xs[0], xb[0], mkT, mv, nextbank)

    B.load_w(w_k, 1152, wb, wst, 0)
    B.load_w(w_v, 768, wb, wst, 1152)
    for i in range(2):
        B.memset("pool", vst[i].ap, 1.0, [vst[i]])
    xsrc = xT_all.ap.rearrange("(kc p) s -> p kc s", p=128)
    for t in range(NT):
        xs_t, xb_t = xs[t % 2], xb[t % 2]
        kt_t, v_t = ktst[0], vst[t % 2]
        for hlf in range(2):
            B.dma(xs_t.ap[:, hlf * 4:(hlf + 1) * 4, :], xsrc[:, hlf * 4:(hlf + 1) * 4, t * TW:(t + 1) * TW], [xT_all], [xs_t])
        for kc in range(KC):
            B.copy(("act", "dve", "act", "dve", "pool", "act", "dve", "pool")[kc], xb_t.ap[:, kc, :], xs_t.ap[:, kc, :], [xs_t], [xb_t])
        B.rope_tables(pos_all, t * TW, TW, posi, posf, t1, cosT, sinS, invf, coefp)
        for cg in range(3):
            ps = nextbank()
            B.proj_fm(ps, wb, cg * 128, xb_t, TW)
            B.copy("act", kt_t.ap[:, cg, :], ps.ap[:, 0:TW], [ps], [kt_t])
        for cg in range(3):
            psK = nextbank()
            B.proj_fm(psK, wb, 384 + cg * 128, xb_t, TW)
            psP = nextbank()
            B.proj_fm(psP, wb, 768 + cg * 128, xb_t, TW)
            B.rope_evac(psK, psP, kt_t.ap[:, 3 + cg, :], TW, cosT, sinS, tmpa, tmpb, kt_t)
        B.dma(kT_scr.ap[:, :, t * TW:(t + 1) * TW], kt_t.ap, [kt_t], [kT_scr])
        for blk in range(TW // 128):
            for half in range(2):
                ps = nextbank()
                for kc in range(KC):
                    B.mm(ps.ap[:, 0:384], xb_t.ap[:, kc, blk * 128:(blk + 1) * 128],
                         wb.ap[:, kc, 1152 + half * 384:1152 + (half + 1) * 384], kc == 0, kc == KC - 1, [xb_t, wb], [ps])
                B.copy("act" if half == 0 else "dve", v_t.ap[:, half * 6:(half + 1) * 6, blk, 0:64],
                       ps.ap[:, 0:384].rearrange("p (h d) -> p h d", h=6), [ps], [v_t])
        nb_t = TW // 128
        B.dma(v_scr.ap[:, :, t * nb_t * 65:(t + 1) * nb_t * 65], v_t.ap.rearrange("p h b c -> p h (b c)"), [v_t], [v_scr])

    xosrc = xT_own.ap.rearrange("(kc p) s -> p kc s", p=128)
    for rnd in range(2):
        if rnd == 0:
            B.load_w(w_q, 1408, wb, wst, 0)
        else:
            B.load_w(w_g, 1024, wb, wst, 0)
        for u, (ea0, OW) in enumerate(etiles):
            xs_t, xb_t = xs[u % 2], xb[u % 2]
            osl = slice(ea0, ea0 + OW)
            for hlf in range(2):
                B.dma(xs_t.ap[:, hlf * 4:(hlf + 1) * 4, 0:OW], xosrc[:, hlf * 4:(hlf + 1) * 4, osl], [xT_own], [xs_t])
            for kc in range(KC):
                B.copy(("act", "dve", "act", "dve", "pool", "act", "dve", "pool")[kc], xb_t.ap[:, kc, 0:OW], xs_t.ap[:, kc, 0:OW], [xs_t], [xb_t])
            if rnd == 0:
                B.rope_tables(pos_own, ea0, OW, posi, posf, t1, cosT, sinS, invf, coefp)
                for cg in range(3):
                    ps = nextbank()
                    B.proj_fm(ps, wb, cg * 128, xb_t, OW)
                    B.act(qT_sb.ap[:, cg, osl], ps.ap[:, 0:OW], AF.Copy, [ps], [qT_sb], scale=0.125)
                for cg in range(3):
                    psK = nextbank()
                    B.proj_fm(psK, wb, 384 + cg * 128, xb_t, OW)
                    psP = nextbank()
                    B.proj_fm(psP, wb, 768 + cg * 128, xb_t, OW)
                    B.rope_evac(psK, psP, qT_df.ap[:, cg, osl], OW, cosT, sinS, tmpa, tmpb, qT_df)
                for cg in range(2):
                    ps = nextbank()
                    B.proj_fm(ps, wb, 1152 + cg * 128, xb_t, OW)
                    B.copy("act", mqT.ap[:, cg, osl], ps.ap[:, 0:OW], [ps], [mqT])
            else:
                for cg in range(8):
                    ps = nextbank()
                    B.proj_fm(ps, wb, cg * 128, xb_t, OW)
                    B.act(mixT.ap[:, cg, osl], ps.ap[:, 0:OW], AF.Silu, [ps], [mixT])

    phaseA_tiles = [wb] + wst + xs + xb + [posi, posf, t1, cosT, sinS, tmpa, tmpb] + ktst + vst
    B.sa.reset(mA)
    kTp = [B.tile([128, S], BF16, "kTp%d" % i) for i in range(2)]
    vtp = [B.tile([128, 2, NB * 65 + 64], BF16, "vtp%d" % i) for i in range(2)]
    E = [B.tile([128, GW], F32, "E%d" % i) for i in range(4)]
    SP = [B.tile([128, GW], BF16, "SP%d" % i) for i in range(6)]
    PT = [B.tile([128, GW], BF16, "PT%d" % i) for i in range(6)]
    R32s = [B.tile([128, GW], F32, "R32_%d" % i) for i in range(2)]
    Rbs = [[B.tile([128, GW], BF16, "Rb%d_%d" % (ch, i)) for i in range(2)] for ch in range(2)]
    tmpo = [B.tile([128, GW], F32, "tmpo%d" % i) for i in range(2)]
    rd = B.tile([128, 2 * GW], F32, "rd")
    bcs = B.tile([128, 2 * GW], F32, "bcs")
    ea = B.tile([128, GW], F32, "ea")
    eb = B.tile([128, GW], F32, "eb")
    ec = B.tile([128, GW], F32, "ec")
    fz = B.tile([128, 16], F32, "fz")
    phaseB_tiles = kTp + vtp + E + SP + PT + R32s + Rbs[0] + Rbs[1] + tmpo + [rd, bcs, ea, eb, ec, fz]
    B.op("pool", lambda e: e.memset(fz.ap, 0.0), [], phaseA_tiles + phaseB_tiles)
    for i in range(2):
        B.memset("pool", vtp[i].ap[:, :, NB * 65:NB * 65 + 64], 0.0, [vtp[i]])

    kview = kT_scr.ap
    vview = v_scr.ap

    def load_pair(kcg, vh0, slot):
        B.dma(kTp[slot].ap, kview[:, kcg, :], [kT_scr], [kTp[slot]])
        B.dma(vtp[slot].ap[:, :, 0:NB * 65], vview[:, vh0:vh0 + 2, :], [v_scr], [vtp[slot]])

    zc = [0]
    accc = [0]

    def kb_list(gi):
        out = []
        if gi < NR:
            for kb in range(16 * gi + 15, -1, -1):
                if kb >= 16 * gi:
                    m = kb - 16 * gi
                    out.append((kb, max(0, m - 12) * 128, m))
                else:
                    out.append((kb, 0, None))
            return gi * 512, 512, qcol.ap[:, 0:512], out
        W = NR * 128
        for kb in range(16 * (NR - 1) + 11, -1, -1):
            c0 = ((kb - 11 + 15) // 16) * 128 if kb > 11 else 0
            out.append((kb, c0, 16 + kb))
        return NO, W, qcol.ap[:, 512:512 + W], out

    def mask_op(Pt, qc, cs, midx, strict):
        B.op("dve", lambda e: e.scalar_tensor_tensor(Pt.ap[:, cs], qc[:, cs], stab.ap[:, midx:midx + 1], Pt.ap[:, cs],
                                                     ALU.is_gt if strict else ALU.is_ge, ALU.mult),
             [Pt, qcol, stab], [Pt])

    def sb_units(h, slot):
        cg, po = h // 2, (h % 2) * 64
        ch = h % 2
        R32, Rb = R32s[ch], Rbs[ch]
        kT, vt = kTp[slot], vtp[slot]
        units = []
        ucount = 0
        for gi in range(NR + 1):
            col0, W, qc, kbs = kb_list(gi)
            acc = PS[4 + ch + 2 * (gi % 2)]
            for ui, (kb, c0, midx) in enumerate(kbs):
                first, last = ui == 0, ui == len(kbs) - 1
                zi = 2 * ucount + ch
                ucount += 1
                Z, ARG = PS[zi % 2], PS[2 + zi % 2]
                Et, SPt, PTt = E[zi % 4], SP[zi % 6], PT[zi % 6]
                Rcur, Rnext = Rb[(zi // 2) % 2], Rb[(zi // 2 + 1) % 2]
                kap = kT.ap[po:po + 64, kb * 128:(kb + 1) * 128]
                qap = qT_sb.ap[po:po + 64, cg, col0 + c0:col0 + W]
                cs = slice(c0, W)

                def s0(Z=Z, Et=Et, SPt=SPt, kap=kap, qap=qap, cs=cs, midx=midx, kT=kT, qc=qc):
                    B.mm(Z.ap[:, cs], kap, qap, True, True, [kT, qT_sb], [Z])
                    B.act(Et.ap[:, cs], Z.ap[:, cs], AF.Exp, [Z], [Et])
                    B.act(SPt.ap[:, cs], Et.ap[:, cs], AF.Ln, [Et], [SPt], bias=1.0)
                    if midx is not None:
                        mask_op(SPt, qc, cs, midx, True)

                def s1(ARG=ARG, SPt=SPt, PTt=PTt, kap=kap, qap=qap, cs=cs, midx=midx, first=first, last=last,
                       Rcur=Rcur, Rnext=Rnext, kT=kT, qc=qc, W=W):
                    B.mm(ARG.ap[:, cs], kap, qap, True, False, [kT, qT_sb], [ARG])
                    B.mm(ARG.ap[:, cs], negtri.ap, SPt.ap[:, cs], False, first, [negtri, SPt], [ARG])
                    if not first:
                        B.mm(ARG.ap[:, cs], negones.ap, Rcur.ap[:, cs], False, True, [negones, Rcur], [ARG])
                    if first:
                        B.memset("pool", R32.ap, 0.0, [R32])
                    if not last:
                        B.tt("dve", R32.ap[:, cs], R32.ap[:, cs], SPt.ap[:, cs], ALU.add, [R32, SPt], [R32])
                        B.copy("act", Rnext.ap[:, 0:W], R32.ap[:, 0:W], [R32], [Rnext])
                    B.act(PTt.ap[:, cs], ARG.ap[:, cs], AF.Exp, [ARG], [PTt])
                    if midx is not None:
                        mask_op(PTt, qc, cs, midx, True)

                def s2(acc=acc, PTt=PTt, cs=cs, kb=kb, first=first, last=last, gi=gi, vt=vt, col0=col0, W=W):
                    B.mm(acc.ap[:, cs], vt.ap[:, h % 2, kb * 65:kb * 65 + 128], PTt.ap[:, cs], first, last, [vt, PTt], [acc], skip=True)
                    if last:
                        tq = tmpo[ch]
                        gsl = slice(col0, col0 + W)
                        B.copy("act", tq.ap[po:po + 64, 0:W], acc.ap[0:64, 0:W], [acc], [tq])
                        B.tt("dve", mixT.ap[po:po + 64, cg, gsl], tq.ap[po:po + 64, 0:W], mixT.ap[po:po + 64, cg, gsl],
                             ALU.mult, [tq, mixT], [mixT])

                units.append([s0, s1, s2])
        return units

    B.rd, B.bcs, B.ones_f, B.b4, B.lamw = rd, bcs, ones_f, [0], lamw
    softmax_norm = B.softmax_norm
    nextbank4 = B.nextbank4

    def df_units(h, slot):
        cgk, po = h // 2, (h % 2) * 64
        kT, vt = kTp[slot], vtp[slot]
        units = []
        scale = 32 ** -0.5
        for gi in range(NR + 1):
            col0, W, qc, kbs = kb_list(gi)
            accs = [PS[4 + 2 * (accc[0] % 2)], PS[5 + 2 * (accc[0] % 2)]]
            accc[0] += 1
            gsl = slice(col0, col0 + W)
            for ui, (kb, c0, midx) in enumerate(kbs):
                first, last = ui == 0, ui == len(kbs) - 1
                cs = slice(c0, W)
                Sbs, PTs, kaps, qaps, tps = [], [], [], [], []
                for cm in range(2):
                    zi = zc[0]
                    zc[0] += 1
                    Sbs.append(nextbank4())
                    PTs.append(PT[zi % 4])
                    rows = slice(po + cm * 32, po + cm * 32 + 32)
                    kaps.append(kT.ap[rows, kb * 128:(kb + 1) * 128])
                    qaps.append(qT_df.ap[rows, cgk, col0 + c0:col0 + W])
                    tps.append((po + cm * 32, 0))

                def s0(Sbs=Sbs, PTs=PTs, kaps=kaps, qaps=qaps, cs=cs, midx=midx, tps=tps, kT=kT, qc=qc):
                    for cm in range(2):
                        B.op("pe", lambda e, cm=cm: e.matmul(Sbs[cm].ap[:, cs], lhsT=kaps[cm], rhs=qaps[cm], start=True, stop=True,
                                                             tile_position=tps[cm]), [kT, qT_df], [Sbs[cm]])
                    for cm in range(2):
                        B.act(PTs[cm].ap[:, cs], Sbs[cm].ap[:, cs], AF.Exp, [Sbs[cm]], [PTs[cm]], scale=scale)
                        if midx is not None:
                            mask_op(PTs[cm], qc, cs, midx, False)

                def s1(PTs=PTs, cs=cs, kb=kb, first=first, last=last, accs=accs, gsl=gsl, vt=vt, W=W):
                    for cm in range(2):
                        B.mm(accs[cm].ap[:, cs], vt.ap[:, h % 2, kb * 65:kb * 65 + 128], PTs[cm].ap[:, cs], first, last,
                             [vt, PTs[cm]], [accs[cm]], skip=True)
                    if last:
                        bc0 = softmax_norm(accs[0], W)
                        B.tt("dve", ea.ap[0:64, 0:W], accs[0].ap[0:64, 0:W], bcs.ap[0:64, bc0], ALU.mult, [accs[0], bcs], [ea])
                        bc1 = softmax_norm(accs[1], W)
                        B.tt("dve", eb.ap[0:64, 0:W], accs[1].ap[0:64, 0:W], bcs.ap[0:64, bc1], ALU.mult, [accs[1], bcs], [eb])
                        B.op("dve", lambda e: e.scalar_tensor_tensor(ea.ap[0:64, 0:W], eb.ap[0:64, 0:W], nlcol.ap[0:64, 1:2],
                                                                     ea.ap[0:64, 0:W], ALU.mult, ALU.add), [ea, eb, nlcol], [ea])
                        B.tt("pool", eb.ap[0:64, 0:W], ea.ap[0:64, 0:W], ea.ap[0:64, 0:W], ALU.mult, [ea], [eb])
                        ss = nextbank4()
                        B.mm(ss.ap[0:64, 0:W], ones_f.ap[0:64, 0:64], eb.ap[0:64, 0:W], True, True, [ones_f, eb], [ss])
                        B.act(ec.ap[0:64, 0:W], ss.ap[0:64, 0:W], AF.Ln, [ss, B.eps_col], [ec], scale=1.0 / 64, bias=B.eps_col.ap[0:64, 0:1])
                        B.act(ec.ap[0:64, 0:W], ec.ap[0:64, 0:W], AF.Exp, [ec], [ec], scale=-0.5)
                        B.tt("dve", ea.ap[0:64, 0:W], ea.ap[0:64, 0:W], ec.ap[0:64, 0:W], ALU.mult, [ea, ec], [ea])
                        tq = tmpo[0]
                        B.copy("act", tq.ap[po:po + 64, 0:W], ea.ap[0:64, 0:W], [ea], [tq])
                        mcg = 3 + h // 2
                        B.op("dve", lambda e: e.scalar_tensor_tensor(
                            mixT.ap[po:po + 64, mcg, gsl], tq.ap[po:po + 64, 0:W], g1col.ap[po:po + 64, 0:1],
                            mixT.ap[po:po + 64, mcg, gsl], ALU.mult, ALU.mult), [tq, g1col, mixT], [mixT])

                units.append([s0, s1])
        return units

    pairs = [("sb", p) for p in range(3)] + [("df", p) for p in range(3)]
    load_pair(0, 0, 0)
    run_pipeline(B.mem_units(mkT, mv, mqT, mixT, PT, ea, tmpo[1], zc, accc, B.groups), 2)
    for pi, (kind, p) in enumerate(pairs):
        slot = pi % 2
        if pi + 1 < len(pairs):
            k2, p2 = pairs[pi + 1]
            load_pair(p2 if k2 == "sb" else 3 + p2, 2 * p2 if k2 == "sb" else 6 + 2 * p2, (pi + 1) % 2)
        if kind == "sb":
            ua, ub = sb_units(2 * p, slot), sb_units(2 * p + 1, slot)
            run_pipeline([u for pair in zip(ua, ub) for u in pair], 3)
        else:
            for h in (2 * p, 2 * p + 1):
                run_pipeline(df_units(h, slot), 2)

    B.sa.reset(mA)
    wo = B.tile([128, KC, D], BF16, "wo")
    wst2 = [B.tile([128, KC, 128], F32, "wst2_%d" % i) for i in range(2)]
    gbc = B.tile([128, D], F32, "gbc")
    bbc = B.tile([128, D], F32, "bbc")
    xr = [B.tile([128, D], F32, "xr%d" % i) for i in range(2)]
    vv = [B.tile([128, D], F32, "vv%d" % i) for i in range(2)]
    st = B.tile([128, 8], F32, "st")
    fz2 = B.tile([128, 16], F32, "fz2")
    phaseC_tiles = [wo, gbc, bbc, st, fz2] + wst2 + xr + vv
    B.op("pool", lambda e: e.memset(fz2.ap, 0.0), [], phaseB_tiles + phaseC_tiles)
    B.load_w(w_o, D, wo, wst2)
    B.dma(gbc.ap, ln_g.ap[0:1, :].partition_broadcast(128), [ln_g], [gbc])
    B.dma(bbc.ap, ln_b.ap[0:1, :].partition_broadcast(128), [ln_b], [bbc])
    outs = []
    for j in range(NEB):
        outs.append(B.out_block(j, mixT, wo, x_own, xr[j % 2], vv[j % 2], st, gbc, bbc, y_out, nextbank))
    B.l0_tiles = phaseB_tiles + phaseC_tiles + [qT_sb, qT_df, mqT, mixT, mkT, mv]
    B.eps_ready = True
    return outs


DBG = {}
QCOLS = [0, 4, 1, 5, 2, 6, 3, 7, 8, 8, 9, 9, 10, 10, 11, 11]


def _qpos(hq):
    if hq < 4:
        return hq, 0
    if hq < 8:
        return hq - 4, 64
    return 4 + hq - 8, 0


def l1_body(B):
    nc = B.nc
    S = B.S
    NB, NJ, NO, NR, NE, NEB = B.NB, B.NJ, B.NO, B.NR, B.NE, B.NEB
    PS = B.psum
    etiles = [(a, min(512, NE - a)) for a in range(0, NE, 512)]
    x1_ext = B.x1_ext_d
    pos_ext = B.dram["pos_ext"]
    memT = B.dram["memT"]
    w1_kv = B.din("w1_kv", [D, 960])
    w1_q = B.din("w1_q", [D, 2048])
    w1_g = B.din("w1_g", [D, 1024])
    w1_mkv = B.din("w1_mkv", [D, 512])
    w1_o = B.din("w1_o", [D, D])
    ln1_g = B.din("ln1_g", [1, D])
    ln1_b = B.din("ln1_b", [1, D])
    sinks_x = B.din("sinks_x", [1, 1536])
    invf1_d = B.din("invf1", [128, 1])
    coef1_d = B.din("coefp1", [128, 1])
    masks1_d = B.din("masks1", [128, 3 * 128], BF16)
    ident_d = B.din("ident", [128, 128])
    y_out = B.dout("y", [NO, D])

    pbc = [0]

    def nextbank():
        b = PS[pbc[0] % 8]
        pbc[0] += 1
        return b

    B.sa.reset(B.mark_consts)
    new_tiles = []

    def tl(shape, dt, name):
        t = B.tile(shape, dt, name)
        new_tiles.append(t)
        return t

    ident = tl([128, 128], F32, "ident")
    masks1 = tl([128, 3, 128], BF16, "masks1")
    invf1 = tl([128, 1], F32, "invf1")
    coef1 = tl([128, 1], F32, "coef1")
    qT1 = tl([128, 8, NO], BF16, "qT1")
    kT1 = tl([128, 2, NE], BF16, "kT1")
    V1 = tl([128, NEB, 3, 65], BF16, "V1")
    mqT1 = tl([128, 2, NO], BF16, "mqT1")
    mixT1 = tl([128, 8, NO], BF16, "mixT1")
    mkT1 = tl([128, 2, MEM_LEN], BF16, "mkT1")
    mv1 = tl([128, 2, 4, 65], BF16, "mv1")
    fz = tl([128, 16], F32, "fz1")
    mP = B.sa.mark()
    xT1 = tl([128, KC, NE], BF16, "xT1")
    xin = [tl([128, D], F32, "xin%d" % i) for i in range(2)]
    wb1 = tl([128, KC, 1024], BF16, "wb1")
    wst = [tl([128, KC, 64], F32, "wst1_%d" % i) for i in range(2)]
    posi = tl([128, 512], I32, "posi1")
    posf = tl([128, 512], F32, "posf1")
    t1 = tl([128, 512], F32, "t11")
    cosT = tl([128, 512], F32, "cosT1")
    sinS = tl([128, 512], F32, "sinS1")
    tmpa = tl([128, 512], F32, "tmpa1")
    tmpb = tl([128, 512], F32, "tmpb1")
    xs_m = tl([128, KC, MEM_LEN], F32, "xs_m")
    memb = tl([128, KC, MEM_LEN], BF16, "memb1")
    B.op("pool", lambda e: e.memset(fz.ap, 0.0), [], B.l0_tiles + new_tiles)
    B.dma(ident.ap, ident_d.ap[:, :], [ident_d], [ident])
    B.dma(masks1.ap.rearrange("p a b -> p (a b)"), masks1_d.ap[:, :], [masks1_d], [masks1])
    B.dma(invf1.ap, invf1_d.ap[:, :], [invf1_d], [invf1])
    B.dma(coef1.ap, coef1_d.ap[:, :], [coef1_d], [coef1])

    B.mem_kv(memT, w1_mkv, wb1, wst, xs_m, memb, mkT1, mv1, nextbank)
    B.memset("pool", V1.ap, 1.0, [V1])

    xc = [0]

    def transpose_block(e):
        xin_t = xin[xc[0] % 2]
        xc[0] += 1
        B.dma(xin_t.ap, x1_ext.ap[e * 128:(e + 1) * 128, :], [x1_ext], [xin_t])
        for half in range(2):
            ps = nextbank()
            for q in range(4):
                kc = half * 4 + q
                B.op("pe", lambda en, ps=ps, q=q, kc=kc: en.transpose(ps.ap[:, q * 128:(q + 1) * 128],
                                                                     xin_t.ap[:, kc * 128:(kc + 1) * 128], ident.ap),
                     [xin_t, ident], [ps])
            B.copy("act" if half == 0 else "dve", xT1.ap[:, half * 4:(half + 1) * 4, e * 128:(e + 1) * 128],
                   ps.ap[:, 0:512].rearrange("p (a b) -> p a b", a=4), [ps], [xT1])

    def xtile(a0, w):
        t = T(xT1.ap[:, :, a0:a0 + w], "xo_t")
        t.res = xT1.res
        return t

    B.load_w(w1_kv, 960, wb1, wst, 0)
    for (a0, w) in etiles:
        for e in range(a0 // 128, (a0 + w) // 128):
            transpose_block(e)
        xo_t = xtile(a0, w)
        B.rope_tables(pos_ext, a0, w, posi, posf, t1, cosT, sinS, invf1, coef1)
        for cg in range(2):
            psK = nextbank()
            B.proj_fm(psK, wb1, cg * 128, xo_t, w)
            psP = nextbank()
            B.proj_fm(psP, wb1, 256 + cg * 128, xo_t, w)
            B.rope_evac(psK, psP, kT1.ap[:, cg, a0:a0 + w], w, cosT, sinS, tmpa, tmpb, kT1)
        for blk in range(w // 128):
            e = a0 // 128 + blk
            ps = nextbank()
            for kc in range(KC):
                B.mm(ps.ap[:, 0:192], xo_t.ap[:, kc, blk * 128:(blk + 1) * 128], wb1.ap[:, kc, 512:704],
                     kc == 0, kc == KC - 1, [xo_t, wb1], [ps])
            B.copy("act", V1.ap[:, e, :, 0:64], ps.ap[:, 0:192].rearrange("p (h d) -> p h d", h=3), [ps], [V1])
        if a0 < NO:
            for cg in range(2):
                ps = nextbank()
                B.proj_fm(ps, wb1, 704 + cg * 128, xo_t, w)
                B.copy("act", mqT1.ap[:, cg, a0:a0 + w], ps.ap[:, 0:w], [ps], [mqT1])
    for rnd in range(3):
        if rnd < 2:
            B.load_w(T(w1_q.ap[:, rnd * 1024:(rnd + 1) * 1024], "w1q"), 1024, wb1, wst, 0)
        else:
            B.load_w(w1_g, 1024, wb1, wst, 0)
        for (a0, w) in etiles:
            if a0 >= NO:
                continue
            osl = slice(a0, a0 + w)
            xo_t = xtile(a0, w)
            if rnd < 2:
                B.rope_tables(pos_ext, a0, w, posi, posf, t1, cosT, sinS, invf1, coef1)
                for cg in range(4):
                    psK = nextbank()
                    B.proj_fm(psK, wb1, cg * 128, xo_t, w)
                    psP = nextbank()
                    B.proj_fm(psP, wb1, 512 + cg * 128, xo_t, w)
                    B.rope_evac(psK, psP, qT1.ap[:, rnd * 4 + cg, osl], w, cosT, sinS, tmpa, tmpb, qT1)
            else:
                for cg in range(8):
                    ps = nextbank()
                    B.proj_fm(ps, wb1, cg * 128, xo_t, w)
                    B.act(mixT1.ap[:, cg, osl], ps.ap[:, 0:w], AF.Silu, [ps], [mixT1])

    proj_tiles = [xT1] + xin + [wb1] + wst + [posi, posf, t1, cosT, sinS, tmpa, tmpb, xs_m, memb]
    B.sa.reset(mP)
    esink = B.tile([128, 1536], F32, "esink")
    esraw = B.tile([128, 1536], F32, "esraw")
    PT1 = [B.tile([128, 512], BF16, "PT1_%d" % i) for i in range(4)]
    rd = B.tile([128, 1024], F32, "rd1")
    bcs = B.tile([128, 1024], F32, "bcs1")
    eas = [B.tile([128, 512], F32, "ea1_%d" % i) for i in range(2)]
    ea = eas[0]
    tq = [B.tile([128, 512], F32, "tq1_%d" % i) for i in range(3)]
    wo1 = B.tile([128, KC, D], BF16, "wo1")
    wst2 = [B.tile([128, KC, 128], F32, "wst1b_%d" % i) for i in range(2)]
    gbc = B.tile([128, D], F32, "gbc1")
    bbc = B.tile([128, D], F32, "bbc1")
    xr = [B.tile([128, D], F32, "xr1_%d" % i) for i in range(2)]
    vv = [B.tile([128, D], F32, "vv1_%d" % i) for i in range(2)]
    st = B.tile([128, 8], F32, "st1")
    att_tiles = [esink, esraw] + PT1 + [rd, bcs] + eas + tq + [wo1] + wst2 + [gbc, bbc] + xr + vv + [st]
    B.op("pool", lambda e: e.memset(fz.ap, 0.0), [], proj_tiles + att_tiles + [fz])
    B.rd, B.bcs, B.b4 = rd, bcs, [0]
    B.dma(esraw.ap, sinks_x.ap[0:1, :].partition_broadcast(128), [sinks_x], [esraw])
    B.act(esink.ap, esraw.ap, AF.Exp, [esraw], [esink])
    B.load_w(w1_o, D, wo1, wst2)
    B.dma(gbc.ap, ln1_g.ap[0:1, :].partition_broadcast(128), [ln1_g], [gbc])
    B.dma(bbc.ap, ln1_b.ap[0:1, :].partition_broadcast(128), [ln1_b], [bbc])

    zc, accc = [0], [0]
    run_pipeline(B.mem_units(mkT1, mv1, mqT1, mixT1, PT1, ea, tq[2], zc, accc, [(i * 512, 512) for i in range(NR)]), 2)
    units = []
    for j in range(NJ):
        i_run, r = j // 4, j % 4
        e_prev = j - 1 if r > 0 else NJ + i_run
        jsl = slice(j * 128, (j + 1) * 128)
        for kvh in range(3):
            acc = PS[4 + accc[0] % 4]
            accc[0] += 1
            for bi, e_k in enumerate((e_prev, j)):
                zi = zc[0]
                zc[0] += 1
                Sb = B.nextbank4()
                PTt = PT1[zi % 4]
                mi = 0 if bi == 1 else (2 if j == 0 else 1)
                ksl = slice(e_k * 128, (e_k + 1) * 128)

                def s0(Sb=Sb, PTt=PTt, kvh=kvh, jsl=jsl, ksl=ksl, mi=mi):
                    for g in range(4):
                        hq = kvh * 4 + g
                        cgq, po = _qpos(hq)
                        cgk = 0 if kvh < 2 else 1
                        B.mm(Sb.ap[:, g * 128:(g + 1) * 128], kT1.ap[po:po + 64, cgk, ksl], qT1.ap[po:po + 64, cgq, jsl],
                             True, True, [kT1, qT1], [Sb])
                    B.act(PTt.ap[:, 0:512], Sb.ap[:, 0:512], AF.Exp, [Sb], [PTt], scale=0.125)
                    for g in range(4):
                        B.tt("dve" if g % 2 == 0 else "pool", PTt.ap[:, g * 128:(g + 1) * 128], PTt.ap[:, g * 128:(g + 1) * 128],
                             masks1.ap[:, mi, :], ALU.mult, [PTt, masks1], [PTt])

                def s1(acc=acc, PTt=PTt, kvh=kvh, j=j, jsl=jsl, bi=bi, e_k=e_k):
                    B.mm(acc.ap[0:65, 0:512], V1.ap[:, e_k, kvh, 0:65], PTt.ap[:, 0:512], bi == 0, bi == 1, [V1, PTt], [acc], skip=True)
                    if bi == 1:
                        bcol = B.softmax_norm(acc, 512, add_row=(esink.ap[64:65, kvh * 512:(kvh + 1) * 512], esink))
                        ea = eas[accc[0] % 2]
                        B.tt("dve", ea.ap[0:64, :], acc.ap[0:64, 0:512], bcs.ap[0:64, bcol], ALU.mult, [acc, bcs], [ea])
                        tqt = tq[accc[0] % 2]
                        accc[0] += 1
                        for g in range(4):
                            hq = kvh * 4 + g
                            mcg, po = hq // 2, (hq % 2) * 64
                            gs = slice(g * 128, (g + 1) * 128)
                            B.copy("act", tqt.ap[po:po + 64, gs], ea.ap[0:64, gs], [ea], [tqt])
                            B.tt("dve", mixT1.ap[po:po + 64, mcg, jsl], tqt.ap[po:po + 64, gs],
                                 mixT1.ap[po:po + 64, mcg, jsl], ALU.mult, [tqt, mixT1], [mixT1])

                units.append([s0, s1])
    run_pipeline(units, 2)

    outs = []
    for j in range(NJ):
        outs.append(B.out_block(j, mixT1, wo1, x1_ext, xr[j % 2], vv[j % 2], st, gbc, bbc, y_out, nextbank))
    return outs


def layer_norm_store(B, vv_t, scratch, st, gbc, bbc, y_out, j):
    B.op("dve", lambda e: e.tensor_reduce(st.ap[:, 0:1], vv_t.ap, AX.X, ALU.add), [vv_t], [st])
    B.ts("dve", st.ap[:, 1:2], st.ap[:, 0:1], -1.0 / D, None, ALU.mult, None, [st], [st])
    B.op("act", lambda e: e.activation(out=scratch.ap, in_=vv_t.ap, func=AF.Square, bias=st.ap[:, 1:2], scale=1.0,
                                       accum_out=st.ap[:, 2:3]), [vv_t, st], [scratch, st])
    B.act(st.ap[:, 3:4], st.ap[:, 2:3], AF.Ln, [st, B.eps_col], [st], scale=1.0 / D, bias=B.eps_col.ap[:, 0:1])
    B.act(st.ap[:, 3:4], st.ap[:, 3:4], AF.Exp, [st], [st], scale=-0.5)
    B.ts("dve", vv_t.ap, vv_t.ap, st.ap[:, 1:2], st.ap[:, 3:4], ALU.add, ALU.mult, [vv_t, st], [vv_t])
    B.tt("pool", vv_t.ap, vv_t.ap, gbc.ap, ALU.mult, [vv_t, gbc], [vv_t])
    B.tt("pool", vv_t.ap, vv_t.ap, bbc.ap, ALU.add, [vv_t, bbc], [vv_t])
    return B.dma(y_out.ap[j * 128:(j + 1) * 128, :], vv_t.ap, [vv_t], [y_out])


def _own_idx(S, c):
    NR = S // 2048
    return np.concatenate([np.arange((16 * i + 4 * c) * 128, (16 * i + 4 * c + 4) * 128) for i in range(NR)])


def _halo_idx(S, c):
    NR = S // 2048
    out = []
    for i in range(NR):
        g = 16 * i + 4 * c - 1
        out.append(np.arange(g * 128, (g + 1) * 128) if g >= 0 else np.full(128, -1))
    return np.concatenate(out)


def _mask_dram(c):
    m = _masks(c)
    return np.ascontiguousarray(np.transpose(m, (1, 0, 2)).reshape(128, 9 * 128))


def _mask_tables(c):
    col = np.arange(512, dtype=np.float32)
    qcol = np.zeros((128, 1024), np.float32)
    qcol[:, 0:512] = col[None, :]
    qcol[:, 512:1024] = (col + 1920.0 * np.floor(col / 128.0))[None, :]
    k = np.arange(128, dtype=np.float32)[:, None]
    stab = np.zeros((128, 80), np.float32)
    stab[:, 0:16] = k + 128.0 * np.arange(16, dtype=np.float32)[None, :] - 512.0 * c
    stab[:, 16:80] = k + 128.0 * np.arange(64, dtype=np.float32)[None, :] + 128.0 - 512.0 * c
    return qcol, stab


def _masks1(c):
    k = np.arange(128)[:, None]
    q = np.arange(128)[None, :]
    m = np.zeros((3, 128, 128), np.float32)
    m[0] = (k <= q)
    m[1] = (k > q)
    m[2] = (k > q) if c > 0 else 0.0
    return np.ascontiguousarray(np.transpose(m, (1, 0, 2)).reshape(128, 3 * 128)).astype(ml_dtypes.bfloat16)


def l1_weights(w_in, w_memkv, sinks, w_out, ln_g, ln_b):
    cq, ck, cv = w_in[:, 0:768], w_in[:, 768:960], w_in[:, 960:1152]
    mq, gate = w_in[:, 1152:1408], w_in[:, 1408:2432]
    kd = np.concatenate([ck[:, 0:64], ck[:, 64:128], ck[:, 128:192], ck[:, 128:192]], axis=1)
    pk = _partner_perm(256, 64, 16)
    qr = np.concatenate([cq[:, h * 64:(h + 1) * 64] for h in QCOLS], axis=1)
    pq = _partner_perm(512, 64, 16)
    q0, q1 = qr[:, 0:512], qr[:, 512:1024]
    invf1, coef1 = _host_consts(1)
    return {
        "w1_kv": np.ascontiguousarray(np.concatenate([kd, kd[:, pk], cv, mq], axis=1)),
        "w1_q": np.ascontiguousarray(np.concatenate([q0, q0[:, pq], q1, q1[:, pq]], axis=1)),
        "w1_g": np.ascontiguousarray(gate),
        "w1_mkv": np.ascontiguousarray(w_memkv),
        "w1_o": np.ascontiguousarray(w_out),
        "ln1_g": np.ascontiguousarray(ln_g[None, :]),
        "ln1_b": np.ascontiguousarray(ln_b[None, :]),
        "sinks_x": np.ascontiguousarray(np.repeat(sinks, 128)[None, :]),
        "invf1": invf1, "coefp1": coef1,
        "ident": np.eye(128, dtype=np.float32),
    }


def prep_fused(inp):
    f = lambda a: np.asarray(a)
    x, mem, positions = f(inp["x"]), f(inp["mem"]), f(inp["positions"])
    S = x.shape[1]
    w_in = f(inp["w_in_even"])[0]
    sbq, sbk, sbv = w_in[:, 0:384], w_in[:, 384:768], w_in[:, 768:1152]
    dfq, dfk, dfv = w_in[:, 1152:1536], w_in[:, 1536:1920], w_in[:, 1920:2304]
    mq, gate = w_in[:, 2304:2560], w_in[:, 2560:3584]
    perm = _partner_perm(384, 32, 8)
    invf, coefp = _host_consts(0)
    dsub = f(inp["diff_subln_even"])[0]
    shared = {
        "w_k": np.ascontiguousarray(np.concatenate([sbk, dfk, dfk[:, perm]], axis=1)),
        "w_v": np.ascontiguousarray(np.concatenate([sbv, dfv], axis=1)),
        "w_q": np.ascontiguousarray(np.concatenate([sbq, dfq, dfq[:, perm], mq], axis=1)),
        "w_g": np.ascontiguousarray(gate),
        "w_mkv": np.ascontiguousarray(f(inp["w_memkv_even"])[0]),
        "w_o": np.ascontiguousarray(f(inp["w_out_even"])[0]),
        "ln_g": np.ascontiguousarray(f(inp["ln_g_even"])[0][None, :]),
        "ln_b": np.ascontiguousarray(f(inp["ln_b_even"])[0][None, :]),
        "dlam": np.ascontiguousarray(f(inp["diff_lambda_even"])[0].reshape(1, 128)),
        "subln": np.ascontiguousarray(np.concatenate([dsub, dsub])[:, None]),
        "invf": invf, "coefp": coefp,
    }
    shared.update(l1_weights(f(inp["w_in_odd"])[0], f(inp["w_memkv_odd"])[0], f(inp["sinks_odd"])[0], f(inp["w_out_odd"])[0],
                             f(inp["ln_g_odd"])[0], f(inp["ln_b_odd"])[0]))
    xT = [np.ascontiguousarray(x[b].T) for b in range(x.shape[0])]
    maps = []
    for core in range(8):
        b, c = core // 4, core % 4
        own = _own_idx(S, c)
        hidx = _halo_idx(S, c)
        ext = np.concatenate([own, np.maximum(hidx, 0)])
        qcol, stab = _mask_tables(c)
        m = dict(shared)
        m.update({
            "xT_all": xT[b],
            "xT_ext": np.ascontiguousarray(x[b][ext].T),
            "x_ext": np.ascontiguousarray(x[b][ext]),
            "pos_all": np.ascontiguousarray(positions[b][None, :]).astype(np.int32),
            "pos_ext": np.ascontiguousarray(positions[b][ext][None, :]).astype(np.int32),
            "memT": np.ascontiguousarray(mem[b].T),
            "masks": _mask_dram(c),
            "qcol": qcol, "stab": stab,
            "masks1": _masks1(c),
        })
        maps.append(m)
    return maps


def gather_own(results, key, S, nb=2):
    out = np.zeros((nb, S, D), np.float32)
    for core in range(8):
        b, c = core // 4, core % 4
        out[b, _own_idx(S, c)] = results[core][key]
    return out


def build_fused(S, lambda_init):
    B = Builder(S, 0)
    l0_body(B, lambda_init, True)
    outs = l1_body(B)
    B.sch.emit(final_wait_ops=outs)
    return B


LAMBDA_INIT0 = 0.8 - 0.6 * math.exp(-0.3 * 0)


def kernel(**inputs):
    S = np.asarray(inputs["x"]).shape[1]
    B = build_fused(S, LAMBDA_INIT0)
    maps = prep_fused(inputs)
    res = run_bass_kernel_spmd(B.nc, maps, core_ids=list(range(8)))
    return gather_own(res.results, "y", S)
```

```python
import math
import contextlib
import numpy as np
import ml_dtypes
import concourse.bass as bass
import concourse.mybir as mybir
from concourse.bass_utils import run_bass_kernel_spmd

F32 = mybir.dt.float32
BF16 = mybir.dt.bfloat16
I32 = mybir.dt.int32
AF = mybir.ActivationFunctionType
ALU = mybir.AluOpType
AX = mybir.AxisListType

D = 1024
KC = 8
DEPTH = 2
ALPHA = (2 * DEPTH) ** 0.25
LN_EPS = 1e-5
ROPE_THETA = 500000.0
MEM_LEN = 256
PI = math.pi


class Res:
    __slots__ = ("lw", "rd", "name")

    def __init__(self, name=""):
        self.lw = None
        self.rd = []
        self.name = name


class Sched:
    ENGS = ("pe", "act", "dve", "pool", "sp")
    NSLOT = 14

    def __init__(self, nc):
        self.nc = nc
        self.ops = []

    def add(self, eng, fn, reads=(), writes=(), dma=False):
        idx = len(self.ops)
        deps = set()
        for r in reads:
            if r.lw is not None:
                deps.add(r.lw)
        for w in writes:
            if w.lw is not None:
                deps.add(w.lw)
            deps.update(w.rd)
        for r in reads:
            r.rd.append(idx)
        for w in writes:
            w.lw = idx
            w.rd = []
        deps.discard(idx)
        self.ops.append([eng, fn, deps, dma])
        return idx

    def emit(self, final_wait_ops=()):
        nc = self.nc
        ops = self.ops
        n = len(ops)
        has_dep = [False] * n
        for i, (eng, fn, deps, dma) in enumerate(ops):
            for d in deps:
                if ops[d][0] == "pe" and eng == "pe" and not ops[d][3] and not dma:
                    continue
                has_dep[d] = True
        for d in final_wait_ops:
            has_dep[d] = True
        cnt = {e: 0 for e in self.ENGS}
        dcnt = {e: 0 for e in self.ENGS}
        sig = [None] * n
        for i, (eng, fn, deps, dma) in enumerate(ops):
            if dma:
                k = dcnt[eng]
                dcnt[eng] += 1
                sig[i] = ("d", eng, k % self.NSLOT, 16 * (k // self.NSLOT + 1))
            elif has_dep[i]:
                cnt[eng] += 1
                sig[i] = ("c", eng, cnt[eng])
        engs_used = [e for e in self.ENGS if any(o[0] == e for o in ops)]
        with contextlib.ExitStack() as st:
            csem = {e: st.enter_context(nc.semaphore("c_" + e)) for e in engs_used}
            dsem = {}
            for e in engs_used:
                if dcnt[e] > 0:
                    dsem[e] = [st.enter_context(nc.semaphore("d_%s_%d" % (e, s)))
                               for s in range(min(self.NSLOT, dcnt[e]))]
            block = st.enter_context(nc.Block())
            engobj = {"pe": "tensor", "act": "scalar", "dve": "vector", "pool": "gpsimd", "sp": "sync"}

            def make_stream(ename):
                def stream(eng):
                    waited_c = {}
                    waited_d = {}
                    for i, (e, fn, deps, dma) in enumerate(ops):
                        if e != ename:
                            continue
                        need_c = {}
                        need_d = {}
                        for d in deps:
                            s = sig[d]
                            if s is None:
                                continue
                            if s[0] == "c":
                                if s[1] == "pe" and ename == "pe" and not dma:
                                    continue
                                need_c[s[1]] = max(need_c.get(s[1], 0), s[2])
                            else:
                                key = (s[1], s[2])
                                need_d[key] = max(need_d.get(key, 0), s[3])
                        if dma:
                            s = sig[i]
                            if s[3] > 16:
                                key = (s[1], s[2])
                                need_d[key] = max(need_d.get(key, 0), s[3] - 16)
                        for se, v in need_c.items():
                            if waited_c.get(se, 0) < v:
                                eng.wait_ge(csem[se], v)
                                waited_c[se] = v
                        for key, v in need_d.items():
                            if waited_d.get(key, 0) < v:
                                eng.wait_ge(dsem[key[0]][key[1]], v)
                                waited_d[key] = v
                        ins = fn(eng)
                        s = sig[i]
                        if s is not None:
                            if s[0] == "c":
                                ins.then_inc(csem[ename], 1)
                            else:
                                ins.then_inc(dsem[ename][s[2]], 16)
                    if ename == "sp":
                        for d in final_wait_ops:
                            s = sig[d]
                            if s[0] == "c":
                                eng.wait_ge(csem[s[1]], s[2])
                            else:
                                eng.wait_ge(dsem[s[1]][s[2]], s[3])
                return stream

            for e in engs_used:
                getattr(block, engobj[e])(make_stream(e))


class SbufAlloc:
    def __init__(self, nc, nbytes=207 * 1024):
        self.nc = nc
        self.arena = nc.alloc_sbuf_tensor("arena", [128, nbytes], mybir.dt.uint8)
        self.off = 0
        self.limit = nbytes
        self.peak = 0

    def mark(self):
        return self.off

    def reset(self, m):
        self.off = m

    def tile(self, shape, dtype):
        assert shape[0] == 128
        esz = {F32: 4, BF16: 2, I32: 4}[dtype]
        nel = int(np.prod(shape[1:]))
        nbytes = (esz * nel + 63) // 64 * 64
        off = self.off
        self.off += nbytes
        self.peak = max(self.peak, self.off)
        assert self.off <= self.limit, ("SBUF overflow", self.off)
        ap = self.arena.ap()[:, off:off + esz * nel].bitcast(dtype)
        if len(shape) == 3:
            ap = ap.rearrange("p (a b) -> p a b", a=shape[1])
        elif len(shape) == 4:
            ap = ap.rearrange("p (a b c) -> p a b c", a=shape[1], b=shape[2])
        return ap


class T:
    def __init__(self, ap, name=""):
        self.ap = ap
        self.res = Res(name)


class Builder:
    def __init__(self, S, layer):
        self.S = S
        self.layer = layer
        self.NB = S // 128
        self.NJ = self.NB // 4
        self.NO = self.NJ * 128
        self.NR = self.NB // 16
        self.NE = self.NO + self.NR * 128
        self.NEB = self.NE // 128
        self.QGB = 4
        self.GW = 512
        self.NQG = self.NR
        self.groups = [(i * 512, 512) for i in range(self.NR)] + [(self.NO, self.NR * 128)]
        self.TW = min(512, S)
        self.NT = S // self.TW
        self.nc = bass.Bass("TRN2", target_bir_lowering=False)
        self.sch = Sched(self.nc)
        self.sa = SbufAlloc(self.nc)
        self.dram = {}
        self.psum = [T(self.nc.alloc_psum_tensor("ps%d" % i, [128, 512], F32).ap(), "ps%d" % i) for i in range(8)]

    def din(self, name, shape, dtype=F32):
        t = T(self.nc.dram_tensor(name, list(shape), dtype, kind="ExternalInput").ap(), name)
        self.dram[name] = t
        return t

    def dout(self, name, shape, dtype=F32):
        t = T(self.nc.dram_tensor(name, list(shape), dtype, kind="ExternalOutput").ap(), name)
        self.dram[name] = t
        return t

    def dscr(self, name, shape, dtype):
        t = T(self.nc.dram_tensor(name, list(shape), dtype).ap(), name)
        self.dram[name] = t
        return t

    def tile(self, shape, dtype, name=""):
        return T(self.sa.tile(shape, dtype), name)

    def op(self, eng, fn, reads=(), writes=(), dma=False):
        return self.sch.add(eng, fn, [t.res for t in reads], [t.res for t in writes], dma)

    def dma(self, out_ap, in_ap, reads, writes, q="sp"):
        return self.op(q, lambda e: e.dma_start(out=out_ap, in_=in_ap), reads, writes, dma=True)

    def mm(self, out_ap, lhsT, rhs, start, stop, reads, writes, skip=False):
        if skip:
            return self.op("pe", lambda e: e.matmul(out_ap, lhsT=lhsT, rhs=rhs, start=start, stop=stop,
                                                    skip_group_check=True), reads, writes)
        return self.op("pe", lambda e: e.matmul(out_ap, lhsT=lhsT, rhs=rhs, start=start, stop=stop), reads, writes)

    def act(self, out_ap, in_ap, func, reads, writes, scale=1.0, bias=0.0):
        return self.op("act", lambda e: e.activation(out=out_ap, in_=in_ap, func=func, bias=bias, scale=scale),
                       reads, writes)

    def tt(self, eng, out_ap, a, b, op, reads, writes):
        return self.op(eng, lambda e: e.tensor_tensor(out_ap, a, b, op), reads, writes)

    def ts(self, eng, out_ap, a, s1, s2, op0, op1, reads, writes):
        if op1 is None:
            return self.op(eng, lambda e: e.tensor_scalar(out_ap, a, s1, None, op0), reads, writes)
        return self.op(eng, lambda e: e.tensor_scalar(out_ap, a, s1, s2, op0, op1), reads, writes)

    def copy(self, eng, out_ap, in_ap, reads, writes):
        if eng == "act":
            return self.op("act", lambda e: e.copy(out_ap, in_ap), reads, writes)
        return self.op(eng, lambda e: e.tensor_copy(out_ap, in_ap), reads, writes)

    def memset(self, eng, ap, val, writes):
        return self.op(eng, lambda e: e.memset(ap, val), (), writes)

    def load_w(self, wd, n, wb, stage=None, c0=0):
        src = wd.ap.rearrange("(kc p) n -> p kc n", p=128)
        for k2 in range(0, KC, 2):
            self.dma(wb.ap[:, k2:k2 + 2, c0:c0 + n], src[:, k2:k2 + 2, :], [wd], [wb], q="pool")

    def rope_tables(self, pos_d, a, w, posi, posf, t1, cosT, sinS, invf, coefp):
        SC = 2 * PI * (1.0 - 1e-6)
        self.dma(posi.ap[:, 0:w], pos_d.ap[0:1, a:a + w].partition_broadcast(128), [pos_d], [posi])
        self.copy("act", posf.ap[:, 0:w], posi.ap[:, 0:w], [posi], [posf])
        self.ts("dve", posf.ap[:, 0:w], posf.ap[:, 0:w], invf.ap[:, 0:1], None, ALU.mult, None, [posf, invf], [posf])
        for (dst, off) in ((sinS, 0.0), (cosT, 0.25)):
            if off:
                self.ts("dve", posf.ap[:, 0:w], posf.ap[:, 0:w], off, None, ALU.add, None, [posf], [posf])
            self.copy("dve", posi.ap[:, 0:w], posf.ap[:, 0:w], [posf], [posi])
            self.copy("act", t1.ap[:, 0:w], posi.ap[:, 0:w], [posi], [t1])
            self.op("dve", lambda e: e.scalar_tensor_tensor(t1.ap[:, 0:w], t1.ap[:, 0:w], -1.0, posf.ap[:, 0:w],
                                                            ALU.mult, ALU.add), [t1, posf], [t1])
            self.act(dst.ap[:, 0:w], t1.ap[:, 0:w], AF.Sin, [t1], [dst], scale=SC)
        self.ts("dve", sinS.ap[:, 0:w], sinS.ap[:, 0:w], coefp.ap[:, 0:1], None, ALU.mult, None, [sinS, coefp], [sinS])

    def proj_fm(self, ps, wb, c0, xb, w, m=128):
        for kc in range(KC):
            self.mm(ps.ap[0:m, 0:w], wb.ap[:, kc, c0:c0 + m], xb.ap[:, kc, 0:w], kc == 0, kc == KC - 1, [wb, xb], [ps])


    def nextbank4(self):
        b = self.psum[self.b4[0] % 4]
        self.b4[0] += 1
        return b

    def softmax_norm(self, acc, W, add_row=None):
        rd, bcs, ones_f = self.rd, self.bcs, self.ones_f
        k = self.b4[0] % 2
        cols = slice(k * 512, k * 512 + W)
        if add_row is not None:
            self.tt("dve", rd.ap[64:65, cols], acc.ap[64:65, 0:W], add_row[0], ALU.add, [acc, add_row[1]], [rd])
        else:
            self.ts("dve", rd.ap[64:65, cols], acc.ap[64:65, 0:W], 1e-18, None, ALU.add, None, [acc], [rd])
        bc = self.nextbank4()
        self.mm(bc.ap[0:64, 0:W], ones_f.ap[64:65, 0:64], rd.ap[64:65, cols], True, True, [ones_f, rd], [bc])
        self.act(bcs.ap[0:64, cols], bc.ap[0:64, 0:W], AF.Ln, [bc], [bcs])
        self.act(bcs.ap[0:64, cols], bcs.ap[0:64, cols], AF.Exp, [bcs], [bcs], scale=-1.0)
        return cols

    def mem_units(self, mkT, mv, mqT, mixT, PT, ea, tq, zc, accc, groups):
        B = self
        PS = self.psum
        units = []
        for hm in range(4):
            cg, po = hm // 2, (hm % 2) * 64
            for (col0, W) in groups:
                acc = PS[4 + accc[0] % 4]
                accc[0] += 1
                gsl = slice(col0, col0 + W)
                for mb in range(2):
                    zi = zc[0]
                    zc[0] += 1
                    Sb = B.nextbank4()
                    PTt = PT[zi % len(PT)]

                    def s0(Sb=Sb, PTt=PTt, mb=mb, cg=cg, po=po, gsl=gsl, W=W):
                        B.mm(Sb.ap[:, 0:W], mkT.ap[po:po + 64, cg, mb * 128:(mb + 1) * 128], mqT.ap[po:po + 64, cg, gsl],
                             True, True, [mkT, mqT], [Sb])
                        B.act(PTt.ap[:, 0:W], Sb.ap[:, 0:W], AF.Exp, [Sb], [PTt], scale=0.125)

                    def s1(acc=acc, PTt=PTt, mb=mb, hm=hm, cg=cg, po=po, gsl=gsl, W=W):
                        B.mm(acc.ap[0:65, 0:W], mv.ap[:, mb, hm, 0:65], PTt.ap[:, 0:W], mb == 0, mb == 1, [mv, PTt], [acc], skip=True)
                        if mb == 1:
                            bcol = B.softmax_norm(acc, W)
                            B.tt("dve", ea.ap[0:64, 0:W], acc.ap[0:64, 0:W], B.bcs.ap[0:64, bcol], ALU.mult, [acc, B.bcs], [ea])
                            B.copy("act", tq.ap[po:po + 64, 0:W], ea.ap[0:64, 0:W], [ea], [tq])
                            B.tt("dve", mixT.ap[po:po + 64, 6 + cg, gsl], tq.ap[po:po + 64, 0:W], mixT.ap[po:po + 64, 6 + cg, gsl],
                                 ALU.mult, [tq, mixT], [mixT])

                    units.append([s0, s1])
        return units

    def mem_kv(self, memT, w_mkv, wb, wst, xs0, memb, mkT, mv, nextbank):
        B = self
        B.load_w(w_mkv, 512, wb, wst)
        B.dma(xs0.ap[:, :, 0:MEM_LEN], memT.ap.rearrange("(kc p) m -> p kc m", p=128), [memT], [xs0])
        B.copy("dve", memb.ap[:, :, 0:MEM_LEN], xs0.ap[:, :, 0:MEM_LEN], [xs0], [memb])
        for cg in range(2):
            ps = nextbank()
            B.proj_fm(ps, wb, cg * 128, memb, MEM_LEN)
            B.copy("act", mkT.ap[:, cg, :], ps.ap[:, 0:MEM_LEN], [ps], [mkT])
        B.memset("pool", mv.ap, 1.0, [mv])
        for mb in range(2):
            ps = nextbank()
            for kc in range(KC):
                B.mm(ps.ap[:, 0:256], memb.ap[:, kc, mb * 128:(mb + 1) * 128], wb.ap[:, kc, 256:512], kc == 0, kc == KC - 1,
                     [memb, wb], [ps])
            B.copy("act", mv.ap[:, mb, :, 0:64], ps.ap[:, 0:256].rearrange("p (h d) -> p h d", h=4), [ps], [mv])

    def out_block(self, j, mixT, wo, x_src, xr_t, vv_t, st, gbc, bbc, y_dst, nextbank):
        B = self
        B.dma(xr_t.ap, x_src.ap[j * 128:(j + 1) * 128, :], [x_src], [xr_t])
        for n in range(2):
            ps = nextbank()
            for kc in range(KC):
                B.mm(ps.ap[:, 0:512], mixT.ap[:, kc, j * 128:(j + 1) * 128], wo.ap[:, kc, n * 512:(n + 1) * 512],
                     kc == 0, kc == KC - 1, [mixT, wo], [ps])
            B.op("dve", lambda e, ps=ps, n=n: e.scalar_tensor_tensor(
                vv_t.ap[:, n * 512:(n + 1) * 512], xr_t.ap[:, n * 512:(n + 1) * 512], ALPHA, ps.ap[:, 0:512],
                ALU.mult, ALU.add), [ps, xr_t], [vv_t])
        return layer_norm_store(B, vv_t, xr_t, st, gbc, bbc, y_dst, j)

    def rope_evac(self, psK, psP, out_ap, w, cosT, sinS, tmpa, tmpb, out_t, scale=None):
        self.tt("dve", tmpa.ap[:, 0:w], psK.ap[:, 0:w], cosT.ap[:, 0:w], ALU.mult, [psK, cosT], [tmpa])
        self.tt("dve", tmpb.ap[:, 0:w], psP.ap[:, 0:w], sinS.ap[:, 0:w], ALU.mult, [psP, sinS], [tmpb])
        self.tt("pool", out_ap, tmpa.ap[:, 0:w], tmpb.ap[:, 0:w], ALU.add, [tmpa, tmpb], [out_t])


def _host_consts(layer):
    if layer == 0:
        hd, rot = 32, 8
    else:
        hd, rot = 64, 16
    half = rot // 2
    inv = np.exp(-(np.arange(half, dtype=np.float32) / half) * math.log(ROPE_THETA)).astype(np.float32)
    invf = np.zeros((128, 1), np.float32)
    coef = np.zeros((128, 1), np.float32)
    for r in range(128):
        d = r % hd
        if d < rot:
            invf[r, 0] = inv[d % half] / np.float32(2 * PI)
            coef[r, 0] = -1.0 if d < half else 1.0
    return invf, coef


def _partner_perm(ncols, hd, rot):
    half = rot // 2
    perm = np.arange(ncols)
    for c in range(ncols):
        d = c % hd
        if d < half:
            perm[c] = c + half
        elif d < rot:
            perm[c] = c - half
    return perm


def _masks(c):
    k = np.arange(128)[:, None]
    q = np.arange(128)[None, :]
    out = np.zeros((9, 128, 128), np.float32)
    out[8] = -1.0 * (k >= q)
    for r in range(4):
        if r < c:
            out[r] = 1.0
            out[4 + r] = 1.0
        elif r == c:
            out[r] = (k <= q)
            out[4 + r] = (k < q)
    return out.astype(ml_dtypes.bfloat16)


def run_pipeline_chains(chains, skew):
    n = len(chains[0])
    depth = max(skew) + 1
    for step in range(n + depth - 1):
        for s, d in enumerate(skew):
            u = step - d
            if 0 <= u < n:
                for ch in chains:
                    if ch[u][s] is not None:
                        ch[u][s]()


def run_pipeline(units, nst):
    n = len(units)
    for step in range(n + nst - 1):
        for s in range(nst):
            u = step - s
            if 0 <= u < n and units[u][s] is not None:
                units[u][s]()


def build_l0(S, lambda_init):
    B = Builder(S, 0)
    outs = l0_body(B, lambda_init, False)
    B.sch.emit(final_wait_ops=outs)
    return B


def l0_body(B, lambda_init, fused):
    nc = B.nc
    S = B.S
    NB, NJ, NO, QGB, GW, NQG, TW, NT = B.NB, B.NJ, B.NO, B.QGB, B.GW, B.NQG, B.TW, B.NT
    NR, NE, NEB = B.NR, B.NE, B.NEB
    etiles = [(a, min(512, NE - a)) for a in range(0, NE, 512)]
    xT_all = B.din("xT_all", [D, S])
    xT_own = B.din("xT_ext", [D, NE])
    x_own = B.din("x_ext", [NE, D])
    pos_all = B.din("pos_all", [1, S], I32)
    pos_own = B.din("pos_ext", [1, NE], I32)
    w_k = B.din("w_k", [D, 1152])
    w_v = B.din("w_v", [D, 768])
    w_q = B.din("w_q", [D, 1408])
    w_g = B.din("w_g", [D, 1024])
    memT = B.din("memT", [D, MEM_LEN])
    w_mkv = B.din("w_mkv", [D, 512])
    w_o = B.din("w_o", [D, D])
    ln_g = B.din("ln_g", [1, D])
    ln_b = B.din("ln_b", [1, D])
    dlam = B.din("dlam", [1, 128])
    subln = B.din("subln", [128, 1])
    invf_d = B.din("invf", [128, 1])
    coef_d = B.din("coefp", [128, 1])
    masks_d = B.din("masks", [128, 9 * 128], BF16)
    qcol_d = B.din("qcol", [128, 1024])
    stab_d = B.din("stab", [128, 16 + 64])
    y_out = B.dscr("x1_ext", [NE, D], F32)
    B.x1_ext_d = y_out
    kT_scr = B.dscr("kT_scr", [128, 6, S], BF16)
    v_scr = B.dscr("v_scr", [128, 12, NB * 65], BF16)

    masks = B.tile([128, 9, 128], BF16, "masks")
    negtri = T(masks.ap[:, 8, :], "negtri")
    negtri.res = masks.res
    negones = B.tile([128, 128], BF16, "negones")
    ones_f = B.tile([128, 128], F32, "ones_f")
    invf = B.tile([128, 1], F32, "invf")
    coefp = B.tile([128, 1], F32, "coefp")
    B.pi_col = B.tile([128, 1], F32, "pi")
    g1col = B.tile([128, 1], F32, "g1col")
    lam_t = B.tile([128, 128], F32, "lam")
    lamw = B.tile([128, 8], F32, "lamw")
    nlcol = B.tile([128, 2], F32, "nlcol")
    B.eps_col = B.tile([128, 1], F32, "eps")
    prod = B.tile([128, 64], F32, "prod")
    qcol = B.tile([128, 1024], F32, "qcol")
    stab = B.tile([128, 80], F32, "stab")
    B.mark_consts = B.sa.mark()
    qT_sb = B.tile([128, 3, NE], BF16, "qT_sb")
    qT_df = B.tile([128, 3, NE], BF16, "qT_df")
    mqT = B.tile([128, 2, NE], BF16, "mqT")
    mixT = B.tile([128, 8, NE], BF16, "mixT")
    mkT = B.tile([128, 2, MEM_LEN], BF16, "mkT")
    mv = B.tile([128, 2, 4, 65], BF16, "mv")
    PS = B.psum
    pbc = [0]

    def nextbank():
        b = PS[pbc[0] % 8]
        pbc[0] += 1
        return b

    B.dma(masks.ap.rearrange("p a b -> p (a b)"), masks_d.ap[:, :], [masks_d], [masks])
    B.dma(invf.ap, invf_d.ap[:, :], [invf_d], [invf])
    B.dma(coefp.ap, coef_d.ap[:, :], [coef_d], [coefp])
    B.dma(qcol.ap, qcol_d.ap[:, :], [qcol_d], [qcol])
    B.dma(stab.ap, stab_d.ap[:, :], [stab_d], [stab])
    B.memset("pool", B.pi_col.ap, PI, [B.pi_col])
    B.memset("pool", B.eps_col.ap, LN_EPS, [B.eps_col])
    B.memset("pool", ones_f.ap, 1.0, [ones_f])
    B.memset("pool", negones.ap, -1.0, [negones])
    B.dma(lam_t.ap[64:65, 0:128], dlam.ap[0:1, :], [dlam], [lam_t])
    B.tt("dve", prod.ap[64:65, 0:32], lam_t.ap[64:65, 0:32], lam_t.ap[64:65, 32:64], ALU.mult, [lam_t], [prod])
    B.tt("dve", prod.ap[64:65, 32:64], lam_t.ap[64:65, 64:96], lam_t.ap[64:65, 96:128], ALU.mult, [lam_t], [prod])
    B.op("dve", lambda e: e.tensor_reduce(lamw.ap[64:65, 0:1], prod.ap[64:65, 0:32], AX.X, ALU.add), [prod], [lamw])
    B.op("dve", lambda e: e.tensor_reduce(lamw.ap[64:65, 1:2], prod.ap[64:65, 32:64], AX.X, ALU.add), [prod], [lamw])
    B.act(lamw.ap[64:65, 2:4], lamw.ap[64:65, 0:2], AF.Exp, [lamw], [lamw])
    B.tt("dve", lamw.ap[64:65, 4:5], lamw.ap[64:65, 2:3], lamw.ap[64:65, 3:4], ALU.subtract, [lamw], [lamw])
    B.ts("dve", lamw.ap[64:65, 5:6], lamw.ap[64:65, 4:5], lambda_init, -1.0, ALU.add, ALU.mult, [lamw], [lamw])
    nlps = nextbank()
    B.mm(nlps.ap[0:64, 0:2], ones_f.ap[64:65, 0:64], lamw.ap[64:65, 4:6], True, True, [ones_f, lamw], [nlps])
    B.copy("dve", nlcol.ap[0:64, 0:2], nlps.ap[0:64, 0:2], [nlps], [nlcol])
    B.dma(g1col.ap, subln.ap[:, :], [subln], [g1col])
    B.ts("dve", g1col.ap, g1col.ap, 1.0 - lambda_init, None, ALU.mult, None, [g1col], [g1col])

    mA = B.sa.mark()
    wb = B.tile([128, KC, 1920], BF16, "wb")
    wst = [B.tile([128, KC, 64], F32, "wst%d" % i) for i in range(2)]
    xs = [B.tile([128, KC, TW], F32, "xs%d" % i) for i in range(2)]
    xb = [B.tile([128, KC, TW], BF16, "xb%d" % i) for i in range(2)]
    posi = B.tile([128, TW], I32, "posi")
    posf = B.tile([128, TW], F32, "posf")
    t1 = B.tile([128, TW], F32, "t1")
    cosT = B.tile([128, TW], F32, "cosT")
    sinS = B.tile([128, TW], F32, "sinS")
    tmpa = B.tile([128, TW], F32, "tmpa")
    tmpb = B.tile([128, TW], F32, "tmpb")
    ktst = [B.tile([128, 6, TW], BF16, "ktst%d" % i) for i in range(1)]
    vst = [B.tile([128, 12, TW // 128, 65], BF16, "vst%d" % i) for i in range(2)]

    B.mem_kv(memT, w_mkv, wb, wst, xs[0], xb[0], mkT, mv, nextbank)

    B.load_w(w_k, 1152, wb, wst, 0)
    B.load_w(w_v, 768, wb, wst, 1152)
    for i in range(2):
        B.memset("pool", vst[i].ap, 1.0, [vst[i]])
    xsrc = xT_all.ap.rearrange("(kc p) s -> p kc s", p=128)
    for t in range(NT):
        xs_t, xb_t = xs[t % 2], xb[t % 2]
        kt_t, v_t = ktst[0], vst[t % 2]
        for hlf in range(2):
            B.dma(xs_t.ap[:, hlf * 4:(hlf + 1) * 4, :], xsrc[:, hlf * 4:(hlf + 1) * 4, t * TW:(t + 1) * TW], [xT_all], [xs_t])
        for kc in range(KC):
            B.copy(("act", "dve", "act", "dve", "pool", "act", "dve", "pool")[kc], xb_t.ap[:, kc, :], xs_t.ap[:, kc, :], [xs_t], [xb_t])
        B.rope_tables(pos_all, t * TW, TW, posi, posf, t1, cosT, sinS, invf, coefp)
        for cg in range(3):
            ps = nextbank()
            B.proj_fm(ps, wb, cg * 128, xb_t, TW)
            B.copy("act", kt_t.ap[:, cg, :], ps.ap[:, 0:TW], [ps], [kt_t])
        for cg in range(3):
            psK = nextbank()
            B.proj_fm(psK, wb, 384 + cg * 128, xb_t, TW)
            psP = nextbank()
            B.proj_fm(psP, wb, 768 + cg * 128, xb_t, TW)
            B.rope_evac(psK, psP, kt_t.ap[:, 3 + cg, :], TW, cosT, sinS, tmpa, tmpb, kt_t)
        B.dma(kT_scr.ap[:, :, t * TW:(t + 1) * TW], kt_t.ap, [kt_t], [kT_scr])
        for blk in range(TW // 128):
            for half in range(2):
                ps = nextbank()
                for kc in range(KC):
                    B.mm(ps.ap[:, 0:384], xb_t.ap[:, kc, blk * 128:(blk + 1) * 128],
                         wb.ap[:, kc, 1152 + half * 384:1152 + (half + 1) * 384], kc == 0, kc == KC - 1, [xb_t, wb], [ps])
                B.copy("act" if half == 0 else "dve", v_t.ap[:, half * 6:(half + 1) * 6, blk, 0:64],
                       ps.ap[:, 0:384].rearrange("p (h d) -> p h d", h=6), [ps], [v_t])
        nb_t = TW // 128
        B.dma(v_scr.ap[:, :, t * nb_t * 65:(t + 1) * nb_t * 65], v_t.ap.rearrange("p h b c -> p h (b c)"), [v_t], [v_scr])

    xosrc = xT_own.ap.rearrange("(kc p) s -> p kc s", p=128)
    for rnd in range(2):
        if rnd == 0:
            B.load_w(w_q, 1408, wb, wst, 0)
        else:
            B.load_w(w_g, 1024, wb, wst, 0)
        for u, (ea0, OW) in enumerate(etiles):
            xs_t, xb_t = xs[u % 2], xb[u % 2]
            osl = slice(ea0, ea0 + OW)
            for hlf in range(2):
                B.dma(xs_t.ap[:, hlf * 4:(hlf + 1) * 4, 0:OW], xosrc[:, hlf * 4:(hlf + 1) * 4, osl], [xT_own], [xs_t])
            for kc in range(KC):
                B.copy(("act", "dve", "act", "dve", "pool", "act", "dve", "pool")[kc], xb_t.ap[:, kc, 0:OW], xs_t.ap[:, kc, 0:OW], [xs_t], [xb_t])
            if rnd == 0:
                B.rope_tables(pos_own, ea0, OW, posi, posf, t1, cosT, sinS, invf, coefp)
                for cg in range(3):
                    ps = nextbank()
                    B.proj_fm(ps, wb, cg * 128, xb_t, OW)
                    B.act(qT_sb.ap[:, cg, osl], ps.ap[:, 0:OW], AF.Copy, [ps], [qT_sb], scale=0.125)
                for cg in range(3):
                    psK = nextbank()
                    B.proj_fm(psK, wb, 384 + cg * 128, xb_t, OW)
                    psP = nextbank()
                    B.proj_fm(psP, wb, 768 + cg * 128, xb_t, OW)
                    B.rope_evac(psK, psP, qT_df.ap[:, cg, osl], OW, cosT, sinS, tmpa, tmpb, qT_df)
                for cg in range(2):
                    ps = nextbank()
                    B.proj_fm(ps, wb, 1152 + cg * 128, xb_t, OW)
                    B.copy("act", mqT.ap[:, cg, osl], ps.ap[:, 0:OW], [ps], [mqT])
            else:
                for cg in range(8):
                    ps = nextbank()
                    B.proj_fm(ps, wb, cg * 128, xb_t, OW)
                    B.act(mixT.ap[:, cg, osl], ps.ap[:, 0:OW], AF.Silu, [ps], [mixT])

    phaseA_tiles = [wb] + wst + xs + xb + [posi, posf, t1, cosT, sinS, tmpa, tmpb] + ktst + vst
    B.sa.reset(mA)
    kTp = [B.tile([128, S], BF16, "kTp%d" % i) for i in range(2)]
    vtp = [B.tile([128, 2, NB * 65 + 64], BF16, "vtp%d" % i) for i in range(2)]
    E = [B.tile([128, GW], F32, "E%d" % i) for i in range(4)]
    SP = [B.tile([128, GW], BF16, "SP%d" % i) for i in range(6)]
    PT = [B.tile([128, GW], BF16, "PT%d" % i) for i in range(8)]
    R32s = [B.tile([128, GW], F32, "R32_%d" % i) for i in range(2)]
    Rbs = [[B.tile([128, GW], BF16, "Rb%d_%d" % (ch, i)) for i in range(2)] for ch in range(2)]
    tmpo = [B.tile([128, GW], F32, "tmpo%d" % i) for i in range(2)]
    rd = B.tile([128, 2 * GW], F32, "rd")
    bcs = B.tile([128, 2 * GW], F32, "bcs")
    ea = B.tile([128, GW], F32, "ea")
    eb = B.tile([128, GW], F32, "eb")
    ec = B.tile([128, GW], F32, "ec")
    fz = B.tile([128, 16], F32, "fz")
    phaseB_tiles = kTp + vtp + E + SP + PT + R32s + Rbs[0] + Rbs[1] + tmpo + [rd, bcs, ea, eb, ec, fz]
    B.op("pool", lambda e: e.memset(fz.ap, 0.0), [], phaseA_tiles + phaseB_tiles)
    for i in range(2):
        B.memset("pool", vtp[i].ap[:, :, NB * 65:NB * 65 + 64], 0.0, [vtp[i]])

    kview = kT_scr.ap
    vview = v_scr.ap

    def load_pair(kcg, vh0, slot):
        B.dma(kTp[slot].ap, kview[:, kcg, :], [kT_scr], [kTp[slot]])
        B.dma(vtp[slot].ap[:, :, 0:NB * 65], vview[:, vh0:vh0 + 2, :], [v_scr], [vtp[slot]])

    zc = [0]
    accc = [0]

    def kb_list(gi):
        out = []
        if gi < NR:
            for kb in range(16 * gi + 15, -1, -1):
                if kb >= 16 * gi:
                    m = kb - 16 * gi
                    out.append((kb, max(0, m - 12) * 128, m))
                else:
                    out.append((kb, 0, None))
            return gi * 512, 512, qcol.ap[:, 0:512], out
        W = NR * 128
        for kb in range(16 * (NR - 1) + 11, -1, -1):
            c0 = ((kb - 11 + 15) // 16) * 128 if kb > 11 else 0
            out.append((kb, c0, 16 + kb))
        return NO, W, qcol.ap[:, 512:512 + W], out

    def mask_op(Pt, qc, cs, midx, strict):
        B.op("dve", lambda e: e.scalar_tensor_tensor(Pt.ap[:, cs], qc[:, cs], stab.ap[:, midx:midx + 1], Pt.ap[:, cs],
                                                     ALU.is_gt if strict else ALU.is_ge, ALU.mult),
             [Pt, qcol, stab], [Pt])

    def sb_units(h, slot):
        cg, po = h // 2, (h % 2) * 64
        ch = h % 2
        R32, Rb = R32s[ch], Rbs[ch]
        kT, vt = kTp[slot], vtp[slot]
        units = []
        ucount = 0
        for gi in range(NR + 1):
            col0, W, qc, kbs = kb_list(gi)
            acc = PS[4 + ch + 2 * (gi % 2)]
            for ui, (kb, c0, midx) in enumerate(kbs):
                first, last = ui == 0, ui == len(kbs) - 1
                zi = 2 * ucount + ch
                ucount += 1
                Z, ARG = PS[zi % 2], PS[2 + zi % 2]
                Et, SPt, PTt = E[zi % 4], SP[zi % 6], PT[zi % 8]
                Rcur, Rnext = Rb[(zi // 2) % 2], Rb[(zi // 2 + 1) % 2]
                kap = kT.ap[po:po + 64, kb * 128:(kb + 1) * 128]
                qap = qT_sb.ap[po:po + 64, cg, col0 + c0:col0 + W]
                cs = slice(c0, W)

                def s0(Z=Z, Et=Et, SPt=SPt, kap=kap, qap=qap, cs=cs, midx=midx, kT=kT, qc=qc):
                    B.mm(Z.ap[:, cs], kap, qap, True, True, [kT, qT_sb], [Z])
                    B.act(Et.ap[:, cs], Z.ap[:, cs], AF.Exp, [Z], [Et])
                    B.act(SPt.ap[:, cs], Et.ap[:, cs], AF.Ln, [Et], [SPt], bias=1.0)
                    if midx is not None:
                        mask_op(SPt, qc, cs, midx, True)

                def s1a(ARG=ARG, kap=kap, qap=qap, cs=cs, kT=kT):
                    B.mm(ARG.ap[:, cs], kap, qap, True, False, [kT, qT_sb], [ARG])

                def s1(ARG=ARG, SPt=SPt, PTt=PTt, cs=cs, midx=midx, first=first, last=last,
                       Rcur=Rcur, Rnext=Rnext, qc=qc, W=W):
                    B.mm(ARG.ap[:, cs], negtri.ap, SPt.ap[:, cs], False, first, [negtri, SPt], [ARG])
                    if not first:
                        B.mm(ARG.ap[:, cs], negones.ap, Rcur.ap[:, cs], False, True, [negones, Rcur], [ARG])
                    if first:
                        B.memset("pool", R32.ap, 0.0, [R32])
                        B.memset("pool", Rcur.ap, 0.0, [Rcur])
                        B.memset("pool", Rnext.ap, 0.0, [Rnext])
                    if not last:
                        B.tt("dve", Rnext.ap[:, cs], R32.ap[:, cs], SPt.ap[:, cs], ALU.add, [R32, SPt], [Rnext])
                        B.tt("dve", R32.ap[:, cs], R32.ap[:, cs], SPt.ap[:, cs], ALU.add, [R32, SPt], [R32])
                    B.act(PTt.ap[:, cs], ARG.ap[:, cs], AF.Exp, [ARG], [PTt])
                    if midx is not None:
                        mask_op(PTt, qc, cs, midx, True)

                def s2(acc=acc, PTt=PTt, cs=cs, kb=kb, first=first, last=last, gi=gi, vt=vt, col0=col0, W=W):
                    B.mm(acc.ap[:, cs], vt.ap[:, h % 2, kb * 65:kb * 65 + 128], PTt.ap[:, cs], first, last, [vt, PTt], [acc], skip=True)
                    if last:
                        tq = tmpo[ch]
                        gsl = slice(col0, col0 + W)
                        B.copy("act", tq.ap[po:po + 64, 0:W], acc.ap[0:64, 0:W], [acc], [tq])
                        B.tt("dve", mixT.ap[po:po + 64, cg, gsl], tq.ap[po:po + 64, 0:W], mixT.ap[po:po + 64, cg, gsl],
                             ALU.mult, [tq, mixT], [mixT])

                units.append([s0, s1a, s1, s2])
        return units

    B.rd, B.bcs, B.ones_f, B.b4, B.lamw = rd, bcs, ones_f, [0], lamw
    softmax_norm = B.softmax_norm
    nextbank4 = B.nextbank4

    def df_units_pair(p, slot):
        kT, vt = kTp[slot], vtp[slot]
        cgk = p
        units = []
        scale = 32 ** -0.5
        ptc = [0]
        for gi in range(NR + 1):
            col0, W, qc, kbs = kb_list(gi)
            accs = [[PS[4 + 2 * hh + cm] for cm in range(2)] for hh in range(2)]
            gsl = slice(col0, col0 + W)
            for ui, (kb, c0, midx) in enumerate(kbs):
                first, last = ui == 0, ui == len(kbs) - 1
                cs = slice(c0, W)
                combos = []
                for hh in range(2):
                    for cm in range(2):
                        r0 = hh * 64 + cm * 32
                        combos.append((hh, cm, nextbank4(), PT[ptc[0] % 8], kT.ap[r0:r0 + 32, kb * 128:(kb + 1) * 128],
                                       qT_df.ap[r0:r0 + 32, cgk, col0 + c0:col0 + W], (r0, 0)))
                        ptc[0] += 1

                def s0(combos=combos, cs=cs, midx=midx, qc=qc):
                    for (hh, cm, Sb, PTt, kap, qap, tp) in combos:
                        B.op("pe", lambda e, Sb=Sb, kap=kap, qap=qap, tp=tp: e.matmul(Sb.ap[:, cs], lhsT=kap, rhs=qap, start=True,
                                                                                     stop=True, tile_position=tp), [kT, qT_df], [Sb])
                    for (hh, cm, Sb, PTt, kap, qap, tp) in combos:
                        B.act(PTt.ap[:, cs], Sb.ap[:, cs], AF.Exp, [Sb], [PTt], scale=scale)
                        if midx is not None:
                            mask_op(PTt, qc, cs, midx, False)

                def s1(combos=combos, cs=cs, kb=kb, first=first, last=last, accs=accs, gsl=gsl, W=W):
                    for (hh, cm, Sb, PTt, kap, qap, tp) in combos:
                        B.mm(accs[hh][cm].ap[:, cs], vt.ap[:, hh, kb * 65:kb * 65 + 128], PTt.ap[:, cs], first, last,
                             [vt, PTt], [accs[hh][cm]], skip=True)
                    if last:
                        for hh in range(2):
                            h = 2 * p + hh
                            po = hh * 64
                            a0, a1 = accs[hh]
                            bc0 = softmax_norm(a0, W)
                            B.tt("dve", ea.ap[0:64, 0:W], a0.ap[0:64, 0:W], bcs.ap[0:64, bc0], ALU.mult, [a0, bcs], [ea])
                            bc1 = softmax_norm(a1, W)
                            B.tt("dve", eb.ap[0:64, 0:W], a1.ap[0:64, 0:W], bcs.ap[0:64, bc1], ALU.mult, [a1, bcs], [eb])
                            B.op("dve", lambda e: e.scalar_tensor_tensor(ea.ap[0:64, 0:W], eb.ap[0:64, 0:W], nlcol.ap[0:64, 1:2],
                                                                         ea.ap[0:64, 0:W], ALU.mult, ALU.add), [ea, eb, nlcol], [ea])
                            B.tt("pool", eb.ap[0:64, 0:W], ea.ap[0:64, 0:W], ea.ap[0:64, 0:W], ALU.mult, [ea], [eb])
                            ss = nextbank4()
                            B.mm(ss.ap[0:64, 0:W], ones_f.ap[0:64, 0:64], eb.ap[0:64, 0:W], True, True, [ones_f, eb], [ss])
                            B.act(ec.ap[0:64, 0:W], ss.ap[0:64, 0:W], AF.Ln, [ss, B.eps_col], [ec], scale=1.0 / 64, bias=B.eps_col.ap[0:64, 0:1])
                            B.act(ec.ap[0:64, 0:W], ec.ap[0:64, 0:W], AF.Exp, [ec], [ec], scale=-0.5)
                            B.tt("dve", ea.ap[0:64, 0:W], ea.ap[0:64, 0:W], ec.ap[0:64, 0:W], ALU.mult, [ea, ec], [ea])
                            tq = tmpo[hh]
                            B.copy("act", tq.ap[po:po + 64, 0:W], ea.ap[0:64, 0:W], [ea], [tq])
                            mcg = 3 + p
                            B.op("dve", lambda e, po=po, tq=tq, mcg=mcg: e.scalar_tensor_tensor(
                                mixT.ap[po:po + 64, mcg, gsl], tq.ap[po:po + 64, 0:W], g1col.ap[po:po + 64, 0:1],
                                mixT.ap[po:po + 64, mcg, gsl], ALU.mult, ALU.mult), [tq, g1col, mixT], [mixT])

                units.append([s0, s1])
        return units

    pairs = [("sb", p) for p in range(3)] + [("df", p) for p in range(3)]
    load_pair(0, 0, 0)
    run_pipeline(B.mem_units(mkT, mv, mqT, mixT, PT, ea, tmpo[1], zc, accc, B.groups), 2)
    for pi, (kind, p) in enumerate(pairs):
        slot = pi % 2
        if pi + 1 < len(pairs):
            k2, p2 = pairs[pi + 1]
            load_pair(p2 if k2 == "sb" else 3 + p2, 2 * p2 if k2 == "sb" else 6 + 2 * p2, (pi + 1) % 2)
        if kind == "sb":
            run_pipeline_chains([sb_units(2 * p, slot), sb_units(2 * p + 1, slot)], [0, 1, 1, 2])
        else:
            run_pipeline(df_units_pair(p, slot), 2)

    B.sa.reset(mA)
    wo = B.tile([128, KC, D], BF16, "wo")
    wst2 = [B.tile([128, KC, 128], F32, "wst2_%d" % i) for i in range(2)]
    gbc = B.tile([128, D], F32, "gbc")
    bbc = B.tile([128, D], F32, "bbc")
    xr = [B.tile([128, D], F32, "xr%d" % i) for i in range(2)]
    vv = [B.tile([128, D], F32, "vv%d" % i) for i in range(2)]
    st = B.tile([128, 8], F32, "st")
    fz2 = B.tile([128, 16], F32, "fz2")
    phaseC_tiles = [wo, gbc, bbc, st, fz2] + wst2 + xr + vv
    B.op("pool", lambda e: e.memset(fz2.ap, 0.0), [], phaseB_tiles + phaseC_tiles)
    B.load_w(w_o, D, wo, wst2)
    B.dma(gbc.ap, ln_g.ap[0:1, :].partition_broadcast(128), [ln_g], [gbc])
    B.dma(bbc.ap, ln_b.ap[0:1, :].partition_broadcast(128), [ln_b], [bbc])
    outs = []
    for j in range(NEB):
        outs.append(B.out_block(j, mixT, wo, x_own, xr[j % 2], vv[j % 2], st, gbc, bbc, y_out, nextbank))
    B.l0_tiles = phaseB_tiles + phaseC_tiles + [qT_sb, qT_df, mqT, mixT, mkT, mv]
    B.eps_ready = True
    return outs


DBG = {}
QCOLS = [0, 4, 1, 5, 2, 6, 3, 7, 8, 8, 9, 9, 10, 10, 11, 11]


def _qpos(hq):
    if hq < 4:
        return hq, 0
    if hq < 8:
        return hq - 4, 64
    return 4 + hq - 8, 0


def l1_body(B):
    nc = B.nc
    S = B.S
    NB, NJ, NO, NR, NE, NEB = B.NB, B.NJ, B.NO, B.NR, B.NE, B.NEB
    PS = B.psum
    etiles = [(a, min(512, NE - a)) for a in range(0, NE, 512)]
    x1_ext = B.x1_ext_d
    pos_ext = B.dram["pos_ext"]
    memT = B.dram["memT"]
    w1_kv = B.din("w1_kv", [D, 960])
    w1_q = B.din("w1_q", [D, 2048])
    w1_g = B.din("w1_g", [D, 1024])
    w1_mkv = B.din("w1_mkv", [D, 512])
    w1_o = B.din("w1_o", [D, D])
    ln1_g = B.din("ln1_g", [1, D])
    ln1_b = B.din("ln1_b", [1, D])
    sinks_x = B.din("sinks_x", [1, 1536])
    invf1_d = B.din("invf1", [128, 1])
    coef1_d = B.din("coefp1", [128, 1])
    masks1_d = B.din("masks1", [128, 3 * 128], BF16)
    ident_d = B.din("ident", [128, 128])
    y_out = B.dout("y", [NO, D])

    pbc = [0]

    def nextbank():
        b = PS[pbc[0] % 8]
        pbc[0] += 1
        return b

    B.sa.reset(B.mark_consts)
    new_tiles = []

    def tl(shape, dt, name):
        t = B.tile(shape, dt, name)
        new_tiles.append(t)
        return t

    ident = tl([128, 128], F32, "ident")
    masks1 = tl([128, 3, 128], BF16, "masks1")
    invf1 = tl([128, 1], F32, "invf1")
    coef1 = tl([128, 1], F32, "coef1")
    qT1 = tl([128, 8, NO], BF16, "qT1")
    kT1 = tl([128, 2, NE], BF16, "kT1")
    V1 = tl([128, NEB, 3, 65], BF16, "V1")
    mqT1 = tl([128, 2, NO], BF16, "mqT1")
    mixT1 = tl([128, 8, NO], BF16, "mixT1")
    mkT1 = tl([128, 2, MEM_LEN], BF16, "mkT1")
    mv1 = tl([128, 2, 4, 65], BF16, "mv1")
    fz = tl([128, 16], F32, "fz1")
    mP = B.sa.mark()
    xT1 = tl([128, KC, NE], BF16, "xT1")
    xin = [tl([128, D], F32, "xin%d" % i) for i in range(2)]
    wb1 = tl([128, KC, 1024], BF16, "wb1")
    wst = [tl([128, KC, 64], F32, "wst1_%d" % i) for i in range(2)]
    posi = tl([128, 512], I32, "posi1")
    posf = tl([128, 512], F32, "posf1")
    t1 = tl([128, 512], F32, "t11")
    cosT = tl([128, 512], F32, "cosT1")
    sinS = tl([128, 512], F32, "sinS1")
    tmpa = tl([128, 512], F32, "tmpa1")
    tmpb = tl([128, 512], F32, "tmpb1")
    xs_m = tl([128, KC, MEM_LEN], F32, "xs_m")
    memb = tl([128, KC, MEM_LEN], BF16, "memb1")
    B.op("pool", lambda e: e.memset(fz.ap, 0.0), [], B.l0_tiles + new_tiles)
    B.dma(ident.ap, ident_d.ap[:, :], [ident_d], [ident])
    B.dma(masks1.ap.rearrange("p a b -> p (a b)"), masks1_d.ap[:, :], [masks1_d], [masks1])
    B.dma(invf1.ap, invf1_d.ap[:, :], [invf1_d], [invf1])
    B.dma(coef1.ap, coef1_d.ap[:, :], [coef1_d], [coef1])

    B.mem_kv(memT, w1_mkv, wb1, wst, xs_m, memb, mkT1, mv1, nextbank)
    B.memset("pool", V1.ap, 1.0, [V1])

    xc = [0]

    def transpose_block(e):
        xin_t = xin[xc[0] % 2]
        xc[0] += 1
        B.dma(xin_t.ap, x1_ext.ap[e * 128:(e + 1) * 128, :], [x1_ext], [xin_t])
        for half in range(2):
            ps = nextbank()
            for q in range(4):
                kc = half * 4 + q
                B.op("pe", lambda en, ps=ps, q=q, kc=kc: en.transpose(ps.ap[:, q * 128:(q + 1) * 128],
                                                                     xin_t.ap[:, kc * 128:(kc + 1) * 128], ident.ap),
                     [xin_t, ident], [ps])
            B.copy("act" if half == 0 else "dve", xT1.ap[:, half * 4:(half + 1) * 4, e * 128:(e + 1) * 128],
                   ps.ap[:, 0:512].rearrange("p (a b) -> p a b", a=4), [ps], [xT1])

    def xtile(a0, w):
        t = T(xT1.ap[:, :, a0:a0 + w], "xo_t")
        t.res = xT1.res
        return t

    B.load_w(w1_kv, 960, wb1, wst, 0)
    for (a0, w) in etiles:
        for e in range(a0 // 128, (a0 + w) // 128):
            transpose_block(e)
        xo_t = xtile(a0, w)
        B.rope_tables(pos_ext, a0, w, posi, posf, t1, cosT, sinS, invf1, coef1)
        for cg in range(2):
            psK = nextbank()
            B.proj_fm(psK, wb1, cg * 128, xo_t, w)
            psP = nextbank()
            B.proj_fm(psP, wb1, 256 + cg * 128, xo_t, w)
            B.rope_evac(psK, psP, kT1.ap[:, cg, a0:a0 + w], w, cosT, sinS, tmpa, tmpb, kT1)
        for blk in range(w // 128):
            e = a0 // 128 + blk
            ps = nextbank()
            for kc in range(KC):
                B.mm(ps.ap[:, 0:192], xo_t.ap[:, kc, blk * 128:(blk + 1) * 128], wb1.ap[:, kc, 512:704],
                     kc == 0, kc == KC - 1, [xo_t, wb1], [ps])
            B.copy("act", V1.ap[:, e, :, 0:64], ps.ap[:, 0:192].rearrange("p (h d) -> p h d", h=3), [ps], [V1])
        if a0 < NO:
            for cg in range(2):
                ps = nextbank()
                B.proj_fm(ps, wb1, 704 + cg * 128, xo_t, w)
                B.copy("act", mqT1.ap[:, cg, a0:a0 + w], ps.ap[:, 0:w], [ps], [mqT1])
    for rnd in range(3):
        if rnd < 2:
            B.load_w(T(w1_q.ap[:, rnd * 1024:(rnd + 1) * 1024], "w1q"), 1024, wb1, wst, 0)
        else:
            B.load_w(w1_g, 1024, wb1, wst, 0)
        for (a0, w) in etiles:
            if a0 >= NO:
                continue
            osl = slice(a0, a0 + w)
            xo_t = xtile(a0, w)
            if rnd < 2:
                B.rope_tables(pos_ext, a0, w, posi, posf, t1, cosT, sinS, invf1, coef1)
                for cg in range(4):
                    psK = nextbank()
                    B.proj_fm(psK, wb1, cg * 128, xo_t, w)
                    psP = nextbank()
                    B.proj_fm(psP, wb1, 512 + cg * 128, xo_t, w)
                    B.rope_evac(psK, psP, qT1.ap[:, rnd * 4 + cg, osl], w, cosT, sinS, tmpa, tmpb, qT1)
            else:
                for cg in range(8):
                    ps = nextbank()
                    B.proj_fm(ps, wb1, cg * 128, xo_t, w)
                    B.act(mixT1.ap[:, cg, osl], ps.ap[:, 0:w], AF.Silu, [ps], [mixT1])

    proj_tiles = [xT1] + xin + [wb1] + wst + [posi, posf, t1, cosT, sinS, tmpa, tmpb, xs_m, memb]
    B.sa.reset(mP)
    esink = B.tile([128, 1536], F32, "esink")
    esraw = B.tile([128, 1536], F32, "esraw")
    PT1 = [B.tile([128, 512], BF16, "PT1_%d" % i) for i in range(4)]
    rd = B.tile([128, 1024], F32, "rd1")
    bcs = B.tile([128, 1024], F32, "bcs1")
    eas = [B.tile([128, 512], F32, "ea1_%d" % i) for i in range(2)]
    ea = eas[0]
    tq = [B.tile([128, 512], F32, "tq1_%d" % i) for i in range(3)]
    wo1 = B.tile([128, KC, D], BF16, "wo1")
    wst2 = [B.tile([128, KC, 128], F32, "wst1b_%d" % i) for i in range(2)]
    gbc = B.tile([128, D], F32, "gbc1")
    bbc = B.tile([128, D], F32, "bbc1")
    xr = [B.tile([128, D], F32, "xr1_%d" % i) for i in range(2)]
    vv = [B.tile([128, D], F32, "vv1_%d" % i) for i in range(2)]
    st = B.tile([128, 8], F32, "st1")
    att_tiles = [esink, esraw] + PT1 + [rd, bcs] + eas + tq + [wo1] + wst2 + [gbc, bbc] + xr + vv + [st]
    B.op("pool", lambda e: e.memset(fz.ap, 0.0), [], proj_tiles + att_tiles + [fz])
    B.rd, B.bcs, B.b4 = rd, bcs, [0]
    B.dma(esraw.ap, sinks_x.ap[0:1, :].partition_broadcast(128), [sinks_x], [esraw])
    B.act(esink.ap, esraw.ap, AF.Exp, [esraw], [esink])
    B.load_w(w1_o, D, wo1, wst2)
    B.dma(gbc.ap, ln1_g.ap[0:1, :].partition_broadcast(128), [ln1_g], [gbc])
    B.dma(bbc.ap, ln1_b.ap[0:1, :].partition_broadcast(128), [ln1_b], [bbc])

    zc, accc = [0], [0]
    run_pipeline(B.mem_units(mkT1, mv1, mqT1, mixT1, PT1, ea, tq[2], zc, accc, [(i * 512, 512) for i in range(NR)]), 2)
    units = []
    for j in range(NJ):
        i_run, r = j // 4, j % 4
        e_prev = j - 1 if r > 0 else NJ + i_run
        jsl = slice(j * 128, (j + 1) * 128)
        for kvh in range(3):
            acc = PS[4 + accc[0] % 4]
            accc[0] += 1
            for bi, e_k in enumerate((e_prev, j)):
                zi = zc[0]
                zc[0] += 1
                Sb = B.nextbank4()
                PTt = PT1[zi % 4]
                mi = 0 if bi == 1 else (2 if j == 0 else 1)
                ksl = slice(e_k * 128, (e_k + 1) * 128)

                def s0(Sb=Sb, PTt=PTt, kvh=kvh, jsl=jsl, ksl=ksl, mi=mi):
                    for g in range(4):
                        hq = kvh * 4 + g
                        cgq, po = _qpos(hq)
                        cgk = 0 if kvh < 2 else 1
                        B.mm(Sb.ap[:, g * 128:(g + 1) * 128], kT1.ap[po:po + 64, cgk, ksl], qT1.ap[po:po + 64, cgq, jsl],
                             True, True, [kT1, qT1], [Sb])
                    B.act(PTt.ap[:, 0:512], Sb.ap[:, 0:512], AF.Exp, [Sb], [PTt], scale=0.125)
                    for g in range(4):
                        B.tt("dve" if g % 2 == 0 else "pool", PTt.ap[:, g * 128:(g + 1) * 128], PTt.ap[:, g * 128:(g + 1) * 128],
                             masks1.ap[:, mi, :], ALU.mult, [PTt, masks1], [PTt])

                def s1(acc=acc, PTt=PTt, kvh=kvh, j=j, jsl=jsl, bi=bi, e_k=e_k):
                    B.mm(acc.ap[0:65, 0:512], V1.ap[:, e_k, kvh, 0:65], PTt.ap[:, 0:512], bi == 0, bi == 1, [V1, PTt], [acc], skip=True)
                    if bi == 1:
                        bcol = B.softmax_norm(acc, 512, add_row=(esink.ap[64:65, kvh * 512:(kvh + 1) * 512], esink))
                        ea = eas[accc[0] % 2]
                        B.tt("dve", ea.ap[0:64, :], acc.ap[0:64, 0:512], bcs.ap[0:64, bcol], ALU.mult, [acc, bcs], [ea])
                        tqt = tq[accc[0] % 2]
                        accc[0] += 1
                        for g in range(4):
                            hq = kvh * 4 + g
                            mcg, po = hq // 2, (hq % 2) * 64
                            gs = slice(g * 128, (g + 1) * 128)
                            B.copy("act", tqt.ap[po:po + 64, gs], ea.ap[0:64, gs], [ea], [tqt])
                            B.tt("dve", mixT1.ap[po:po + 64, mcg, jsl], tqt.ap[po:po + 64, gs],
                                 mixT1.ap[po:po + 64, mcg, jsl], ALU.mult, [tqt, mixT1], [mixT1])

                units.append([s0, s1])
    run_pipeline(units, 2)

    outs = []
    for j in range(NJ):
        outs.append(B.out_block(j, mixT1, wo1, x1_ext, xr[j % 2], vv[j % 2], st, gbc, bbc, y_out, nextbank))
    return outs


def layer_norm_store(B, vv_t, scratch, st, gbc, bbc, y_out, j):
    B.op("dve", lambda e: e.tensor_reduce(st.ap[:, 0:1], vv_t.ap, AX.X, ALU.add), [vv_t], [st])
    B.ts("dve", st.ap[:, 1:2], st.ap[:, 0:1], -1.0 / D, None, ALU.mult, None, [st], [st])
    B.op("act", lambda e: e.activation(out=scratch.ap, in_=vv_t.ap, func=AF.Square, bias=st.ap[:, 1:2], scale=1.0,
                                       accum_out=st.ap[:, 2:3]), [vv_t, st], [scratch, st])
    B.act(st.ap[:, 3:4], st.ap[:, 2:3], AF.Ln, [st, B.eps_col], [st], scale=1.0 / D, bias=B.eps_col.ap[:, 0:1])
    B.act(st.ap[:, 3:4], st.ap[:, 3:4], AF.Exp, [st], [st], scale=-0.5)
    B.ts("dve", vv_t.ap, vv_t.ap, st.ap[:, 1:2], st.ap[:, 3:4], ALU.add, ALU.mult, [vv_t, st], [vv_t])
    B.tt("pool", vv_t.ap, vv_t.ap, gbc.ap, ALU.mult, [vv_t, gbc], [vv_t])
    B.tt("pool", vv_t.ap, vv_t.ap, bbc.ap, ALU.add, [vv_t, bbc], [vv_t])
    return B.dma(y_out.ap[j * 128:(j + 1) * 128, :], vv_t.ap, [vv_t], [y_out])


def _own_idx(S, c):
    NR = S // 2048
    return np.concatenate([np.arange((16 * i + 4 * c) * 128, (16 * i + 4 * c + 4) * 128) for i in range(NR)])


def _halo_idx(S, c):
    NR = S // 2048
    out = []
    for i in range(NR):
        g = 16 * i + 4 * c - 1
        out.append(np.arange(g * 128, (g + 1) * 128) if g >= 0 else np.full(128, -1))
    return np.concatenate(out)


def _mask_dram(c):
    m = _masks(c)
    return np.ascontiguousarray(np.transpose(m, (1, 0, 2)).reshape(128, 9 * 128))


def _mask_tables(c):
    col = np.arange(512, dtype=np.float32)
    qcol = np.zeros((128, 1024), np.float32)
    qcol[:, 0:512] = col[None, :]
    qcol[:, 512:1024] = (col + 1920.0 * np.floor(col / 128.0))[None, :]
    k = np.arange(128, dtype=np.float32)[:, None]
    stab = np.zeros((128, 80), np.float32)
    stab[:, 0:16] = k + 128.0 * np.arange(16, dtype=np.float32)[None, :] - 512.0 * c
    stab[:, 16:80] = k + 128.0 * np.arange(64, dtype=np.float32)[None, :] + 128.0 - 512.0 * c
    return qcol, stab


def _masks1(c):
    k = np.arange(128)[:, None]
    q = np.arange(128)[None, :]
    m = np.zeros((3, 128, 128), np.float32)
    m[0] = (k <= q)
    m[1] = (k > q)
    m[2] = (k > q) if c > 0 else 0.0
    return np.ascontiguousarray(np.transpose(m, (1, 0, 2)).reshape(128, 3 * 128)).astype(ml_dtypes.bfloat16)


def l1_weights(w_in, w_memkv, sinks, w_out, ln_g, ln_b):
    cq, ck, cv = w_in[:, 0:768], w_in[:, 768:960], w_in[:, 960:1152]
    mq, gate = w_in[:, 1152:1408], w_in[:, 1408:2432]
    kd = np.concatenate([ck[:, 0:64], ck[:, 64:128], ck[:, 128:192], ck[:, 128:192]], axis=1)
    pk = _partner_perm(256, 64, 16)
    qr = np.concatenate([cq[:, h * 64:(h + 1) * 64] for h in QCOLS], axis=1)
    pq = _partner_perm(512, 64, 16)
    q0, q1 = qr[:, 0:512], qr[:, 512:1024]
    invf1, coef1 = _host_consts(1)
    return {
        "w1_kv": np.ascontiguousarray(np.concatenate([kd, kd[:, pk], cv, mq], axis=1)),
        "w1_q": np.ascontiguousarray(np.concatenate([q0, q0[:, pq], q1, q1[:, pq]], axis=1)),
        "w1_g": np.ascontiguousarray(gate),
        "w1_mkv": np.ascontiguousarray(w_memkv),
        "w1_o": np.ascontiguousarray(w_out),
        "ln1_g": np.ascontiguousarray(ln_g[None, :]),
        "ln1_b": np.ascontiguousarray(ln_b[None, :]),
        "sinks_x": np.ascontiguousarray(np.repeat(sinks, 128)[None, :]),
        "invf1": invf1, "coefp1": coef1,
        "ident": np.eye(128, dtype=np.float32),
    }


def prep_fused(inp):
    f = lambda a: np.asarray(a)
    x, mem, positions = f(inp["x"]), f(inp["mem"]), f(inp["positions"])
    S = x.shape[1]
    w_in = f(inp["w_in_even"])[0]
    sbq, sbk, sbv = w_in[:, 0:384], w_in[:, 384:768], w_in[:, 768:1152]
    dfq, dfk, dfv = w_in[:, 1152:1536], w_in[:, 1536:1920], w_in[:, 1920:2304]
    mq, gate = w_in[:, 2304:2560], w_in[:, 2560:3584]
    perm = _partner_perm(384, 32, 8)
    invf, coefp = _host_consts(0)
    dsub = f(inp["diff_subln_even"])[0]
    shared = {
        "w_k": np.ascontiguousarray(np.concatenate([sbk, dfk, dfk[:, perm]], axis=1)),
        "w_v": np.ascontiguousarray(np.concatenate([sbv, dfv], axis=1)),
        "w_q": np.ascontiguousarray(np.concatenate([sbq, dfq, dfq[:, perm], mq], axis=1)),
        "w_g": np.ascontiguousarray(gate),
        "w_mkv": np.ascontiguousarray(f(inp["w_memkv_even"])[0]),
        "w_o": np.ascontiguousarray(f(inp["w_out_even"])[0]),
        "ln_g": np.ascontiguousarray(f(inp["ln_g_even"])[0][None, :]),
        "ln_b": np.ascontiguousarray(f(inp["ln_b_even"])[0][None, :]),
        "dlam": np.ascontiguousarray(f(inp["diff_lambda_even"])[0].reshape(1, 128)),
        "subln": np.ascontiguousarray(np.concatenate([dsub, dsub])[:, None]),
        "invf": invf, "coefp": coefp,
    }
    shared.update(l1_weights(f(inp["w_in_odd"])[0], f(inp["w_memkv_odd"])[0], f(inp["sinks_odd"])[0], f(inp["w_out_odd"])[0],
                             f(inp["ln_g_odd"])[0], f(inp["ln_b_odd"])[0]))
    xT = [np.ascontiguousarray(x[b].T) for b in range(x.shape[0])]
    maps = []
    for core in range(8):
        b, c = core // 4, core % 4
        own = _own_idx(S, c)
        hidx = _halo_idx(S, c)
        ext = np.concatenate([own, np.maximum(hidx, 0)])
        qcol, stab = _mask_tables(c)
        m = dict(shared)
        m.update({
            "xT_all": xT[b],
            "xT_ext": np.ascontiguousarray(x[b][ext].T),
            "x_ext": np.ascontiguousarray(x[b][ext]),
            "pos_all": np.ascontiguousarray(positions[b][None, :]).astype(np.int32),
            "pos_ext": np.ascontiguousarray(positions[b][ext][None, :]).astype(np.int32),
            "memT": np.ascontiguousarray(mem[b].T),
            "masks": _mask_dram(c),
            "qcol": qcol, "stab": stab,
            "masks1": _masks1(c),
        })
        maps.append(m)
    return maps


def gather_own(results, key, S, nb=2):
    out = np.zeros((nb, S, D), np.float32)
    for core in range(8):
        b, c = core // 4, core % 4
        out[b, _own_idx(S, c)] = results[core][key]
    return out


def build_fused(S, lambda_init):
    B = Builder(S, 0)
    l0_body(B, lambda_init, True)
    outs = l1_body(B)
    B.sch.emit(final_wait_ops=outs)
    return B


LAMBDA_INIT0 = 0.8 - 0.6 * math.exp(-0.3 * 0)


def kernel(**inputs):
    S = np.asarray(inputs["x"]).shape[1]
    B = build_fused(S, LAMBDA_INIT0)
    maps = prep_fused(inputs)
    res = run_bass_kernel_spmd(B.nc, maps, core_ids=list(range(8)))
    return gather_own(res.results, "y", S)
```

```python
import math
import contextlib
import numpy as np
import ml_dtypes
import concourse.bass as bass
import concourse.mybir as mybir
from concourse.bass_utils import run_bass_kernel_spmd

F32 = mybir.dt.float32
BF16 = mybir.dt.bfloat16
I32 = mybir.dt.int32
AF = mybir.ActivationFunctionType
ALU = mybir.AluOpType
AX = mybir.AxisListType

D = 1024
KC = 8
DEPTH = 2
ALPHA = (2 * DEPTH) ** 0.25
LN_EPS = 1e-5
ROPE_THETA = 500000.0
MEM_LEN = 256
PI = math.pi


class Res:
    __slots__ = ("lw", "rd", "name")

    def __init__(self, name=""):
        self.lw = None
        self.rd = []
        self.name = name


class Sched:
    ENGS = ("pe", "act", "dve", "pool", "sp")
    NSLOT = 14

    def __init__(self, nc):
        self.nc = nc
        self.ops = []

    def add(self, eng, fn, reads=(), writes=(), dma=False):
        idx = len(self.ops)
        deps = set()
        for r in reads:
            if r.lw is not None:
                deps.add(r.lw)
        for w in writes:
            if w.lw is not None:
                deps.add(w.lw)
            deps.update(w.rd)
        for r in reads:
            r.rd.append(idx)
        for w in writes:
            w.lw = idx
            w.rd = []
        deps.discard(idx)
        self.ops.append([eng, fn, deps, dma])
        return idx

    def emit(self, final_wait_ops=()):
        nc = self.nc
        ops = self.ops
        n = len(ops)
        has_dep = [False] * n
        for i, (eng, fn, deps, dma) in enumerate(ops):
            for d in deps:
                if ops[d][0] == "pe" and eng == "pe" and not ops[d][3] and not dma:
                    continue
                has_dep[d] = True
        for d in final_wait_ops:
            has_dep[d] = True
        cnt = {e: 0 for e in self.ENGS}
        dcnt = {e: 0 for e in self.ENGS}
        sig = [None] * n
        for i, (eng, fn, deps, dma) in enumerate(ops):
            if dma:
                k = dcnt[eng]
                dcnt[eng] += 1
                sig[i] = ("d", eng, k % self.NSLOT, 16 * (k // self.NSLOT + 1))
            elif has_dep[i]:
                cnt[eng] += 1
                sig[i] = ("c", eng, cnt[eng])
        engs_used = [e for e in self.ENGS if any(o[0] == e for o in ops)]
        with contextlib.ExitStack() as st:
            csem = {e: st.enter_context(nc.semaphore("c_" + e)) for e in engs_used}
            dsem = {}
            for e in engs_used:
                if dcnt[e] > 0:
                    dsem[e] = [st.enter_context(nc.semaphore("d_%s_%d" % (e, s)))
                               for s in range(min(self.NSLOT, dcnt[e]))]
            block = st.enter_context(nc.Block())
            engobj = {"pe": "tensor", "act": "scalar", "dve": "vector", "pool": "gpsimd", "sp": "sync"}

            def make_stream(ename):
                def stream(eng):
                    waited_c = {}
                    waited_d = {}
                    for i, (e, fn, deps, dma) in enumerate(ops):
                        if e != ename:
                            continue
                        need_c = {}
                        need_d = {}
                        for d in deps:
                            s = sig[d]
                            if s is None:
                                continue
                            if s[0] == "c":
                                if s[1] == "pe" and ename == "pe" and not dma:
                                    continue
                                need_c[s[1]] = max(need_c.get(s[1], 0), s[2])
                            else:
                                key = (s[1], s[2])
                                need_d[key] = max(need_d.get(key, 0), s[3])
                        if dma:
                            s = sig[i]
                            if s[3] > 16:
                                key = (s[1], s[2])
                                need_d[key] = max(need_d.get(key, 0), s[3] - 16)
                        for se, v in need_c.items():
                            if waited_c.get(se, 0) < v:
                                eng.wait_ge(csem[se], v)
                                waited_c[se] = v
                        for key, v in need_d.items():
                            if waited_d.get(key, 0) < v:
                                eng.wait_ge(dsem[key[0]][key[1]], v)
                                waited_d[key] = v
                        ins = fn(eng)
                        s = sig[i]
                        if s is not None:
                            if s[0] == "c":
                                ins.then_inc(csem[ename], 1)
                            else:
                                ins.then_inc(dsem[ename][s[2]], 16)
                    if ename == "sp":
                        for d in final_wait_ops:
                            s = sig[d]
                            if s[0] == "c":
                                eng.wait_ge(csem[s[1]], s[2])
                            else:
                                eng.wait_ge(dsem[s[1]][s[2]], s[3])
                return stream

            for e in engs_used:
                getattr(block, engobj[e])(make_stream(e))


class SbufAlloc:
    def __init__(self, nc, nbytes=207 * 1024):
        self.nc = nc
        self.arena = nc.alloc_sbuf_tensor("arena", [128, nbytes], mybir.dt.uint8)
        self.off = 0
        self.limit = nbytes
        self.peak = 0

    def mark(self):
        return self.off

    def reset(self, m):
        self.off = m

    def tile(self, shape, dtype):
        assert shape[0] == 128
        esz = {F32: 4, BF16: 2, I32: 4}[dtype]
        nel = int(np.prod(shape[1:]))
        nbytes = (esz * nel + 63) // 64 * 64
        off = self.off
        self.off += nbytes
        self.peak = max(self.peak, self.off)
        assert self.off <= self.limit, ("SBUF overflow", self.off)
        ap = self.arena.ap()[:, off:off + esz * nel].bitcast(dtype)
        if len(shape) == 3:
            ap = ap.rearrange("p (a b) -> p a b", a=shape[1])
        elif len(shape) == 4:
            ap = ap.rearrange("p (a b c) -> p a b c", a=shape[1], b=shape[2])
        return ap


class T:
    def __init__(self, ap, name=""):
        self.ap = ap
        self.res = Res(name)


class Builder:
    def __init__(self, S, layer):
        self.S = S
        self.layer = layer
        self.NB = S // 128
        self.NJ = self.NB // 4
        self.NO = self.NJ * 128
        self.NR = self.NB // 16
        self.NE = self.NO + self.NR * 128
        self.NEB = self.NE // 128
        self.QGB = 4
        self.GW = 512
        self.NQG = self.NR
        self.groups = [(i * 512, 512) for i in range(self.NR)] + [(self.NO, self.NR * 128)]
        self.TW = min(512, S)
        self.NT = S // self.TW
        self.nc = bass.Bass("TRN2", target_bir_lowering=False)
        self.sch = Sched(self.nc)
        self.sa = SbufAlloc(self.nc)
        self.dram = {}
        self.psum = [T(self.nc.alloc_psum_tensor("ps%d" % i, [128, 512], F32).ap(), "ps%d" % i) for i in range(8)]

    def din(self, name, shape, dtype=F32):
        t = T(self.nc.dram_tensor(name, list(shape), dtype, kind="ExternalInput").ap(), name)
        self.dram[name] = t
        return t

    def dout(self, name, shape, dtype=F32):
        t = T(self.nc.dram_tensor(name, list(shape), dtype, kind="ExternalOutput").ap(), name)
        self.dram[name] = t
        return t

    def dscr(self, name, shape, dtype):
        t = T(self.nc.dram_tensor(name, list(shape), dtype).ap(), name)
        self.dram[name] = t
        return t

    def tile(self, shape, dtype, name=""):
        return T(self.sa.tile(shape, dtype), name)

    def op(self, eng, fn, reads=(), writes=(), dma=False):
        return self.sch.add(eng, fn, [t.res for t in reads], [t.res for t in writes], dma)

    def dma(self, out_ap, in_ap, reads, writes, q="sp"):
        return self.op(q, lambda e: e.dma_start(out=out_ap, in_=in_ap), reads, writes, dma=True)

    def mm(self, out_ap, lhsT, rhs, start, stop, reads, writes, skip=False):
        if skip:
            return self.op("pe", lambda e: e.matmul(out_ap, lhsT=lhsT, rhs=rhs, start=start, stop=stop,
                                                    skip_group_check=True), reads, writes)
        return self.op("pe", lambda e: e.matmul(out_ap, lhsT=lhsT, rhs=rhs, start=start, stop=stop), reads, writes)

    def act(self, out_ap, in_ap, func, reads, writes, scale=1.0, bias=0.0):
        return self.op("act", lambda e: e.activation(out=out_ap, in_=in_ap, func=func, bias=bias, scale=scale),
                       reads, writes)

    def tt(self, eng, out_ap, a, b, op, reads, writes):
        return self.op(eng, lambda e: e.tensor_tensor(out_ap, a, b, op), reads, writes)

    def ts(self, eng, out_ap, a, s1, s2, op0, op1, reads, writes):
        if op1 is None:
            return self.op(eng, lambda e: e.tensor_scalar(out_ap, a, s1, None, op0), reads, writes)
        return self.op(eng, lambda e: e.tensor_scalar(out_ap, a, s1, s2, op0, op1), reads, writes)

    def copy(self, eng, out_ap, in_ap, reads, writes):
        if eng == "act":
            return self.op("act", lambda e: e.copy(out_ap, in_ap), reads, writes)
        return self.op(eng, lambda e: e.tensor_copy(out_ap, in_ap), reads, writes)

    def memset(self, eng, ap, val, writes):
        return self.op(eng, lambda e: e.memset(ap, val), (), writes)

    def load_w(self, wd, n, wb, stage=None, c0=0):
        src = wd.ap.rearrange("(kc p) n -> p kc n", p=128)
        for k2 in range(0, KC, 2):
            self.dma(wb.ap[:, k2:k2 + 2, c0:c0 + n], src[:, k2:k2 + 2, :], [wd], [wb], q="pool")

    def rope_tables(self, pos_d, a, w, posi, posf, t1, cosT, sinS, invf, coefp):
        SC = 2 * PI * (1.0 - 1e-6)
        self.dma(posi.ap[:, 0:w], pos_d.ap[0:1, a:a + w].partition_broadcast(128), [pos_d], [posi])
        self.copy("act", posf.ap[:, 0:w], posi.ap[:, 0:w], [posi], [posf])
        self.ts("dve", posf.ap[:, 0:w], posf.ap[:, 0:w], invf.ap[:, 0:1], None, ALU.mult, None, [posf, invf], [posf])
        for (dst, off) in ((sinS, 0.0), (cosT, 0.25)):
            if off:
                self.ts("dve", posf.ap[:, 0:w], posf.ap[:, 0:w], off, None, ALU.add, None, [posf], [posf])
            self.copy("dve", posi.ap[:, 0:w], posf.ap[:, 0:w], [posf], [posi])
            self.copy("act", t1.ap[:, 0:w], posi.ap[:, 0:w], [posi], [t1])
            self.op("dve", lambda e: e.scalar_tensor_tensor(t1.ap[:, 0:w], t1.ap[:, 0:w], -1.0, posf.ap[:, 0:w],
                                                            ALU.mult, ALU.add), [t1, posf], [t1])
            self.act(dst.ap[:, 0:w], t1.ap[:, 0:w], AF.Sin, [t1], [dst], scale=SC)
        self.ts("dve", sinS.ap[:, 0:w], sinS.ap[:, 0:w], coefp.ap[:, 0:1], None, ALU.mult, None, [sinS, coefp], [sinS])

    def proj_fm(self, ps, wb, c0, xb, w, m=128):
        for kc in range(KC):
            self.mm(ps.ap[0:m, 0:w], wb.ap[:, kc, c0:c0 + m], xb.ap[:, kc, 0:w], kc == 0, kc == KC - 1, [wb, xb], [ps])


    def nextbank4(self):
        b = self.psum[self.b4[0] % 4]
        self.b4[0] += 1
        return b

    def softmax_norm(self, acc, W, add_row=None):
        rd, bcs, ones_f = self.rd, self.bcs, self.ones_f
        k = self.b4[0] % 2
        cols = slice(k * 512, k * 512 + W)
        if add_row is not None:
            self.tt("dve", rd.ap[64:65, cols], acc.ap[64:65, 0:W], add_row[0], ALU.add, [acc, add_row[1]], [rd])
        else:
            self.ts("dve", rd.ap[64:65, cols], acc.ap[64:65, 0:W], 1e-18, None, ALU.add, None, [acc], [rd])
        bc = self.nextbank4()
        self.mm(bc.ap[0:64, 0:W], ones_f.ap[64:65, 0:64], rd.ap[64:65, cols], True, True, [ones_f, rd], [bc])
        self.act(bcs.ap[0:64, cols], bc.ap[0:64, 0:W], AF.Ln, [bc], [bcs])
        self.act(bcs.ap[0:64, cols], bcs.ap[0:64, cols], AF.Exp, [bcs], [bcs], scale=-1.0)
        return cols

    def mem_units(self, mkT, mv, mqT, mixT, PT, ea, tq, zc, accc, groups):
        B = self
        PS = self.psum
        units = []
        for hm in range(4):
            cg, po = hm // 2, (hm % 2) * 64
            for (col0, W) in groups:
                acc = PS[4 + accc[0] % 4]
                accc[0] += 1
                gsl = slice(col0, col0 + W)
                for mb in range(2):
                    zi = zc[0]
                    zc[0] += 1
                    Sb = B.nextbank4()
                    PTt = PT[zi % len(PT)]

                    def s0(Sb=Sb, PTt=PTt, mb=mb, cg=cg, po=po, gsl=gsl, W=W):
                        B.mm(Sb.ap[:, 0:W], mkT.ap[po:po + 64, cg, mb * 128:(mb + 1) * 128], mqT.ap[po:po + 64, cg, gsl],
                             True, True, [mkT, mqT], [Sb])
                        B.act(PTt.ap[:, 0:W], Sb.ap[:, 0:W], AF.Exp, [Sb], [PTt], scale=0.125)

                    def s1(acc=acc, PTt=PTt, mb=mb, hm=hm, cg=cg, po=po, gsl=gsl, W=W):
                        B.mm(acc.ap[0:65, 0:W], mv.ap[:, mb, hm, 0:65], PTt.ap[:, 0:W], mb == 0, mb == 1, [mv, PTt], [acc], skip=True)
                        if mb == 1:
                            bcol = B.softmax_norm(acc, W)
                            B.tt("dve", ea.ap[0:64, 0:W], acc.ap[0:64, 0:W], B.bcs.ap[0:64, bcol], ALU.mult, [acc, B.bcs], [ea])
                            B.copy("act", tq.ap[po:po + 64, 0:W], ea.ap[0:64, 0:W], [ea], [tq])
                            B.tt("dve", mixT.ap[po:po + 64, 6 + cg, gsl], tq.ap[po:po + 64, 0:W], mixT.ap[po:po + 64, 6 + cg, gsl],
                                 ALU.mult, [tq, mixT], [mixT])

                    units.append([s0, s1])
        return units

    def mem_kv(self, memT, w_mkv, wb, wst, xs0, memb, mkT, mv, nextbank):
        B = self
        B.load_w(w_mkv, 512, wb, wst)
        B.dma(xs0.ap[:, :, 0:MEM_LEN], memT.ap.rearrange("(kc p) m -> p kc m", p=128), [memT], [xs0])
        B.copy("dve", memb.ap[:, :, 0:MEM_LEN], xs0.ap[:, :, 0:MEM_LEN], [xs0], [memb])
        for cg in range(2):
            ps = nextbank()
            B.proj_fm(ps, wb, cg * 128, memb, MEM_LEN)
            B.copy("act", mkT.ap[:, cg, :], ps.ap[:, 0:MEM_LEN], [ps], [mkT])
        B.memset("pool", mv.ap, 1.0, [mv])
        for mb in range(2):
            ps = nextbank()
            for kc in range(KC):
                B.mm(ps.ap[:, 0:256], memb.ap[:, kc, mb * 128:(mb + 1) * 128], wb.ap[:, kc, 256:512], kc == 0, kc == KC - 1,
                     [memb, wb], [ps])
            B.copy("act", mv.ap[:, mb, :, 0:64], ps.ap[:, 0:256].rearrange("p (h d) -> p h d", h=4), [ps], [mv])

    def out_block(self, j, mixT, wo, x_src, xr_t, vv_t, st, gbc, bbc, y_dst, nextbank):
        B = self
        B.dma(xr_t.ap, x_src.ap[j * 128:(j + 1) * 128, :], [x_src], [xr_t])
        for n in range(2):
            ps = nextbank()
            for kc in range(KC):
                B.mm(ps.ap[:, 0:512], mixT.ap[:, kc, j * 128:(j + 1) * 128], wo.ap[:, kc, n * 512:(n + 1) * 512],
                     kc == 0, kc == KC - 1, [mixT, wo], [ps])
            B.op("dve", lambda e, ps=ps, n=n: e.scalar_tensor_tensor(
                vv_t.ap[:, n * 512:(n + 1) * 512], xr_t.ap[:, n * 512:(n + 1) * 512], ALPHA, ps.ap[:, 0:512],
                ALU.mult, ALU.add), [ps, xr_t], [vv_t])
        return layer_norm_store(B, vv_t, xr_t, st, gbc, bbc, y_dst, j)

    def rope_evac(self, psK, psP, out_ap, w, cosT, sinS, tmpa, tmpb, out_t, scale=None):
        self.tt("dve", tmpa.ap[:, 0:w], psK.ap[:, 0:w], cosT.ap[:, 0:w], ALU.mult, [psK, cosT], [tmpa])
        self.tt("dve", tmpb.ap[:, 0:w], psP.ap[:, 0:w], sinS.ap[:, 0:w], ALU.mult, [psP, sinS], [tmpb])
        self.tt("dve", out_ap, tmpa.ap[:, 0:w], tmpb.ap[:, 0:w], ALU.add, [tmpa, tmpb], [out_t])


def _host_consts(layer):
    if layer == 0:
        hd, rot = 32, 8
    else:
        hd, rot = 64, 16
    half = rot // 2
    inv = np.exp(-(np.arange(half, dtype=np.float32) / half) * math.log(ROPE_THETA)).astype(np.float32)
    invf = np.zeros((128, 1), np.float32)
    coef = np.zeros((128, 1), np.float32)
    for r in range(128):
        d = r % hd
        if d < rot:
            invf[r, 0] = inv[d % half] / np.float32(2 * PI)
            coef[r, 0] = -1.0 if d < half else 1.0
    return invf, coef


def _partner_perm(ncols, hd, rot):
    half = rot // 2
    perm = np.arange(ncols)
    for c in range(ncols):
        d = c % hd
        if d < half:
            perm[c] = c + half
        elif d < rot:
            perm[c] = c - half
    return perm


def _masks(c):
    k = np.arange(128)[:, None]
    q = np.arange(128)[None, :]
    out = np.zeros((9, 128, 128), np.float32)
    out[8] = -1.0 * (k >= q)
    for r in range(4):
        if r < c:
            out[r] = 1.0
            out[4 + r] = 1.0
        elif r == c:
            out[r] = (k <= q)
            out[4 + r] = (k < q)
    return out.astype(ml_dtypes.bfloat16)


def run_pipeline_chains(chains, skew):
    n = len(chains[0])
    depth = max(skew) + 1
    for step in range(n + depth - 1):
        for s, d in enumerate(skew):
            u = step - d
            if 0 <= u < n:
                for ch in chains:
                    if ch[u][s] is not None:
                        ch[u][s]()


def run_pipeline(units, nst):
    n = len(units)
    for step in range(n + nst - 1):
        for s in range(nst):
            u = step - s
            if 0 <= u < n and units[u][s] is not None:
                units[u][s]()


def build_l0(S, lambda_init):
    B = Builder(S, 0)
    outs = l0_body(B, lambda_init, False)
    B.sch.emit(final_wait_ops=outs)
    return B


def l0_body(B, lambda_init, fused):
    nc = B.nc
    S = B.S
    NB, NJ, NO, QGB, GW, NQG, TW, NT = B.NB, B.NJ, B.NO, B.QGB, B.GW, B.NQG, B.TW, B.NT
    NR, NE, NEB = B.NR, B.NE, B.NEB
    etiles = [(a, min(512, NE - a)) for a in range(0, NE, 512)]
    xT_all = B.din("xT_all", [D, S])
    xT_own = B.din("xT_ext", [D, NE])
    x_own = B.din("x_ext", [NE, D])
    pos_all = B.din("pos_all", [1, S], I32)
    pos_own = B.din("pos_ext", [1, NE], I32)
    w_k = B.din("w_k", [D, 1152])
    w_v = B.din("w_v", [D, 768])
    w_q = B.din("w_q", [D, 1408])
    w_g = B.din("w_g", [D, 1024])
    memT = B.din("memT", [D, MEM_LEN])
    w_mkv = B.din("w_mkv", [D, 512])
    w_o = B.din("w_o", [D, D])
    ln_g = B.din("ln_g", [1, D])
    ln_b = B.din("ln_b", [1, D])
    dlam = B.din("dlam", [1, 128])
    subln = B.din("subln", [128, 1])
    invf_d = B.din("invf", [128, 1])
    coef_d = B.din("coefp", [128, 1])
    masks_d = B.din("masks", [128, 9 * 128], BF16)
    qcol_d = B.din("qcol", [128, 1024])
    stab_d = B.din("stab", [128, 16 + 64])
    y_out = B.dscr("x1_ext", [NE, D], F32)
    B.x1_ext_d = y_out
    kT_scr = B.dscr("kT_scr", [128, 6, S], BF16)
    v_scr = B.dscr("v_scr", [128, 12, NB * 65], BF16)

    masks = B.tile([128, 9, 128], BF16, "masks")
    negtri = T(masks.ap[:, 8, :], "negtri")
    negtri.res = masks.res
    negones = B.tile([128, 128], BF16, "negones")
    ones_f = B.tile([128, 128], F32, "ones_f")
    invf = B.tile([128, 1], F32, "invf")
    coefp = B.tile([128, 1], F32, "coefp")
    B.pi_col = B.tile([128, 1], F32, "pi")
    g1col = B.tile([128, 1], F32, "g1col")
    lam_t = B.tile([128, 128], F32, "lam")
    lamw = B.tile([128, 8], F32, "lamw")
    nlcol = B.tile([128, 2], F32, "nlcol")
    B.eps_col = B.tile([128, 1], F32, "eps")
    prod = B.tile([128, 64], F32, "prod")
    qcol = B.tile([128, 1024], F32, "qcol")
    stab = B.tile([128, 80], F32, "stab")
    B.mark_consts = B.sa.mark()
    qT_sb = B.tile([128, 3, NE], BF16, "qT_sb")
    qT_df = B.tile([128, 3, NE], BF16, "qT_df")
    mqT = B.tile([128, 2, NE], BF16, "mqT")
    mixT = B.tile([128, 8, NE], BF16, "mixT")
    mkT = B.tile([128, 2, MEM_LEN], BF16, "mkT")
    mv = B.tile([128, 2, 4, 65], BF16, "mv")
    PS = B.psum
    pbc = [0]

    def nextbank():
        b = PS[pbc[0] % 8]
        pbc[0] += 1
        return b

    B.dma(masks.ap.rearrange("p a b -> p (a b)"), masks_d.ap[:, :], [masks_d], [masks])
    B.dma(invf.ap, invf_d.ap[:, :], [invf_d], [invf])
    B.dma(coefp.ap, coef_d.ap[:, :], [coef_d], [coefp])
    B.dma(qcol.ap, qcol_d.ap[:, :], [qcol_d], [qcol])
    B.dma(stab.ap, stab_d.ap[:, :], [stab_d], [stab])
    B.memset("pool", B.pi_col.ap, PI, [B.pi_col])
    B.memset("pool", B.eps_col.ap, LN_EPS, [B.eps_col])
    B.memset("pool", ones_f.ap, 1.0, [ones_f])
    B.memset("pool", negones.ap, -1.0, [negones])
    B.dma(lam_t.ap[64:65, 0:128], dlam.ap[0:1, :], [dlam], [lam_t])
    B.tt("dve", prod.ap[64:65, 0:32], lam_t.ap[64:65, 0:32], lam_t.ap[64:65, 32:64], ALU.mult, [lam_t], [prod])
    B.tt("dve", prod.ap[64:65, 32:64], lam_t.ap[64:65, 64:96], lam_t.ap[64:65, 96:128], ALU.mult, [lam_t], [prod])
    B.op("dve", lambda e: e.tensor_reduce(lamw.ap[64:65, 0:1], prod.ap[64:65, 0:32], AX.X, ALU.add), [prod], [lamw])
    B.op("dve", lambda e: e.tensor_reduce(lamw.ap[64:65, 1:2], prod.ap[64:65, 32:64], AX.X, ALU.add), [prod], [lamw])
    B.act(lamw.ap[64:65, 2:4], lamw.ap[64:65, 0:2], AF.Exp, [lamw], [lamw])
    B.tt("dve", lamw.ap[64:65, 4:5], lamw.ap[64:65, 2:3], lamw.ap[64:65, 3:4], ALU.subtract, [lamw], [lamw])
    B.ts("dve", lamw.ap[64:65, 5:6], lamw.ap[64:65, 4:5], lambda_init, -1.0, ALU.add, ALU.mult, [lamw], [lamw])
    nlps = nextbank()
    B.mm(nlps.ap[0:64, 0:2], ones_f.ap[64:65, 0:64], lamw.ap[64:65, 4:6], True, True, [ones_f, lamw], [nlps])
    B.copy("dve", nlcol.ap[0:64, 0:2], nlps.ap[0:64, 0:2], [nlps], [nlcol])
    B.dma(g1col.ap, subln.ap[:, :], [subln], [g1col])
    B.ts("dve", g1col.ap, g1col.ap, 1.0 - lambda_init, None, ALU.mult, None, [g1col], [g1col])

    mA = B.sa.mark()
    wb = B.tile([128, KC, 1920], BF16, "wb")
    wst = [B.tile([128, KC, 64], F32, "wst%d" % i) for i in range(2)]
    xs = [B.tile([128, KC, TW], F32, "xs%d" % i) for i in range(2)]
    xb = [B.tile([128, KC, TW], BF16, "xb%d" % i) for i in range(2)]
    posi = B.tile([128, TW], I32, "posi")
    posf = B.tile([128, TW], F32, "posf")
    t1 = B.tile([128, TW], F32, "t1")
    cosT = B.tile([128, TW], F32, "cosT")
    sinS = B.tile([128, TW], F32, "sinS")
    tmpa = B.tile([128, TW], F32, "tmpa")
    tmpb = B.tile([128, TW], F32, "tmpb")
    ktst = [B.tile([128, 6, TW], BF16, "ktst%d" % i) for i in range(1)]
    vst = [B.tile([128, 12, TW // 128, 65], BF16, "vst%d" % i) for i in range(2)]

    B.mem_kv(memT, w_mkv, wb, wst, xs[0], xb[0], mkT, mv, nextbank)

    B.load_w(w_k, 1152, wb, wst, 0)
    B.load_w(w_v, 768, wb, wst, 1152)
    for i in range(2):
        B.memset("pool", vst[i].ap, 1.0, [vst[i]])
    xsrc = xT_all.ap.rearrange("(kc p) s -> p kc s", p=128)
    for t in range(NT):
        xs_t, xb_t = xs[t % 2], xb[t % 2]
        kt_t, v_t = ktst[0], vst[t % 2]
        for hlf in range(2):
            B.dma(xs_t.ap[:, hlf * 4:(hlf + 1) * 4, :], xsrc[:, hlf * 4:(hlf + 1) * 4, t * TW:(t + 1) * TW], [xT_all], [xs_t])
        for kc in range(KC):
            B.copy(("act", "dve", "act", "dve", "pool", "act", "dve", "pool")[kc], xb_t.ap[:, kc, :], xs_t.ap[:, kc, :], [xs_t], [xb_t])
        B.rope_tables(pos_all, t * TW, TW, posi, posf, t1, cosT, sinS, invf, coefp)
        for cg in range(3):
            ps = nextbank()
            B.proj_fm(ps, wb, cg * 128, xb_t, TW)
            B.copy("act", kt_t.ap[:, cg, :], ps.ap[:, 0:TW], [ps], [kt_t])
        for cg in range(3):
            psK = nextbank()
            B.proj_fm(psK, wb, 384 + cg * 128, xb_t, TW)
            psP = nextbank()
            B.proj_fm(psP, wb, 768 + cg * 128, xb_t, TW)
            B.rope_evac(psK, psP, kt_t.ap[:, 3 + cg, :], TW, cosT, sinS, tmpa, tmpb, kt_t)
        B.dma(kT_scr.ap[:, :, t * TW:(t + 1) * TW], kt_t.ap, [kt_t], [kT_scr])
        for blk in range(TW // 128):
            for half in range(2):
                ps = nextbank()
                for kc in range(KC):
                    B.mm(ps.ap[:, 0:384], xb_t.ap[:, kc, blk * 128:(blk + 1) * 128],
                         wb.ap[:, kc, 1152 + half * 384:1152 + (half + 1) * 384], kc == 0, kc == KC - 1, [xb_t, wb], [ps])
                B.copy("act" if half == 0 else "dve", v_t.ap[:, half * 6:(half + 1) * 6, blk, 0:64],
                       ps.ap[:, 0:384].rearrange("p (h d) -> p h d", h=6), [ps], [v_t])
        nb_t = TW // 128
        B.dma(v_scr.ap[:, :, t * nb_t * 65:(t + 1) * nb_t * 65], v_t.ap.rearrange("p h b c -> p h (b c)"), [v_t], [v_scr])

    xosrc = xT_own.ap.rearrange("(kc p) s -> p kc s", p=128)
    for rnd in range(2):
        if rnd == 0:
            B.load_w(w_q, 1408, wb, wst, 0)
        else:
            B.load_w(w_g, 1024, wb, wst, 0)
        for u, (ea0, OW) in enumerate(etiles):
            xs_t, xb_t = xs[u % 2], xb[u % 2]
            osl = slice(ea0, ea0 + OW)
            for hlf in range(2):
                B.dma(xs_t.ap[:, hlf * 4:(hlf + 1) * 4, 0:OW], xosrc[:, hlf * 4:(hlf + 1) * 4, osl], [xT_own], [xs_t])
            for kc in range(KC):
                B.copy(("act", "dve", "act", "dve", "pool", "act", "dve", "pool")[kc], xb_t.ap[:, kc, 0:OW], xs_t.ap[:, kc, 0:OW], [xs_t], [xb_t])
            if rnd == 0:
                B.rope_tables(pos_own, ea0, OW, posi, posf, t1, cosT, sinS, invf, coefp)
                for cg in range(3):
                    ps = nextbank()
                    B.proj_fm(ps, wb, cg * 128, xb_t, OW)
                    B.act(qT_sb.ap[:, cg, osl], ps.ap[:, 0:OW], AF.Copy, [ps], [qT_sb], scale=0.125)
                for cg in range(3):
                    psK = nextbank()
                    B.proj_fm(psK, wb, 384 + cg * 128, xb_t, OW)
                    psP = nextbank()
                    B.proj_fm(psP, wb, 768 + cg * 128, xb_t, OW)
                    B.rope_evac(psK, psP, qT_df.ap[:, cg, osl], OW, cosT, sinS, tmpa, tmpb, qT_df)
                for cg in range(2):
                    ps = nextbank()
                    B.proj_fm(ps, wb, 1152 + cg * 128, xb_t, OW)
                    B.copy("act", mqT.ap[:, cg, osl], ps.ap[:, 0:OW], [ps], [mqT])
            else:
                for cg in range(8):
                    ps = nextbank()
                    B.proj_fm(ps, wb, cg * 128, xb_t, OW)
                    B.act(mixT.ap[:, cg, osl], ps.ap[:, 0:OW], AF.Silu, [ps], [mixT])

    phaseA_tiles = [wb] + wst + xs + xb + [posi, posf, t1, cosT, sinS, tmpa, tmpb] + ktst + vst
    B.sa.reset(mA)
    kTp = [B.tile([128, S], BF16, "kTp%d" % i) for i in range(2)]
    vtp = [B.tile([128, 2, NB * 65 + 64], BF16, "vtp%d" % i) for i in range(2)]
    E = [B.tile([128, GW], F32, "E%d" % i) for i in range(4)]
    SP = [B.tile([128, GW], BF16, "SP%d" % i) for i in range(6)]
    PT = [B.tile([128, GW], BF16, "PT%d" % i) for i in range(8)]
    R32s = [B.tile([128, GW], F32, "R32_%d" % i) for i in range(2)]
    Rbs = [[B.tile([128, GW], BF16, "Rb%d_%d" % (ch, i)) for i in range(2)] for ch in range(2)]
    tmpo = [B.tile([128, GW], F32, "tmpo%d" % i) for i in range(2)]
    rd = B.tile([128, 2 * GW], F32, "rd")
    bcs = B.tile([128, 2 * GW], F32, "bcs")
    ea = B.tile([128, GW], F32, "ea")
    eb = B.tile([128, GW], F32, "eb")
    ec = B.tile([128, GW], F32, "ec")
    fz = B.tile([128, 16], F32, "fz")
    phaseB_tiles = kTp + vtp + E + SP + PT + R32s + Rbs[0] + Rbs[1] + tmpo + [rd, bcs, ea, eb, ec, fz]
    B.op("pool", lambda e: e.memset(fz.ap, 0.0), [], phaseA_tiles + phaseB_tiles)
    for i in range(2):
        B.memset("pool", vtp[i].ap[:, :, NB * 65:NB * 65 + 64], 0.0, [vtp[i]])

    kview = kT_scr.ap
    vview = v_scr.ap

    def load_pair(kcg, vh0, slot):
        B.dma(kTp[slot].ap, kview[:, kcg, :], [kT_scr], [kTp[slot]])
        B.dma(vtp[slot].ap[:, :, 0:NB * 65], vview[:, vh0:vh0 + 2, :], [v_scr], [vtp[slot]])

    zc = [0]
    accc = [0]

    def kb_list(gi):
        out = []
        if gi < NR:
            for kb in range(16 * gi + 15, -1, -1):
                if kb >= 16 * gi:
                    m = kb - 16 * gi
                    out.append((kb, max(0, m - 12) * 128, m))
                else:
                    out.append((kb, 0, None))
            return gi * 512, 512, qcol.ap[:, 0:512], out
        W = NR * 128
        for kb in range(16 * (NR - 1) + 11, -1, -1):
            c0 = ((kb - 11 + 15) // 16) * 128 if kb > 11 else 0
            out.append((kb, c0, 16 + kb))
        return NO, W, qcol.ap[:, 512:512 + W], out

    def mask_op(Pt, qc, cs, midx, strict):
        B.op("dve", lambda e: e.scalar_tensor_tensor(Pt.ap[:, cs], qc[:, cs], stab.ap[:, midx:midx + 1], Pt.ap[:, cs],
                                                     ALU.is_gt if strict else ALU.is_ge, ALU.mult),
             [Pt, qcol, stab], [Pt])

    def sb_units(h, slot):
        cg, po = h // 2, (h % 2) * 64
        ch = h % 2
        R32, Rb = R32s[ch], Rbs[ch]
        kT, vt = kTp[slot], vtp[slot]
        units = []
        ucount = 0
        for gi in range(NR + 1):
            col0, W, qc, kbs = kb_list(gi)
            acc = PS[4 + ch + 2 * (gi % 2)]
            for ui, (kb, c0, midx) in enumerate(kbs):
                first, last = ui == 0, ui == len(kbs) - 1
                zi = 2 * ucount + ch
                ucount += 1
                Z, ARG = PS[zi % 2], PS[2 + zi % 2]
                Et, SPt, PTt = E[zi % 4], SP[zi % 6], PT[zi % 8]
                Rcur, Rnext = Rb[(zi // 2) % 2], Rb[(zi // 2 + 1) % 2]
                kap = kT.ap[po:po + 64, kb * 128:(kb + 1) * 128]
                qap = qT_sb.ap[po:po + 64, cg, col0 + c0:col0 + W]
                cs = slice(c0, W)

                def s0(Z=Z, Et=Et, SPt=SPt, kap=kap, qap=qap, cs=cs, midx=midx, kT=kT, qc=qc):
                    B.mm(Z.ap[:, cs], kap, qap, True, True, [kT, qT_sb], [Z])
                    B.act(Et.ap[:, cs], Z.ap[:, cs], AF.Exp, [Z], [Et])
                    B.act(SPt.ap[:, cs], Et.ap[:, cs], AF.Ln, [Et], [SPt], bias=1.0)
                    if midx is not None:
                        mask_op(SPt, qc, cs, midx, True)

                def s1a(ARG=ARG, kap=kap, qap=qap, cs=cs, kT=kT):
                    B.mm(ARG.ap[:, cs], kap, qap, True, False, [kT, qT_sb], [ARG])

                def s1(ARG=ARG, SPt=SPt, PTt=PTt, cs=cs, midx=midx, first=first, last=last,
                       Rcur=Rcur, Rnext=Rnext, qc=qc, W=W):
                    B.mm(ARG.ap[:, cs], negtri.ap, SPt.ap[:, cs], False, first, [negtri, SPt], [ARG])
                    if not first:
                        B.mm(ARG.ap[:, cs], negones.ap, Rcur.ap[:, cs], False, True, [negones, Rcur], [ARG])
                    if first:
                        B.memset("pool", R32.ap, 0.0, [R32])
                        B.memset("pool", Rcur.ap, 0.0, [Rcur])
                        B.memset("pool", Rnext.ap, 0.0, [Rnext])
                    if not last:
                        B.tt("dve", Rnext.ap[:, cs], R32.ap[:, cs], SPt.ap[:, cs], ALU.add, [R32, SPt], [Rnext])
                        B.tt("pool", R32.ap[:, cs], R32.ap[:, cs], SPt.ap[:, cs], ALU.add, [R32, SPt], [R32])
                    B.act(PTt.ap[:, cs], ARG.ap[:, cs], AF.Exp, [ARG], [PTt])
                    if midx is not None:
                        mask_op(PTt, qc, cs, midx, True)

                def s2(acc=acc, PTt=PTt, cs=cs, kb=kb, first=first, last=last, gi=gi, vt=vt, col0=col0, W=W):
                    B.mm(acc.ap[:, cs], vt.ap[:, h % 2, kb * 65:kb * 65 + 128], PTt.ap[:, cs], first, last, [vt, PTt], [acc], skip=True)
                    if last:
                        tq = tmpo[ch]
                        gsl = slice(col0, col0 + W)
                        B.copy("act", tq.ap[po:po + 64, 0:W], acc.ap[0:64, 0:W], [acc], [tq])
                        B.tt("dve", mixT.ap[po:po + 64, cg, gsl], tq.ap[po:po + 64, 0:W], mixT.ap[po:po + 64, cg, gsl],
                             ALU.mult, [tq, mixT], [mixT])

                units.append([s0, s1a, s1, s2])
        return units

    B.rd, B.bcs, B.ones_f, B.b4, B.lamw = rd, bcs, ones_f, [0], lamw
    softmax_norm = B.softmax_norm
    nextbank4 = B.nextbank4

    def df_units_pair(p, slot):
        kT, vt = kTp[slot], vtp[slot]
        cgk = p
        units = []
        scale = 32 ** -0.5
        ptc = [0]
        for gi in range(NR + 1):
            col0, W, qc, kbs = kb_list(gi)
            accs = [[PS[4 + 2 * hh + cm] for cm in range(2)] for hh in range(2)]
            gsl = slice(col0, col0 + W)
            for ui, (kb, c0, midx) in enumerate(kbs):
                first, last = ui == 0, ui == len(kbs) - 1
                cs = slice(c0, W)
                combos = []
                for hh in range(2):
                    for cm in range(2):
                        r0 = hh * 64 + cm * 32
                        combos.append((hh, cm, nextbank4(), PT[ptc[0] % 8], kT.ap[r0:r0 + 32, kb * 128:(kb + 1) * 128],
                                       qT_df.ap[r0:r0 + 32, cgk, col0 + c0:col0 + W], (r0, 0)))
                        ptc[0] += 1

                def s0(combos=combos, cs=cs, midx=midx, qc=qc):
                    for (hh, cm, Sb, PTt, kap, qap, tp) in combos:
                        B.op("pe", lambda e, Sb=Sb, kap=kap, qap=qap, tp=tp: e.matmul(Sb.ap[:, cs], lhsT=kap, rhs=qap, start=True,
                                                                                     stop=True, tile_position=tp), [kT, qT_df], [Sb])
                    for (hh, cm, Sb, PTt, kap, qap, tp) in combos:
                        B.act(PTt.ap[:, cs], Sb.ap[:, cs], AF.Exp, [Sb], [PTt], scale=scale)
                        if midx is not None:
                            mask_op(PTt, qc, cs, midx, False)

                def s1(combos=combos, cs=cs, kb=kb, first=first, last=last, accs=accs, gsl=gsl, W=W):
                    for (hh, cm, Sb, PTt, kap, qap, tp) in combos:
                        B.mm(accs[hh][cm].ap[:, cs], vt.ap[:, hh, kb * 65:kb * 65 + 128], PTt.ap[:, cs], first, last,
                             [vt, PTt], [accs[hh][cm]], skip=True)
                    if last:
                        for hh in range(2):
                            h = 2 * p + hh
                            po = hh * 64
                            a0, a1 = accs[hh]
                            bc0 = softmax_norm(a0, W)
                            B.tt("dve", ea.ap[0:64, 0:W], a0.ap[0:64, 0:W], bcs.ap[0:64, bc0], ALU.mult, [a0, bcs], [ea])
                            bc1 = softmax_norm(a1, W)
                            B.tt("dve", eb.ap[0:64, 0:W], a1.ap[0:64, 0:W], bcs.ap[0:64, bc1], ALU.mult, [a1, bcs], [eb])
                            B.op("dve", lambda e: e.scalar_tensor_tensor(ea.ap[0:64, 0:W], eb.ap[0:64, 0:W], nlcol.ap[0:64, 1:2],
                                                                         ea.ap[0:64, 0:W], ALU.mult, ALU.add), [ea, eb, nlcol], [ea])
                            B.tt("pool", eb.ap[0:64, 0:W], ea.ap[0:64, 0:W], ea.ap[0:64, 0:W], ALU.mult, [ea], [eb])
                            ss = nextbank4()
                            B.mm(ss.ap[0:64, 0:W], ones_f.ap[0:64, 0:64], eb.ap[0:64, 0:W], True, True, [ones_f, eb], [ss])
                            B.act(ec.ap[0:64, 0:W], ss.ap[0:64, 0:W], AF.Ln, [ss, B.eps_col], [ec], scale=1.0 / 64, bias=B.eps_col.ap[0:64, 0:1])
                            B.act(ec.ap[0:64, 0:W], ec.ap[0:64, 0:W], AF.Exp, [ec], [ec], scale=-0.5)
                            B.tt("dve", ea.ap[0:64, 0:W], ea.ap[0:64, 0:W], ec.ap[0:64, 0:W], ALU.mult, [ea, ec], [ea])
                            tq = tmpo[hh]
                            B.copy("act", tq.ap[po:po + 64, 0:W], ea.ap[0:64, 0:W], [ea], [tq])
                            mcg = 3 + p
                            B.op("dve", lambda e, po=po, tq=tq, mcg=mcg: e.scalar_tensor_tensor(
                                mixT.ap[po:po + 64, mcg, gsl], tq.ap[po:po + 64, 0:W], g1col.ap[po:po + 64, 0:1],
                                mixT.ap[po:po + 64, mcg, gsl], ALU.mult, ALU.mult), [tq, g1col, mixT], [mixT])

                units.append([s0, s1])
        return units

    pairs = [("sb", p) for p in range(3)] + [("df", p) for p in range(3)]
    load_pair(0, 0, 0)
    run_pipeline(B.mem_units(mkT, mv, mqT, mixT, PT, ea, tmpo[1], zc, accc, B.groups), 2)
    for pi, (kind, p) in enumerate(pairs):
        slot = pi % 2
        if pi + 1 < len(pairs):
            k2, p2 = pairs[pi + 1]
            load_pair(p2 if k2 == "sb" else 3 + p2, 2 * p2 if k2 == "sb" else 6 + 2 * p2, (pi + 1) % 2)
        if kind == "sb":
            run_pipeline_chains([sb_units(2 * p, slot), sb_units(2 * p + 1, slot)], [0, 1, 1, 2])
        else:
            run_pipeline(df_units_pair(p, slot), 2)

    B.sa.reset(mA)
    wo = B.tile([128, KC, D], BF16, "wo")
    wst2 = [B.tile([128, KC, 128], F32, "wst2_%d" % i) for i in range(2)]
    gbc = B.tile([128, D], F32, "gbc")
    bbc = B.tile([128, D], F32, "bbc")
    xr = [B.tile([128, D], F32, "xr%d" % i) for i in range(3)]
    vv = [B.tile([128, D], F32, "vv%d" % i) for i in range(3)]
    st = B.tile([128, 8], F32, "st")
    fz2 = B.tile([128, 16], F32, "fz2")
    phaseC_tiles = [wo, gbc, bbc, st, fz2] + wst2 + xr + vv
    B.op("pool", lambda e: e.memset(fz2.ap, 0.0), [], phaseB_tiles + phaseC_tiles)
    B.load_w(w_o, D, wo, wst2)
    B.dma(gbc.ap, ln_g.ap[0:1, :].partition_broadcast(128), [ln_g], [gbc])
    B.dma(bbc.ap, ln_b.ap[0:1, :].partition_broadcast(128), [ln_b], [bbc])
    outs = []
    for j in range(NEB):
        outs.append(B.out_block(j, mixT, wo, x_own, xr[j % 3], vv[j % 3], st, gbc, bbc, y_out, nextbank))
    B.l0_tiles = phaseB_tiles + phaseC_tiles + [qT_sb, qT_df, mqT, mixT, mkT, mv]
    B.eps_ready = True
    return outs


DBG = {}
QCOLS = [0, 4, 1, 5, 2, 6, 3, 7, 8, 8, 9, 9, 10, 10, 11, 11]


def _qpos(hq):
    if hq < 4:
        return hq, 0
    if hq < 8:
        return hq - 4, 64
    return 4 + hq - 8, 0


def l1_body(B):
    nc = B.nc
    S = B.S
    NB, NJ, NO, NR, NE, NEB = B.NB, B.NJ, B.NO, B.NR, B.NE, B.NEB
    PS = B.psum
    etiles = [(a, min(512, NE - a)) for a in range(0, NE, 512)]
    x1_ext = B.x1_ext_d
    pos_ext = B.dram["pos_ext"]
    memT = B.dram["memT"]
    w1_kv = B.din("w1_kv", [D, 960])
    w1_q = B.din("w1_q", [D, 2048])
    w1_g = B.din("w1_g", [D, 1024])
    w1_mkv = B.din("w1_mkv", [D, 512])
    w1_o = B.din("w1_o", [D, D])
    ln1_g = B.din("ln1_g", [1, D])
    ln1_b = B.din("ln1_b", [1, D])
    sinks_x = B.din("sinks_x", [1, 1536])
    invf1_d = B.din("invf1", [128, 1])
    coef1_d = B.din("coefp1", [128, 1])
    masks1_d = B.din("masks1", [128, 3 * 128], BF16)
    ident_d = B.din("ident", [128, 128])
    y_out = B.dout("y", [NO, D])

    pbc = [0]

    def nextbank():
        b = PS[pbc[0] % 8]
        pbc[0] += 1
        return b

    B.sa.reset(B.mark_consts)
    new_tiles = []

    def tl(shape, dt, name):
        t = B.tile(shape, dt, name)
        new_tiles.append(t)
        return t

    ident = tl([128, 128], F32, "ident")
    masks1 = tl([128, 3, 128], BF16, "masks1")
    invf1 = tl([128, 1], F32, "invf1")
    coef1 = tl([128, 1], F32, "coef1")
    qT1 = tl([128, 8, NO], BF16, "qT1")
    kT1 = tl([128, 2, NE], BF16, "kT1")
    V1 = tl([128, NEB, 3, 65], BF16, "V1")
    mqT1 = tl([128, 2, NO], BF16, "mqT1")
    mixT1 = tl([128, 8, NO], BF16, "mixT1")
    mkT1 = tl([128, 2, MEM_LEN], BF16, "mkT1")
    mv1 = tl([128, 2, 4, 65], BF16, "mv1")
    fz = tl([128, 16], F32, "fz1")
    mP = B.sa.mark()
    xT1 = tl([128, KC, NE], BF16, "xT1")
    xin = [tl([128, D], F32, "xin%d" % i) for i in range(2)]
    wb1 = tl([128, KC, 1024], BF16, "wb1")
    wst = [tl([128, KC, 64], F32, "wst1_%d" % i) for i in range(2)]
    posi = tl([128, 512], I32, "posi1")
    posf = tl([128, 512], F32, "posf1")
    t1 = tl([128, 512], F32, "t11")
    cosT = tl([128, 512], F32, "cosT1")
    sinS = tl([128, 512], F32, "sinS1")
    tmpa = tl([128, 512], F32, "tmpa1")
    tmpb = tl([128, 512], F32, "tmpb1")
    xs_m = tl([128, KC, MEM_LEN], F32, "xs_m")
    memb = tl([128, KC, MEM_LEN], BF16, "memb1")
    B.op("pool", lambda e: e.memset(fz.ap, 0.0), [], B.l0_tiles + new_tiles)
    B.dma(ident.ap, ident_d.ap[:, :], [ident_d], [ident])
    B.dma(masks1.ap.rearrange("p a b -> p (a b)"), masks1_d.ap[:, :], [masks1_d], [masks1])
    B.dma(invf1.ap, invf1_d.ap[:, :], [invf1_d], [invf1])
    B.dma(coef1.ap, coef1_d.ap[:, :], [coef1_d], [coef1])

    B.mem_kv(memT, w1_mkv, wb1, wst, xs_m, memb, mkT1, mv1, nextbank)
    B.memset("pool", V1.ap, 1.0, [V1])

    xc = [0]

    def transpose_block(e):
        xin_t = xin[xc[0] % 2]
        xc[0] += 1
        B.dma(xin_t.ap, x1_ext.ap[e * 128:(e + 1) * 128, :], [x1_ext], [xin_t])
        for half in range(2):
            ps = nextbank()
            for q in range(4):
                kc = half * 4 + q
                B.op("pe", lambda en, ps=ps, q=q, kc=kc: en.transpose(ps.ap[:, q * 128:(q + 1) * 128],
                                                                     xin_t.ap[:, kc * 128:(kc + 1) * 128], ident.ap),
                     [xin_t, ident], [ps])
            B.copy("act" if half == 0 else "dve", xT1.ap[:, half * 4:(half + 1) * 4, e * 128:(e + 1) * 128],
                   ps.ap[:, 0:512].rearrange("p (a b) -> p a b", a=4), [ps], [xT1])

    def xtile(a0, w):
        t = T(xT1.ap[:, :, a0:a0 + w], "xo_t")
        t.res = xT1.res
        return t

    B.load_w(w1_kv, 960, wb1, wst, 0)
    for (a0, w) in etiles:
        for e in range(a0 // 128, (a0 + w) // 128):
            transpose_block(e)
        xo_t = xtile(a0, w)
        B.rope_tables(pos_ext, a0, w, posi, posf, t1, cosT, sinS, invf1, coef1)
        for cg in range(2):
            psK = nextbank()
            B.proj_fm(psK, wb1, cg * 128, xo_t, w)
            psP = nextbank()
            B.proj_fm(psP, wb1, 256 + cg * 128, xo_t, w)
            B.rope_evac(psK, psP, kT1.ap[:, cg, a0:a0 + w], w, cosT, sinS, tmpa, tmpb, kT1)
        for blk in range(w // 128):
            e = a0 // 128 + blk
            ps = nextbank()
            for kc in range(KC):
                B.mm(ps.ap[:, 0:192], xo_t.ap[:, kc, blk * 128:(blk + 1) * 128], wb1.ap[:, kc, 512:704],
                     kc == 0, kc == KC - 1, [xo_t, wb1], [ps])
            B.copy("act", V1.ap[:, e, :, 0:64], ps.ap[:, 0:192].rearrange("p (h d) -> p h d", h=3), [ps], [V1])
        if a0 < NO:
            for cg in range(2):
                ps = nextbank()
                B.proj_fm(ps, wb1, 704 + cg * 128, xo_t, w)
                B.copy("act", mqT1.ap[:, cg, a0:a0 + w], ps.ap[:, 0:w], [ps], [mqT1])
    for rnd in range(3):
        if rnd < 2:
            B.load_w(T(w1_q.ap[:, rnd * 1024:(rnd + 1) * 1024], "w1q"), 1024, wb1, wst, 0)
        else:
            B.load_w(w1_g, 1024, wb1, wst, 0)
        for (a0, w) in etiles:
            if a0 >= NO:
                continue
            osl = slice(a0, a0 + w)
            xo_t = xtile(a0, w)
            if rnd < 2:
                B.rope_tables(pos_ext, a0, w, posi, posf, t1, cosT, sinS, invf1, coef1)
                for cg in range(4):
                    psK = nextbank()
                    B.proj_fm(psK, wb1, cg * 128, xo_t, w)
                    psP = nextbank()
                    B.proj_fm(psP, wb1, 512 + cg * 128, xo_t, w)
                    B.rope_evac(psK, psP, qT1.ap[:, rnd * 4 + cg, osl], w, cosT, sinS, tmpa, tmpb, qT1)
            else:
                for cg in range(8):
                    ps = nextbank()
                    B.proj_fm(ps, wb1, cg * 128, xo_t, w)
                    B.act(mixT1.ap[:, cg, osl], ps.ap[:, 0:w], AF.Silu, [ps], [mixT1])

    proj_tiles = [xT1] + xin + [wb1] + wst + [posi, posf, t1, cosT, sinS, tmpa, tmpb, xs_m, memb]
    B.sa.reset(mP)
    esink = B.tile([128, 1536], F32, "esink")
    esraw = B.tile([128, 1536], F32, "esraw")
    PT1 = [B.tile([128, 512], BF16, "PT1_%d" % i) for i in range(4)]
    rd = B.tile([128, 1024], F32, "rd1")
    bcs = B.tile([128, 1024], F32, "bcs1")
    eas = [B.tile([128, 512], F32, "ea1_%d" % i) for i in range(2)]
    ea = eas[0]
    tq = [B.tile([128, 512], F32, "tq1_%d" % i) for i in range(3)]
    wo1 = B.tile([128, KC, D], BF16, "wo1")
    wst2 = [B.tile([128, KC, 128], F32, "wst1b_%d" % i) for i in range(2)]
    gbc = B.tile([128, D], F32, "gbc1")
    bbc = B.tile([128, D], F32, "bbc1")
    xr = [B.tile([128, D], F32, "xr1_%d" % i) for i in range(3)]
    vv = [B.tile([128, D], F32, "vv1_%d" % i) for i in range(3)]
    st = B.tile([128, 8], F32, "st1")
    att_tiles = [esink, esraw] + PT1 + [rd, bcs] + eas + tq + [wo1] + wst2 + [gbc, bbc] + xr + vv + [st]
    B.op("pool", lambda e: e.memset(fz.ap, 0.0), [], proj_tiles + att_tiles + [fz])
    B.rd, B.bcs, B.b4 = rd, bcs, [0]
    B.dma(esraw.ap, sinks_x.ap[0:1, :].partition_broadcast(128), [sinks_x], [esraw])
    B.act(esink.ap, esraw.ap, AF.Exp, [esraw], [esink])
    B.load_w(w1_o, D, wo1, wst2)
    B.dma(gbc.ap, ln1_g.ap[0:1, :].partition_broadcast(128), [ln1_g], [gbc])
    B.dma(bbc.ap, ln1_b.ap[0:1, :].partition_broadcast(128), [ln1_b], [bbc])

    zc, accc = [0], [0]
    run_pipeline(B.mem_units(mkT1, mv1, mqT1, mixT1, PT1, ea, tq[2], zc, accc, [(i * 512, 512) for i in range(NR)]), 2)
    units = []
    for j in range(NJ):
        i_run, r = j // 4, j % 4
        e_prev = j - 1 if r > 0 else NJ + i_run
        jsl = slice(j * 128, (j + 1) * 128)
        for kvh in range(3):
            acc = PS[4 + accc[0] % 4]
            accc[0] += 1
            for bi, e_k in enumerate((e_prev, j)):
                zi = zc[0]
                zc[0] += 1
                Sb = B.nextbank4()
                PTt = PT1[zi % 4]
                mi = 0 if bi == 1 else (2 if j == 0 else 1)
                ksl = slice(e_k * 128, (e_k + 1) * 128)

                def s0(Sb=Sb, PTt=PTt, kvh=kvh, jsl=jsl, ksl=ksl, mi=mi):
                    for g in range(4):
                        hq = kvh * 4 + g
                        cgq, po = _qpos(hq)
                        cgk = 0 if kvh < 2 else 1
                        B.mm(Sb.ap[:, g * 128:(g + 1) * 128], kT1.ap[po:po + 64, cgk, ksl], qT1.ap[po:po + 64, cgq, jsl],
                             True, True, [kT1, qT1], [Sb])
                    B.act(PTt.ap[:, 0:512], Sb.ap[:, 0:512], AF.Exp, [Sb], [PTt], scale=0.125)
                    for g in range(4):
                        B.tt("dve", PTt.ap[:, g * 128:(g + 1) * 128], PTt.ap[:, g * 128:(g + 1) * 128],
                             masks1.ap[:, mi, :], ALU.mult, [PTt, masks1], [PTt])

                def s1(acc=acc, PTt=PTt, kvh=kvh, j=j, jsl=jsl, bi=bi, e_k=e_k):
                    B.mm(acc.ap[0:65, 0:512], V1.ap[:, e_k, kvh, 0:65], PTt.ap[:, 0:512], bi == 0, bi == 1, [V1, PTt], [acc], skip=True)

                def s2(acc=acc, kvh=kvh, jsl=jsl):
                        bcol = B.softmax_norm(acc, 512, add_row=(esink.ap[64:65, kvh * 512:(kvh + 1) * 512], esink))
                        ea = eas[accc[0] % 2]
                        B.tt("dve", ea.ap[0:64, :], acc.ap[0:64, 0:512], bcs.ap[0:64, bcol], ALU.mult, [acc, bcs], [ea])
                        tqt = tq[accc[0] % 2]
                        accc[0] += 1
                        for g in range(4):
                            hq = kvh * 4 + g
                            mcg, po = hq // 2, (hq % 2) * 64
                            gs = slice(g * 128, (g + 1) * 128)
                            B.copy("act", tqt.ap[po:po + 64, gs], ea.ap[0:64, gs], [ea], [tqt])
                            B.tt("dve", mixT1.ap[po:po + 64, mcg, jsl], tqt.ap[po:po + 64, gs],
                                 mixT1.ap[po:po + 64, mcg, jsl], ALU.mult, [tqt, mixT1], [mixT1])

                units.append([s0, s1, s2 if bi == 1 else None])
    run_pipeline(units, 3)

    outs = []
    for j in range(NJ):
        outs.append(B.out_block(j, mixT1, wo1, x1_ext, xr[j % 3], vv[j % 3], st, gbc, bbc, y_out, nextbank))
    return outs


def layer_norm_store(B, vv_t, scratch, st, gbc, bbc, y_out, j):
    B.op("dve", lambda e: e.tensor_reduce(st.ap[:, 0:1], vv_t.ap, AX.X, ALU.add), [vv_t], [st])
    B.ts("dve", st.ap[:, 1:2], st.ap[:, 0:1], -1.0 / D, None, ALU.mult, None, [st], [st])
    B.op("act", lambda e: e.activation(out=scratch.ap, in_=vv_t.ap, func=AF.Square, bias=st.ap[:, 1:2], scale=1.0,
                                       accum_out=st.ap[:, 2:3]), [vv_t, st], [scratch, st])
    B.act(st.ap[:, 3:4], st.ap[:, 2:3], AF.Ln, [st, B.eps_col], [st], scale=1.0 / D, bias=B.eps_col.ap[:, 0:1])
    B.act(st.ap[:, 3:4], st.ap[:, 3:4], AF.Exp, [st], [st], scale=-0.5)
    B.ts("dve", vv_t.ap, vv_t.ap, st.ap[:, 1:2], st.ap[:, 3:4], ALU.add, ALU.mult, [vv_t, st], [vv_t])
    B.tt("dve", vv_t.ap, vv_t.ap, gbc.ap, ALU.mult, [vv_t, gbc], [vv_t])
    B.tt("pool", vv_t.ap, vv_t.ap, bbc.ap, ALU.add, [vv_t, bbc], [vv_t])
    return B.dma(y_out.ap[j * 128:(j + 1) * 128, :], vv_t.ap, [vv_t], [y_out])


def _own_idx(S, c):
    NR = S // 2048
    return np.concatenate([np.arange((16 * i + 4 * c) * 128, (16 * i + 4 * c + 4) * 128) for i in range(NR)])


def _halo_idx(S, c):
    NR = S // 2048
    out = []
    for i in range(NR):
        g = 16 * i + 4 * c - 1
        out.append(np.arange(g * 128, (g + 1) * 128) if g >= 0 else np.full(128, -1))
    return np.concatenate(out)


def _mask_dram(c):
    m = _masks(c)
    return np.ascontiguousarray(np.transpose(m, (1, 0, 2)).reshape(128, 9 * 128))


def _mask_tables(c):
    col = np.arange(512, dtype=np.float32)
    qcol = np.zeros((128, 1024), np.float32)
    qcol[:, 0:512] = col[None, :]
    qcol[:, 512:1024] = (col + 1920.0 * np.floor(col / 128.0))[None, :]
    k = np.arange(128, dtype=np.float32)[:, None]
    stab = np.zeros((128, 80), np.float32)
    stab[:, 0:16] = k + 128.0 * np.arange(16, dtype=np.float32)[None, :] - 512.0 * c
    stab[:, 16:80] = k + 128.0 * np.arange(64, dtype=np.float32)[None, :] + 128.0 - 512.0 * c
    return qcol, stab


def _masks1(c):
    k = np.arange(128)[:, None]
    q = np.arange(128)[None, :]
    m = np.zeros((3, 128, 128), np.float32)
    m[0] = (k <= q)
    m[1] = (k > q)
    m[2] = (k > q) if c > 0 else 0.0
    return np.ascontiguousarray(np.transpose(m, (1, 0, 2)).reshape(128, 3 * 128)).astype(ml_dtypes.bfloat16)


def l1_weights(w_in, w_memkv, sinks, w_out, ln_g, ln_b):
    cq, ck, cv = w_in[:, 0:768], w_in[:, 768:960], w_in[:, 960:1152]
    mq, gate = w_in[:, 1152:1408], w_in[:, 1408:2432]
    kd = np.concatenate([ck[:, 0:64], ck[:, 64:128], ck[:, 128:192], ck[:, 128:192]], axis=1)
    pk = _partner_perm(256, 64, 16)
    qr = np.concatenate([cq[:, h * 64:(h + 1) * 64] for h in QCOLS], axis=1)
    pq = _partner_perm(512, 64, 16)
    q0, q1 = qr[:, 0:512], qr[:, 512:1024]
    invf1, coef1 = _host_consts(1)
    return {
        "w1_kv": np.ascontiguousarray(np.concatenate([kd, kd[:, pk], cv, mq], axis=1)),
        "w1_q": np.ascontiguousarray(np.concatenate([q0, q0[:, pq], q1, q1[:, pq]], axis=1)),
        "w1_g": np.ascontiguousarray(gate),
        "w1_mkv": np.ascontiguousarray(w_memkv),
        "w1_o": np.ascontiguousarray(w_out),
        "ln1_g": np.ascontiguousarray(ln_g[None, :]),
        "ln1_b": np.ascontiguousarray(ln_b[None, :]),
        "sinks_x": np.ascontiguousarray(np.repeat(sinks, 128)[None, :]),
        "invf1": invf1, "coefp1": coef1,
        "ident": np.eye(128, dtype=np.float32),
    }


def prep_fused(inp):
    f = lambda a: np.asarray(a)
    x, mem, positions = f(inp["x"]), f(inp["mem"]), f(inp["positions"])
    S = x.shape[1]
    w_in = f(inp["w_in_even"])[0]
    sbq, sbk, sbv = w_in[:, 0:384], w_in[:, 384:768], w_in[:, 768:1152]
    dfq, dfk, dfv = w_in[:, 1152:1536], w_in[:, 1536:1920], w_in[:, 1920:2304]
    mq, gate = w_in[:, 2304:2560], w_in[:, 2560:3584]
    perm = _partner_perm(384, 32, 8)
    invf, coefp = _host_consts(0)
    dsub = f(inp["diff_subln_even"])[0]
    shared = {
        "w_k": np.ascontiguousarray(np.concatenate([sbk, dfk, dfk[:, perm]], axis=1)),
        "w_v": np.ascontiguousarray(np.concatenate([sbv, dfv], axis=1)),
        "w_q": np.ascontiguousarray(np.concatenate([sbq, dfq, dfq[:, perm], mq], axis=1)),
        "w_g": np.ascontiguousarray(gate),
        "w_mkv": np.ascontiguousarray(f(inp["w_memkv_even"])[0]),
        "w_o": np.ascontiguousarray(f(inp["w_out_even"])[0]),
        "ln_g": np.ascontiguousarray(f(inp["ln_g_even"])[0][None, :]),
        "ln_b": np.ascontiguousarray(f(inp["ln_b_even"])[0][None, :]),
        "dlam": np.ascontiguousarray(f(inp["diff_lambda_even"])[0].reshape(1, 128)),
        "subln": np.ascontiguousarray(np.concatenate([dsub, dsub])[:, None]),
        "invf": invf, "coefp": coefp,
    }
    shared.update(l1_weights(f(inp["w_in_odd"])[0], f(inp["w_memkv_odd"])[0], f(inp["sinks_odd"])[0], f(inp["w_out_odd"])[0],
                             f(inp["ln_g_odd"])[0], f(inp["ln_b_odd"])[0]))
    xT = [np.ascontiguousarray(x[b].T) for b in range(x.shape[0])]
    maps = []
    for core in range(8):
        b, c = core // 4, core % 4
        own = _own_idx(S, c)
        hidx = _halo_idx(S, c)
        ext = np.concatenate([own, np.maximum(hidx, 0)])
        qcol, stab = _mask_tables(c)
        m = dict(shared)
        m.update({
            "xT_all": xT[b],
            "xT_ext": np.ascontiguousarray(x[b][ext].T),
            "x_ext": np.ascontiguousarray(x[b][ext]),
            "pos_all": np.ascontiguousarray(positions[b][None, :]).astype(np.int32),
            "pos_ext": np.ascontiguousarray(positions[b][ext][None, :]).astype(np.int32),
            "memT": np.ascontiguousarray(mem[b].T),
            "masks": _mask_dram(c),
            "qcol": qcol, "stab": stab,
            "masks1": _masks1(c),
        })
        maps.append(m)
    return maps


def gather_own(results, key, S, nb=2):
    out = np.zeros((nb, S, D), np.float32)
    for core in range(8):
        b, c = core // 4, core % 4
        out[b, _own_idx(S, c)] = results[core][key]
    return out


def build_fused(S, lambda_init):
    B = Builder(S, 0)
    l0_body(B, lambda_init, True)
    outs = l1_body(B)
    B.sch.emit(final_wait_ops=outs)
    return B


LAMBDA_INIT0 = 0.8 - 0.6 * math.exp(-0.3 * 0)


def kernel(**inputs):
    S = np.asarray(inputs["x"]).shape[1]
    B = build_fused(S, LAMBDA_INIT0)
    maps = prep_fused(inputs)
    res = run_bass_kernel_spmd(B.nc, maps, core_ids=list(range(8)))
    return gather_own(res.results, "y", S)
```

```python
import math
import contextlib
import numpy as np
import ml_dtypes
import concourse.bass as bass
import concourse.mybir as mybir
from concourse.bass_utils import run_bass_kernel_spmd

F32 = mybir.dt.float32
BF16 = mybir.dt.bfloat16
I32 = mybir.dt.int32
AF = mybir.ActivationFunctionType
ALU = mybir.AluOpType
AX = mybir.AxisListType

D = 1024
KC = 8
DEPTH = 2
ALPHA = (2 * DEPTH) ** 0.25
LN_EPS = 1e-5
ROPE_THETA = 500000.0
MEM_LEN = 256
PI = math.pi


class Res:
    __slots__ = ("lw", "rd", "name")

    def __init__(self, name=""):
        self.lw = None
        self.rd = []
        self.name = name


class Sched:
    ENGS = ("pe", "act", "dve", "pool", "sp")
    NSLOT = 14

    def __init__(self, nc):
        self.nc = nc
        self.ops = []

    def add(self, eng, fn, reads=(), writes=(), dma=False):
        idx = len(self.ops)
        deps = set()
        for r in reads:
            if r.lw is not None:
                deps.add(r.lw)
        for w in writes:
            if w.lw is not None:
                deps.add(w.lw)
            deps.update(w.rd)
        for r in reads:
            r.rd.append(idx)
        for w in writes:
            w.lw = idx
            w.rd = []
        deps.discard(idx)
        self.ops.append([eng, fn, deps, dma])
        return idx

    def emit(self, final_wait_ops=()):
        nc = self.nc
        ops = self.ops
        n = len(ops)
        has_dep = [False] * n
        for i, (eng, fn, deps, dma) in enumerate(ops):
            for d in deps:
                if ops[d][0] == "pe" and eng == "pe" and not ops[d][3] and not dma:
                    continue
                has_dep[d] = True
        for d in final_wait_ops:
            has_dep[d] = True
        cnt = {e: 0 for e in self.ENGS}
        dcnt = {e: 0 for e in self.ENGS}
        sig = [None] * n
        for i, (eng, fn, deps, dma) in enumerate(ops):
            if dma:
                k = dcnt[eng]
                dcnt[eng] += 1
                sig[i] = ("d", eng, k % self.NSLOT, 16 * (k // self.NSLOT + 1))
            elif has_dep[i]:
                cnt[eng] += 1
                sig[i] = ("c", eng, cnt[eng])
        engs_used = [e for e in self.ENGS if any(o[0] == e for o in ops)]
        with contextlib.ExitStack() as st:
            csem = {e: st.enter_context(nc.semaphore("c_" + e)) for e in engs_used}
            dsem = {}
            for e in engs_used:
                if dcnt[e] > 0:
                    dsem[e] = [st.enter_context(nc.semaphore("d_%s_%d" % (e, s)))
                               for s in range(min(self.NSLOT, dcnt[e]))]
            block = st.enter_context(nc.Block())
            engobj = {"pe": "tensor", "act": "scalar", "dve": "vector", "pool": "gpsimd", "sp": "sync"}

            def make_stream(ename):
                def stream(eng):
                    waited_c = {}
                    waited_d = {}
                    for i, (e, fn, deps, dma) in enumerate(ops):
                        if e != ename:
                            continue
                        need_c = {}
                        need_d = {}
                        for d in deps:
                            s = sig[d]
                            if s is None:
                                continue
                            if s[0] == "c":
                                if s[1] == "pe" and ename == "pe" and not dma:
                                    continue
                                need_c[s[1]] = max(need_c.get(s[1], 0), s[2])
                            else:
                                key = (s[1], s[2])
                                need_d[key] = max(need_d.get(key, 0), s[3])
                        if dma:
                            s = sig[i]
                            if s[3] > 16:
                                key = (s[1], s[2])
                                need_d[key] = max(need_d.get(key, 0), s[3] - 16)
                        for se, v in need_c.items():
                            if waited_c.get(se, 0) < v:
                                eng.wait_ge(csem[se], v)
                                waited_c[se] = v
                        for key, v in need_d.items():
                            if waited_d.get(key, 0) < v:
                                eng.wait_ge(dsem[key[0]][key[1]], v)
                                waited_d[key] = v
                        ins = fn(eng)
                        s = sig[i]
                        if s is not None:
                            if s[0] == "c":
                                ins.then_inc(csem[ename], 1)
                            else:
                                ins.then_inc(dsem[ename][s[2]], 16)
                    if ename == "sp":
                        for d in final_wait_ops:
                            s = sig[d]
                            if s[0] == "c":
                                eng.wait_ge(csem[s[1]], s[2])
                            else:
                                eng.wait_ge(dsem[s[1]][s[2]], s[3])
                return stream

            for e in engs_used:
                getattr(block, engobj[e])(make_stream(e))


class SbufAlloc:
    def __init__(self, nc, nbytes=207 * 1024):
        self.nc = nc
        self.arena = nc.alloc_sbuf_tensor("arena", [128, nbytes], mybir.dt.uint8)
        self.off = 0
        self.limit = nbytes
        self.peak = 0

    def mark(self):
        return self.off

    def reset(self, m):
        self.off = m

    def tile(self, shape, dtype):
        assert shape[0] == 128
        esz = {F32: 4, BF16: 2, I32: 4}[dtype]
        nel = int(np.prod(shape[1:]))
        nbytes = (esz * nel + 63) // 64 * 64
        off = self.off
        self.off += nbytes
        self.peak = max(self.peak, self.off)
        assert self.off <= self.limit, ("SBUF overflow", self.off)
        ap = self.arena.ap()[:, off:off + esz * nel].bitcast(dtype)
        if len(shape) == 3:
            ap = ap.rearrange("p (a b) -> p a b", a=shape[1])
        elif len(shape) == 4:
            ap = ap.rearrange("p (a b c) -> p a b c", a=shape[1], b=shape[2])
        return ap


class T:
    def __init__(self, ap, name=""):
        self.ap = ap
        self.res = Res(name)


class Builder:
    def __init__(self, S, layer):
        self.S = S
        self.layer = layer
        self.NB = S // 128
        self.NJ = self.NB // 4
        self.NO = self.NJ * 128
        self.NR = self.NB // 16
        self.NE = self.NO + self.NR * 128
        self.NEB = self.NE // 128
        self.QGB = 4
        self.GW = 512
        self.NQG = self.NR
        self.groups = [(i * 512, 512) for i in range(self.NR)] + [(self.NO, self.NR * 128)]
        self.TW = min(512, S)
        self.NT = S // self.TW
        self.nc = bass.Bass("TRN2", target_bir_lowering=False)
        self.sch = Sched(self.nc)
        self.sa = SbufAlloc(self.nc)
        self.dram = {}
        self.psum = [T(self.nc.alloc_psum_tensor("ps%d" % i, [128, 512], F32).ap(), "ps%d" % i) for i in range(8)]

    def din(self, name, shape, dtype=F32):
        t = T(self.nc.dram_tensor(name, list(shape), dtype, kind="ExternalInput").ap(), name)
        self.dram[name] = t
        return t

    def dout(self, name, shape, dtype=F32):
        t = T(self.nc.dram_tensor(name, list(shape), dtype, kind="ExternalOutput").ap(), name)
        self.dram[name] = t
        return t

    def dscr(self, name, shape, dtype):
        t = T(self.nc.dram_tensor(name, list(shape), dtype).ap(), name)
        self.dram[name] = t
        return t

    def tile(self, shape, dtype, name=""):
        return T(self.sa.tile(shape, dtype), name)

    def op(self, eng, fn, reads=(), writes=(), dma=False):
        return self.sch.add(eng, fn, [t.res for t in reads], [t.res for t in writes], dma)

    def dma(self, out_ap, in_ap, reads, writes, q="sp"):
        return self.op(q, lambda e: e.dma_start(out=out_ap, in_=in_ap), reads, writes, dma=True)

    def mm(self, out_ap, lhsT, rhs, start, stop, reads, writes, skip=False):
        if skip:
            return self.op("pe", lambda e: e.matmul(out_ap, lhsT=lhsT, rhs=rhs, start=start, stop=stop,
                                                    skip_group_check=True), reads, writes)
        return self.op("pe", lambda e: e.matmul(out_ap, lhsT=lhsT, rhs=rhs, start=start, stop=stop), reads, writes)

    def act(self, out_ap, in_ap, func, reads, writes, scale=1.0, bias=0.0):
        return self.op("act", lambda e: e.activation(out=out_ap, in_=in_ap, func=func, bias=bias, scale=scale),
                       reads, writes)

    def tt(self, eng, out_ap, a, b, op, reads, writes):
        return self.op(eng, lambda e: e.tensor_tensor(out_ap, a, b, op), reads, writes)

    def ts(self, eng, out_ap, a, s1, s2, op0, op1, reads, writes):
        if op1 is None:
            return self.op(eng, lambda e: e.tensor_scalar(out_ap, a, s1, None, op0), reads, writes)
        return self.op(eng, lambda e: e.tensor_scalar(out_ap, a, s1, s2, op0, op1), reads, writes)

    def copy(self, eng, out_ap, in_ap, reads, writes):
        if eng == "act":
            return self.op("act", lambda e: e.copy(out_ap, in_ap), reads, writes)
        return self.op(eng, lambda e: e.tensor_copy(out_ap, in_ap), reads, writes)

    def memset(self, eng, ap, val, writes):
        return self.op(eng, lambda e: e.memset(ap, val), (), writes)

    def load_w(self, wd, n, wb, stage=None, c0=0):
        src = wd.ap.rearrange("(kc p) n -> p kc n", p=128)
        for k2 in range(0, KC, 2):
            self.dma(wb.ap[:, k2:k2 + 2, c0:c0 + n], src[:, k2:k2 + 2, :], [wd], [wb], q="pool")

    def rope_tables(self, pos_d, a, w, posi, posf, t1, cosT, sinS, invf, coefp):
        SC = 2 * PI * (1.0 - 1e-6)
        self.dma(posi.ap[:, 0:w], pos_d.ap[0:1, a:a + w].partition_broadcast(128), [pos_d], [posi])
        self.copy("act", posf.ap[:, 0:w], posi.ap[:, 0:w], [posi], [posf])
        self.ts("dve", posf.ap[:, 0:w], posf.ap[:, 0:w], invf.ap[:, 0:1], None, ALU.mult, None, [posf, invf], [posf])
        for (dst, off) in ((sinS, 0.0), (cosT, 0.25)):
            if off:
                self.ts("dve", posf.ap[:, 0:w], posf.ap[:, 0:w], off, None, ALU.add, None, [posf], [posf])
            self.copy("dve", posi.ap[:, 0:w], posf.ap[:, 0:w], [posf], [posi])
            self.copy("act", t1.ap[:, 0:w], posi.ap[:, 0:w], [posi], [t1])
            self.op("dve", lambda e: e.scalar_tensor_tensor(t1.ap[:, 0:w], t1.ap[:, 0:w], -1.0, posf.ap[:, 0:w],
                                                            ALU.mult, ALU.add), [t1, posf], [t1])
            self.act(dst.ap[:, 0:w], t1.ap[:, 0:w], AF.Sin, [t1], [dst], scale=SC)
        self.ts("dve", sinS.ap[:, 0:w], sinS.ap[:, 0:w], coefp.ap[:, 0:1], None, ALU.mult, None, [sinS, coefp], [sinS])

    def proj_fm(self, ps, wb, c0, xb, w, m=128):
        for kc in range(KC):
            self.mm(ps.ap[0:m, 0:w], wb.ap[:, kc, c0:c0 + m], xb.ap[:, kc, 0:w], kc == 0, kc == KC - 1, [wb, xb], [ps])


    def nextbank4(self):
        b = self.psum[self.b4[0] % 4]
        self.b4[0] += 1
        return b

    def softmax_norm(self, acc, W, add_row=None):
        rd, bcs, ones_f = self.rd, self.bcs, self.ones_f
        k = self.b4[0] % 2
        cols = slice(k * 512, k * 512 + W)
        if add_row is not None:
            self.tt("dve", rd.ap[64:65, cols], acc.ap[64:65, 0:W], add_row[0], ALU.add, [acc, add_row[1]], [rd])
        else:
            self.ts("dve", rd.ap[64:65, cols], acc.ap[64:65, 0:W], 1e-18, None, ALU.add, None, [acc], [rd])
        bc = self.nextbank4()
        self.mm(bc.ap[0:64, 0:W], ones_f.ap[64:65, 0:64], rd.ap[64:65, cols], True, True, [ones_f, rd], [bc])
        self.act(bcs.ap[0:64, cols], bc.ap[0:64, 0:W], AF.Ln, [bc], [bcs])
        self.act(bcs.ap[0:64, cols], bcs.ap[0:64, cols], AF.Exp, [bcs], [bcs], scale=-1.0)
        return cols

    def mem_units(self, mkT, mv, mqT, mixT, PT, ea, tq, zc, accc, groups):
        B = self
        PS = self.psum
        units = []
        for hm in range(4):
            cg, po = hm // 2, (hm % 2) * 64
            for (col0, W) in groups:
                acc = PS[4 + accc[0] % 4]
                accc[0] += 1
                gsl = slice(col0, col0 + W)
                for mb in range(2):
                    zi = zc[0]
                    zc[0] += 1
                    Sb = B.nextbank4()
                    PTt = PT[zi % len(PT)]

                    def s0(Sb=Sb, PTt=PTt, mb=mb, cg=cg, po=po, gsl=gsl, W=W):
                        B.mm(Sb.ap[:, 0:W], mkT.ap[po:po + 64, cg, mb * 128:(mb + 1) * 128], mqT.ap[po:po + 64, cg, gsl],
                             True, True, [mkT, mqT], [Sb])
                        B.act(PTt.ap[:, 0:W], Sb.ap[:, 0:W], AF.Exp, [Sb], [PTt], scale=0.125)

                    def s1(acc=acc, PTt=PTt, mb=mb, hm=hm, cg=cg, po=po, gsl=gsl, W=W):
                        B.mm(acc.ap[0:65, 0:W], mv.ap[:, mb, hm, 0:65], PTt.ap[:, 0:W], mb == 0, mb == 1, [mv, PTt], [acc], skip=True)
                        if mb == 1:
                            bcol = B.softmax_norm(acc, W)
                            B.tt("dve", ea.ap[0:64, 0:W], acc.ap[0:64, 0:W], B.bcs.ap[0:64, bcol], ALU.mult, [acc, B.bcs], [ea])
                            B.copy("act", tq.ap[po:po + 64, 0:W], ea.ap[0:64, 0:W], [ea], [tq])
                            B.tt("dve", mixT.ap[po:po + 64, 6 + cg, gsl], tq.ap[po:po + 64, 0:W], mixT.ap[po:po + 64, 6 + cg, gsl],
                                 ALU.mult, [tq, mixT], [mixT])

                    units.append([s0, s1])
        return units

    def mem_kv(self, memT, w_mkv, wb, wst, xs0, memb, mkT, mv, nextbank):
        B = self
        B.load_w(w_mkv, 512, wb, wst)
        B.dma(xs0.ap[:, :, 0:MEM_LEN], memT.ap.rearrange("(kc p) m -> p kc m", p=128), [memT], [xs0])
        B.copy("dve", memb.ap[:, :, 0:MEM_LEN], xs0.ap[:, :, 0:MEM_LEN], [xs0], [memb])
        for cg in range(2):
            ps = nextbank()
            B.proj_fm(ps, wb, cg * 128, memb, MEM_LEN)
            B.copy("act", mkT.ap[:, cg, :], ps.ap[:, 0:MEM_LEN], [ps], [mkT])
        B.memset("pool", mv.ap, 1.0, [mv])
        for mb in range(2):
            ps = nextbank()
            for kc in range(KC):
                B.mm(ps.ap[:, 0:256], memb.ap[:, kc, mb * 128:(mb + 1) * 128], wb.ap[:, kc, 256:512], kc == 0, kc == KC - 1,
                     [memb, wb], [ps])
            B.copy("act", mv.ap[:, mb, :, 0:64], ps.ap[:, 0:256].rearrange("p (h d) -> p h d", h=4), [ps], [mv])

    def out_block(self, j, mixT, wo, x_src, xr_t, vv_t, st, gbc, bbc, y_dst, nextbank):
        B = self
        B.dma(xr_t.ap, x_src.ap[j * 128:(j + 1) * 128, :], [x_src], [xr_t])
        for n in range(2):
            ps = nextbank()
            for kc in range(KC):
                B.mm(ps.ap[:, 0:512], mixT.ap[:, kc, j * 128:(j + 1) * 128], wo.ap[:, kc, n * 512:(n + 1) * 512],
                     kc == 0, kc == KC - 1, [mixT, wo], [ps])
            B.op("dve", lambda e, ps=ps, n=n: e.scalar_tensor_tensor(
                vv_t.ap[:, n * 512:(n + 1) * 512], xr_t.ap[:, n * 512:(n + 1) * 512], ALPHA, ps.ap[:, 0:512],
                ALU.mult, ALU.add), [ps, xr_t], [vv_t])
        return layer_norm_store(B, vv_t, xr_t, st, gbc, bbc, y_dst, j)

    def rope_evac(self, psK, psP, out_ap, w, cosT, sinS, tmpa, tmpb, out_t, scale=None):
        self.tt("dve", tmpa.ap[:, 0:w], psK.ap[:, 0:w], cosT.ap[:, 0:w], ALU.mult, [psK, cosT], [tmpa])
        self.tt("dve", tmpb.ap[:, 0:w], psP.ap[:, 0:w], sinS.ap[:, 0:w], ALU.mult, [psP, sinS], [tmpb])
        self.tt("dve", out_ap, tmpa.ap[:, 0:w], tmpb.ap[:, 0:w], ALU.add, [tmpa, tmpb], [out_t])


def _host_consts(layer):
    if layer == 0:
        hd, rot = 32, 8
    else:
        hd, rot = 64, 16
    half = rot // 2
    inv = np.exp(-(np.arange(half, dtype=np.float32) / half) * math.log(ROPE_THETA)).astype(np.float32)
    invf = np.zeros((128, 1), np.float32)
    coef = np.zeros((128, 1), np.float32)
    for r in range(128):
        d = r % hd
        if d < rot:
            invf[r, 0] = inv[d % half] / np.float32(2 * PI)
            coef[r, 0] = -1.0 if d < half else 1.0
    return invf, coef


def _partner_perm(ncols, hd, rot):
    half = rot // 2
    perm = np.arange(ncols)
    for c in range(ncols):
        d = c % hd
        if d < half:
            perm[c] = c + half
        elif d < rot:
            perm[c] = c - half
    return perm


def _masks(c):
    k = np.arange(128)[:, None]
    q = np.arange(128)[None, :]
    out = np.zeros((9, 128, 128), np.float32)
    out[8] = -1.0 * (k >= q)
    for r in range(4):
        if r < c:
            out[r] = 1.0
            out[4 + r] = 1.0
        elif r == c:
            out[r] = (k <= q)
            out[4 + r] = (k < q)
    return out.astype(ml_dtypes.bfloat16)


def run_pipeline_chains(chains, skew):
    n = len(chains[0])
    depth = max(skew) + 1
    for step in range(n + depth - 1):
        for s, d in enumerate(skew):
            u = step - d
            if 0 <= u < n:
                for ch in chains:
                    if ch[u][s] is not None:
                        ch[u][s]()


def run_pipeline(units, nst):
    n = len(units)
    for step in range(n + nst - 1):
        for s in range(nst):
            u = step - s
            if 0 <= u < n and units[u][s] is not None:
                units[u][s]()


def build_l0(S, lambda_init):
    B = Builder(S, 0)
    outs = l0_body(B, lambda_init, False)
    B.sch.emit(final_wait_ops=outs)
    return B


def l0_body(B, lambda_init, fused):
    nc = B.nc
    S = B.S
    NB, NJ, NO, QGB, GW, NQG, TW, NT = B.NB, B.NJ, B.NO, B.QGB, B.GW, B.NQG, B.TW, B.NT
    NR, NE, NEB = B.NR, B.NE, B.NEB
    etiles = [(a, min(512, NE - a)) for a in range(0, NE, 512)]
    xT_all = B.din("xT_all", [D, S])
    xT_own = B.din("xT_ext", [D, NE])
    x_own = B.din("x_ext", [NE, D])
    pos_all = B.din("pos_all", [1, S], I32)
    pos_own = B.din("pos_ext", [1, NE], I32)
    w_k = B.din("w_k", [D, 1152])
    w_v = B.din("w_v", [D, 768])
    w_q = B.din("w_q", [D, 1408])
    w_g = B.din("w_g", [D, 1024])
    memT = B.din("memT", [D, MEM_LEN])
    w_mkv = B.din("w_mkv", [D, 512])
    w_o = B.din("w_o", [D, D])
    ln_g = B.din("ln_g", [1, D])
    ln_b = B.din("ln_b", [1, D])
    dlam = B.din("dlam", [1, 128])
    subln = B.din("subln", [128, 1])
    invf_d = B.din("invf", [128, 1])
    coef_d = B.din("coefp", [128, 1])
    masks_d = B.din("masks", [128, 9 * 128], BF16)
    qcol_d = B.din("qcol", [128, 1024])
    stab_d = B.din("stab", [128, 16 + 64])
    y_out = B.dscr("x1_ext", [NE, D], F32)
    B.x1_ext_d = y_out
    kT_scr = B.dscr("kT_scr", [128, 6, S], BF16)
    v_scr = B.dscr("v_scr", [128, 12, NB * 65], BF16)

    masks = B.tile([128, 9, 128], BF16, "masks")
    negtri = T(masks.ap[:, 8, :], "negtri")
    negtri.res = masks.res
    negones = B.tile([128, 128], BF16, "negones")
    ones_f = B.tile([128, 128], F32, "ones_f")
    invf = B.tile([128, 1], F32, "invf")
    coefp = B.tile([128, 1], F32, "coefp")
    B.pi_col = B.tile([128, 1], F32, "pi")
    g1col = B.tile([128, 1], F32, "g1col")
    lam_t = B.tile([128, 128], F32, "lam")
    lamw = B.tile([128, 8], F32, "lamw")
    nlcol = B.tile([128, 2], F32, "nlcol")
    B.eps_col = B.tile([128, 1], F32, "eps")
    prod = B.tile([128, 64], F32, "prod")
    qcol = B.tile([128, 1024], F32, "qcol")
    stab = B.tile([128, 80], F32, "stab")
    B.mark_consts = B.sa.mark()
    qT_sb = B.tile([128, 3, NE], BF16, "qT_sb")
    qT_df = B.tile([128, 3, NE], BF16, "qT_df")
    mqT = B.tile([128, 2, NE], BF16, "mqT")
    mixT = B.tile([128, 8, NE], BF16, "mixT")
    mkT = B.tile([128, 2, MEM_LEN], BF16, "mkT")
    mv = B.tile([128, 2, 4, 65], BF16, "mv")
    PS = B.psum
    pbc = [0]

    def nextbank():
        b = PS[pbc[0] % 8]
        pbc[0] += 1
        return b

    B.dma(masks.ap.rearrange("p a b -> p (a b)"), masks_d.ap[:, :], [masks_d], [masks])
    B.dma(invf.ap, invf_d.ap[:, :], [invf_d], [invf])
    B.dma(coefp.ap, coef_d.ap[:, :], [coef_d], [coefp])
    B.dma(qcol.ap, qcol_d.ap[:, :], [qcol_d], [qcol])
    B.dma(stab.ap, stab_d.ap[:, :], [stab_d], [stab])
    B.memset("pool", B.pi_col.ap, PI, [B.pi_col])
    B.memset("pool", B.eps_col.ap, LN_EPS, [B.eps_col])
    B.memset("pool", ones_f.ap, 1.0, [ones_f])
    B.memset("pool", negones.ap, -1.0, [negones])
    B.dma(lam_t.ap[64:65, 0:128], dlam.ap[0:1, :], [dlam], [lam_t])
    B.tt("dve", prod.ap[64:65, 0:32], lam_t.ap[64:65, 0:32], lam_t.ap[64:65, 32:64], ALU.mult, [lam_t], [prod])
    B.tt("dve", prod.ap[64:65, 32:64], lam_t.ap[64:65, 64:96], lam_t.ap[64:65, 96:128], ALU.mult, [lam_t], [prod])
    B.op("dve", lambda e: e.tensor_reduce(lamw.ap[64:65, 0:1], prod.ap[64:65, 0:32], AX.X, ALU.add), [prod], [lamw])
    B.op("dve", lambda e: e.tensor_reduce(lamw.ap[64:65, 1:2], prod.ap[64:65, 32:64], AX.X, ALU.add), [prod], [lamw])
    B.act(lamw.ap[64:65, 2:4], lamw.ap[64:65, 0:2], AF.Exp, [lamw], [lamw])
    B.tt("dve", lamw.ap[64:65, 4:5], lamw.ap[64:65, 2:3], lamw.ap[64:65, 3:4], ALU.subtract, [lamw], [lamw])
    B.ts("dve", lamw.ap[64:65, 5:6], lamw.ap[64:65, 4:5], lambda_init, -1.0, ALU.add, ALU.mult, [lamw], [lamw])
    nlps = nextbank()
    B.mm(nlps.ap[0:64, 0:2], ones_f.ap[64:65, 0:64], lamw.ap[64:65, 4:6], True, True, [ones_f, lamw], [nlps])
    B.copy("dve", nlcol.ap[0:64, 0:2], nlps.ap[0:64, 0:2], [nlps], [nlcol])
    B.dma(g1col.ap, subln.ap[:, :], [subln], [g1col])
    B.ts("dve", g1col.ap, g1col.ap, 1.0 - lambda_init, None, ALU.mult, None, [g1col], [g1col])

    mA = B.sa.mark()
    wb = B.tile([128, KC, 1920], BF16, "wb")
    wst = [B.tile([128, 1, 16], F32, "wst%d" % i) for i in range(2)]
    xs = [B.tile([128, KC, TW], F32, "xs%d" % i) for i in range(2)]
    xb = [B.tile([128, KC, TW], BF16, "xb%d" % i) for i in range(2)]
    posi = B.tile([128, TW], I32, "posi")
    posf = B.tile([128, TW], F32, "posf")
    t1 = B.tile([128, TW], F32, "t1")
    cosT = B.tile([128, TW], F32, "cosT")
    sinS = B.tile([128, TW], F32, "sinS")
    tmpa = B.tile([128, TW], F32, "tmpa")
    tmpb = B.tile([128, TW], F32, "tmpb")
    ktst = [B.tile([128, 6, TW], BF16, "ktst%d" % i) for i in range(1)]
    vst = [B.tile([128, 12, TW // 128, 65], BF16, "vst%d" % i) for i in range(2)]

    B.mem_kv(memT, w_mkv, wb, wst, xs[0], xb[0], mkT, mv, nextbank)

    B.load_w(w_k, 1152, wb, wst, 0)
    B.load_w(w_v, 768, wb, wst, 1152)
    for i in range(2):
        B.memset("pool", vst[i].ap, 1.0, [vst[i]])
    xsrc = xT_all.ap.rearrange("(kc p) s -> p kc s", p=128)
    cosTs = [cosT, B.tile([128, TW], F32, "cosT_b")]
    sinSs = [sinS, B.tile([128, TW], F32, "sinS_b")]

    def a1_prefetch(t):
        xs_t, xb_t = xs[t % 2], xb[t % 2]
        for hlf in range(2):
            B.dma(xs_t.ap[:, hlf * 4:(hlf + 1) * 4, :], xsrc[:, hlf * 4:(hlf + 1) * 4, t * TW:(t + 1) * TW], [xT_all], [xs_t])
        for kc in range(KC):
            B.copy(("act", "dve", "act", "dve", "pool", "act", "dve", "pool")[kc], xb_t.ap[:, kc, :], xs_t.ap[:, kc, :], [xs_t], [xb_t])
        B.rope_tables(pos_all, t * TW, TW, posi, posf, t1, cosTs[t % 2], sinSs[t % 2], invf, coefp)

    a1_prefetch(0)
    for t in range(NT):
        xs_t, xb_t = xs[t % 2], xb[t % 2]
        kt_t, v_t = ktst[0], vst[t % 2]
        cosT, sinS = cosTs[t % 2], sinSs[t % 2]
        if t + 1 < NT:
            a1_prefetch(t + 1)
        for cg in range(3):
            ps = nextbank()
            B.proj_fm(ps, wb, cg * 128, xb_t, TW)
            B.copy("act", kt_t.ap[:, cg, :], ps.ap[:, 0:TW], [ps], [kt_t])
        for cg in range(3):
            psK = nextbank()
            B.proj_fm(psK, wb, 384 + cg * 128, xb_t, TW)
            psP = nextbank()
            B.proj_fm(psP, wb, 768 + cg * 128, xb_t, TW)
            B.rope_evac(psK, psP, kt_t.ap[:, 3 + cg, :], TW, cosT, sinS, tmpa, tmpb, kt_t)
        B.dma(kT_scr.ap[:, :, t * TW:(t + 1) * TW], kt_t.ap, [kt_t], [kT_scr])
        for blk in range(TW // 128):
            for half in range(2):
                ps = nextbank()
                for kc in range(KC):
                    B.mm(ps.ap[:, 0:384], xb_t.ap[:, kc, blk * 128:(blk + 1) * 128],
                         wb.ap[:, kc, 1152 + half * 384:1152 + (half + 1) * 384], kc == 0, kc == KC - 1, [xb_t, wb], [ps])
                B.copy("act" if half == 0 else "dve", v_t.ap[:, half * 6:(half + 1) * 6, blk, 0:64],
                       ps.ap[:, 0:384].rearrange("p (h d) -> p h d", h=6), [ps], [v_t])
        nb_t = TW // 128
        B.dma(v_scr.ap[:, :, t * nb_t * 65:(t + 1) * nb_t * 65], v_t.ap.rearrange("p h b c -> p h (b c)"), [v_t], [v_scr])

    cosT, sinS = cosTs[0], sinSs[0]
    xosrc = xT_own.ap.rearrange("(kc p) s -> p kc s", p=128)
    for rnd in range(2):
        if rnd == 0:
            B.load_w(w_q, 1408, wb, wst, 0)
        else:
            B.load_w(w_g, 1024, wb, wst, 0)
        for u, (ea0, OW) in enumerate(etiles):
            xs_t, xb_t = xs[u % 2], xb[u % 2]
            osl = slice(ea0, ea0 + OW)
            for hlf in range(2):
                B.dma(xs_t.ap[:, hlf * 4:(hlf + 1) * 4, 0:OW], xosrc[:, hlf * 4:(hlf + 1) * 4, osl], [xT_own], [xs_t])
            for kc in range(KC):
                B.copy(("act", "dve", "act", "dve", "pool", "act", "dve", "pool")[kc], xb_t.ap[:, kc, 0:OW], xs_t.ap[:, kc, 0:OW], [xs_t], [xb_t])
            if rnd == 0:
                B.rope_tables(pos_own, ea0, OW, posi, posf, t1, cosT, sinS, invf, coefp)
                for cg in range(3):
                    ps = nextbank()
                    B.proj_fm(ps, wb, cg * 128, xb_t, OW)
                    B.act(qT_sb.ap[:, cg, osl], ps.ap[:, 0:OW], AF.Copy, [ps], [qT_sb], scale=0.125)
                for cg in range(3):
                    psK = nextbank()
                    B.proj_fm(psK, wb, 384 + cg * 128, xb_t, OW)
                    psP = nextbank()
                    B.proj_fm(psP, wb, 768 + cg * 128, xb_t, OW)
                    B.rope_evac(psK, psP, qT_df.ap[:, cg, osl], OW, cosT, sinS, tmpa, tmpb, qT_df)
                for cg in range(2):
                    ps = nextbank()
                    B.proj_fm(ps, wb, 1152 + cg * 128, xb_t, OW)
                    B.copy("act", mqT.ap[:, cg, osl], ps.ap[:, 0:OW], [ps], [mqT])
            else:
                for cg in range(8):
                    ps = nextbank()
                    B.proj_fm(ps, wb, cg * 128, xb_t, OW)
                    B.act(mixT.ap[:, cg, osl], ps.ap[:, 0:OW], AF.Silu, [ps], [mixT])

    phaseA_tiles = [wb] + wst + xs + xb + [posi, posf, t1, tmpa, tmpb] + cosTs + sinSs + ktst + vst
    B.sa.reset(mA)
    kTp = [B.tile([128, S], BF16, "kTp%d" % i) for i in range(2)]
    vtp = [B.tile([128, 2, NB * 65 + 64], BF16, "vtp%d" % i) for i in range(2)]
    E = [B.tile([128, GW], F32, "E%d" % i) for i in range(4)]
    SP = [B.tile([128, GW], BF16, "SP%d" % i) for i in range(6)]
    PT = [B.tile([128, GW], BF16, "PT%d" % i) for i in range(8)]
    R32s = [B.tile([128, GW], F32, "R32_%d" % i) for i in range(2)]
    Rbs = [[B.tile([128, GW], BF16, "Rb%d_%d" % (ch, i)) for i in range(2)] for ch in range(2)]
    tmpo = [B.tile([128, GW], F32, "tmpo%d" % i) for i in range(2)]
    rd = B.tile([128, 2 * GW], F32, "rd")
    bcs = B.tile([128, 2 * GW], F32, "bcs")
    ea = B.tile([128, GW], F32, "ea")
    eb = B.tile([128, GW], F32, "eb")
    ec = B.tile([128, GW], F32, "ec")
    fz = B.tile([128, 16], F32, "fz")
    phaseB_tiles = kTp + vtp + E + SP + PT + R32s + Rbs[0] + Rbs[1] + tmpo + [rd, bcs, ea, eb, ec, fz]
    B.op("pool", lambda e: e.memset(fz.ap, 0.0), [], phaseA_tiles + phaseB_tiles)
    for i in range(2):
        B.memset("pool", vtp[i].ap[:, :, NB * 65:NB * 65 + 64], 0.0, [vtp[i]])

    kview = kT_scr.ap
    vview = v_scr.ap

    def load_pair(kcg, vh0, slot):
        B.dma(kTp[slot].ap, kview[:, kcg, :], [kT_scr], [kTp[slot]])
        B.dma(vtp[slot].ap[:, :, 0:NB * 65], vview[:, vh0:vh0 + 2, :], [v_scr], [vtp[slot]])

    zc = [0]
    accc = [0]

    def kb_list(gi):
        out = []
        if gi < NR:
            for kb in range(16 * gi + 15, -1, -1):
                if kb >= 16 * gi:
                    m = kb - 16 * gi
                    out.append((kb, max(0, m - 12) * 128, m))
                else:
                    out.append((kb, 0, None))
            return gi * 512, 512, qcol.ap[:, 0:512], out
        W = NR * 128
        for kb in range(16 * (NR - 1) + 11, -1, -1):
            c0 = ((kb - 11 + 15) // 16) * 128 if kb > 11 else 0
            out.append((kb, c0, 16 + kb))
        return NO, W, qcol.ap[:, 512:512 + W], out

    def mask_op(Pt, qc, cs, midx, strict):
        B.op("dve", lambda e: e.scalar_tensor_tensor(Pt.ap[:, cs], qc[:, cs], stab.ap[:, midx:midx + 1], Pt.ap[:, cs],
                                                     ALU.is_gt if strict else ALU.is_ge, ALU.mult),
             [Pt, qcol, stab], [Pt])

    def sb_units(h, slot):
        cg, po = h // 2, (h % 2) * 64
        ch = h % 2
        R32, Rb = R32s[ch], Rbs[ch]
        kT, vt = kTp[slot], vtp[slot]
        units = []
        ucount = 0
        for gi in range(NR + 1):
            col0, W, qc, kbs = kb_list(gi)
            acc = PS[4 + ch + 2 * (gi % 2)]
            for ui, (kb, c0, midx) in enumerate(kbs):
                first, last = ui == 0, ui == len(kbs) - 1
                zi = 2 * ucount + ch
                ucount += 1
                Z, ARG = PS[zi % 2], PS[2 + zi % 2]
                Et, SPt, PTt = E[zi % 4], SP[zi % 6], PT[zi % 8]
                Rcur, Rnext = Rb[(zi // 2) % 2], Rb[(zi // 2 + 1) % 2]
                kap = kT.ap[po:po + 64, kb * 128:(kb + 1) * 128]
                qap = qT_sb.ap[po:po + 64, cg, col0 + c0:col0 + W]
                cs = slice(c0, W)

                def s0(Z=Z, Et=Et, SPt=SPt, kap=kap, qap=qap, cs=cs, midx=midx, kT=kT, qc=qc):
                    B.mm(Z.ap[:, cs], kap, qap, True, True, [kT, qT_sb], [Z])
                    B.act(Et.ap[:, cs], Z.ap[:, cs], AF.Exp, [Z], [Et])
                    B.act(SPt.ap[:, cs], Et.ap[:, cs], AF.Ln, [Et], [SPt], bias=1.0)
                    if midx is not None:
                        mask_op(SPt, qc, cs, midx, True)

                def s1a(ARG=ARG, kap=kap, qap=qap, cs=cs, kT=kT):
                    B.mm(ARG.ap[:, cs], kap, qap, True, False, [kT, qT_sb], [ARG])

                def s1(ARG=ARG, SPt=SPt, PTt=PTt, cs=cs, midx=midx, first=first, last=last,
                       Rcur=Rcur, Rnext=Rnext, qc=qc, W=W):
                    B.mm(ARG.ap[:, cs], negtri.ap, SPt.ap[:, cs], False, first, [negtri, SPt], [ARG])
                    if not first:
                        B.mm(ARG.ap[:, cs], negones.ap, Rcur.ap[:, cs], False, True, [negones, Rcur], [ARG])
                    if first:
                        B.memset("pool", Rcur.ap, 0.0, [Rcur])
                        B.memset("pool", Rnext.ap, 0.0, [Rnext])
                    if not last:
                        B.tt("dve", Rnext.ap[:, cs], Rcur.ap[:, cs], SPt.ap[:, cs], ALU.add, [Rcur, SPt], [Rnext])
                    B.act(PTt.ap[:, cs], ARG.ap[:, cs], AF.Exp, [ARG], [PTt])
                    if midx is not None:
                        mask_op(PTt, qc, cs, midx, True)

                def s2(acc=acc, PTt=PTt, cs=cs, kb=kb, first=first, last=last, gi=gi, vt=vt, col0=col0, W=W):
                    B.mm(acc.ap[:, cs], vt.ap[:, h % 2, kb * 65:kb * 65 + 128], PTt.ap[:, cs], first, last, [vt, PTt], [acc], skip=True)
                    if last:
                        tq = tmpo[ch]
                        gsl = slice(col0, col0 + W)
                        B.copy("act", tq.ap[po:po + 64, 0:W], acc.ap[0:64, 0:W], [acc], [tq])
                        B.tt("dve", mixT.ap[po:po + 64, cg, gsl], tq.ap[po:po + 64, 0:W], mixT.ap[po:po + 64, cg, gsl],
                             ALU.mult, [tq, mixT], [mixT])

                units.append([s0, s1a, s1, s2])
        return units

    B.rd, B.bcs, B.ones_f, B.b4, B.lamw = rd, bcs, ones_f, [0], lamw
    softmax_norm = B.softmax_norm
    nextbank4 = B.nextbank4

    def df_units_pair(p, slot):
        kT, vt = kTp[slot], vtp[slot]
        cgk = p
        units = []
        scale = 32 ** -0.5
        ptc = [0]
        for gi in range(NR + 1):
            col0, W, qc, kbs = kb_list(gi)
            accs = [[PS[4 + 2 * hh + cm] for cm in range(2)] for hh in range(2)]
            gsl = slice(col0, col0 + W)
            for ui, (kb, c0, midx) in enumerate(kbs):
                first, last = ui == 0, ui == len(kbs) - 1
                cs = slice(c0, W)
                combos = []
                for hh in range(2):
                    for cm in range(2):
                        r0 = hh * 64 + cm * 32
                        combos.append((hh, cm, nextbank4(), PT[ptc[0] % 8], kT.ap[r0:r0 + 32, kb * 128:(kb + 1) * 128],
                                       qT_df.ap[r0:r0 + 32, cgk, col0 + c0:col0 + W], (r0, 0)))
                        ptc[0] += 1

                def s0(combos=combos, cs=cs, midx=midx, qc=qc):
                    for (hh, cm, Sb, PTt, kap, qap, tp) in combos:
                        B.op("pe", lambda e, Sb=Sb, kap=kap, qap=qap, tp=tp: e.matmul(Sb.ap[:, cs], lhsT=kap, rhs=qap, start=True,
                                                                                     stop=True, tile_position=tp), [kT, qT_df], [Sb])
                    for (hh, cm, Sb, PTt, kap, qap, tp) in combos:
                        B.act(PTt.ap[:, cs], Sb.ap[:, cs], AF.Exp, [Sb], [PTt], scale=scale)
                        if midx is not None:
                            mask_op(PTt, qc, cs, midx, False)

                def s1(combos=combos, cs=cs, kb=kb, first=first, last=last, accs=accs, gsl=gsl, W=W):
                    for (hh, cm, Sb, PTt, kap, qap, tp) in combos:
                        B.mm(accs[hh][cm].ap[:, cs], vt.ap[:, hh, kb * 65:kb * 65 + 128], PTt.ap[:, cs], first, last,
                             [vt, PTt], [accs[hh][cm]], skip=True)
                    if last:
                        for hh in range(2):
                            h = 2 * p + hh
                            po = hh * 64
                            a0, a1 = accs[hh]
                            bc0 = softmax_norm(a0, W)
                            B.tt("dve", ea.ap[0:64, 0:W], a0.ap[0:64, 0:W], bcs.ap[0:64, bc0], ALU.mult, [a0, bcs], [ea])
                            bc1 = softmax_norm(a1, W)
                            B.tt("dve", eb.ap[0:64, 0:W], a1.ap[0:64, 0:W], bcs.ap[0:64, bc1], ALU.mult, [a1, bcs], [eb])
                            B.op("dve", lambda e: e.scalar_tensor_tensor(ea.ap[0:64, 0:W], eb.ap[0:64, 0:W], nlcol.ap[0:64, 1:2],
                                                                         ea.ap[0:64, 0:W], ALU.mult, ALU.add), [ea, eb, nlcol], [ea])
                            B.tt("pool", eb.ap[0:64, 0:W], ea.ap[0:64, 0:W], ea.ap[0:64, 0:W], ALU.mult, [ea], [eb])
                            ss = nextbank4()
                            B.mm(ss.ap[0:64, 0:W], ones_f.ap[0:64, 0:64], eb.ap[0:64, 0:W], True, True, [ones_f, eb], [ss])
                            B.act(ec.ap[0:64, 0:W], ss.ap[0:64, 0:W], AF.Ln, [ss, B.eps_col], [ec], scale=1.0 / 64, bias=B.eps_col.ap[0:64, 0:1])
                            B.act(ec.ap[0:64, 0:W], ec.ap[0:64, 0:W], AF.Exp, [ec], [ec], scale=-0.5)
                            B.tt("dve", ea.ap[0:64, 0:W], ea.ap[0:64, 0:W], ec.ap[0:64, 0:W], ALU.mult, [ea, ec], [ea])
                            tq = tmpo[hh]
                            B.copy("act", tq.ap[po:po + 64, 0:W], ea.ap[0:64, 0:W], [ea], [tq])
                            mcg = 3 + p
                            B.op("dve", lambda e, po=po, tq=tq, mcg=mcg: e.scalar_tensor_tensor(
                                mixT.ap[po:po + 64, mcg, gsl], tq.ap[po:po + 64, 0:W], g1col.ap[po:po + 64, 0:1],
                                mixT.ap[po:po + 64, mcg, gsl], ALU.mult, ALU.mult), [tq, g1col, mixT], [mixT])

                units.append([s0, s1])
        return units

    pairs = [("sb", p) for p in range(3)] + [("df", p) for p in range(3)]
    load_pair(0, 0, 0)
    run_pipeline(B.mem_units(mkT, mv, mqT, mixT, PT, ea, tmpo[1], zc, accc, B.groups), 2)
    for pi, (kind, p) in enumerate(pairs):
        slot = pi % 2
        if pi + 1 < len(pairs):
            k2, p2 = pairs[pi + 1]
            load_pair(p2 if k2 == "sb" else 3 + p2, 2 * p2 if k2 == "sb" else 6 + 2 * p2, (pi + 1) % 2)
        if kind == "sb":
            run_pipeline_chains([sb_units(2 * p, slot), sb_units(2 * p + 1, slot)], [0, 1, 1, 2])
        else:
            run_pipeline(df_units_pair(p, slot), 2)

    B.sa.reset(mA)
    wo = B.tile([128, KC, D], BF16, "wo")
    wst2 = [B.tile([128, 1, 16], F32, "wst2_%d" % i) for i in range(2)]
    gbc = B.tile([128, D], F32, "gbc")
    bbc = B.tile([128, D], F32, "bbc")
    xr = [B.tile([128, D], F32, "xr%d" % i) for i in range(3)]
    vv = [B.tile([128, D], F32, "vv%d" % i) for i in range(3)]
    st = B.tile([128, 8], F32, "st")
    fz2 = B.tile([128, 16], F32, "fz2")
    phaseC_tiles = [wo, gbc, bbc, st, fz2] + wst2 + xr + vv
    B.op("pool", lambda e: e.memset(fz2.ap, 0.0), [], phaseB_tiles + phaseC_tiles)
    B.load_w(w_o, D, wo, wst2)
    B.dma(gbc.ap, ln_g.ap[0:1, :].partition_broadcast(128), [ln_g], [gbc])
    B.dma(bbc.ap, ln_b.ap[0:1, :].partition_broadcast(128), [ln_b], [bbc])
    outs = []
    for j in range(NEB):
        outs.append(B.out_block(j, mixT, wo, x_own, xr[j % 3], vv[j % 3], st, gbc, bbc, y_out, nextbank))
    B.l0_tiles = phaseB_tiles + phaseC_tiles + [qT_sb, qT_df, mqT, mixT, mkT, mv]
    B.eps_ready = True
    return outs


DBG = {}
QCOLS = [0, 4, 1, 5, 2, 6, 3, 7, 8, 8, 9, 9, 10, 10, 11, 11]


def _qpos(hq):
    if hq < 4:
        return hq, 0
    if hq < 8:
        return hq - 4, 64
    return 4 + hq - 8, 0


def l1_body(B):
    nc = B.nc
    S = B.S
    NB, NJ, NO, NR, NE, NEB = B.NB, B.NJ, B.NO, B.NR, B.NE, B.NEB
    PS = B.psum
    etiles = [(a, min(512, NE - a)) for a in range(0, NE, 512)]
    x1_ext = B.x1_ext_d
    pos_ext = B.dram["pos_ext"]
    memT = B.dram["memT"]
    w1_kv = B.din("w1_kv", [D, 960])
    w1_q = B.din("w1_q", [D, 2048])
    w1_g = B.din("w1_g", [D, 1024])
    w1_mkv = B.din("w1_mkv", [D, 512])
    w1_o = B.din("w1_o", [D, D])
    ln1_g = B.din("ln1_g", [1, D])
    ln1_b = B.din("ln1_b", [1, D])
    sinks_x = B.din("sinks_x", [1, 1536])
    invf1_d = B.din("invf1", [128, 1])
    coef1_d = B.din("coefp1", [128, 1])
    masks1_d = B.din("masks1", [128, 3 * 128], BF16)
    ident_d = B.din("ident", [128, 128])
    y_out = B.dout("y", [NO, D])

    pbc = [0]

    def nextbank():
        b = PS[pbc[0] % 8]
        pbc[0] += 1
        return b

    B.sa.reset(B.mark_consts)
    new_tiles = []

    def tl(shape, dt, name):
        t = B.tile(shape, dt, name)
        new_tiles.append(t)
        return t

    ident = tl([128, 128], F32, "ident")
    masks1 = tl([128, 3, 128], BF16, "masks1")
    invf1 = tl([128, 1], F32, "invf1")
    coef1 = tl([128, 1], F32, "coef1")
    qT1 = tl([128, 8, NO], BF16, "qT1")
    kT1 = tl([128, 2, NE], BF16, "kT1")
    V1 = tl([128, NEB, 3, 65], BF16, "V1")
    mqT1 = tl([128, 2, NO], BF16, "mqT1")
    mixT1 = tl([128, 8, NO], BF16, "mixT1")
    mkT1 = tl([128, 2, MEM_LEN], BF16, "mkT1")
    mv1 = tl([128, 2, 4, 65], BF16, "mv1")
    fz = tl([128, 16], F32, "fz1")
    mP = B.sa.mark()
    xT1 = tl([128, KC, NE], BF16, "xT1")
    xin = [tl([128, D], F32, "xin%d" % i) for i in range(2)]
    wb1 = tl([128, KC, 1024], BF16, "wb1")
    wst = [tl([128, 1, 16], F32, "wst1_%d" % i) for i in range(2)]
    posi = tl([128, 512], I32, "posi1")
    posf = tl([128, 512], F32, "posf1")
    t1 = tl([128, 512], F32, "t11")
    cosT = tl([128, 512], F32, "cosT1")
    sinS = tl([128, 512], F32, "sinS1")
    tmpa = tl([128, 512], F32, "tmpa1")
    tmpb = tl([128, 512], F32, "tmpb1")
    xs_m = tl([128, KC, MEM_LEN], F32, "xs_m")
    memb = tl([128, KC, MEM_LEN], BF16, "memb1")
    B.op("pool", lambda e: e.memset(fz.ap, 0.0), [], B.l0_tiles + new_tiles)
    B.dma(ident.ap, ident_d.ap[:, :], [ident_d], [ident])
    B.dma(masks1.ap.rearrange("p a b -> p (a b)"), masks1_d.ap[:, :], [masks1_d], [masks1])
    B.dma(invf1.ap, invf1_d.ap[:, :], [invf1_d], [invf1])
    B.dma(coef1.ap, coef1_d.ap[:, :], [coef1_d], [coef1])

    B.mem_kv(memT, w1_mkv, wb1, wst, xs_m, memb, mkT1, mv1, nextbank)
    B.memset("pool", V1.ap, 1.0, [V1])

    xc = [0]

    def transpose_block(e):
        xin_t = xin[xc[0] % 2]
        xc[0] += 1
        B.dma(xin_t.ap, x1_ext.ap[e * 128:(e + 1) * 128, :], [x1_ext], [xin_t])
        for half in range(2):
            ps = nextbank()
            for q in range(4):
                kc = half * 4 + q
                B.op("pe", lambda en, ps=ps, q=q, kc=kc: en.transpose(ps.ap[:, q * 128:(q + 1) * 128],
                                                                     xin_t.ap[:, kc * 128:(kc + 1) * 128], ident.ap),
                     [xin_t, ident], [ps])
            B.copy("act" if half == 0 else "dve", xT1.ap[:, half * 4:(half + 1) * 4, e * 128:(e + 1) * 128],
                   ps.ap[:, 0:512].rearrange("p (a b) -> p a b", a=4), [ps], [xT1])

    def xtile(a0, w):
        t = T(xT1.ap[:, :, a0:a0 + w], "xo_t")
        t.res = xT1.res
        return t

    B.load_w(w1_kv, 960, wb1, wst, 0)
    for (a0, w) in etiles:
        for e in range(a0 // 128, (a0 + w) // 128):
            transpose_block(e)
        xo_t = xtile(a0, w)
        B.rope_tables(pos_ext, a0, w, posi, posf, t1, cosT, sinS, invf1, coef1)
        for cg in range(2):
            psK = nextbank()
            B.proj_fm(psK, wb1, cg * 128, xo_t, w)
            psP = nextbank()
            B.proj_fm(psP, wb1, 256 + cg * 128, xo_t, w)
            B.rope_evac(psK, psP, kT1.ap[:, cg, a0:a0 + w], w, cosT, sinS, tmpa, tmpb, kT1)
        for blk in range(w // 128):
            e = a0 // 128 + blk
            ps = nextbank()
            for kc in range(KC):
                B.mm(ps.ap[:, 0:192], xo_t.ap[:, kc, blk * 128:(blk + 1) * 128], wb1.ap[:, kc, 512:704],
                     kc == 0, kc == KC - 1, [xo_t, wb1], [ps])
            B.copy("act", V1.ap[:, e, :, 0:64], ps.ap[:, 0:192].rearrange("p (h d) -> p h d", h=3), [ps], [V1])
        if a0 < NO:
            for cg in range(2):
                ps = nextbank()
                B.proj_fm(ps, wb1, 704 + cg * 128, xo_t, w)
                B.copy("act", mqT1.ap[:, cg, a0:a0 + w], ps.ap[:, 0:w], [ps], [mqT1])
    for rnd in range(3):
        if rnd < 2:
            B.load_w(T(w1_q.ap[:, rnd * 1024:(rnd + 1) * 1024], "w1q"), 1024, wb1, wst, 0)
        else:
            B.load_w(w1_g, 1024, wb1, wst, 0)
        for (a0, w) in etiles:
            if a0 >= NO:
                continue
            osl = slice(a0, a0 + w)
            xo_t = xtile(a0, w)
            if rnd < 2:
                B.rope_tables(pos_ext, a0, w, posi, posf, t1, cosT, sinS, invf1, coef1)
                for cg in range(4):
                    psK = nextbank()
                    B.proj_fm(psK, wb1, cg * 128, xo_t, w)
                    psP = nextbank()
                    B.proj_fm(psP, wb1, 512 + cg * 128, xo_t, w)
                    B.rope_evac(psK, psP, qT1.ap[:, rnd * 4 + cg, osl], w, cosT, sinS, tmpa, tmpb, qT1)
            else:
                for cg in range(8):
                    ps = nextbank()
                    B.proj_fm(ps, wb1, cg * 128, xo_t, w)
                    B.act(mixT1.ap[:, cg, osl], ps.ap[:, 0:w], AF.Silu, [ps], [mixT1])

    proj_tiles = [xT1] + xin + [wb1] + wst + [posi, posf, t1, cosT, sinS, tmpa, tmpb, xs_m, memb]
    B.sa.reset(mP)
    esink = B.tile([128, 1536], F32, "esink")
    esraw = B.tile([128, 1536], F32, "esraw")
    PT1 = [B.tile([128, 512], BF16, "PT1_%d" % i) for i in range(4)]
    rd = B.tile([128, 1024], F32, "rd1")
    bcs = B.tile([128, 1024], F32, "bcs1")
    eas = [B.tile([128, 512], F32, "ea1_%d" % i) for i in range(2)]
    ea = eas[0]
    tq = [B.tile([128, 512], F32, "tq1_%d" % i) for i in range(3)]
    wo1 = B.tile([128, KC, D], BF16, "wo1")
    wst2 = [B.tile([128, 1, 16], F32, "wst1b_%d" % i) for i in range(2)]
    gbc = B.tile([128, D], F32, "gbc1")
    bbc = B.tile([128, D], F32, "bbc1")
    xr = [B.tile([128, D], F32, "xr1_%d" % i) for i in range(3)]
    vv = [B.tile([128, D], F32, "vv1_%d" % i) for i in range(3)]
    st = B.tile([128, 8], F32, "st1")
    att_tiles = [esink, esraw] + PT1 + [rd, bcs] + eas + tq + [wo1] + wst2 + [gbc, bbc] + xr + vv + [st]
    B.op("pool", lambda e: e.memset(fz.ap, 0.0), [], proj_tiles + att_tiles + [fz])
    B.rd, B.bcs, B.b4 = rd, bcs, [0]
    B.dma(esraw.ap, sinks_x.ap[0:1, :].partition_broadcast(128), [sinks_x], [esraw])
    B.act(esink.ap, esraw.ap, AF.Exp, [esraw], [esink])
    B.load_w(w1_o, D, wo1, wst2)
    B.dma(gbc.ap, ln1_g.ap[0:1, :].partition_broadcast(128), [ln1_g], [gbc])
    B.dma(bbc.ap, ln1_b.ap[0:1, :].partition_broadcast(128), [ln1_b], [bbc])

    zc, accc = [0], [0]
    run_pipeline(B.mem_units(mkT1, mv1, mqT1, mixT1, PT1, ea, tq[2], zc, accc, [(i * 512, 512) for i in range(NR)]), 2)
    units = []
    for j in range(NJ):
        i_run, r = j // 4, j % 4
        e_prev = j - 1 if r > 0 else NJ + i_run
        jsl = slice(j * 128, (j + 1) * 128)
        for kvh in range(3):
            acc = PS[4 + accc[0] % 4]
            accc[0] += 1
            for bi, e_k in enumerate((e_prev, j)):
                zi = zc[0]
                zc[0] += 1
                Sb = B.nextbank4()
                PTt = PT1[zi % 4]
                mi = 0 if bi == 1 else (2 if j == 0 else 1)
                ksl = slice(e_k * 128, (e_k + 1) * 128)

                def s0(Sb=Sb, PTt=PTt, kvh=kvh, jsl=jsl, ksl=ksl, mi=mi):
                    for g in range(4):
                        hq = kvh * 4 + g
                        cgq, po = _qpos(hq)
                        cgk = 0 if kvh < 2 else 1
                        B.mm(Sb.ap[:, g * 128:(g + 1) * 128], kT1.ap[po:po + 64, cgk, ksl], qT1.ap[po:po + 64, cgq, jsl],
                             True, True, [kT1, qT1], [Sb])
                    B.act(PTt.ap[:, 0:512], Sb.ap[:, 0:512], AF.Exp, [Sb], [PTt], scale=0.125)
                    for g in range(4):
                        B.tt("dve", PTt.ap[:, g * 128:(g + 1) * 128], PTt.ap[:, g * 128:(g + 1) * 128],
                             masks1.ap[:, mi, :], ALU.mult, [PTt, masks1], [PTt])

                def s1(acc=acc, PTt=PTt, kvh=kvh, j=j, jsl=jsl, bi=bi, e_k=e_k):
                    B.mm(acc.ap[0:65, 0:512], V1.ap[:, e_k, kvh, 0:65], PTt.ap[:, 0:512], bi == 0, bi == 1, [V1, PTt], [acc], skip=True)

                def s2(acc=acc, kvh=kvh, jsl=jsl):
                        bcol = B.softmax_norm(acc, 512, add_row=(esink.ap[64:65, kvh * 512:(kvh + 1) * 512], esink))
                        ea = eas[accc[0] % 2]
                        B.tt("dve", ea.ap[0:64, :], acc.ap[0:64, 0:512], bcs.ap[0:64, bcol], ALU.mult, [acc, bcs], [ea])
                        tqt = tq[accc[0] % 2]
                        accc[0] += 1
                        for g in range(4):
                            hq = kvh * 4 + g
                            mcg, po = hq // 2, (hq % 2) * 64
                            gs = slice(g * 128, (g + 1) * 128)
                            B.copy("act", tqt.ap[po:po + 64, gs], ea.ap[0:64, gs], [ea], [tqt])
                            B.tt("dve", mixT1.ap[po:po + 64, mcg, jsl], tqt.ap[po:po + 64, gs],
                                 mixT1.ap[po:po + 64, mcg, jsl], ALU.mult, [tqt, mixT1], [mixT1])

                units.append([s0, s1, s2 if bi == 1 else None])
    run_pipeline(units, 3)

    outs = []
    for j in range(NJ):
        outs.append(B.out_block(j, mixT1, wo1, x1_ext, xr[j % 3], vv[j % 3], st, gbc, bbc, y_out, nextbank))
    return outs


def layer_norm_store(B, vv_t, scratch, st, gbc, bbc, y_out, j):
    B.op("dve", lambda e: e.tensor_reduce(st.ap[:, 0:1], vv_t.ap, AX.X, ALU.add), [vv_t], [st])
    B.ts("dve", st.ap[:, 1:2], st.ap[:, 0:1], -1.0 / D, None, ALU.mult, None, [st], [st])
    B.op("act", lambda e: e.activation(out=scratch.ap, in_=vv_t.ap, func=AF.Square, bias=st.ap[:, 1:2], scale=1.0,
                                       accum_out=st.ap[:, 2:3]), [vv_t, st], [scratch, st])
    B.act(st.ap[:, 3:4], st.ap[:, 2:3], AF.Ln, [st, B.eps_col], [st], scale=1.0 / D, bias=B.eps_col.ap[:, 0:1])
    B.act(st.ap[:, 3:4], st.ap[:, 3:4], AF.Exp, [st], [st], scale=-0.5)
    B.ts("dve", vv_t.ap, vv_t.ap, st.ap[:, 1:2], st.ap[:, 3:4], ALU.add, ALU.mult, [vv_t, st], [vv_t])
    B.tt("dve", vv_t.ap, vv_t.ap, gbc.ap, ALU.mult, [vv_t, gbc], [vv_t])
    B.tt("pool", vv_t.ap, vv_t.ap, bbc.ap, ALU.add, [vv_t, bbc], [vv_t])
    return B.dma(y_out.ap[j * 128:(j + 1) * 128, :], vv_t.ap, [vv_t], [y_out])


def _own_idx(S, c):
    NR = S // 2048
    return np.concatenate([np.arange((16 * i + 4 * c) * 128, (16 * i + 4 * c + 4) * 128) for i in range(NR)])


def _halo_idx(S, c):
    NR = S // 2048
    out = []
    for i in range(NR):
        g = 16 * i + 4 * c - 1
        out.append(np.arange(g * 128, (g + 1) * 128) if g >= 0 else np.full(128, -1))
    return np.concatenate(out)


def _mask_dram(c):
    m = _masks(c)
    return np.ascontiguousarray(np.transpose(m, (1, 0, 2)).reshape(128, 9 * 128))


def _mask_tables(c):
    col = np.arange(512, dtype=np.float32)
    qcol = np.zeros((128, 1024), np.float32)
    qcol[:, 0:512] = col[None, :]
    qcol[:, 512:1024] = (col + 1920.0 * np.floor(col / 128.0))[None, :]
    k = np.arange(128, dtype=np.float32)[:, None]
    stab = np.zeros((128, 80), np.float32)
    stab[:, 0:16] = k + 128.0 * np.arange(16, dtype=np.float32)[None, :] - 512.0 * c
    stab[:, 16:80] = k + 128.0 * np.arange(64, dtype=np.float32)[None, :] + 128.0 - 512.0 * c
    return qcol, stab


def _masks1(c):
    k = np.arange(128)[:, None]
    q = np.arange(128)[None, :]
    m = np.zeros((3, 128, 128), np.float32)
    m[0] = (k <= q)
    m[1] = (k > q)
    m[2] = (k > q) if c > 0 else 0.0
    return np.ascontiguousarray(np.transpose(m, (1, 0, 2)).reshape(128, 3 * 128)).astype(ml_dtypes.bfloat16)


def l1_weights(w_in, w_memkv, sinks, w_out, ln_g, ln_b):
    cq, ck, cv = w_in[:, 0:768], w_in[:, 768:960], w_in[:, 960:1152]
    mq, gate = w_in[:, 1152:1408], w_in[:, 1408:2432]
    kd = np.concatenate([ck[:, 0:64], ck[:, 64:128], ck[:, 128:192], ck[:, 128:192]], axis=1)
    pk = _partner_perm(256, 64, 16)
    qr = np.concatenate([cq[:, h * 64:(h + 1) * 64] for h in QCOLS], axis=1)
    pq = _partner_perm(512, 64, 16)
    q0, q1 = qr[:, 0:512], qr[:, 512:1024]
    invf1, coef1 = _host_consts(1)
    return {
        "w1_kv": np.ascontiguousarray(np.concatenate([kd, kd[:, pk], cv, mq], axis=1)),
        "w1_q": np.ascontiguousarray(np.concatenate([q0, q0[:, pq], q1, q1[:, pq]], axis=1)),
        "w1_g": np.ascontiguousarray(gate),
        "w1_mkv": np.ascontiguousarray(w_memkv),
        "w1_o": np.ascontiguousarray(w_out),
        "ln1_g": np.ascontiguousarray(ln_g[None, :]),
        "ln1_b": np.ascontiguousarray(ln_b[None, :]),
        "sinks_x": np.ascontiguousarray(np.repeat(sinks, 128)[None, :]),
        "invf1": invf1, "coefp1": coef1,
        "ident": np.eye(128, dtype=np.float32),
    }


def prep_fused(inp):
    f = lambda a: np.asarray(a)
    x, mem, positions = f(inp["x"]), f(inp["mem"]), f(inp["positions"])
    S = x.shape[1]
    w_in = f(inp["w_in_even"])[0]
    sbq, sbk, sbv = w_in[:, 0:384], w_in[:, 384:768], w_in[:, 768:1152]
    dfq, dfk, dfv = w_in[:, 1152:1536], w_in[:, 1536:1920], w_in[:, 1920:2304]
    mq, gate = w_in[:, 2304:2560], w_in[:, 2560:3584]
    perm = _partner_perm(384, 32, 8)
    invf, coefp = _host_consts(0)
    dsub = f(inp["diff_subln_even"])[0]
    shared = {
        "w_k": np.ascontiguousarray(np.concatenate([sbk, dfk, dfk[:, perm]], axis=1)),
        "w_v": np.ascontiguousarray(np.concatenate([sbv, dfv], axis=1)),
        "w_q": np.ascontiguousarray(np.concatenate([sbq, dfq, dfq[:, perm], mq], axis=1)),
        "w_g": np.ascontiguousarray(gate),
        "w_mkv": np.ascontiguousarray(f(inp["w_memkv_even"])[0]),
        "w_o": np.ascontiguousarray(f(inp["w_out_even"])[0]),
        "ln_g": np.ascontiguousarray(f(inp["ln_g_even"])[0][None, :]),
        "ln_b": np.ascontiguousarray(f(inp["ln_b_even"])[0][None, :]),
        "dlam": np.ascontiguousarray(f(inp["diff_lambda_even"])[0].reshape(1, 128)),
        "subln": np.ascontiguousarray(np.concatenate([dsub, dsub])[:, None]),
        "invf": invf, "coefp": coefp,
    }
    shared.update(l1_weights(f(inp["w_in_odd"])[0], f(inp["w_memkv_odd"])[0], f(inp["sinks_odd"])[0], f(inp["w_out_odd"])[0],
                             f(inp["ln_g_odd"])[0], f(inp["ln_b_odd"])[0]))
    xT = [np.ascontiguousarray(x[b].T) for b in range(x.shape[0])]
    maps = []
    for core in range(8):
        b, c = core // 4, core % 4
        own = _own_idx(S, c)
        hidx = _halo_idx(S, c)
        ext = np.concatenate([own, np.maximum(hidx, 0)])
        qcol, stab = _mask_tables(c)
        m = dict(shared)
        m.update({
            "xT_all": xT[b],
            "xT_ext": np.ascontiguousarray(x[b][ext].T),
            "x_ext": np.ascontiguousarray(x[b][ext]),
            "pos_all": np.ascontiguousarray(positions[b][None, :]).astype(np.int32),
            "pos_ext": np.ascontiguousarray(positions[b][ext][None, :]).astype(np.int32),
            "memT": np.ascontiguousarray(mem[b].T),
            "masks": _mask_dram(c),
            "qcol": qcol, "stab": stab,
            "masks1": _masks1(c),
        })
        maps.append(m)
    return maps


def gather_own(results, key, S, nb=2):
    out = np.zeros((nb, S, D), np.float32)
    for core in range(8):
        b, c = core // 4, core % 4
        out[b, _own_idx(S, c)] = results[core][key]
    return out


def build_fused(S, lambda_init):
    B = Builder(S, 0)
    l0_body(B, lambda_init, True)
    outs = l1_body(B)
    B.sch.emit(final_wait_ops=outs)
    return B


LAMBDA_INIT0 = 0.8 - 0.6 * math.exp(-0.3 * 0)


def kernel(**inputs):
    S = np.asarray(inputs["x"]).shape[1]
    B = build_fused(S, LAMBDA_INIT0)
    maps = prep_fused(inputs)
    res = run_bass_kernel_spmd(B.nc, maps, core_ids=list(range(8)))
    return gather_own(res.results, "y", S)
```

```python
import math
import contextlib
import numpy as np
import ml_dtypes
import concourse.bass as bass
import concourse.mybir as mybir
from concourse.bass_utils import run_bass_kernel_spmd

F32 = mybir.dt.float32
BF16 = mybir.dt.bfloat16
I32 = mybir.dt.int32
AF = mybir.ActivationFunctionType
ALU = mybir.AluOpType
AX = mybir.AxisListType

D = 1024
KC = 8
DEPTH = 2
ALPHA = (2 * DEPTH) ** 0.25
LN_EPS = 1e-5
ROPE_THETA = 500000.0
MEM_LEN = 256
PI = math.pi


class Res:
    __slots__ = ("lw", "rd", "name")

    def __init__(self, name=""):
        self.lw = None
        self.rd = []
        self.name = name


class Sched:
    ENGS = ("pe", "act", "dve", "pool", "sp")
    NSLOT = 14

    def __init__(self, nc):
        self.nc = nc
        self.ops = []

    def add(self, eng, fn, reads=(), writes=(), dma=False):
        idx = len(self.ops)
        deps = set()
        for r in reads:
            if r.lw is not None:
                deps.add(r.lw)
        for w in writes:
            if w.lw is not None:
                deps.add(w.lw)
            deps.update(w.rd)
        for r in reads:
            r.rd.append(idx)
        for w in writes:
            w.lw = idx
            w.rd = []
        deps.discard(idx)
        self.ops.append([eng, fn, deps, dma])
        return idx

    def emit(self, final_wait_ops=()):
        nc = self.nc
        ops = self.ops
        n = len(ops)
        has_dep = [False] * n
        for i, (eng, fn, deps, dma) in enumerate(ops):
            for d in deps:
                if ops[d][0] == "pe" and eng == "pe" and not ops[d][3] and not dma:
                    continue
                has_dep[d] = True
        for d in final_wait_ops:
            has_dep[d] = True
        cnt = {e: 0 for e in self.ENGS}
        dcnt = {e: 0 for e in self.ENGS}
        sig = [None] * n
        for i, (eng, fn, deps, dma) in enumerate(ops):
            if dma:
                k = dcnt[eng]
                dcnt[eng] += 1
                sig[i] = ("d", eng, k % self.NSLOT, 16 * (k // self.NSLOT + 1))
            elif has_dep[i]:
                cnt[eng] += 1
                sig[i] = ("c", eng, cnt[eng])
        engs_used = [e for e in self.ENGS if any(o[0] == e for o in ops)]
        with contextlib.ExitStack() as st:
            csem = {e: st.enter_context(nc.semaphore("c_" + e)) for e in engs_used}
            dsem = {}
            for e in engs_used:
                if dcnt[e] > 0:
                    dsem[e] = [st.enter_context(nc.semaphore("d_%s_%d" % (e, s)))
                               for s in range(min(self.NSLOT, dcnt[e]))]
            block = st.enter_context(nc.Block())
            engobj = {"pe": "tensor", "act": "scalar", "dve": "vector", "pool": "gpsimd", "sp": "sync"}

            def make_stream(ename):
                def stream(eng):
                    waited_c = {}
                    waited_d = {}
                    for i, (e, fn, deps, dma) in enumerate(ops):
                        if e != ename:
                            continue
                        need_c = {}
                        need_d = {}
                        for d in deps:
                            s = sig[d]
                            if s is None:
                                continue
                            if s[0] == "c":
                                if s[1] == "pe" and ename == "pe" and not dma:
                                    continue
                                need_c[s[1]] = max(need_c.get(s[1], 0), s[2])
                            else:
                                key = (s[1], s[2])
                                need_d[key] = max(need_d.get(key, 0), s[3])
                        if dma:
                            s = sig[i]
                            if s[3] > 16:
                                key = (s[1], s[2])
                                need_d[key] = max(need_d.get(key, 0), s[3] - 16)
                        for se, v in need_c.items():
                            if waited_c.get(se, 0) < v:
                                eng.wait_ge(csem[se], v)
                                waited_c[se] = v
                        for key, v in need_d.items():
                            if waited_d.get(key, 0) < v:
                                eng.wait_ge(dsem[key[0]][key[1]], v)
                                waited_d[key] = v
                        ins = fn(eng)
                        s = sig[i]
                        if s is not None:
                            if s[0] == "c":
                                ins.then_inc(csem[ename], 1)
                            else:
                                ins.then_inc(dsem[ename][s[2]], 16)
                    if ename == "sp":
                        for d in final_wait_ops:
                            s = sig[d]
                            if s[0] == "c":
                                eng.wait_ge(csem[s[1]], s[2])
                            else:
                                eng.wait_ge(dsem[s[1]][s[2]], s[3])
                return stream

            for e in engs_used:
                getattr(block, engobj[e])(make_stream(e))


class SbufAlloc:
    def __init__(self, nc, nbytes=207 * 1024):
        self.nc = nc
        self.arena = nc.alloc_sbuf_tensor("arena", [128, nbytes], mybir.dt.uint8)
        self.off = 0
        self.limit = nbytes
        self.peak = 0

    def mark(self):
        return self.off

    def reset(self, m):
        self.off = m

    def tile(self, shape, dtype):
        assert shape[0] == 128
        esz = {F32: 4, BF16: 2, I32: 4}[dtype]
        nel = int(np.prod(shape[1:]))
        nbytes = (esz * nel + 63) // 64 * 64
        off = self.off
        self.off += nbytes
        self.peak = max(self.peak, self.off)
        assert self.off <= self.limit, ("SBUF overflow", self.off)
        ap = self.arena.ap()[:, off:off + esz * nel].bitcast(dtype)
        if len(shape) == 3:
            ap = ap.rearrange("p (a b) -> p a b", a=shape[1])
        elif len(shape) == 4:
            ap = ap.rearrange("p (a b c) -> p a b c", a=shape[1], b=shape[2])
        return ap


class T:
    def __init__(self, ap, name=""):
        self.ap = ap
        self.res = Res(name)


class Builder:
    def __init__(self, S, layer):
        self.S = S
        self.layer = layer
        self.NB = S // 128
        self.NJ = self.NB // 4
        self.NO = self.NJ * 128
        self.NR = self.NB // 16
        self.NE = self.NO + self.NR * 128
        self.NEB = self.NE // 128
        self.QGB = 4
        self.GW = 512
        self.NQG = self.NR
        self.groups = [(i * 512, 512) for i in range(self.NR)] + [(self.NO, self.NR * 128)]
        self.TW = min(512, S)
        self.NT = S // self.TW
        self.nc = bass.Bass("TRN2", target_bir_lowering=False)
        self.snc = 0
        self.sch = Sched(self.nc)
        self.sa = SbufAlloc(self.nc)
        self.dram = {}
        self.psum = [T(self.nc.alloc_psum_tensor("ps%d" % i, [128, 512], F32).ap(), "ps%d" % i) for i in range(8)]

    def din(self, name, shape, dtype=F32):
        t = T(self.nc.dram_tensor(name, list(shape), dtype, kind="ExternalInput").ap(), name)
        self.dram[name] = t
        return t

    def dout(self, name, shape, dtype=F32):
        t = T(self.nc.dram_tensor(name, list(shape), dtype, kind="ExternalOutput").ap(), name)
        self.dram[name] = t
        return t

    def dscr(self, name, shape, dtype):
        t = T(self.nc.dram_tensor(name, list(shape), dtype).ap(), name)
        self.dram[name] = t
        return t

    def tile(self, shape, dtype, name=""):
        return T(self.sa.tile(shape, dtype), name)

    def op(self, eng, fn, reads=(), writes=(), dma=False):
        return self.sch.add(eng, fn, [t.res for t in reads], [t.res for t in writes], dma)

    def dma(self, out_ap, in_ap, reads, writes, q="sp"):
        return self.op(q, lambda e: e.dma_start(out=out_ap, in_=in_ap), reads, writes, dma=True)

    def mm(self, out_ap, lhsT, rhs, start, stop, reads, writes, skip=False):
        if skip:
            return self.op("pe", lambda e: e.matmul(out_ap, lhsT=lhsT, rhs=rhs, start=start, stop=stop,
                                                    skip_group_check=True), reads, writes)
        return self.op("pe", lambda e: e.matmul(out_ap, lhsT=lhsT, rhs=rhs, start=start, stop=stop), reads, writes)

    def act(self, out_ap, in_ap, func, reads, writes, scale=1.0, bias=0.0):
        return self.op("act", lambda e: e.activation(out=out_ap, in_=in_ap, func=func, bias=bias, scale=scale),
                       reads, writes)

    def tt(self, eng, out_ap, a, b, op, reads, writes):
        return self.op(eng, lambda e: e.tensor_tensor(out_ap, a, b, op), reads, writes)

    def ts(self, eng, out_ap, a, s1, s2, op0, op1, reads, writes):
        if op1 is None:
            return self.op(eng, lambda e: e.tensor_scalar(out_ap, a, s1, None, op0), reads, writes)
        return self.op(eng, lambda e: e.tensor_scalar(out_ap, a, s1, s2, op0, op1), reads, writes)

    def copy(self, eng, out_ap, in_ap, reads, writes):
        if eng == "act":
            return self.op("act", lambda e: e.copy(out_ap, in_ap), reads, writes)
        return self.op(eng, lambda e: e.tensor_copy(out_ap, in_ap), reads, writes)

    def memset(self, eng, ap, val, writes):
        return self.op(eng, lambda e: e.memset(ap, val), (), writes)

    def load_w(self, wd, n, wb, stage=None, c0=0):
        src = wd.ap.rearrange("(kc p) n -> p kc n", p=128)
        for k2 in range(0, KC, 2):
            self.dma(wb.ap[:, k2:k2 + 2, c0:c0 + n], src[:, k2:k2 + 2, :], [wd], [wb], q="pool")

    def rope_tables(self, pos_d, a, w, posi, posf, t1, cosT, sinS, invf, coefp):
        SC = 2 * PI * (1.0 - 1e-6)
        self.dma(posi.ap[:, 0:w], pos_d.ap[0:1, a:a + w].partition_broadcast(128), [pos_d], [posi])
        self.copy("act", posf.ap[:, 0:w], posi.ap[:, 0:w], [posi], [posf])
        self.ts("dve", posf.ap[:, 0:w], posf.ap[:, 0:w], invf.ap[:, 0:1], None, ALU.mult, None, [posf, invf], [posf])
        for (dst, off) in ((sinS, 0.0), (cosT, 0.25)):
            if off:
                self.ts("dve", posf.ap[:, 0:w], posf.ap[:, 0:w], off, None, ALU.add, None, [posf], [posf])
            self.copy("dve", posi.ap[:, 0:w], posf.ap[:, 0:w], [posf], [posi])
            self.copy("act", t1.ap[:, 0:w], posi.ap[:, 0:w], [posi], [t1])
            self.op("dve", lambda e: e.scalar_tensor_tensor(t1.ap[:, 0:w], t1.ap[:, 0:w], -1.0, posf.ap[:, 0:w],
                                                            ALU.mult, ALU.add), [t1, posf], [t1])
            self.act(dst.ap[:, 0:w], t1.ap[:, 0:w], AF.Sin, [t1], [dst], scale=SC)
        self.ts("dve", sinS.ap[:, 0:w], sinS.ap[:, 0:w], coefp.ap[:, 0:1], None, ALU.mult, None, [sinS, coefp], [sinS])

    def proj_fm(self, ps, wb, c0, xb, w, m=128):
        for kc in range(KC):
            self.mm(ps.ap[0:m, 0:w], wb.ap[:, kc, c0:c0 + m], xb.ap[:, kc, 0:w], kc == 0, kc == KC - 1, [wb, xb], [ps])


    def nextbank4(self):
        b = self.psum[self.b4[0] % 4]
        self.b4[0] += 1
        return b

    def softmax_norm(self, acc, W, add_row=None, split=False):
        rd, bcs, ones_f = self.rd, self.bcs, self.ones_f
        k = self.snc % 2
        self.snc += 1
        cols = slice(k * 512, k * 512 + W)
        if add_row is not None:
            self.tt("dve", rd.ap[64:65, cols], acc.ap[64:65, 0:W], add_row[0], ALU.add, [acc, add_row[1]], [rd])
        else:
            self.ts("dve", rd.ap[64:65, cols], acc.ap[64:65, 0:W], 1e-18, None, ALU.add, None, [acc], [rd])
        def part2():
            bc = self.nextbank4()
            self.mm(bc.ap[0:64, 0:W], ones_f.ap[64:65, 0:64], rd.ap[64:65, cols], True, True, [ones_f, rd], [bc])
            self.act(bcs.ap[0:64, cols], bc.ap[0:64, 0:W], AF.Ln, [bc], [bcs])
            self.act(bcs.ap[0:64, cols], bcs.ap[0:64, cols], AF.Exp, [bcs], [bcs], scale=-1.0)
            return cols
        if split:
            return part2
        return part2()

    def mem_units(self, mkT, mv, mqT, mixT, PT, ea, tq, zc, accc, groups):
        B = self
        PS = self.psum
        units = []
        for hm in range(4):
            cg, po = hm // 2, (hm % 2) * 64
            for (col0, W) in groups:
                acc = PS[4 + accc[0] % 4]
                accc[0] += 1
                gsl = slice(col0, col0 + W)
                for mb in range(2):
                    zi = zc[0]
                    zc[0] += 1
                    Sb = B.nextbank4()
                    PTt = PT[zi % len(PT)]

                    def s0(Sb=Sb, PTt=PTt, mb=mb, cg=cg, po=po, gsl=gsl, W=W):
                        B.mm(Sb.ap[:, 0:W], mkT.ap[po:po + 64, cg, mb * 128:(mb + 1) * 128], mqT.ap[po:po + 64, cg, gsl],
                             True, True, [mkT, mqT], [Sb])
                        B.act(PTt.ap[:, 0:W], Sb.ap[:, 0:W], AF.Exp, [Sb], [PTt], scale=0.125)

                    def s1(acc=acc, PTt=PTt, mb=mb, hm=hm, cg=cg, po=po, gsl=gsl, W=W):
                        B.mm(acc.ap[0:65, 0:W], mv.ap[:, mb, hm, 0:65], PTt.ap[:, 0:W], mb == 0, mb == 1, [mv, PTt], [acc], skip=True)
                        if mb == 1:
                            bcol = B.softmax_norm(acc, W)
                            B.tt("dve", ea.ap[0:64, 0:W], acc.ap[0:64, 0:W], B.bcs.ap[0:64, bcol], ALU.mult, [acc, B.bcs], [ea])
                            B.copy("act", tq.ap[po:po + 64, 0:W], ea.ap[0:64, 0:W], [ea], [tq])
                            B.tt("dve", mixT.ap[po:po + 64, 6 + cg, gsl], tq.ap[po:po + 64, 0:W], mixT.ap[po:po + 64, 6 + cg, gsl],
                                 ALU.mult, [tq, mixT], [mixT])

                    units.append([s0, s1])
        return units

    def mem_kv(self, memT, w_mkv, wb, wst, xs0, memb, mkT, mv, nextbank):
        B = self
        B.load_w(w_mkv, 512, wb, wst)
        B.dma(xs0.ap[:, :, 0:MEM_LEN], memT.ap.rearrange("(kc p) m -> p kc m", p=128), [memT], [xs0])
        B.copy("dve", memb.ap[:, :, 0:MEM_LEN], xs0.ap[:, :, 0:MEM_LEN], [xs0], [memb])
        for cg in range(2):
            ps = nextbank()
            B.proj_fm(ps, wb, cg * 128, memb, MEM_LEN)
            B.copy("act", mkT.ap[:, cg, :], ps.ap[:, 0:MEM_LEN], [ps], [mkT])
        B.memset("pool", mv.ap, 1.0, [mv])
        for mb in range(2):
            ps = nextbank()
            for kc in range(KC):
                B.mm(ps.ap[:, 0:256], memb.ap[:, kc, mb * 128:(mb + 1) * 128], wb.ap[:, kc, 256:512], kc == 0, kc == KC - 1,
                     [memb, wb], [ps])
            B.copy("act", mv.ap[:, mb, :, 0:64], ps.ap[:, 0:256].rearrange("p (h d) -> p h d", h=4), [ps], [mv])

    def out_blocks(self, nblk, mixT, wo, x_src, xr, vv, sts, gbc, bbc, y_dst, nextbank):
        B = self
        outs = []
        units = []
        nb = len(xr)
        for j in range(nblk):
            xr_t, vv_t, st = xr[j % nb], vv[j % nb], sts[j % nb]

            def s0(j=j, xr_t=xr_t, vv_t=vv_t, st=st):
                B.dma(xr_t.ap, x_src.ap[j * 128:(j + 1) * 128, :], [x_src], [xr_t])
                for n in range(2):
                    ps = nextbank()
                    for kc in range(KC):
                        B.mm(ps.ap[:, 0:512], mixT.ap[:, kc, j * 128:(j + 1) * 128], wo.ap[:, kc, n * 512:(n + 1) * 512],
                             kc == 0, kc == KC - 1, [mixT, wo], [ps])
                    B.op("dve", lambda e, ps=ps, n=n: e.scalar_tensor_tensor(
                        vv_t.ap[:, n * 512:(n + 1) * 512], xr_t.ap[:, n * 512:(n + 1) * 512], ALPHA, ps.ap[:, 0:512],
                        ALU.mult, ALU.add), [ps, xr_t], [vv_t])
                B.op("dve", lambda e: e.tensor_reduce(st.ap[:, 0:1], vv_t.ap, AX.X, ALU.add), [vv_t], [st])
                B.ts("dve", st.ap[:, 1:2], st.ap[:, 0:1], -1.0 / D, None, ALU.mult, None, [st], [st])
                B.op("act", lambda e: e.activation(out=xr_t.ap, in_=vv_t.ap, func=AF.Square, bias=st.ap[:, 1:2], scale=1.0,
                                                   accum_out=st.ap[:, 2:3]), [vv_t, st], [xr_t, st])
                B.act(st.ap[:, 3:4], st.ap[:, 2:3], AF.Ln, [st, B.eps_col], [st], scale=1.0 / D, bias=B.eps_col.ap[:, 0:1])
                B.act(st.ap[:, 3:4], st.ap[:, 3:4], AF.Exp, [st], [st], scale=-0.5)

            def s1(j=j, vv_t=vv_t, st=st):
                B.ts("dve", vv_t.ap, vv_t.ap, st.ap[:, 1:2], st.ap[:, 3:4], ALU.add, ALU.mult, [vv_t, st], [vv_t])
                B.tt("dve", vv_t.ap, vv_t.ap, gbc.ap, ALU.mult, [vv_t, gbc], [vv_t])
                B.tt("pool", vv_t.ap, vv_t.ap, bbc.ap, ALU.add, [vv_t, bbc], [vv_t])
                outs.append(B.dma(y_dst.ap[j * 128:(j + 1) * 128, :], vv_t.ap, [vv_t], [y_dst]))

            units.append([s0, s1])
        run_pipeline(units, 2)
        return outs

    def rope_evac(self, psK, psP, out_ap, w, cosT, sinS, tmpa, tmpb, out_t, scale=None):
        self.tt("dve", tmpa.ap[:, 0:w], psK.ap[:, 0:w], cosT.ap[:, 0:w], ALU.mult, [psK, cosT], [tmpa])
        self.tt("dve", tmpb.ap[:, 0:w], psP.ap[:, 0:w], sinS.ap[:, 0:w], ALU.mult, [psP, sinS], [tmpb])
        self.tt("dve", out_ap, tmpa.ap[:, 0:w], tmpb.ap[:, 0:w], ALU.add, [tmpa, tmpb], [out_t])


def _host_consts(layer):
    if layer == 0:
        hd, rot = 32, 8
    else:
        hd, rot = 64, 16
    half = rot // 2
    inv = np.exp(-(np.arange(half, dtype=np.float32) / half) * math.log(ROPE_THETA)).astype(np.float32)
    invf = np.zeros((128, 1), np.float32)
    coef = np.zeros((128, 1), np.float32)
    for r in range(128):
        d = r % hd
        if d < rot:
            invf[r, 0] = inv[d % half] / np.float32(2 * PI)
            coef[r, 0] = -1.0 if d < half else 1.0
    return invf, coef


def _partner_perm(ncols, hd, rot):
    half = rot // 2
    perm = np.arange(ncols)
    for c in range(ncols):
        d = c % hd
        if d < half:
            perm[c] = c + half
        elif d < rot:
            perm[c] = c - half
    return perm


def _masks(c):
    k = np.arange(128)[:, None]
    q = np.arange(128)[None, :]
    out = np.zeros((9, 128, 128), np.float32)
    out[8] = -1.0 * (k >= q)
    for r in range(4):
        if r < c:
            out[r] = 1.0
            out[4 + r] = 1.0
        elif r == c:
            out[r] = (k <= q)
            out[4 + r] = (k < q)
    return out.astype(ml_dtypes.bfloat16)


def run_pipeline_chains(chains, skew):
    n = len(chains[0])
    depth = max(skew) + 1
    for step in range(n + depth - 1):
        for s, d in enumerate(skew):
            u = step - d
            if 0 <= u < n:
                for ch in chains:
                    if ch[u][s] is not None:
                        ch[u][s]()


def run_pipeline(units, nst):
    n = len(units)
    for step in range(n + nst - 1):
        for s in range(nst):
            u = step - s
            if 0 <= u < n and units[u][s] is not None:
                units[u][s]()


def build_l0(S, lambda_init):
    B = Builder(S, 0)
    outs = l0_body(B, lambda_init, False)
    B.sch.emit(final_wait_ops=outs)
    return B


def l0_body(B, lambda_init, fused):
    nc = B.nc
    S = B.S
    NB, NJ, NO, QGB, GW, NQG, TW, NT = B.NB, B.NJ, B.NO, B.QGB, B.GW, B.NQG, B.TW, B.NT
    NR, NE, NEB = B.NR, B.NE, B.NEB
    etiles = [(a, min(512, NE - a)) for a in range(0, NE, 512)]
    xT_all = B.din("xT_all", [D, S])
    xT_own = B.din("xT_ext", [D, NE])
    x_own = B.din("x_ext", [NE, D])
    pos_all = B.din("pos_all", [1, S], I32)
    pos_own = B.din("pos_ext", [1, NE], I32)
    w_k = B.din("w_k", [D, 1152])
    w_v = B.din("w_v", [D, 768])
    w_q = B.din("w_q", [D, 1408])
    w_g = B.din("w_g", [D, 1024])
    memT = B.din("memT", [D, MEM_LEN])
    w_mkv = B.din("w_mkv", [D, 512])
    w_o = B.din("w_o", [D, D])
    ln_g = B.din("ln_g", [1, D])
    ln_b = B.din("ln_b", [1, D])
    dlam = B.din("dlam", [1, 128])
    subln = B.din("subln", [128, 1])
    invf_d = B.din("invf", [128, 1])
    coef_d = B.din("coefp", [128, 1])
    masks_d = B.din("masks", [128, 9 * 128], BF16)
    qcol_d = B.din("qcol", [128, 1024])
    stab_d = B.din("stab", [128, 16 + 64])
    y_out = B.dscr("x1_ext", [NE, D], F32)
    B.x1_ext_d = y_out
    kT_scr = B.dscr("kT_scr", [128, 6, S], BF16)
    v_scr = B.dscr("v_scr", [128, 12, NB * 65], BF16)

    masks = B.tile([128, 9, 128], BF16, "masks")
    negtri = T(masks.ap[:, 8, :], "negtri")
    negtri.res = masks.res
    negones = B.tile([128, 128], BF16, "negones")
    ones_f = B.tile([128, 128], F32, "ones_f")
    invf = B.tile([128, 1], F32, "invf")
    coefp = B.tile([128, 1], F32, "coefp")
    B.pi_col = B.tile([128, 1], F32, "pi")
    g1col = B.tile([128, 1], F32, "g1col")
    lam_t = B.tile([128, 128], F32, "lam")
    lamw = B.tile([128, 8], F32, "lamw")
    nlcol = B.tile([128, 2], F32, "nlcol")
    B.eps_col = B.tile([128, 1], F32, "eps")
    prod = B.tile([128, 64], F32, "prod")
    qcol = B.tile([128, 1024], F32, "qcol")
    stab = B.tile([128, 80], F32, "stab")
    B.mark_consts = B.sa.mark()
    qT_sb = B.tile([128, 3, NE], BF16, "qT_sb")
    qT_df = B.tile([128, 3, NE], BF16, "qT_df")
    mqT = B.tile([128, 2, NE], BF16, "mqT")
    mixT = B.tile([128, 8, NE], BF16, "mixT")
    mkT = B.tile([128, 2, MEM_LEN], BF16, "mkT")
    mv = B.tile([128, 2, 4, 65], BF16, "mv")
    PS = B.psum
    pbc = [0]

    def nextbank():
        b = PS[pbc[0] % 8]
        pbc[0] += 1
        return b

    B.dma(masks.ap.rearrange("p a b -> p (a b)"), masks_d.ap[:, :], [masks_d], [masks])
    B.dma(invf.ap, invf_d.ap[:, :], [invf_d], [invf])
    B.dma(coefp.ap, coef_d.ap[:, :], [coef_d], [coefp])
    B.dma(qcol.ap, qcol_d.ap[:, :], [qcol_d], [qcol])
    B.dma(stab.ap, stab_d.ap[:, :], [stab_d], [stab])
    B.memset("pool", B.pi_col.ap, PI, [B.pi_col])
    B.memset("pool", B.eps_col.ap, LN_EPS, [B.eps_col])
    B.memset("pool", ones_f.ap, 1.0, [ones_f])
    B.memset("pool", negones.ap, -1.0, [negones])
    B.dma(lam_t.ap[64:65, 0:128], dlam.ap[0:1, :], [dlam], [lam_t])
    B.tt("dve", prod.ap[64:65, 0:32], lam_t.ap[64:65, 0:32], lam_t.ap[64:65, 32:64], ALU.mult, [lam_t], [prod])
    B.tt("dve", prod.ap[64:65, 32:64], lam_t.ap[64:65, 64:96], lam_t.ap[64:65, 96:128], ALU.mult, [lam_t], [prod])
    B.op("dve", lambda e: e.tensor_reduce(lamw.ap[64:65, 0:1], prod.ap[64:65, 0:32], AX.X, ALU.add), [prod], [lamw])
    B.op("dve", lambda e: e.tensor_reduce(lamw.ap[64:65, 1:2], prod.ap[64:65, 32:64], AX.X, ALU.add), [prod], [lamw])
    B.act(lamw.ap[64:65, 2:4], lamw.ap[64:65, 0:2], AF.Exp, [lamw], [lamw])
    B.tt("dve", lamw.ap[64:65, 4:5], lamw.ap[64:65, 2:3], lamw.ap[64:65, 3:4], ALU.subtract, [lamw], [lamw])
    B.ts("dve", lamw.ap[64:65, 5:6], lamw.ap[64:65, 4:5], lambda_init, -1.0, ALU.add, ALU.mult, [lamw], [lamw])
    nlps = nextbank()
    B.mm(nlps.ap[0:64, 0:2], ones_f.ap[64:65, 0:64], lamw.ap[64:65, 4:6], True, True, [ones_f, lamw], [nlps])
    B.copy("dve", nlcol.ap[0:64, 0:2], nlps.ap[0:64, 0:2], [nlps], [nlcol])
    B.dma(g1col.ap, subln.ap[:, :], [subln], [g1col])
    B.ts("dve", g1col.ap, g1col.ap, 1.0 - lambda_init, None, ALU.mult, None, [g1col], [g1col])

    mA = B.sa.mark()
    wb = B.tile([128, KC, 1920], BF16, "wb")
    wst = [B.tile([128, 1, 16], F32, "wst%d" % i) for i in range(2)]
    xs = [B.tile([128, KC, TW], F32, "xs%d" % i) for i in range(2)]
    xb = [B.tile([128, KC, TW], BF16, "xb%d" % i) for i in range(2)]
    posi = B.tile([128, TW], I32, "posi")
    posf = B.tile([128, TW], F32, "posf")
    t1 = B.tile([128, TW], F32, "t1")
    cosT = B.tile([128, TW], F32, "cosT")
    sinS = B.tile([128, TW], F32, "sinS")
    tmpa = B.tile([128, TW], F32, "tmpa")
    tmpb = B.tile([128, TW], F32, "tmpb")
    ktst = [B.tile([128, 6, TW], BF16, "ktst%d" % i) for i in range(1)]
    vst = [B.tile([128, 12, TW // 128, 65], BF16, "vst%d" % i) for i in range(2)]

    B.mem_kv(memT, w_mkv, wb, wst, xs[0], xb[0], mkT, mv, nextbank)

    B.load_w(w_k, 1152, wb, wst, 0)
    B.load_w(w_v, 768, wb, wst, 1152)
    for i in range(2):
        B.memset("pool", vst[i].ap, 1.0, [vst[i]])
    xsrc = xT_all.ap.rearrange("(kc p) s -> p kc s", p=128)
    cosTs = [cosT, B.tile([128, TW], F32, "cosT_b")]
    sinSs = [sinS, B.tile([128, TW], F32, "sinS_b")]

    def a1_prefetch(t):
        xs_t, xb_t = xs[t % 2], xb[t % 2]
        for hlf in range(2):
            B.dma(xs_t.ap[:, hlf * 4:(hlf + 1) * 4, :], xsrc[:, hlf * 4:(hlf + 1) * 4, t * TW:(t + 1) * TW], [xT_all], [xs_t])
        for kc in range(KC):
            B.copy(("act", "dve", "act", "dve", "pool", "act", "dve", "pool")[kc], xb_t.ap[:, kc, :], xs_t.ap[:, kc, :], [xs_t], [xb_t])
        B.rope_tables(pos_all, t * TW, TW, posi, posf, t1, cosTs[t % 2], sinSs[t % 2], invf, coefp)

    a1_prefetch(0)
    for t in range(NT):
        xs_t, xb_t = xs[t % 2], xb[t % 2]
        kt_t, v_t = ktst[0], vst[t % 2]
        cosT, sinS = cosTs[t % 2], sinSs[t % 2]
        if t + 1 < NT:
            a1_prefetch(t + 1)
        for cg in range(3):
            ps = nextbank()
            B.proj_fm(ps, wb, cg * 128, xb_t, TW)
            B.copy("act", kt_t.ap[:, cg, :], ps.ap[:, 0:TW], [ps], [kt_t])
        for cg in range(3):
            psK = nextbank()
            B.proj_fm(psK, wb, 384 + cg * 128, xb_t, TW)
            psP = nextbank()
            B.proj_fm(psP, wb, 768 + cg * 128, xb_t, TW)
            B.rope_evac(psK, psP, kt_t.ap[:, 3 + cg, :], TW, cosT, sinS, tmpa, tmpb, kt_t)
        B.dma(kT_scr.ap[:, :, t * TW:(t + 1) * TW], kt_t.ap, [kt_t], [kT_scr])
        for blk in range(TW // 128):
            for half in range(2):
                ps = nextbank()
                for kc in range(KC):
                    B.mm(ps.ap[:, 0:384], xb_t.ap[:, kc, blk * 128:(blk + 1) * 128],
                         wb.ap[:, kc, 1152 + half * 384:1152 + (half + 1) * 384], kc == 0, kc == KC - 1, [xb_t, wb], [ps])
                B.copy("act" if half == 0 else "dve", v_t.ap[:, half * 6:(half + 1) * 6, blk, 0:64],
                       ps.ap[:, 0:384].rearrange("p (h d) -> p h d", h=6), [ps], [v_t])
        nb_t = TW // 128
        B.dma(v_scr.ap[:, :, t * nb_t * 65:(t + 1) * nb_t * 65], v_t.ap.rearrange("p h b c -> p h (b c)"), [v_t], [v_scr])

    cosT, sinS = cosTs[0], sinSs[0]
    xosrc = xT_own.ap.rearrange("(kc p) s -> p kc s", p=128)
    for rnd in range(2):
        if rnd == 0:
            B.load_w(w_q, 1408, wb, wst, 0)
        else:
            B.load_w(w_g, 1024, wb, wst, 0)
        def a2_prefetch(u, rnd=rnd):
            ea0, OW = etiles[u]
            xs_t, xb_t = xs[u % 2], xb[u % 2]
            osl = slice(ea0, ea0 + OW)
            for hlf in range(2):
                B.dma(xs_t.ap[:, hlf * 4:(hlf + 1) * 4, 0:OW], xosrc[:, hlf * 4:(hlf + 1) * 4, osl], [xT_own], [xs_t])
            for kc in range(KC):
                B.copy(("act", "dve", "act", "dve", "pool", "act", "dve", "pool")[kc], xb_t.ap[:, kc, 0:OW], xs_t.ap[:, kc, 0:OW], [xs_t], [xb_t])
            if rnd == 0:
                B.rope_tables(pos_own, ea0, OW, posi, posf, t1, cosTs[u % 2], sinSs[u % 2], invf, coefp)

        a2_prefetch(0)
        for u, (ea0, OW) in enumerate(etiles):
            xs_t, xb_t = xs[u % 2], xb[u % 2]
            osl = slice(ea0, ea0 + OW)
            cosT, sinS = cosTs[u % 2], sinSs[u % 2]
            if u + 1 < len(etiles):
                a2_prefetch(u + 1)
            if rnd == 0:
                for cg in range(3):
                    ps = nextbank()
                    B.proj_fm(ps, wb, cg * 128, xb_t, OW)
                    B.act(qT_sb.ap[:, cg, osl], ps.ap[:, 0:OW], AF.Copy, [ps], [qT_sb], scale=0.125)
                for cg in range(3):
                    psK = nextbank()
                    B.proj_fm(psK, wb, 384 + cg * 128, xb_t, OW)
                    psP = nextbank()
                    B.proj_fm(psP, wb, 768 + cg * 128, xb_t, OW)
                    B.rope_evac(psK, psP, qT_df.ap[:, cg, osl], OW, cosT, sinS, tmpa, tmpb, qT_df)
                for cg in range(2):
                    ps = nextbank()
                    B.proj_fm(ps, wb, 1152 + cg * 128, xb_t, OW)
                    B.copy("act", mqT.ap[:, cg, osl], ps.ap[:, 0:OW], [ps], [mqT])
            else:
                for cg in range(8):
                    ps = nextbank()
                    B.proj_fm(ps, wb, cg * 128, xb_t, OW)
                    B.act(mixT.ap[:, cg, osl], ps.ap[:, 0:OW], AF.Silu, [ps], [mixT])

    phaseA_tiles = [wb] + wst + xs + xb + [posi, posf, t1, tmpa, tmpb] + cosTs + sinSs + ktst + vst
    B.sa.reset(mA)
    kTp = [B.tile([128, S], BF16, "kTp%d" % i) for i in range(2)]
    vtp = [B.tile([128, 2, NB * 65 + 64], BF16, "vtp%d" % i) for i in range(2)]
    E = [B.tile([128, GW], F32, "E%d" % i) for i in range(4)]
    SP = [B.tile([128, GW], BF16, "SP%d" % i) for i in range(6)]
    PT = [B.tile([128, GW], BF16, "PT%d" % i) for i in range(8)]
    R32s = [B.tile([128, GW], F32, "R32_%d" % i) for i in range(2)]
    Rbs = [[B.tile([128, GW], BF16, "Rb%d_%d" % (ch, i)) for i in range(2)] for ch in range(2)]
    tmpo = [B.tile([128, GW], F32, "tmpo%d" % i) for i in range(2)]
    rd = B.tile([128, 2 * GW], F32, "rd")
    bcs = B.tile([128, 2 * GW], F32, "bcs")
    ea = B.tile([128, GW], F32, "ea")
    eb = B.tile([128, GW], F32, "eb")
    ec = B.tile([128, GW], F32, "ec")
    fz = B.tile([128, 16], F32, "fz")
    phaseB_tiles = kTp + vtp + E + SP + PT + R32s + Rbs[0] + Rbs[1] + tmpo + [rd, bcs, ea, eb, ec, fz]
    B.op("pool", lambda e: e.memset(fz.ap, 0.0), [], phaseA_tiles + phaseB_tiles)
    for i in range(2):
        B.memset("pool", vtp[i].ap[:, :, NB * 65:NB * 65 + 64], 0.0, [vtp[i]])

    kview = kT_scr.ap
    vview = v_scr.ap

    def load_pair(kcg, vh0, slot):
        B.dma(kTp[slot].ap, kview[:, kcg, :], [kT_scr], [kTp[slot]])
        B.dma(vtp[slot].ap[:, :, 0:NB * 65], vview[:, vh0:vh0 + 2, :], [v_scr], [vtp[slot]])

    zc = [0]
    accc = [0]

    def kb_list(gi):
        out = []
        if gi < NR:
            for kb in range(16 * gi + 15, -1, -1):
                if kb >= 16 * gi:
                    m = kb - 16 * gi
                    out.append((kb, max(0, m - 12) * 128, m))
                else:
                    out.append((kb, 0, None))
            return gi * 512, 512, qcol.ap[:, 0:512], out
        W = NR * 128
        for kb in range(16 * (NR - 1) + 11, -1, -1):
            c0 = ((kb - 11 + 15) // 16) * 128 if kb > 11 else 0
            out.append((kb, c0, 16 + kb))
        return NO, W, qcol.ap[:, 512:512 + W], out

    def mask_op(Pt, qc, cs, midx, strict):
        B.op("dve", lambda e: e.scalar_tensor_tensor(Pt.ap[:, cs], qc[:, cs], stab.ap[:, midx:midx + 1], Pt.ap[:, cs],
                                                     ALU.is_gt if strict else ALU.is_ge, ALU.mult),
             [Pt, qcol, stab], [Pt])

    def sb_units(h, slot):
        cg, po = h // 2, (h % 2) * 64
        ch = h % 2
        R32, Rb = R32s[ch], Rbs[ch]
        kT, vt = kTp[slot], vtp[slot]
        units = []
        ucount = 0
        for gi in range(NR + 1):
            col0, W, qc, kbs = kb_list(gi)
            acc = PS[4 + ch + 2 * (gi % 2)]
            for ui, (kb, c0, midx) in enumerate(kbs):
                first, last = ui == 0, ui == len(kbs) - 1
                zi = 2 * ucount + ch
                ucount += 1
                Z, ARG = PS[zi % 2], PS[2 + zi % 2]
                Et, SPt, PTt = E[zi % 4], SP[zi % 6], PT[zi % 8]
                Rcur, Rnext = Rb[(zi // 2) % 2], Rb[(zi // 2 + 1) % 2]
                kap = kT.ap[po:po + 64, kb * 128:(kb + 1) * 128]
                qap = qT_sb.ap[po:po + 64, cg, col0 + c0:col0 + W]
                cs = slice(c0, W)

                def s0(Z=Z, Et=Et, SPt=SPt, kap=kap, qap=qap, cs=cs, midx=midx, kT=kT, qc=qc):
                    B.mm(Z.ap[:, cs], kap, qap, True, True, [kT, qT_sb], [Z])
                    B.act(Et.ap[:, cs], Z.ap[:, cs], AF.Exp, [Z], [Et])
                    B.act(SPt.ap[:, cs], Et.ap[:, cs], AF.Ln, [Et], [SPt], bias=1.0)
                    if midx is not None:
                        mask_op(SPt, qc, cs, midx, True)

                def s1a(ARG=ARG, kap=kap, qap=qap, cs=cs, kT=kT):
                    B.mm(ARG.ap[:, cs], kap, qap, True, False, [kT, qT_sb], [ARG])

                def s1(ARG=ARG, SPt=SPt, PTt=PTt, cs=cs, midx=midx, first=first, last=last,
                       Rcur=Rcur, Rnext=Rnext, qc=qc, W=W):
                    B.mm(ARG.ap[:, cs], negtri.ap, SPt.ap[:, cs], False, first, [negtri, SPt], [ARG])
                    if not first:
                        B.mm(ARG.ap[:, cs], negones.ap, Rcur.ap[:, cs], False, True, [negones, Rcur], [ARG])
                    if first:
                        B.memset("pool", Rcur.ap, 0.0, [Rcur])
                        B.memset("pool", Rnext.ap, 0.0, [Rnext])
                    if not last:
                        B.tt("dve", Rnext.ap[:, cs], Rcur.ap[:, cs], SPt.ap[:, cs], ALU.add, [Rcur, SPt], [Rnext])
                    B.act(PTt.ap[:, cs], ARG.ap[:, cs], AF.Exp, [ARG], [PTt])
                    if midx is not None:
                        mask_op(PTt, qc, cs, midx, True)

                def s2(acc=acc, PTt=PTt, cs=cs, kb=kb, first=first, last=last, gi=gi, vt=vt, col0=col0, W=W):
                    B.mm(acc.ap[:, cs], vt.ap[:, h % 2, kb * 65:kb * 65 + 128], PTt.ap[:, cs], first, last, [vt, PTt], [acc], skip=True)
                    if last:
                        tq = tmpo[ch]
                        gsl = slice(col0, col0 + W)
                        B.copy("act", tq.ap[po:po + 64, 0:W], acc.ap[0:64, 0:W], [acc], [tq])
                        B.tt("dve", mixT.ap[po:po + 64, cg, gsl], tq.ap[po:po + 64, 0:W], mixT.ap[po:po + 64, cg, gsl],
                             ALU.mult, [tq, mixT], [mixT])

                units.append([s0, s1a, s1, s2])
        return units

    B.rd, B.bcs, B.ones_f, B.b4, B.lamw = rd, bcs, ones_f, [0], lamw
    softmax_norm = B.softmax_norm
    nextbank4 = B.nextbank4

    def df_units_pair(p, slot):
        kT, vt = kTp[slot], vtp[slot]
        cgk = p
        units = []
        scale = 32 ** -0.5
        ptc = [0]
        for gi in range(NR + 1):
            col0, W, qc, kbs = kb_list(gi)
            accs = [[PS[4 + 2 * hh + cm] for cm in range(2)] for hh in range(2)]
            gsl = slice(col0, col0 + W)
            for ui, (kb, c0, midx) in enumerate(kbs):
                first, last = ui == 0, ui == len(kbs) - 1
                cs = slice(c0, W)
                combos = []
                for hh in range(2):
                    for cm in range(2):
                        r0 = hh * 64 + cm * 32
                        combos.append((hh, cm, nextbank4(), PT[ptc[0] % 8], kT.ap[r0:r0 + 32, kb * 128:(kb + 1) * 128],
                                       qT_df.ap[r0:r0 + 32, cgk, col0 + c0:col0 + W], (r0, 0)))
                        ptc[0] += 1

                def s0(combos=combos, cs=cs, midx=midx, qc=qc):
                    for (hh, cm, Sb, PTt, kap, qap, tp) in combos:
                        B.op("pe", lambda e, Sb=Sb, kap=kap, qap=qap, tp=tp: e.matmul(Sb.ap[:, cs], lhsT=kap, rhs=qap, start=True,
                                                                                     stop=True, tile_position=tp), [kT, qT_df], [Sb])
                    for (hh, cm, Sb, PTt, kap, qap, tp) in combos:
                        B.act(PTt.ap[:, cs], Sb.ap[:, cs], AF.Exp, [Sb], [PTt], scale=scale)
                        if midx is not None:
                            mask_op(PTt, qc, cs, midx, False)

                def s1(combos=combos, cs=cs, kb=kb, first=first, last=last, accs=accs, gsl=gsl, W=W):
                    for (hh, cm, Sb, PTt, kap, qap, tp) in combos:
                        B.mm(accs[hh][cm].ap[:, cs], vt.ap[:, hh, kb * 65:kb * 65 + 128], PTt.ap[:, cs], first, last,
                             [vt, PTt], [accs[hh][cm]], skip=True)
                    if last:
                        for hh in range(2):
                            h = 2 * p + hh
                            po = hh * 64
                            a0, a1 = accs[hh]
                            bc0 = softmax_norm(a0, W)
                            B.tt("dve", ea.ap[0:64, 0:W], a0.ap[0:64, 0:W], bcs.ap[0:64, bc0], ALU.mult, [a0, bcs], [ea])
                            bc1 = softmax_norm(a1, W)
                            B.tt("dve", eb.ap[0:64, 0:W], a1.ap[0:64, 0:W], bcs.ap[0:64, bc1], ALU.mult, [a1, bcs], [eb])
                            B.op("dve", lambda e: e.scalar_tensor_tensor(ea.ap[0:64, 0:W], eb.ap[0:64, 0:W], nlcol.ap[0:64, 1:2],
                                                                         ea.ap[0:64, 0:W], ALU.mult, ALU.add), [ea, eb, nlcol], [ea])
                            B.tt("pool", eb.ap[0:64, 0:W], ea.ap[0:64, 0:W], ea.ap[0:64, 0:W], ALU.mult, [ea], [eb])
                            ss = nextbank4()
                            B.mm(ss.ap[0:64, 0:W], ones_f.ap[0:64, 0:64], eb.ap[0:64, 0:W], True, True, [ones_f, eb], [ss])
                            B.act(ec.ap[0:64, 0:W], ss.ap[0:64, 0:W], AF.Ln, [ss, B.eps_col], [ec], scale=1.0 / 64, bias=B.eps_col.ap[0:64, 0:1])
                            B.act(ec.ap[0:64, 0:W], ec.ap[0:64, 0:W], AF.Exp, [ec], [ec], scale=-0.5)
                            B.tt("dve", ea.ap[0:64, 0:W], ea.ap[0:64, 0:W], ec.ap[0:64, 0:W], ALU.mult, [ea, ec], [ea])
                            tq = tmpo[hh]
                            B.copy("act", tq.ap[po:po + 64, 0:W], ea.ap[0:64, 0:W], [ea], [tq])
                            mcg = 3 + p
                            B.op("dve", lambda e, po=po, tq=tq, mcg=mcg: e.scalar_tensor_tensor(
                                mixT.ap[po:po + 64, mcg, gsl], tq.ap[po:po + 64, 0:W], g1col.ap[po:po + 64, 0:1],
                                mixT.ap[po:po + 64, mcg, gsl], ALU.mult, ALU.mult), [tq, g1col, mixT], [mixT])

                units.append([s0, s1])
        return units

    pairs = [("sb", p) for p in range(3)] + [("df", p) for p in range(3)]
    load_pair(0, 0, 0)
    run_pipeline(B.mem_units(mkT, mv, mqT, mixT, PT, ea, tmpo[1], zc, accc, B.groups), 2)
    for pi, (kind, p) in enumerate(pairs):
        slot = pi % 2
        if pi + 1 < len(pairs):
            k2, p2 = pairs[pi + 1]
            load_pair(p2 if k2 == "sb" else 3 + p2, 2 * p2 if k2 == "sb" else 6 + 2 * p2, (pi + 1) % 2)
        if kind == "sb":
            run_pipeline_chains([sb_units(2 * p, slot), sb_units(2 * p + 1, slot)], [0, 1, 1, 2])
        else:
            run_pipeline(df_units_pair(p, slot), 2)

    B.sa.reset(mA)
    wo = B.tile([128, KC, D], BF16, "wo")
    wst2 = [B.tile([128, 1, 16], F32, "wst2_%d" % i) for i in range(2)]
    gbc = B.tile([128, D], F32, "gbc")
    bbc = B.tile([128, D], F32, "bbc")
    xr = [B.tile([128, D], F32, "xr%d" % i) for i in range(3)]
    vv = [B.tile([128, D], F32, "vv%d" % i) for i in range(3)]
    sts = [B.tile([128, 8], F32, "st%d" % i) for i in range(3)]
    fz2 = B.tile([128, 16], F32, "fz2")
    phaseC_tiles = [wo, gbc, bbc, fz2] + sts + wst2 + xr + vv
    B.op("pool", lambda e: e.memset(fz2.ap, 0.0), [], phaseB_tiles + phaseC_tiles)
    B.load_w(w_o, D, wo, wst2)
    B.dma(gbc.ap, ln_g.ap[0:1, :].partition_broadcast(128), [ln_g], [gbc])
    B.dma(bbc.ap, ln_b.ap[0:1, :].partition_broadcast(128), [ln_b], [bbc])
    outs = B.out_blocks(NEB, mixT, wo, x_own, xr, vv, sts, gbc, bbc, y_out, nextbank)
    B.l0_tiles = phaseB_tiles + phaseC_tiles + [qT_sb, qT_df, mqT, mixT, mkT, mv]
    B.eps_ready = True
    return outs


DBG = {}
QCOLS = [0, 4, 1, 5, 2, 6, 3, 7, 8, 8, 9, 9, 10, 10, 11, 11]


def _qpos(hq):
    if hq < 4:
        return hq, 0
    if hq < 8:
        return hq - 4, 64
    return 4 + hq - 8, 0


def l1_body(B):
    nc = B.nc
    S = B.S
    NB, NJ, NO, NR, NE, NEB = B.NB, B.NJ, B.NO, B.NR, B.NE, B.NEB
    PS = B.psum
    etiles = [(a, min(512, NE - a)) for a in range(0, NE, 512)]
    x1_ext = B.x1_ext_d
    pos_ext = B.dram["pos_ext"]
    memT = B.dram["memT"]
    w1_kv = B.din("w1_kv", [D, 960])
    w1_q = B.din("w1_q", [D, 2048])
    w1_g = B.din("w1_g", [D, 1024])
    w1_mkv = B.din("w1_mkv", [D, 512])
    w1_o = B.din("w1_o", [D, D])
    ln1_g = B.din("ln1_g", [1, D])
    ln1_b = B.din("ln1_b", [1, D])
    sinks_x = B.din("sinks_x", [1, 1536])
    invf1_d = B.din("invf1", [128, 1])
    coef1_d = B.din("coefp1", [128, 1])
    masks1_d = B.din("masks1", [128, 3 * 128], BF16)
    ident_d = B.din("ident", [128, 128])
    y_out = B.dout("y", [NO, D])

    pbc = [0]

    def nextbank():
        b = PS[pbc[0] % 8]
        pbc[0] += 1
        return b

    B.sa.reset(B.mark_consts)
    new_tiles = []

    def tl(shape, dt, name):
        t = B.tile(shape, dt, name)
        new_tiles.append(t)
        return t

    ident = tl([128, 128], F32, "ident")
    masks1 = tl([128, 3, 128], BF16, "masks1")
    invf1 = tl([128, 1], F32, "invf1")
    coef1 = tl([128, 1], F32, "coef1")
    qT1 = tl([128, 8, NO], BF16, "qT1")
    kT1 = tl([128, 2, NE], BF16, "kT1")
    V1 = tl([128, NEB, 3, 65], BF16, "V1")
    mqT1 = tl([128, 2, NO], BF16, "mqT1")
    mixT1 = tl([128, 8, NO], BF16, "mixT1")
    mkT1 = tl([128, 2, MEM_LEN], BF16, "mkT1")
    mv1 = tl([128, 2, 4, 65], BF16, "mv1")
    fz = tl([128, 16], F32, "fz1")
    mP = B.sa.mark()
    xT1 = tl([128, KC, NE], BF16, "xT1")
    xin = [tl([128, D], F32, "xin%d" % i) for i in range(2)]
    wb1 = tl([128, KC, 1024], BF16, "wb1")
    wst = [tl([128, 1, 16], F32, "wst1_%d" % i) for i in range(2)]
    posi = tl([128, 512], I32, "posi1")
    posf = tl([128, 512], F32, "posf1")
    t1 = tl([128, 512], F32, "t11")
    cosT = tl([128, 512], F32, "cosT1")
    sinS = tl([128, 512], F32, "sinS1")
    tmpa = tl([128, 512], F32, "tmpa1")
    tmpb = tl([128, 512], F32, "tmpb1")
    xs_m = tl([128, KC, MEM_LEN], F32, "xs_m")
    memb = tl([128, KC, MEM_LEN], BF16, "memb1")
    B.op("pool", lambda e: e.memset(fz.ap, 0.0), [], B.l0_tiles + new_tiles)
    B.dma(ident.ap, ident_d.ap[:, :], [ident_d], [ident])
    B.dma(masks1.ap.rearrange("p a b -> p (a b)"), masks1_d.ap[:, :], [masks1_d], [masks1])
    B.dma(invf1.ap, invf1_d.ap[:, :], [invf1_d], [invf1])
    B.dma(coef1.ap, coef1_d.ap[:, :], [coef1_d], [coef1])

    B.mem_kv(memT, w1_mkv, wb1, wst, xs_m, memb, mkT1, mv1, nextbank)
    B.memset("pool", V1.ap, 1.0, [V1])

    xc = [0]

    def transpose_block(e):
        xin_t = xin[xc[0] % 2]
        xc[0] += 1
        B.dma(xin_t.ap, x1_ext.ap[e * 128:(e + 1) * 128, :], [x1_ext], [xin_t])
        for half in range(2):
            ps = nextbank()
            for q in range(4):
                kc = half * 4 + q
                B.op("pe", lambda en, ps=ps, q=q, kc=kc: en.transpose(ps.ap[:, q * 128:(q + 1) * 128],
                                                                     xin_t.ap[:, kc * 128:(kc + 1) * 128], ident.ap),
                     [xin_t, ident], [ps])
            B.copy("act" if half == 0 else "dve", xT1.ap[:, half * 4:(half + 1) * 4, e * 128:(e + 1) * 128],
                   ps.ap[:, 0:512].rearrange("p (a b) -> p a b", a=4), [ps], [xT1])

    def xtile(a0, w):
        t = T(xT1.ap[:, :, a0:a0 + w], "xo_t")
        t.res = xT1.res
        return t

    B.load_w(w1_kv, 960, wb1, wst, 0)
    for (a0, w) in etiles:
        for e in range(a0 // 128, (a0 + w) // 128):
            transpose_block(e)
        xo_t = xtile(a0, w)
        B.rope_tables(pos_ext, a0, w, posi, posf, t1, cosT, sinS, invf1, coef1)
        for cg in range(2):
            psK = nextbank()
            B.proj_fm(psK, wb1, cg * 128, xo_t, w)
            psP = nextbank()
            B.proj_fm(psP, wb1, 256 + cg * 128, xo_t, w)
            B.rope_evac(psK, psP, kT1.ap[:, cg, a0:a0 + w], w, cosT, sinS, tmpa, tmpb, kT1)
        for blk in range(w // 128):
            e = a0 // 128 + blk
            ps = nextbank()
            for kc in range(KC):
                B.mm(ps.ap[:, 0:192], xo_t.ap[:, kc, blk * 128:(blk + 1) * 128], wb1.ap[:, kc, 512:704],
                     kc == 0, kc == KC - 1, [xo_t, wb1], [ps])
            B.copy("act", V1.ap[:, e, :, 0:64], ps.ap[:, 0:192].rearrange("p (h d) -> p h d", h=3), [ps], [V1])
        if a0 < NO:
            for cg in range(2):
                ps = nextbank()
                B.proj_fm(ps, wb1, 704 + cg * 128, xo_t, w)
                B.copy("act", mqT1.ap[:, cg, a0:a0 + w], ps.ap[:, 0:w], [ps], [mqT1])
    for rnd in range(3):
        if rnd < 2:
            B.load_w(T(w1_q.ap[:, rnd * 1024:(rnd + 1) * 1024], "w1q"), 1024, wb1, wst, 0)
        else:
            B.load_w(w1_g, 1024, wb1, wst, 0)
        for (a0, w) in etiles:
            if a0 >= NO:
                continue
            osl = slice(a0, a0 + w)
            xo_t = xtile(a0, w)
            if rnd < 2:
                B.rope_tables(pos_ext, a0, w, posi, posf, t1, cosT, sinS, invf1, coef1)
                for cg in range(4):
                    psK = nextbank()
                    B.proj_fm(psK, wb1, cg * 128, xo_t, w)
                    psP = nextbank()
                    B.proj_fm(psP, wb1, 512 + cg * 128, xo_t, w)
                    B.rope_evac(psK, psP, qT1.ap[:, rnd * 4 + cg, osl], w, cosT, sinS, tmpa, tmpb, qT1)
            else:
                for cg in range(8):
                    ps = nextbank()
                    B.proj_fm(ps, wb1, cg * 128, xo_t, w)
                    B.act(mixT1.ap[:, cg, osl], ps.ap[:, 0:w], AF.Silu, [ps], [mixT1])

    proj_tiles = [xT1] + xin + [wb1] + wst + [posi, posf, t1, cosT, sinS, tmpa, tmpb, xs_m, memb]
    B.sa.reset(mP)
    esink = B.tile([128, 1536], F32, "esink")
    esraw = B.tile([128, 1536], F32, "esraw")
    PT1 = [B.tile([128, 512], BF16, "PT1_%d" % i) for i in range(4)]
    rd = B.tile([128, 1024], F32, "rd1")
    bcs = B.tile([128, 1024], F32, "bcs1")
    eas = [B.tile([128, 512], F32, "ea1_%d" % i) for i in range(2)]
    ea = eas[0]
    tq = [B.tile([128, 512], F32, "tq1_%d" % i) for i in range(3)]
    wo1 = B.tile([128, KC, D], BF16, "wo1")
    wst2 = [B.tile([128, 1, 16], F32, "wst1b_%d" % i) for i in range(2)]
    gbc = B.tile([128, D], F32, "gbc1")
    bbc = B.tile([128, D], F32, "bbc1")
    xr = [B.tile([128, D], F32, "xr1_%d" % i) for i in range(3)]
    vv = [B.tile([128, D], F32, "vv1_%d" % i) for i in range(3)]
    sts = [B.tile([128, 8], F32, "st1_%d" % i) for i in range(3)]
    att_tiles = [esink, esraw] + PT1 + [rd, bcs] + eas + tq + [wo1] + wst2 + [gbc, bbc] + xr + vv + sts
    B.op("pool", lambda e: e.memset(fz.ap, 0.0), [], proj_tiles + att_tiles + [fz])
    B.rd, B.bcs, B.b4 = rd, bcs, [0]
    B.dma(esraw.ap, sinks_x.ap[0:1, :].partition_broadcast(128), [sinks_x], [esraw])
    B.act(esink.ap, esraw.ap, AF.Exp, [esraw], [esink])
    B.load_w(w1_o, D, wo1, wst2)
    B.dma(gbc.ap, ln1_g.ap[0:1, :].partition_broadcast(128), [ln1_g], [gbc])
    B.dma(bbc.ap, ln1_b.ap[0:1, :].partition_broadcast(128), [ln1_b], [bbc])

    zc, accc = [0], [0]
    run_pipeline(B.mem_units(mkT1, mv1, mqT1, mixT1, PT1, ea, tq[2], zc, accc, [(i * 512, 512) for i in range(NR)]), 2)
    units = []
    for j in range(NJ):
        i_run, r = j // 4, j % 4
        e_prev = j - 1 if r > 0 else NJ + i_run
        jsl = slice(j * 128, (j + 1) * 128)
        for kvh in range(3):
            acc = PS[4 + accc[0] % 4]
            accc[0] += 1
            for bi, e_k in enumerate((e_prev, j)):
                zi = zc[0]
                zc[0] += 1
                Sb = B.nextbank4()
                PTt = PT1[zi % 4]
                mi = 0 if bi == 1 else (2 if j == 0 else 1)
                ksl = slice(e_k * 128, (e_k + 1) * 128)

                def s0(Sb=Sb, PTt=PTt, kvh=kvh, jsl=jsl, ksl=ksl, mi=mi):
                    for g in range(4):
                        hq = kvh * 4 + g
                        cgq, po = _qpos(hq)
                        cgk = 0 if kvh < 2 else 1
                        B.mm(Sb.ap[:, g * 128:(g + 1) * 128], kT1.ap[po:po + 64, cgk, ksl], qT1.ap[po:po + 64, cgq, jsl],
                             True, True, [kT1, qT1], [Sb])
                    B.act(PTt.ap[:, 0:512], Sb.ap[:, 0:512], AF.Exp, [Sb], [PTt], scale=0.125)
                    for g in range(4):
                        B.tt("dve", PTt.ap[:, g * 128:(g + 1) * 128], PTt.ap[:, g * 128:(g + 1) * 128],
                             masks1.ap[:, mi, :], ALU.mult, [PTt, masks1], [PTt])

                hold = {}

                def s1(acc=acc, PTt=PTt, kvh=kvh, j=j, jsl=jsl, bi=bi, e_k=e_k, hold=hold):
                    B.mm(acc.ap[0:65, 0:512], V1.ap[:, e_k, kvh, 0:65], PTt.ap[:, 0:512], bi == 0, bi == 1, [V1, PTt], [acc], skip=True)
                    if bi == 1:
                        hold["p2"] = B.softmax_norm(acc, 512, add_row=(esink.ap[64:65, kvh * 512:(kvh + 1) * 512], esink), split=True)

                def s2(acc=acc, kvh=kvh, jsl=jsl, hold=hold):
                        bcol = hold["p2"]()
                        ea = eas[accc[0] % 2]
                        B.tt("dve", ea.ap[0:64, :], acc.ap[0:64, 0:512], bcs.ap[0:64, bcol], ALU.mult, [acc, bcs], [ea])
                        tqt = tq[accc[0] % 2]
                        accc[0] += 1
                        for g in range(4):
                            hq = kvh * 4 + g
                            mcg, po = hq // 2, (hq % 2) * 64
                            gs = slice(g * 128, (g + 1) * 128)
                            B.copy("act", tqt.ap[po:po + 64, gs], ea.ap[0:64, gs], [ea], [tqt])
                            B.tt("dve", mixT1.ap[po:po + 64, mcg, jsl], tqt.ap[po:po + 64, gs],
                                 mixT1.ap[po:po + 64, mcg, jsl], ALU.mult, [tqt, mixT1], [mixT1])

                units.append([s0, s1, s2 if bi == 1 else None])
    run_pipeline(units, 3)

    return B.out_blocks(NJ, mixT1, wo1, x1_ext, xr, vv, sts, gbc, bbc, y_out, nextbank)


def layer_norm_store(B, vv_t, scratch, st, gbc, bbc, y_out, j):
    B.op("dve", lambda e: e.tensor_reduce(st.ap[:, 0:1], vv_t.ap, AX.X, ALU.add), [vv_t], [st])
    B.ts("dve", st.ap[:, 1:2], st.ap[:, 0:1], -1.0 / D, None, ALU.mult, None, [st], [st])
    B.op("act", lambda e: e.activation(out=scratch.ap, in_=vv_t.ap, func=AF.Square, bias=st.ap[:, 1:2], scale=1.0,
                                       accum_out=st.ap[:, 2:3]), [vv_t, st], [scratch, st])
    B.act(st.ap[:, 3:4], st.ap[:, 2:3], AF.Ln, [st, B.eps_col], [st], scale=1.0 / D, bias=B.eps_col.ap[:, 0:1])
    B.act(st.ap[:, 3:4], st.ap[:, 3:4], AF.Exp, [st], [st], scale=-0.5)
    B.ts("dve", vv_t.ap, vv_t.ap, st.ap[:, 1:2], st.ap[:, 3:4], ALU.add, ALU.mult, [vv_t, st], [vv_t])
    B.tt("dve", vv_t.ap, vv_t.ap, gbc.ap, ALU.mult, [vv_t, gbc], [vv_t])
    B.tt("pool", vv_t.ap, vv_t.ap, bbc.ap, ALU.add, [vv_t, bbc], [vv_t])
    return B.dma(y_out.ap[j * 128:(j + 1) * 128, :], vv_t.ap, [vv_t], [y_out])


def _own_idx(S, c):
    NR = S // 2048
    return np.concatenate([np.arange((16 * i + 4 * c) * 128, (16 * i + 4 * c + 4) * 128) for i in range(NR)])


def _halo_idx(S, c):
    NR = S // 2048
    out = []
    for i in range(NR):
        g = 16 * i + 4 * c - 1
        out.append(np.arange(g * 128, (g + 1) * 128) if g >= 0 else np.full(128, -1))
    return np.concatenate(out)


def _mask_dram(c):
    m = _masks(c)
    return np.ascontiguousarray(np.transpose(m, (1, 0, 2)).reshape(128, 9 * 128))


def _mask_tables(c):
    col = np.arange(512, dtype=np.float32)
    qcol = np.zeros((128, 1024), np.float32)
    qcol[:, 0:512] = col[None, :]
    qcol[:, 512:1024] = (col + 1920.0 * np.floor(col / 128.0))[None, :]
    k = np.arange(128, dtype=np.float32)[:, None]
    stab = np.zeros((128, 80), np.float32)
    stab[:, 0:16] = k + 128.0 * np.arange(16, dtype=np.float32)[None, :] - 512.0 * c
    stab[:, 16:80] = k + 128.0 * np.arange(64, dtype=np.float32)[None, :] + 128.0 - 512.0 * c
    return qcol, stab


def _masks1(c):
    k = np.arange(128)[:, None]
    q = np.arange(128)[None, :]
    m = np.zeros((3, 128, 128), np.float32)
    m[0] = (k <= q)
    m[1] = (k > q)
    m[2] = (k > q) if c > 0 else 0.0
    return np.ascontiguousarray(np.transpose(m, (1, 0, 2)).reshape(128, 3 * 128)).astype(ml_dtypes.bfloat16)


def l1_weights(w_in, w_memkv, sinks, w_out, ln_g, ln_b):
    cq, ck, cv = w_in[:, 0:768], w_in[:, 768:960], w_in[:, 960:1152]
    mq, gate = w_in[:, 1152:1408], w_in[:, 1408:2432]
    kd = np.concatenate([ck[:, 0:64], ck[:, 64:128], ck[:, 128:192], ck[:, 128:192]], axis=1)
    pk = _partner_perm(256, 64, 16)
    qr = np.concatenate([cq[:, h * 64:(h + 1) * 64] for h in QCOLS], axis=1)
    pq = _partner_perm(512, 64, 16)
    q0, q1 = qr[:, 0:512], qr[:, 512:1024]
    invf1, coef1 = _host_consts(1)
    return {
        "w1_kv": np.ascontiguousarray(np.concatenate([kd, kd[:, pk], cv, mq], axis=1)),
        "w1_q": np.ascontiguousarray(np.concatenate([q0, q0[:, pq], q1, q1[:, pq]], axis=1)),
        "w1_g": np.ascontiguousarray(gate),
        "w1_mkv": np.ascontiguousarray(w_memkv),
        "w1_o": np.ascontiguousarray(w_out),
        "ln1_g": np.ascontiguousarray(ln_g[None, :]),
        "ln1_b": np.ascontiguousarray(ln_b[None, :]),
        "sinks_x": np.ascontiguousarray(np.repeat(sinks, 128)[None, :]),
        "invf1": invf1, "coefp1": coef1,
        "ident": np.eye(128, dtype=np.float32),
    }


def prep_fused(inp):
    f = lambda a: np.asarray(a)
    x, mem, positions = f(inp["x"]), f(inp["mem"]), f(inp["positions"])
    S = x.shape[1]
    w_in = f(inp["w_in_even"])[0]
    sbq, sbk, sbv = w_in[:, 0:384], w_in[:, 384:768], w_in[:, 768:1152]
    dfq, dfk, dfv = w_in[:, 1152:1536], w_in[:, 1536:1920], w_in[:, 1920:2304]
    mq, gate = w_in[:, 2304:2560], w_in[:, 2560:3584]
    perm = _partner_perm(384, 32, 8)
    invf, coefp = _host_consts(0)
    dsub = f(inp["diff_subln_even"])[0]
    shared = {
        "w_k": np.ascontiguousarray(np.concatenate([sbk, dfk, dfk[:, perm]], axis=1)),
        "w_v": np.ascontiguousarray(np.concatenate([sbv, dfv], axis=1)),
        "w_q": np.ascontiguousarray(np.concatenate([sbq, dfq, dfq[:, perm], mq], axis=1)),
        "w_g": np.ascontiguousarray(gate),
        "w_mkv": np.ascontiguousarray(f(inp["w_memkv_even"])[0]),
        "w_o": np.ascontiguousarray(f(inp["w_out_even"])[0]),
        "ln_g": np.ascontiguousarray(f(inp["ln_g_even"])[0][None, :]),
        "ln_b": np.ascontiguousarray(f(inp["ln_b_even"])[0][None, :]),
        "dlam": np.ascontiguousarray(f(inp["diff_lambda_even"])[0].reshape(1, 128)),
        "subln": np.ascontiguousarray(np.concatenate([dsub, dsub])[:, None]),
        "invf": invf, "coefp": coefp,
    }
    shared.update(l1_weights(f(inp["w_in_odd"])[0], f(inp["w_memkv_odd"])[0], f(inp["sinks_odd"])[0], f(inp["w_out_odd"])[0],
                             f(inp["ln_g_odd"])[0], f(inp["ln_b_odd"])[0]))
    xT = [np.ascontiguousarray(x[b].T) for b in range(x.shape[0])]
    maps = []
    for core in range(8):
        b, c = core // 4, core % 4
        own = _own_idx(S, c)
        hidx = _halo_idx(S, c)
        ext = np.concatenate([own, np.maximum(hidx, 0)])
        qcol, stab = _mask_tables(c)
        m = dict(shared)
        m.update({
            "xT_all": xT[b],
            "xT_ext": np.ascontiguousarray(x[b][ext].T),
            "x_ext": np.ascontiguousarray(x[b][ext]),
            "pos_all": np.ascontiguousarray(positions[b][None, :]).astype(np.int32),
            "pos_ext": np.ascontiguousarray(positions[b][ext][None, :]).astype(np.int32),
            "memT": np.ascontiguousarray(mem[b].T),
            "masks": _mask_dram(c),
            "qcol": qcol, "stab": stab,
            "masks1": _masks1(c),
        })
        maps.append(m)
    return maps


def gather_own(results, key, S, nb=2):
    out = np.zeros((nb, S, D), np.float32)
    for core in range(8):
        b, c = core // 4, core % 4
        out[b, _own_idx(S, c)] = results[core][key]
    return out


def build_fused(S, lambda_init):
    B = Builder(S, 0)
    l0_body(B, lambda_init, True)
    outs = l1_body(B)
    B.sch.emit(final_wait_ops=outs)
    return B


LAMBDA_INIT0 = 0.8 - 0.6 * math.exp(-0.3 * 0)


def kernel(**inputs):
    S = np.asarray(inputs["x"]).shape[1]
    B = build_fused(S, LAMBDA_INIT0)
    maps = prep_fused(inputs)
    res = run_bass_kernel_spmd(B.nc, maps, core_ids=list(range(8)))
    return gather_own(res.results, "y", S)
```
